# Optimizing a Trainium2 kernel written in Bass

```python
import jax
import jax.numpy as jnp
from jax import lax
import numpy as np

D_MODEL = 1024
BATCH = 2
SEQ = 8192
DEPTH = 2

N_MIXERS = 2
N_CONV_LAYERS = (DEPTH + N_MIXERS - 1) // N_MIXERS
N_NSA_LAYERS = DEPTH // N_MIXERS

D_FF = 2816
FFN_RESIDUAL_WEIGHT = 0.5
CONV_WIDTH = 31
NSA_HEADS = 16
NSA_KV_GROUPS = 4
NSA_REP = NSA_HEADS // NSA_KV_GROUPS
HEAD_DIM = D_MODEL // NSA_HEADS
ROT_DIM = HEAD_DIM // 4
ROPE_THETA = 500000.0
CMP_LEN = 32
CMP_STRIDE = 16
CMP_HIDDEN = 4 * HEAD_DIM
SEL_BLOCK = 64
N_SEL = 16
WINDOW = 512
Q_BLOCK = 128
N_GATES = 3
Q_WIDTH = NSA_HEADS * HEAD_DIM
KV_WIDTH = NSA_KV_GROUPS * HEAD_DIM
NSA_IN_WIDTH = Q_WIDTH + 6 * KV_WIDTH + NSA_HEADS * N_GATES
RMS_EPS = 1e-6
LN_EPS = 1e-5
NEG_INF = -1e30

kernel_name = 'hybrid_conformer_conv_nsa_macaron'


def rms_norm(x, gain):
    xf = x.astype(jnp.float32)
    y = xf * lax.rsqrt(jnp.mean(xf * xf, axis=-1, keepdims=True) + RMS_EPS)
    return (y * gain.astype(jnp.float32)).astype(x.dtype)


def layer_norm(x, gain, bias):
    xf = x.astype(jnp.float32)
    mu = jnp.mean(xf, axis=-1, keepdims=True)
    var = jnp.mean(jnp.square(xf - mu), axis=-1, keepdims=True)
    y = (xf - mu) * lax.rsqrt(var + LN_EPS)
    return (y * gain.astype(jnp.float32) + bias.astype(jnp.float32)).astype(x.dtype)


def masked_softmax(s, mask):
    s = jnp.where(mask, s.astype(jnp.float32), NEG_INF)
    p = jnp.exp(s - jnp.max(s, axis=-1, keepdims=True)) * mask
    return p / jnp.maximum(jnp.sum(p, axis=-1, keepdims=True), 1e-30)


def half_step_ffn(x, g_pre, g_post, w_gate, w_up, w_down):
    h = rms_norm(x, g_pre)
    h = (jax.nn.silu(h @ w_gate) * (h @ w_up)) @ w_down
    return x + FFN_RESIDUAL_WEIGHT * rms_norm(h, g_post)


def conformer_conv(x, w_pw1, b_pw1, w_dw, b_dw, ln_g, ln_b, w_pw2, b_pw2):
    h = x @ w_pw1 + b_pw1
    a, g = jnp.split(h, 2, axis=-1)
    h = a * jax.nn.sigmoid(g)
    h = lax.conv_general_dilated(
        h, w_dw[:, None, :], window_strides=(1,), padding=[(CONV_WIDTH - 1, 0)],
        dimension_numbers=('NWC', 'WIO', 'NWC'), feature_group_count=h.shape[-1]) + b_dw
    h = jax.nn.silu(layer_norm(h, ln_g, ln_b))
    return h @ w_pw2 + b_pw2


def rope_tables(seq_len, dtype):
    pos = jnp.arange(seq_len, dtype=jnp.float32)
    inv_freq = ROPE_THETA ** (-jnp.arange(0, ROT_DIM, 2, dtype=jnp.float32) / ROT_DIM)
    ang = pos[:, None] * inv_freq[None, :]
    return jnp.cos(ang).astype(dtype), jnp.sin(ang).astype(dtype)


def partial_rope(x, cos, sin):
    xr, xp = x[..., :ROT_DIM], x[..., ROT_DIM:]
    x1, x2 = xr[..., :ROT_DIM // 2], xr[..., ROT_DIM // 2:]
    return jnp.concatenate([x1 * cos - x2 * sin, x2 * cos + x1 * sin, xp], axis=-1)


def compress_tokens(kv, pos_emb, w1, w2):
    b, g, s, dh = kv.shape
    nc = (s - CMP_LEN) // CMP_STRIDE + 1
    idx = jnp.arange(nc)[:, None] * CMP_STRIDE + jnp.arange(CMP_LEN)[None, :]
    blocks = kv[:, :, idx] + pos_emb
    flat = blocks.reshape(b, g, nc, CMP_LEN * dh)
    return jax.nn.silu(flat @ w1) @ w2


def native_sparse_attention(x, w_in, b_gate, cmp_pos, cmp_w1, cmp_w2, w_out):
    b, s, _ = x.shape
    G, R, dh = NSA_KV_GROUPS, NSA_REP, HEAD_DIM
    proj = x @ w_in
    q = proj[..., :Q_WIDTH].reshape(b, s, G, R, dh).transpose(0, 2, 3, 1, 4)
    kv = proj[..., Q_WIDTH:Q_WIDTH + 6 * KV_WIDTH].reshape(b, s, 6, G, dh).transpose(2, 0, 3, 1, 4)
    gates = jax.nn.sigmoid(proj[..., Q_WIDTH + 6 * KV_WIDTH:] + b_gate)
    gates = gates.reshape(b, s, G, R, N_GATES).transpose(0, 2, 3, 1, 4)

    cos, sin = rope_tables(s, x.dtype)
    q_rot = partial_rope(q, cos, sin)
    k_sel = partial_rope(kv[2], cos, sin)
    v_sel = kv[3]
    k_win = partial_rope(kv[4], cos, sin)
    v_win = kv[5]
    k_cmp = compress_tokens(kv[0], cmp_pos[0], cmp_w1[0], cmp_w2[0])
    v_cmp = compress_tokens(kv[1], cmp_pos[1], cmp_w1[1], cmp_w2[1])

    nc = k_cmp.shape[2]
    nb = s // SEL_BLOCK
    nsel = min(N_SEL, nb)
    cmp_start = jnp.arange(nc) * CMP_STRIDE
    cmp_end = cmp_start + CMP_LEN - 1
    blk_ids = jnp.arange(nb)
    sel_start = blk_ids * SEL_BLOCK
    overlap = ((cmp_start[:, None] < sel_start[None, :] + SEL_BLOCK)
               & (cmp_start[:, None] + CMP_LEN > sel_start[None, :])).astype(jnp.float32)
    ks_blocks = k_sel.reshape(b, G, nb, SEL_BLOCK, dh)
    vs_blocks = v_sel.reshape(b, G, nb, SEL_BLOCK, dh)
    pad = ((0, 0), (0, 0), (WINDOW, 0), (0, 0))
    k_win_pad = jnp.pad(k_win, pad)
    v_win_pad = jnp.pad(v_win, pad)
    scale = HEAD_DIM ** -0.5
    b_idx = jnp.arange(b)[:, None, None, None]
    g_idx = jnp.arange(G)[None, :, None, None]

    def query_block(i):
        t0 = i * Q_BLOCK
        pos = t0 + jnp.arange(Q_BLOCK)
        qb = lax.dynamic_slice_in_dim(q, t0, Q_BLOCK, axis=3)
        qrb = lax.dynamic_slice_in_dim(q_rot, t0, Q_BLOCK, axis=3)
        gb = lax.dynamic_slice_in_dim(gates, t0, Q_BLOCK, axis=3)

        s_c = jnp.einsum('bgrqd,bgcd->bgrqc', qb, k_cmp) * scale
        p_c = masked_softmax(s_c, cmp_end[None, :] <= pos[:, None])
        o_cmp = jnp.einsum('bgrqc,bgcd->bgrqd', p_c.astype(v_cmp.dtype), v_cmp)

        imp = jnp.einsum('bgrqc,cn->bgqn', p_c, overlap)
        cur = pos // SEL_BLOCK
        forced = (blk_ids[None, :] == 0) | (blk_ids[None, :] == cur[:, None]) | (blk_ids[None, :] == cur[:, None] - 1)
        causal_blk = sel_start[None, :] <= pos[:, None]
        imp = jnp.where(forced, 1e9, jnp.where(causal_blk, imp, -1e9))
        _, sel = lax.top_k(imp, nsel)

        k_g = ks_blocks[b_idx, g_idx, sel].reshape(b, G, Q_BLOCK, nsel * SEL_BLOCK, dh)
        v_g = vs_blocks[b_idx, g_idx, sel].reshape(b, G, Q_BLOCK, nsel * SEL_BLOCK, dh)
        kpos = (sel[..., None] * SEL_BLOCK + jnp.arange(SEL_BLOCK)).reshape(b, G, Q_BLOCK, nsel * SEL_BLOCK)
        s_s = jnp.einsum('bgrqd,bgqkd->bgrqk', qrb, k_g) * scale
        p_s = masked_softmax(s_s, (kpos <= pos[None, None, :, None])[:, :, None])
        o_sel = jnp.einsum('bgrqk,bgqkd->bgrqd', p_s.astype(v_g.dtype), v_g)

        kw = lax.dynamic_slice_in_dim(k_win_pad, t0, Q_BLOCK + WINDOW, axis=2)
        vw = lax.dynamic_slice_in_dim(v_win_pad, t0, Q_BLOCK + WINDOW, axis=2)
        wpos = t0 - WINDOW + jnp.arange(Q_BLOCK + WINDOW)
        wmask = (wpos[None, :] <= pos[:, None]) & (wpos[None, :] > pos[:, None] - WINDOW) & (wpos[None, :] >= 0)
        s_w = jnp.einsum('bgrqd,bgkd->bgrqk', qrb, kw) * scale
        p_w = masked_softmax(s_w, wmask)
        o_win = jnp.einsum('bgrqk,bgkd->bgrqd', p_w.astype(vw.dtype), vw)

        return gb[..., 0:1] * o_cmp + gb[..., 1:2] * o_sel + gb[..., 2:3] * o_win

    out = lax.map(query_block, jnp.arange(s // Q_BLOCK))
    out = out.transpose(1, 0, 4, 2, 3, 5).reshape(b, s, Q_WIDTH)
    return out @ w_out


def setup_inputs(seed: int = 0) -> dict:
    key = jax.random.key(seed)
    ks = jax.random.split(key, 22)
    nrm = jax.random.normal
    D, F, NC, NN = D_MODEL, D_FF, N_CONV_LAYERS, N_NSA_LAYERS
    return {
        'x': nrm(ks[0], (BATCH, SEQ, D), jnp.float32),
        'mix_norm_pre': 1.0 + 0.05 * nrm(ks[1], (DEPTH, D), jnp.float32),
        'mix_norm_post': 1.0 + 0.05 * nrm(ks[2], (DEPTH, D), jnp.float32),
        'ffn_norm_pre': 1.0 + 0.05 * nrm(ks[3], (DEPTH, 2, D), jnp.float32),
        'ffn_norm_post': 1.0 + 0.05 * nrm(ks[4], (DEPTH, 2, D), jnp.float32),
        'ffn_w_gate': nrm(ks[5], (DEPTH, 2, D, F), jnp.float32) * D ** -0.5,
        'ffn_w_up': nrm(ks[6], (DEPTH, 2, D, F), jnp.float32) * D ** -0.5,
        'ffn_w_down': nrm(ks[7], (DEPTH, 2, F, D), jnp.float32) * F ** -0.5,
        'conv_w_pw1': nrm(ks[8], (NC, D, 2 * D), jnp.float32) * D ** -0.5,
        'conv_b_pw1': 0.02 * nrm(ks[9], (NC, 2 * D), jnp.float32),
        'conv_w_dw': nrm(ks[10], (NC, CONV_WIDTH, D), jnp.float32) * CONV_WIDTH ** -0.5,
        'conv_b_dw': 0.02 * nrm(ks[11], (NC, D), jnp.float32),
        'conv_ln_g': 1.0 + 0.05 * nrm(ks[12], (NC, D), jnp.float32),
        'conv_ln_b': 0.02 * nrm(ks[13], (NC, D), jnp.float32),
        'conv_w_pw2': nrm(ks[14], (NC, D, D), jnp.float32) * D ** -0.5,
        'conv_b_pw2': 0.02 * nrm(ks[15], (NC, D), jnp.float32),
        'nsa_w_in': nrm(ks[16], (NN, D, NSA_IN_WIDTH), jnp.float32) * D ** -0.5,
        'nsa_b_gate': 0.02 * nrm(ks[17], (NN, NSA_HEADS * N_GATES), jnp.float32),
        'nsa_cmp_pos': 0.1 * nrm(ks[18], (NN, 2, CMP_LEN, HEAD_DIM), jnp.float32),
        'nsa_cmp_w1': nrm(ks[19], (NN, 2, CMP_LEN * HEAD_DIM, CMP_HIDDEN), jnp.float32) * (CMP_LEN * HEAD_DIM) ** -0.5,
        'nsa_cmp_w2': nrm(ks[20], (NN, 2, CMP_HIDDEN, HEAD_DIM), jnp.float32) * CMP_HIDDEN ** -0.5,
        'nsa_w_out': nrm(ks[21], (NN, Q_WIDTH, D), jnp.float32) * Q_WIDTH ** -0.5,
    }


def reference(x, mix_norm_pre, mix_norm_post, ffn_norm_pre, ffn_norm_post, ffn_w_gate, ffn_w_up, ffn_w_down,
              conv_w_pw1, conv_b_pw1, conv_w_dw, conv_b_dw, conv_ln_g, conv_ln_b, conv_w_pw2, conv_b_pw2,
              nsa_w_in, nsa_b_gate, nsa_cmp_pos, nsa_cmp_w1, nsa_cmp_w2, nsa_w_out):
    for i in range(DEPTH):
        x = half_step_ffn(x, ffn_norm_pre[i, 0], ffn_norm_post[i, 0], ffn_w_gate[i, 0], ffn_w_up[i, 0], ffn_w_down[i, 0])
        h = rms_norm(x, mix_norm_pre[i])
        j = i // N_MIXERS
        if i % N_MIXERS == 0:
            h = conformer_conv(h, conv_w_pw1[j], conv_b_pw1[j], conv_w_dw[j], conv_b_dw[j],
                               conv_ln_g[j], conv_ln_b[j], conv_w_pw2[j], conv_b_pw2[j])
        else:
            h = native_sparse_attention(h, nsa_w_in[j], nsa_b_gate[j], nsa_cmp_pos[j],
                                        nsa_cmp_w1[j], nsa_cmp_w2[j], nsa_w_out[j])
        x = x + rms_norm(h, mix_norm_post[i])
        x = half_step_ffn(x, ffn_norm_pre[i, 1], ffn_norm_post[i, 1], ffn_w_gate[i, 1], ffn_w_up[i, 1], ffn_w_down[i, 1])
    return x
```

```python
import contextlib
import numpy as np
import ml_dtypes
import concourse.bass as bass
import concourse.mybir as mybir
from concourse.bass_utils import run_bass_kernel_spmd

F32 = mybir.dt.float32
BF16 = mybir.dt.bfloat16
AF = mybir.ActivationFunctionType
ALU = mybir.AluOpType

ENGS = ["tensor", "vector", "scalar", "gpsimd", "sync"]

D = 1024
DFF = 2816
NFC = 22
S = 8192
TC = 2048
HALO = 128
CW = 31
RMS_EPS = 1e-6
LN_EPS = 1e-5
NEG = -30000.0
NIN = 2608


class Buf:
    __slots__ = ("name", "last_w", "readers")

    def __init__(self, name=""):
        self.name = name
        self.last_w = None
        self.readers = []


class _Rec:
    def __init__(self):
        self.call = None

    def __getattr__(self, name):
        def f(*a, **k):
            self.call = (name, a, k)
            return None
        return f


class Prog:
    def __init__(self, nc):
        self.nc = nc
        self.ops = {e: [] for e in ENGS}
        self.cnt = {e: 0 for e in ENGS}
        self.seen = {e: {} for e in ENGS}
        self.dcnt = {}
        self.pending = {e: {} for e in ENGS}

    def barrier(self):
        snap = dict(self.cnt)
        snap.update(self.dcnt)
        for e in ENGS:
            for k, v in snap.items():
                if v > 0 and not (e == "tensor" and k == "tensor"):
                    if self.pending[e].get(k, 0) < v:
                        self.pending[e][k] = v

    def op(self, eng, fn, reads=(), writes=(), dma=None, sig=True):
        rec = _Rec()
        fn(rec)
        name_, args_, kw_ = rec.call
        fn = lambda e: getattr(e, name_)(*args_, **kw_)
        waits = dict(self.pending[eng])
        self.pending[eng] = {}

        def need(ev, war=False):
            if ev is None:
                return
            k, v = ev
            if k == eng:
                if eng == "tensor" or war:
                    return
            if waits.get(k, 0) < v:
                waits[k] = v

        for b in reads:
            need(b.last_w)
        for b in writes:
            need(b.last_w)
            for r in b.readers:
                need(r, war=True)
        w = []
        for k, v in waits.items():
            if self.seen[eng].get(k, 0) < v:
                self.seen[eng][k] = v
                w.append((k, v))
        if dma is not None:
            prev = self.dcnt.get(dma, 0)
            if prev > 0 and self.seen[eng].get(dma, 0) < prev:
                self.seen[eng][dma] = prev
                w.append((dma, prev))
            self.dcnt[dma] = self.dcnt.get(dma, 0) + 16
            ev = (dma, self.dcnt[dma])
            inc = (dma, 16)
        elif eng == "tensor" and not sig:
            ev = (eng, self.cnt[eng] + 1)
            inc = None
        else:
            self.cnt[eng] += 1
            ev = (eng, self.cnt[eng])
            inc = (eng, 1)
        self.ops[eng].append((fn, w, inc))
        for b in reads:
            b.readers.append(ev)
            if len(b.readers) > 64:
                best = {}
                for k, v in b.readers:
                    if best.get(k, 0) < v:
                        best[k] = v
                b.readers = list(best.items())
        for b in writes:
            b.last_w = ev
            b.readers = []
        return ev

    def mm(self, out, lhsT, rhs, start, stop, reads=(), writes=(), sig=None, **kw):
        if sig is None:
            sig = stop
        return self.op("tensor", lambda e: e.matmul(out, lhsT, rhs, start=start, stop=stop, **kw),
                       reads=reads, writes=writes, sig=sig)

    def dma(self, eng, out, in_, sem, reads=(), writes=(), **kw):
        return self.op(eng, lambda e: e.dma_start(out=out, in_=in_, **kw), reads=reads, writes=writes, dma=sem)

    def V(self, fn, reads=(), writes=()):
        return self.op("vector", fn, reads, writes)

    def A(self, fn, reads=(), writes=()):
        return self.op("scalar", fn, reads, writes)

    def check(self):
        sem = {}
        pos = {e: 0 for e in ENGS}
        n = {e: len(self.ops[e]) for e in ENGS}
        progress = True
        while progress:
            progress = False
            for e in ENGS:
                while pos[e] < n[e]:
                    fn, w, inc = self.ops[e][pos[e]]
                    if all(sem.get(k, 0) >= v for k, v in w):
                        if inc is not None:
                            sem[inc[0]] = sem.get(inc[0], 0) + inc[1]
                        pos[e] += 1
                        progress = True
                    else:
                        break
        stuck = {e: (pos[e], n[e], [(k, v, sem.get(k, 0)) for k, v in self.ops[e][pos[e]][1]]) for e in ENGS if pos[e] < n[e]}
        return stuck

    def build(self, final_waits=()):
        nc = self.nc
        names = list(ENGS) + sorted(self.dcnt.keys())
        with contextlib.ExitStack() as st:
            sems = {n: st.enter_context(nc.semaphore("s_" + n)) for n in names}
            block = st.enter_context(nc.Block())
            fw = {}
            for ev in final_waits:
                if ev is not None and fw.get(ev[0], 0) < ev[1]:
                    fw[ev[0]] = ev[1]
            for eng in ENGS:
                ops = self.ops[eng]
                if eng == "sync":
                    ops = ops + [(None, list(fw.items()), None)]
                if not ops:
                    continue

                def body(e, ops=ops):
                    for fn, w, inc in ops:
                        for k, v in w:
                            e.wait_ge(sems[k], v)
                        if fn is None:
                            continue
                        ins = fn(e)
                        if inc is not None:
                            ins.then_inc(sems[inc[0]], inc[1])

                getattr(block, eng)(body)


class Ctx:
    def __init__(self, name):
        self.nc = bass.Bass("TRN2", target_bir_lowering=False)
        self.p = Prog(self.nc)
        self.st = contextlib.ExitStack()
        self.bufs = {}
        self.outs = []
        self.rr = {}

    def din(self, name, shape, dt=F32):
        return self.nc.dram_tensor(name, list(shape), dt, kind="ExternalInput").ap()

    def dout(self, name, shape, dt=F32):
        return self.nc.dram_tensor(name, list(shape), dt, kind="ExternalOutput").ap()

    def sb(self, name, shape, dt):
        return self.st.enter_context(self.nc.sbuf_tensor(name, list(shape), dt))

    def ps(self, name, shape, dt=F32):
        return self.st.enter_context(self.nc.psum_tensor(name, list(shape), dt))

    def arena_init(self, nwords):
        self.arena = self.sb("arena", [128, nwords], F32)
        self.top = 0
        self.nwords = nwords

    def carve(self, shape, dt):
        nfree = 1
        for d in shape[1:]:
            nfree *= d
        words = nfree if dt == F32 else (nfree + 1) // 2
        a = self.top
        self.top += words
        assert self.top <= self.nwords, ("arena overflow", self.top, self.nwords)
        ap = self.arena[:, a:a + words]
        if dt != F32:
            ap = ap.bitcast(dt)[:, :nfree]
        if len(shape) == 3:
            ap = ap.rearrange("p (a b) -> p a b", b=shape[2])
        elif len(shape) == 4:
            ap = ap.rearrange("p (a b c) -> p a b c", b=shape[2], c=shape[3])
        if shape[0] < 128:
            ap = ap[0:shape[0]]
        return ap

    def mark(self):
        return self.top

    def release(self, m):
        self.top = m
        self.p.barrier()

    def B(self, *key):
        b = self.bufs.get(key)
        if b is None:
            b = self.bufs[key] = Buf(str(key))
        return b

    def rot(self, key, n):
        i = self.rr.get(key, 0)
        self.rr[key] = (i + 1) % n
        return i


def pc(v):
    v = np.asarray(v, np.float32)
    return np.ascontiguousarray(v.reshape(-1, 128).T)


def setup_common(cx, n_ps=8):
    cx.arena_init(50 * 1024)
    cx.ones = cx.carve([128, 128], BF16)
    cx.p.V(lambda e: e.memset(cx.ones[:], 1.0), writes=[cx.B("ones")])
    cx.psb = [cx.ps(f"psb{i}", [128, 512], F32) for i in range(n_ps)]
    cx.sq = [cx.carve([128, 512], BF16) for i in range(2)]
    cx.r32 = [cx.carve([128, 512], F32) for i in range(2)]


def rms_stats(cx, src, srcbufs, n, psi, eps_scaled, nch=8):
    p = cx.p
    ps = cx.psb[psi]
    for c in range(nch):
        i = cx.rot("sq", 2)
        sq = cx.sq[i]
        p.A(lambda e, c=c, sq=sq: e.activation(sq[:, :n], src(c), AF.Square), reads=srcbufs, writes=[cx.B("sq", i)])
        p.mm(ps[:, :n], cx.ones[:], sq[:, :n], c == 0, c == nch - 1, reads=[cx.B("sq", i), cx.B("ones")],
             writes=[cx.B("psb", psi)], sig=True)
    j = cx.rot("r32", 2)
    r = cx.r32[j]
    p.A(lambda e: e.activation(r[:, :n], ps[:, :n], AF.Sqrt, bias=float(eps_scaled), scale=1.0),
        reads=[cx.B("psb", psi)], writes=[cx.B("r32", j)])
    p.V(lambda e: e.reciprocal(r[:, :n], r[:, :n]), reads=[cx.B("r32", j)], writes=[cx.B("r32", j)])
    return r, cx.B("r32", j)


def load_x_group(cx, xdram, xbufkey, off, n):
    src = xdram.rearrange("(c p) t -> p c t", p=128)[:, :, off:off + n]
    cx.p.dma("sync", cx.xin[:, :, :n], src, "xin", reads=[cx.B(*xbufkey)], writes=[cx.B("xin")])


def norm_to_h(cx, hdst, hbuf, g32col, n, psi=6):
    r, rb = rms_stats(cx, lambda c: cx.xin[:, c, :n], [cx.B("xin")], n, psi, D * RMS_EPS)
    for c in range(8):
        cx.p.V(lambda e, c=c: e.scalar_tensor_tensor(hdst(c), cx.xin[:, c, :n], g32col(c), r[:, :n], ALU.mult, ALU.mult),
               reads=[cx.B("xin"), rb, cx.B("vecs")], writes=[hbuf])


def norm_residual_store(cx, ysrc, ybufs, xin_dram, xin_key, xout_dram, xout_key, off_in, off_out, n, gcol, psi=6):
    p = cx.p
    r, rb = rms_stats(cx, ysrc, ybufs, n, psi, D * RMS_EPS)
    load_x_group(cx, xin_dram, xin_key, off_in, n)
    for c in range(8):
        i = cx.rot("ntmp", 2)
        t = cx.ntmp[i]
        p.V(lambda e, c=c, t=t: e.scalar_tensor_tensor(t[:, :n], ysrc(c), gcol(c), r[:, :n], ALU.mult, ALU.mult),
            reads=ybufs + [rb, cx.B("vecs")], writes=[cx.B("ntmp", i)])
        p.V(lambda e, c=c, t=t: e.tensor_tensor(cx.xin[:, c, :n], cx.xin[:, c, :n], t[:, :n], ALU.add),
            reads=[cx.B("ntmp", i), cx.B("xin")], writes=[cx.B("xin")])
    dst = xout_dram.rearrange("(c p) t -> p c t", p=128)[:, :, off_out:off_out + n]
    return p.dma("sync", dst, cx.xin[:, :, :n], "xout", reads=[cx.B("xin")], writes=[cx.B(*xout_key)])


def alloc_small(cx):
    cx.xin = cx.carve([128, 8, 512], F32)
    cx.sg = [cx.carve([128, 512], F32) for i in range(2)]
    cx.ntmp = [cx.carve([128, 512], F32) for i in range(2)]


def alloc_ffn(cx, maxtok):
    cx.hT = cx.carve([128, 8, maxtok], BF16)
    cx.aT = cx.carve([128, NFC, maxtok], BF16)
    cx.ytmp = cx.carve([128, 8, maxtok], F32)
    cx.wg = [cx.carve([128, 8, 256], BF16) for i in range(2)]
    cx.wu = [cx.carve([128, 8, 256], BF16) for i in range(2)]
    cx.wd = [cx.carve([128, NFC, 128], BF16) for i in range(2)]


def ffn(cx, tag, xin_dram, xin_key, xout_dram, xout_key, passes, wg_d, wu_d, wd_d, gpre, gpost, out_shift=0):
    p = cx.p
    last = None
    hT, aT, ytmp, wg, wu, wd = cx.hT, cx.aT, cx.ytmp, cx.wg, cx.wu, cx.wd
    for groups in passes:
        loc = []
        o = 0
        for (off, n) in groups:
            loc.append(o)
            o += n
        for gi, (off, n) in enumerate(groups):
            load_x_group(cx, xin_dram, xin_key, off, n)
            lo = loc[gi]
            norm_to_h(cx, lambda c, lo=lo, n=n: hT[:, c, lo:lo + n], cx.B("hT"), gpre, n)
        for fp in range(NFC // 2):
            s = cx.rot("wgu", 2)
            p.dma("gpsimd", wg[s][:], wg_d.rearrange("(c p) f -> p c f", p=128)[:, :, fp * 256:(fp + 1) * 256],
                  f"wg{s}", writes=[cx.B("wg", s)])
            p.dma("gpsimd", wu[s][:], wu_d.rearrange("(c p) f -> p c f", p=128)[:, :, fp * 256:(fp + 1) * 256],
                  f"wu{s}", writes=[cx.B("wu", s)])
            for h in range(2):
                fc = fp * 2 + h
                for gi, (off, n) in enumerate(groups):
                    lo = loc[gi]
                    b = cx.rot("gu", 2)
                    pg, pu = cx.psb[b], cx.psb[2 + b]
                    for c in range(8):
                        p.mm(pg[:, :n], wg[s][:, c, h * 128:(h + 1) * 128], hT[:, c, lo:lo + n], c == 0, c == 7,
                             reads=[cx.B("wg", s), cx.B("hT")], writes=[cx.B("psb", b)])
                    for c in range(8):
                        p.mm(pu[:, :n], wu[s][:, c, h * 128:(h + 1) * 128], hT[:, c, lo:lo + n], c == 0, c == 7,
                             reads=[cx.B("wu", s), cx.B("hT")], writes=[cx.B("psb", 2 + b)])
                    sg = cx.sg[b]
                    p.A(lambda e, sg=sg, pg=pg, n=n: e.activation(sg[:, :n], pg[:, :n], AF.Silu),
                        reads=[cx.B("psb", b)], writes=[cx.B("sg", b)])
                    p.V(lambda e, sg=sg, pu=pu, n=n, fc=fc, lo=lo: e.tensor_tensor(aT[:, fc, lo:lo + n], pu[:, :n], sg[:, :n], ALU.mult),
                        reads=[cx.B("psb", 2 + b), cx.B("sg", b)], writes=[cx.B("aT")])
        for dc in range(8):
            s = cx.rot("wd", 2)
            p.dma("gpsimd", wd[s][:], wd_d.rearrange("(fc p) d -> p fc d", p=128)[:, :, dc * 128:(dc + 1) * 128],
                  f"wd{s}", writes=[cx.B("wd", s)])
            for gi, (off, n) in enumerate(groups):
                lo = loc[gi]
                b = 4 + cx.rot("dn", 2)
                pd = cx.psb[b]
                for fc in range(NFC):
                    p.mm(pd[:, :n], wd[s][:, fc, :], aT[:, fc, lo:lo + n], fc == 0, fc == NFC - 1,
                         reads=[cx.B("wd", s), cx.B("aT")], writes=[cx.B("psb", b)])
                p.A(lambda e, pd=pd, dc=dc, lo=lo, n=n: e.activation(ytmp[:, dc, lo:lo + n], pd[:, :n], AF.Copy),
                    reads=[cx.B("psb", b)], writes=[cx.B("ytmp")])
        for gi, (off, n) in enumerate(groups):
            lo = loc[gi]
            last = norm_residual_store(cx, lambda c, lo=lo, n=n: ytmp[:, c, lo:lo + n], [cx.B("ytmp")],
                                       xin_dram, xin_key, xout_dram, xout_key, off, off - out_shift, n, gpost)
    return last


def l1_vec_layout():
    names = [("f00pre", 8), ("f00post", 8), ("m0pre", 8), ("bpw1", 16), ("wdw", 8 * CW), ("bdw", 8), ("lng", 8),
             ("lnb", 8), ("bpw2", 8), ("m0post", 8), ("f01pre", 8), ("f01post", 8), ("f10pre", 8), ("f10post", 8),
             ("m1pre", 8), ("bgate", 1), ("flag", 1)]
    off = {}
    o = 0
    for n, k in names:
        off[n] = (o, k)
        o += k
    return off, o


def phase_A(cx, T):
    p = cx.p
    TT = TC + HALO
    voff, nv = l1_vec_layout()
    xT, vecs_d, wgs, wus, wds = T["xT"], T["vecs"], T["wgs"], T["wus"], T["wds"]
    wpw1_d, wpw2_d, win_d, ident_d, prot_d, cos_d, sin_d = T["wpw1"], T["wpw2"], T["win"], T["ident"], T["prot"], T["ropecos"], T["ropesin"]
    xa, xb, xc, xd = T["xa"], T["xb"], T["xc"], T["xd"]
    qraw_o, qrot_o, kvT_o, vtok_o, gate_o = T["qraw_o"], T["qrot_o"], T["kvT_o"], T["vtok_o"], T["gate_o"]
    if True:
        vecs = cx.carve([128, nv], F32)
        p.dma("sync", vecs[:], vecs_d, "const", writes=[cx.B("vecs")])

        def vcol(name, scale=None):
            o, k = voff[name]
            if scale is not None:
                p.V(lambda e: e.tensor_scalar(vecs[:, o:o + k], vecs[:, o:o + k], float(scale), None, ALU.mult),
                    reads=[cx.B("vecs")], writes=[cx.B("vecs")])
            return lambda c: vecs[:, o + c:o + c + 1]

        g_f00pre = vcol("f00pre", 32.0)
        g_f00post = vcol("f00post", 16.0)
        g_m0pre = vcol("m0pre", 32.0)
        g_m0post = vcol("m0post", 32.0)
        g_f01pre = vcol("f01pre", 32.0)
        g_f01post = vcol("f01post", 16.0)
        g_f10pre = vcol("f10pre", 32.0)
        g_f10post = vcol("f10post", 16.0)
        g_m1pre = vcol("m1pre", 32.0)
        bpw1 = vcol("bpw1")
        bdw = vcol("bdw")
        lng = vcol("lng")
        lnb = vcol("lnb")
        bpw2 = vcol("bpw2")
        wdw_o = voff["wdw"][0]
        bgate_o = voff["bgate"][0]
        flag_o = voff["flag"][0]

        identb = cx.carve([128, 128], BF16)
        p.dma("gpsimd", identb[:], ident_d, "const2", writes=[cx.B("identb")])
        prot = cx.carve([128, 128], BF16)
        p.dma("gpsimd", prot[:], prot_d, "const2", writes=[cx.B("prot")])
        SKIP = False
        m0 = cx.mark()
        alloc_ffn(cx, 1152)
        passes0 = [] if SKIP else [[(0, 128), (128, 512), (640, 512)], [(1152, 512), (1664, 512)]]
        ffn(cx, "f00", xT, ("xT",), xa, ("xa",), passes0, wgs[0], wus[0], wds[0], g_f00pre, g_f00post)
        cx.release(m0)

        wpw1 = cx.carve([128, 8, 2 * D], BF16)
        wpw2 = cx.carve([128, 8, D], BF16)
        for c in range(8):
            p.dma("gpsimd", wpw1[:, c, :], wpw1_d[c * 128:(c + 1) * 128, :], f"wpw{c % 4}", writes=[cx.B("wpw1")])
        for c in range(8):
            p.dma("gpsimd", wpw2[:, c, :], wpw2_d[c * 128:(c + 1) * 128, :], f"wpw{c % 4}", writes=[cx.B("wpw2")])
        glu = cx.carve([128, 8, TT], BF16)
        vbs = [cx.carve([128, 512], BF16) for q in range(2)]
        y2 = cx.carve([128, 8, 512], F32)
        dg = [cx.carve([128, CW, 128], BF16) for i in range(2)]
        hc = cx.carve([128, 8, 512], BF16)
        vt = cx.carve([128, 8, 512], F32)
        sc = cx.carve([128, 8, 512], BF16)
        for (off, n) in ([] if SKIP else [(0, 128), (128, 512), (640, 512), (1152, 512), (1664, 512)]):
            load_x_group(cx, xa, ("xa",), off, n)
            norm_to_h(cx, lambda c, n=n: hc[:, c, :n], cx.B("hT"), g_m0pre, n)
            for oc in range(8):
                b = cx.rot("gu", 2)
                pa, pg = cx.psb[b], cx.psb[2 + b]
                for c in range(8):
                    p.mm(pa[:, :n], wpw1[:, c, oc * 128:(oc + 1) * 128], hc[:, c, :n], c == 0, c == 7,
                         reads=[cx.B("wpw1"), cx.B("hT")], writes=[cx.B("psb", b)])
                for c in range(8):
                    p.mm(pg[:, :n], wpw1[:, c, D + oc * 128:D + (oc + 1) * 128], hc[:, c, :n], c == 0, c == 7,
                         reads=[cx.B("wpw1"), cx.B("hT")], writes=[cx.B("psb", 2 + b)])
                sg = cx.sg[b]
                p.A(lambda e, sg=sg, pg=pg, n=n, oc=oc: e.activation(sg[:, :n], pg[:, :n], AF.Sigmoid, bias=bpw1(8 + oc)),
                    reads=[cx.B("psb", 2 + b), cx.B("vecs")], writes=[cx.B("sg", b)])
                p.V(lambda e, sg=sg, pa=pa, n=n, oc=oc, off=off: e.scalar_tensor_tensor(
                    glu[:, oc, off:off + n], pa[:, :n], bpw1(oc), sg[:, :n], ALU.add, ALU.mult),
                    reads=[cx.B("psb", b), cx.B("sg", b), cx.B("vecs")], writes=[cx.B("glu")])
            if off == 0:
                for oc in range(8):
                    p.V(lambda e, oc=oc: e.tensor_scalar(glu[:, oc, 0:128], glu[:, oc, 0:128], vecs[:, flag_o:flag_o + 1], None, ALU.mult),
                        reads=[cx.B("glu"), cx.B("vecs")], writes=[cx.B("glu")])
        for g in range(0 if SKIP else 4):
            t0 = HALO + g * 512
            n = 512
            for cc in range(8):
                di = cx.rot("dg", 2)
                for k in range(CW):
                    p.V(lambda e, k=k, cc=cc, di=di: e.tensor_scalar(dg[di][:, k, :], identb[:], vecs[:, wdw_o + k * 8 + cc:wdw_o + k * 8 + cc + 1], None, ALU.mult),
                        reads=[cx.B("identb"), cx.B("vecs")], writes=[cx.B("dg", di)])
                b = 4 + cx.rot("dn", 2)
                pd = cx.psb[b]
                for k in range(CW):
                    s0 = t0 - (CW - 1) + k
                    p.mm(pd[:, :n], dg[di][:, k, :], glu[:, cc, s0:s0 + n], k == 0, k == CW - 1,
                         reads=[cx.B("dg", di), cx.B("glu")], writes=[cx.B("psb", b)])
                p.A(lambda e, pd=pd, cc=cc: e.activation(vt[:, cc, :n], pd[:, :n], AF.Identity, bias=bdw(cc)),
                    reads=[cx.B("psb", b), cx.B("vecs")], writes=[cx.B("ytmp")])
            psm, psq = cx.psb[6], cx.psb[7]
            for cc in range(8):
                i = cx.rot("sq", 2)
                sq = cx.sq[i]
                p.A(lambda e, cc=cc, sq=sq: e.activation(sq[:, :n], vt[:, cc, :n], AF.Square), reads=[cx.B("ytmp")], writes=[cx.B("sq", i)])
                p.mm(psq[:, :n], cx.ones[:], sq[:, :n], cc == 0, cc == 7, reads=[cx.B("sq", i), cx.B("ones")], writes=[cx.B("psb", 7)], sig=True)
                j = cx.rot("vb", 2)
                vb = vbs[j]
                p.V(lambda e, cc=cc, vb=vb: e.tensor_copy(vb[:, :n], vt[:, cc, :n]), reads=[cx.B("ytmp")], writes=[cx.B("vb", j)])
                p.mm(psm[:, :n], cx.ones[:], vb[:, :n], cc == 0, cc == 7, reads=[cx.B("vb", j), cx.B("ones")], writes=[cx.B("psb", 6)], sig=True)
            mean, m2 = cx.r32[0], cx.r32[1]
            rs = cx.ntmp[0]
            p.V(lambda e: e.tensor_scalar(mean[:, :n], psm[:, :n], 1.0 / D, None, ALU.mult), reads=[cx.B("psb", 6)], writes=[cx.B("r32", 0)])
            p.V(lambda e: e.tensor_tensor(m2[:, :n], mean[:, :n], mean[:, :n], ALU.mult), reads=[cx.B("r32", 0)], writes=[cx.B("r32", 1)])
            p.V(lambda e: e.scalar_tensor_tensor(rs[:, :n], psq[:, :n], 1.0 / D, m2[:, :n], ALU.mult, ALU.subtract),
                reads=[cx.B("psb", 7), cx.B("r32", 1)], writes=[cx.B("ntmp", 0)])
            p.A(lambda e: e.activation(rs[:, :n], rs[:, :n], AF.Sqrt, bias=float(LN_EPS), scale=1.0), reads=[cx.B("ntmp", 0)], writes=[cx.B("ntmp", 0)])
            p.V(lambda e: e.reciprocal(rs[:, :n], rs[:, :n]), reads=[cx.B("ntmp", 0)], writes=[cx.B("ntmp", 0)])
            for cc in range(8):
                p.V(lambda e, cc=cc: e.tensor_tensor(vt[:, cc, :n], vt[:, cc, :n], mean[:, :n], ALU.subtract),
                    reads=[cx.B("ytmp"), cx.B("r32", 0)], writes=[cx.B("ytmp")])
                p.V(lambda e, cc=cc: e.tensor_tensor(vt[:, cc, :n], vt[:, cc, :n], rs[:, :n], ALU.mult),
                    reads=[cx.B("ytmp"), cx.B("ntmp", 0)], writes=[cx.B("ytmp")])
                p.A(lambda e, cc=cc: e.activation(sc[:, cc, :n], vt[:, cc, :n], AF.Silu, bias=lnb(cc), scale=lng(cc)),
                    reads=[cx.B("ytmp"), cx.B("vecs")], writes=[cx.B("aT")])
            for oc in range(8):
                b = 4 + cx.rot("dn", 2)
                pd = cx.psb[b]
                for c in range(8):
                    p.mm(pd[:, :n], wpw2[:, c, oc * 128:(oc + 1) * 128], sc[:, c, :n], c == 0, c == 7,
                         reads=[cx.B("wpw2"), cx.B("aT")], writes=[cx.B("psb", b)])
                p.A(lambda e, pd=pd, oc=oc: e.activation(y2[:, oc, :n], pd[:, :n], AF.Identity, bias=bpw2(oc)),
                    reads=[cx.B("psb", b), cx.B("vecs")], writes=[cx.B("y2")])
            norm_residual_store(cx, lambda c: y2[:, c, :n], [cx.B("y2")], xa, ("xa",), xb, ("xb",), t0, t0 - HALO, n, g_m0post)

        cx.release(m0)
        alloc_ffn(cx, 1024)
        passes = [] if SKIP else [[(0, 512), (512, 512)], [(1024, 512), (1536, 512)]]
        ffn(cx, "f01", xb, ("xb",), xc, ("xc",), passes, wgs[1], wus[1], wds[1], g_f01pre, g_f01post)
        ffn(cx, "f10", xc, ("xc",), xd, ("xd",), passes, wgs[2], wus[2], wds[2], g_f10pre, g_f10post)

        cx.release(m0)
        hc = cx.carve([128, 8, 512], BF16)
        win = cx.carve([128, 8, NIN], BF16)
        for c in range(8):
            p.dma("gpsimd", win[:, c, :], win_d[c * 128:(c + 1) * 128, :], f"wpw{c % 4}", writes=[cx.B("win")])
        cosT = cx.carve([128, TC], F32)
        sinT = cx.carve([128, TC], F32)
        p.dma("sync", cosT[:], cos_d, "const", writes=[cx.B("cs")])
        p.dma("sync", sinT[:], sin_d, "const", writes=[cx.B("cs")])
        qrawS = [cx.carve([128, 8, 512], BF16) for i in range(2)]
        qrotS = [cx.carve([128, 8, 512], BF16) for i in range(2)]
        kvS = [cx.carve([128, 8, 512], BF16) for i in range(2)]
        vtk = [cx.carve([128, 512], BF16) for i in range(2)]
        gsb = [cx.carve([128, 512], BF16) for i in range(2)]
        vS = [cx.carve([128, 2, 4, 256], BF16) for i in range(2)]
        outs = []
        PARTS = ["fm", "v", "g"]
        for g in range(4):
            t0 = g * 512
            n = 512
            load_x_group(cx, xd, ("xd",), t0, n)
            norm_to_h(cx, lambda c: hc[:, c, :n], cx.B("hT"), g_m1pre, n)
            so = cx.rot("nsao", 2)
            fm = [(oc, "q") for oc in range(8)] + [(8, "kc"), (9, "kc"), (10, "vc"), (11, "vc"), (12, "ks"), (13, "ks"), (16, "kw"), (17, "kw")]
            kvidx = {"kc": 0, "vc": 2, "ks": 4, "kw": 6}
            for (oc, kind) in (fm if "fm" in PARTS else []):
                b = cx.rot("gu", 2)
                pq = cx.psb[b]
                for c in range(8):
                    p.mm(pq[:, :n], win[:, c, oc * 128:(oc + 1) * 128], hc[:, c, :n], c == 0, c == 7,
                         reads=[cx.B("win"), cx.B("hT")], writes=[cx.B("psb", b)])
                if kind == "q":
                    raw, rawb = qrawS[so][:, oc, :], cx.B("qrawS", so)
                    rot_, rotb = qrotS[so][:, oc, :], cx.B("qrotS", so)
                elif kind in ("kc", "vc"):
                    raw, rawb = kvS[so][:, kvidx[kind] + oc % 2, :], cx.B("kvS", so)
                else:
                    i = cx.rot("qb", 2)
                    raw, rawb = vtk[i][:, :], cx.B("vtk", i)
                    rot_, rotb = kvS[so][:, kvidx[kind] + oc % 2, :], cx.B("kvS", so)
                p.A(lambda e, raw=raw, pq=pq: e.activation(raw, pq[:, :n], AF.Copy), reads=[cx.B("psb", b)], writes=[rawb])
                if kind in ("q", "ks", "kw"):
                    b2 = 2 + cx.rot("rp", 2)
                    pr = cx.psb[b2]
                    p.mm(pr[:, :n], prot[:], raw, True, True, reads=[cx.B("prot"), rawb], writes=[cx.B("psb", b2)])
                    ti = cx.rot("ntmp", 2)
                    t1, t1b = cx.ntmp[ti], cx.B("ntmp", ti)
                    si = cx.rot("sgr", 2)
                    t2, t2b = cx.sg[si], cx.B("sg", si)
                    p.V(lambda e, t1=t1, raw=raw: e.tensor_tensor(t1[:, :n], raw, cosT[:, t0:t0 + n], ALU.mult),
                        reads=[rawb, cx.B("cs")], writes=[t1b])
                    p.V(lambda e, t2=t2, pr=pr: e.tensor_tensor(t2[:, :n], pr[:, :n], sinT[:, t0:t0 + n], ALU.mult),
                        reads=[cx.B("psb", b2), cx.B("cs")], writes=[t2b])
                    p.V(lambda e, t1=t1, t2=t2, rot_=rot_: e.tensor_tensor(rot_, t1[:, :n], t2[:, :n], ALU.add),
                        reads=[t1b, t2b], writes=[rotb])
            if "fm" in PARTS:
                outs.append(p.dma("sync", qraw_o.rearrange("(c p) t -> p c t", p=128)[:, :, t0:t0 + n], qrawS[so][:], f"qo{so}a", reads=[cx.B("qrawS", so)]))
                outs.append(p.dma("sync", qrot_o.rearrange("(c p) t -> p c t", p=128)[:, :, t0:t0 + n], qrotS[so][:], f"qo{so}b", reads=[cx.B("qrotS", so)]))
                outs.append(p.dma("sync", kvT_o.rearrange("s (a p) t -> p (s a) t", p=128)[:, :, t0:t0 + n], kvS[so][:], f"qo{so}c", reads=[cx.B("kvS", so)]))
            for vi, c0 in enumerate((1792, 2304) if "v" in PARTS else ()):
                for tt in range(4):
                    b = 4 + cx.rot("dn", 2)
                    pv = cx.psb[b]
                    for c in range(8):
                        p.mm(pv[:, :256], hc[:, c, tt * 128:(tt + 1) * 128], win[:, c, c0:c0 + 256], c == 0, c == 7,
                             reads=[cx.B("win"), cx.B("hT")], writes=[cx.B("psb", b)])
                    p.A(lambda e, pv=pv, vi=vi, tt=tt: e.activation(vS[so][:, vi, tt, :], pv[:, :256], AF.Copy), reads=[cx.B("psb", b)], writes=[cx.B("vS", so)])
            if "v" in PARTS:
                for vi in range(2):
                    outs.append(p.dma("sync", vtok_o[vi, t0:t0 + n, :].rearrange("(tt p) c -> p tt c", p=128), vS[so][:, vi, :, :], f"qo{so}v{vi}", reads=[cx.B("vS", so)]))
            if "g" not in PARTS:
                continue
            b = cx.rot("gu", 2)
            pq = cx.psb[b]
            for c in range(8):
                p.mm(pq[0:48, :n], win[:, c, 2560:2608], hc[:, c, :n], c == 0, c == 7,
                     reads=[cx.B("win"), cx.B("hT")], writes=[cx.B("psb", b)])
            i = cx.rot("gsb", 2)
            p.A(lambda e, i=i, pq=pq: e.activation(gsb[i][0:48, :n], pq[0:48, :n], AF.Sigmoid, bias=vecs[0:48, bgate_o:bgate_o + 1]),
                reads=[cx.B("psb", b), cx.B("vecs")], writes=[cx.B("gsb", i)])
            outs.append(p.dma("sync", gate_o[:, t0:t0 + n], gsb[i][0:48, :n], "gout", reads=[cx.B("gsb", i)]))

        cx.release(m0)
    return outs


def rope_tables_np(pos0, n):
    pos = (pos0 + np.arange(n)).astype(np.float32)
    inv = (np.float32(500000.0) ** (-np.arange(0, 16, 2, dtype=np.float32) / np.float32(16))).astype(np.float32)
    ang = pos[None, :] * inv[:, None]
    c8, s8 = np.cos(ang).astype(np.float32), np.sin(ang).astype(np.float32)
    cos = np.ones((128, n), np.float32)
    sin = np.zeros((128, n), np.float32)
    for hb in (0, 64):
        cos[hb:hb + 8] = c8
        cos[hb + 8:hb + 16] = c8
        sin[hb:hb + 8] = s8
        sin[hb + 8:hb + 16] = s8
    return cos, sin


def prot_np():
    pr = np.zeros((128, 128), np.float32)
    for hb in (0, 64):
        for j in range(8):
            pr[hb + j + 8, hb + j] = -1.0
            pr[hb + j, hb + 8 + j] = 1.0
    return pr


def prep_launch1(I, core):
    b, j = divmod(core, 4)
    t0 = j * TC
    x = I["x"][b]
    xT = np.zeros((D, TC + HALO), np.float32)
    xT[:, HALO:] = x[t0:t0 + TC].T
    if j > 0:
        xT[:, :HALO] = x[t0 - HALO:t0].T
    voff, nv = l1_vec_layout()
    vecs = np.zeros((128, nv), np.float32)

    def put(name, arr):
        o, k = voff[name]
        vecs[:, o:o + k] = arr

    put("f00pre", pc(I["ffn_norm_pre"][0, 0]))
    put("f00post", pc(I["ffn_norm_post"][0, 0]))
    put("m0pre", pc(I["mix_norm_pre"][0]))
    put("bpw1", pc(I["conv_b_pw1"][0]))
    put("wdw", np.concatenate([pc(I["conv_w_dw"][0, k]) for k in range(CW)], axis=1))
    put("bdw", pc(I["conv_b_dw"][0]))
    put("lng", pc(I["conv_ln_g"][0]))
    put("lnb", pc(I["conv_ln_b"][0]))
    put("bpw2", pc(I["conv_b_pw2"][0]))
    put("m0post", pc(I["mix_norm_post"][0]))
    put("f01pre", pc(I["ffn_norm_pre"][0, 1]))
    put("f01post", pc(I["ffn_norm_post"][0, 1]))
    put("f10pre", pc(I["ffn_norm_pre"][1, 0]))
    put("f10post", pc(I["ffn_norm_post"][1, 0]))
    put("m1pre", pc(I["mix_norm_pre"][1]))
    bg = np.zeros((128, 1), np.float32)
    bg[:48, 0] = I["nsa_b_gate"][0]
    put("bgate", bg)
    put("flag", np.full((128, 1), 0.0 if j == 0 else 1.0, np.float32))
    cos, sin = rope_tables_np(t0, TC)
    m = {"xT": xT, "vecs": vecs, "wpw1": I["conv_w_pw1"][0], "wpw2": I["conv_w_pw2"][0], "win": I["nsa_w_in"][0],
         "ident": np.eye(128, dtype=np.float32), "prot": prot_np(), "ropecos": cos, "ropesin": sin}
    for i, (l, h) in enumerate(((0, 0), (0, 1), (1, 0))):
        m[f"wg{i}"] = I["ffn_w_gate"][l, h]
        m[f"wu{i}"] = I["ffn_w_up"][l, h]
        m[f"wd{i}"] = I["ffn_w_down"][l, h]
    return {k: np.ascontiguousarray(v, dtype=np.float32) for k, v in m.items()}


def l2_consts():
    bf = ml_dtypes.bfloat16
    E = np.zeros((128, 64, 128), np.float32)
    for jt in range(64):
        E[2 * jt, jt, 0:64] = 1.0
        E[2 * jt + 1, jt, 64:128] = 1.0
    i = np.arange(128)[:, None]
    j = np.arange(512)[None, :]
    cmask = np.zeros((128, 5, 512), np.float32)
    for d in range(5):
        cmask[:, d, :] = np.where(16 * i + 31 - 512 * d <= j, 0.0, NEG)
    wmask = np.zeros((128, 8, 512), np.float32)
    for oi in range(8):
        k = 128 * (oi - 4) + i
        wmask[:, oi, :] = np.where((k <= j) & (k > j - 512), 0.0, NEG)
    AB = np.zeros((128, 2, 256), np.float32)
    jj = np.arange(128)[:, None]
    m = np.arange(256)[None, :]
    x = (m - 128) - (jj >= 64)
    forced = (x == 0) | (x == -1)
    nonc = x > 0
    AB[:, 0, :] = np.where(forced | nonc, 0.0, 1.0)
    AB[:, 1, :] = np.where(forced, 1e9, np.where(nonc, -1e9, 0.0))
    ov = np.zeros((128, 4, 130), np.float32)
    for ct in range(4):
        for ii in range(128):
            c = 128 * ct + ii
            if c > 510:
                continue
            for n in range(128):
                if 16 * c < 64 * n + 64 and 16 * c + 32 > 64 * n:
                    ov[ii, ct, n] = 1.0
            ov[ii, ct, 128] = 1.0
    selG = np.zeros((12, 12, 128), np.float32)
    for r in range(12):
        selG[r, r, :] = 1.0
    return {"E": E.astype(bf), "identb": np.eye(128, dtype=np.float32).astype(bf), "cmask": cmask.astype(bf),
            "wmask": wmask.astype(bf), "AB": AB, "ov": ov.astype(bf), "selG": selG.astype(bf)}


NQG = 16
SCALE = 0.125


def phase_B(cx, T):
    p = cx.p
    oh_d = T["oh"]
    w1_d, w2k_d, w2v_d, pe_d = T["w1"], T["w2kD"], T["w2vD"], T["pe"]
    EX2 = T["EX2"]

    GXL = T["GX1L"]
    oT4 = EX2[0, :].rearrange("(j r t) -> j r t", j=4, t=TC)

    def gq(j, name, g_):
        base = 0 if name == "qraw_o" else 4
        return GXL[base + g_][j, :].rearrange("(r t) -> r t", t=TC)

    def gkv(j, kind):
        return GXL[8 + kind][j, :].rearrange("(r t) -> r t", t=TC)

    def gv(j, vi):
        return GXL[12 + vi][j, :].rearrange("(t c) -> t c", c=256)

    def ggate(j):
        return GXL[14][j, :].rearrange("(r t) -> r t", t=TC)

    if True:
        pst = cx.psb[7][:].bitcast(BF16)
        ones = cx.ones
        oh = cx.carve([128, 4], F32)
        p.dma("sync", oh[:], oh_d, "c_oh", writes=[cx.B("oh")])
        identb = cx.carve([128, 128], BF16)
        E = cx.carve([128, 64, 128], BF16)
        cmask = cx.carve([128, 5, 512], BF16)
        wmask = cx.carve([128, 8, 512], BF16)
        AB = cx.carve([128, 2, 256], F32)
        ov = cx.carve([128, 4, 130], BF16)
        selG = cx.carve([12, 12, 128], BF16)
        for dst, nm in ((identb, "identb"), (E, "E"), (cmask, "cmask"), (wmask, "wmask"), (AB, "AB"), (ov, "ov"), (selG, "selG")):
            p.dma("sync", dst[:], T[nm], f"c_k{nm}", writes=[cx.B(nm)])
        kselD = cx.carve([128, S], BF16)
        kwinD = cx.carve([128, S], BF16)
        vA = {nm: cx.carve([128, 64, 192], BF16) for nm in ("sel", "win")}
        for t_ in vA.values():
            p.V(lambda e: e.memset(t_[:], 1.0), writes=[cx.B("vA")])
        kcmpT = cx.carve([128, 512], BF16)
        vcmp = cx.carve([128, 4, 128], BF16)
        p.V(lambda e: e.memset(kcmpT[:], 0.0), writes=[cx.B("kcmpT")])
        p.V(lambda e: e.memset(vcmp[:], 0.0), writes=[cx.B("vcmp")])

        def select4(dst, stage, n_part, stagebuf, dstbuf):
            ps_ = slice(0, n_part)
            p.V(lambda e: e.tensor_scalar(dst, stage(0), oh[ps_, 0:1], None, ALU.mult), reads=[stagebuf, cx.B("oh")], writes=[dstbuf])
            for g_ in range(1, 4):
                p.V(lambda e: e.scalar_tensor_tensor(dst, stage(g_), oh[ps_, g_:g_ + 1], dst, ALU.mult, ALU.add),
                    reads=[stagebuf, cx.B("oh"), dstbuf], writes=[dstbuf])

        m0 = cx.mark()
        stg = cx.carve([128, 4, 2048], BF16)
        kv2 = cx.carve([128, S], BF16)
        w1 = cx.carve([128, 16, 256], BF16)
        w2 = cx.carve([128, 2, 128], BF16)
        pe = cx.carve([128, 16], BF16)
        hid = cx.carve([128, 2, 512], BF16)
        bias = cx.carve([128, 2], F32)
        vstg = cx.carve([128, 4, 16, 64], BF16)

        def load_sel_kv(kind, dstT, nm, shifted):
            for j in range(4):
                for g_ in range(4):
                    src = gkv(j, kind)[g_ * 64:(g_ + 1) * 64, :]
                    p.dma("sync", stg[0:64, g_, :], src, f"sg{g_}", writes=[cx.B("stg")])
                    if not shifted:
                        p.dma("sync", stg[64:128, g_, :], src, f"sh{g_}", writes=[cx.B("stg")])
                    else:
                        p.dma("sync", stg[64:128, g_, 0:2047], src[:, 1:2048], f"sh{g_}", writes=[cx.B("stg")])
                        if j < 3:
                            p.dma("sync", stg[64:128, g_, 2047:2048], gkv(j + 1, kind)[g_ * 64:(g_ + 1) * 64, 0:1], f"sh{g_}", writes=[cx.B("stg")], allow_slow_non_contiguous=True)
                        else:
                            p.V(lambda e: e.memset(stg[64:128, g_, 2047:2048], 0.0), writes=[cx.B("stg")])
                select4(dstT[:, j * 2048:(j + 1) * 2048], lambda g_: stg[:, g_, :], 128, cx.B("stg"), cx.B(nm))

        load_sel_kv(2, kselD, "kselD", False)
        load_sel_kv(3, kwinD, "kwinD", False)
        for vi, nm in enumerate(("sel", "win")):
            for j in range(4):
                for g_ in range(4):
                    p.dma("sync", vstg[:, g_, :, :], gv(j, vi)[:, g_ * 64:(g_ + 1) * 64].rearrange("(t p) c -> p t c", p=128),
                          f"sg{g_}", writes=[cx.B("vstg")])
                select4(vA[nm][:, j * 16:(j + 1) * 16, 64:128], lambda g_: vstg[:, g_, :, :], 128, cx.B("vstg"), cx.B("vA"))

        for which, w2_d in enumerate((w2k_d, w2v_d)):
            load_sel_kv(which, kv2, "kv2", True)
            p.dma("gpsimd", w1[:], w1_d[which].rearrange("(c p) j -> p c j", p=128), "c_w1", writes=[cx.B("w1")])
            p.dma("gpsimd", w2[:], w2_d.rearrange("(c p) j -> p c j", p=128), "c_w2", writes=[cx.B("w2")])
            p.dma("gpsimd", pe[:], pe_d[which], "c_pe", writes=[cx.B("pe")])
            for jc in range(2):
                pb = cx.psb[2]
                for ch in range(16):
                    p.mm(pb[:, 0:1], w1[:, ch, jc * 128:(jc + 1) * 128], pe[:, ch:ch + 1], ch == 0, ch == 15,
                         reads=[cx.B("w1"), cx.B("pe")], writes=[cx.B("psb", 2)])
                p.V(lambda e: e.tensor_copy(bias[:, jc:jc + 1], pb[:, 0:1]), reads=[cx.B("psb", 2)], writes=[cx.B("bias")])
                ph = cx.psb[jc]
                for lp in range(16):
                    p.mm(ph[:, 0:511], w1[:, lp, jc * 128:(jc + 1) * 128], kv2[:, 2 * lp:2 * lp + 16 * 510 + 1:16], lp == 0, lp == 15,
                         reads=[cx.B("w1"), cx.B("kv2")], writes=[cx.B("psb", jc)])
                p.A(lambda e: e.activation(hid[:, jc, 0:511], ph[:, 0:511], AF.Silu, bias=bias[:, jc:jc + 1]),
                    reads=[cx.B("psb", jc), cx.B("bias")], writes=[cx.B("hid")])
            if which == 0:
                pk = cx.psb[3]
                for jc in range(2):
                    p.mm(pk[:, 0:511], w2[:, jc, :], hid[:, jc, 0:511], jc == 0, jc == 1, reads=[cx.B("w2"), cx.B("hid")], writes=[cx.B("psb", 3)])
                p.A(lambda e: e.activation(kcmpT[:, 0:511], pk[:, 0:511], AF.Copy), reads=[cx.B("psb", 3)], writes=[cx.B("kcmpT")])
            else:
                for ct in range(4):
                    M = 128 if ct < 3 else 127
                    pv = cx.psb[3 + (ct % 2)]
                    for jc in range(2):
                        p.mm(pv[0:M, 0:128], hid[:, jc, ct * 128:ct * 128 + M], w2[:, jc, :], jc == 0, jc == 1,
                             reads=[cx.B("w2"), cx.B("hid")], writes=[cx.B("psb", 3 + (ct % 2))])
                    p.A(lambda e: e.activation(vcmp[0:M, ct, :], pv[0:M, 0:128], AF.Copy),
                        reads=[cx.B("psb", 3 + (ct % 2))], writes=[cx.B("vcmp")])
        cx.release(m0)

        qstg = cx.carve([128, 4, 2, 512], BF16)
        gstg = cx.carve([12, 4, 512], BF16)
        qraw = [cx.carve([128, 2, 512], BF16) for i in range(2)]
        qrot = [cx.carve([128, 2, 512], BF16) for i in range(2)]
        gts = [cx.carve([12, 512], BF16) for i in range(2)]
        eT = [cx.carve([128, 4, 512], BF16) for i in range(2)]
        pT = [cx.carve([128, 512], BF16) for i in range(3)]
        rz = [cx.carve([128, 512], F32) for i in range(2)]
        wv = [cx.carve([128, 512], F32) for i in range(2)]
        tmp = [cx.carve([128, 512], F32) for i in range(2)]
        acc = cx.carve([128, 2, 512], F32)
        accb = [cx.carve([128, 2, 512], BF16) for i in range(2)]
        impacc = cx.carve([128, 4, 128], F32)
        imod = cx.carve([128, 128], F32)
        scr = cx.carve([128, 128], F32)
        m8 = cx.carve([128, 16], F32)
        rzc = cx.carve([128, 1], F32)
        negm = cx.carve([128, 128], BF16)
        negT = cx.carve([128, 512], BF16)
        outs = []

        def finish_branch(r, gi, pacc, paccbuf, zrows, first, gt, gtb):
            a, half = divmod(r, 2)
            hs = slice(64 * half, 64 * half + 64)
            pG = cx.psb[6]
            p.mm(pG[:, :], selG[:, 3 * r + gi, :], gt[:, :], True, True, reads=[cx.B("selG"), gtb], writes=[cx.B("psb", 6)])
            i = cx.rot("rz", 2)
            p.V(lambda e: e.tensor_scalar(rz[i][zrows, :], pacc[zrows, :], 1e-30, None, ALU.max), reads=[paccbuf], writes=[cx.B("rz", i)])
            p.V(lambda e: e.reciprocal(rz[i][zrows, :], rz[i][zrows, :]), reads=[cx.B("rz", i)], writes=[cx.B("rz", i)])
            p.V(lambda e: e.tensor_tensor(wv[i][zrows, :], rz[i][zrows, :], pG[zrows, :], ALU.mult),
                reads=[cx.B("rz", i), cx.B("psb", 6)], writes=[cx.B("wv", i)])
            if first:
                p.V(lambda e: e.tensor_tensor(acc[hs, a, :], pacc[hs, :], wv[i][zrows, :], ALU.mult),
                    reads=[paccbuf, cx.B("wv", i)], writes=[cx.B("acc")])
            else:
                p.V(lambda e: e.tensor_tensor(tmp[i][hs, :], pacc[hs, :], wv[i][zrows, :], ALU.mult),
                    reads=[paccbuf, cx.B("wv", i)], writes=[cx.B("tmp", i)])
                p.V(lambda e: e.tensor_tensor(acc[hs, a, :], acc[hs, a, :], tmp[i][hs, :], ALU.add),
                    reads=[cx.B("acc"), cx.B("tmp", i)], writes=[cx.B("acc")])

        for qg in range(NQG):
            q0 = qg * 512
            s = cx.rot("qld", 2)
            jq, tl = divmod(qg, 4)
            tl *= 512
            for nmq, dstq, bq in (("qraw_o", qraw[s], cx.B("qraw", s)), ("qrot_o", qrot[s], cx.B("qrot", s))):
                for g_ in range(4):
                    p.dma("sync", qstg[:, g_, :, :], gq(jq, nmq, g_)[:, tl:tl + 512].rearrange("(a p) t -> p a t", p=128),
                          f"qs{g_}", writes=[cx.B("qstg")])
                select4(dstq[:], lambda g_: qstg[:, g_, :, :], 128, cx.B("qstg"), bq)
            for g_ in range(4):
                p.dma("sync", gstg[:, g_, :], ggate(jq)[g_ * 12:(g_ + 1) * 12, tl:tl + 512], f"qs{g_}", writes=[cx.B("gstg")])
            select4(gts[s][:], lambda g_: gstg[:, g_, :], 12, cx.B("gstg"), cx.B("gts", s))
            gt, gtb = gts[s], cx.B("gts", s)
            nct = (32 * qg + 30) // 128 + 1
            for r in range(4):
                a, half = divmod(r, 2)
                hs = slice(64 * half, 64 * half + 64)
                es = cx.rot("eT", 2)
                for ct in range(nct):
                    d = qg - 4 * ct
                    b = cx.rot("S", 2)
                    ps_ = cx.psb[b]
                    p.mm(ps_[:, :], kcmpT[hs, ct * 128:(ct + 1) * 128], qraw[s][hs, a, :], True, d >= 5,
                         reads=[cx.B("kcmpT"), cx.B("qraw", s)], writes=[cx.B("psb", b)])
                    if d < 5:
                        p.mm(ps_[:, :], identb[:], cmask[:, d, :], False, True, reads=[cx.B("identb"), cx.B("cmask")], writes=[cx.B("psb", b)])
                    p.A(lambda e, ps_=ps_, es=es, ct=ct: e.activation(eT[es][:, ct, :], ps_[:, :], AF.Exp, scale=SCALE),
                        reads=[cx.B("psb", b)], writes=[cx.B("eT", es)])
                pO, pZ = cx.psb[2], cx.psb[3]
                for ct in range(nct):
                    p.mm(pO[:, :], vcmp[:, ct, :], eT[es][:, ct, :], ct == 0, ct == nct - 1, reads=[cx.B("vcmp"), cx.B("eT", es)], writes=[cx.B("psb", 2)])
                for ct in range(nct):
                    p.mm(pZ[:, :], ones[:], eT[es][:, ct, :], ct == 0, ct == nct - 1, reads=[cx.B("ones"), cx.B("eT", es)], writes=[cx.B("psb", 3)])
                zrows = slice(64 * (1 - half), 64 * (1 - half) + 64)
                pG = cx.psb[6]
                p.mm(pG[:, :], selG[:, 3 * r + 0, :], gt[:, :], True, True, reads=[cx.B("selG"), gtb], writes=[cx.B("psb", 6)])
                i = cx.rot("rz", 2)
                p.V(lambda e, i=i: e.tensor_scalar(rz[i][hs, :], pZ[hs, :], 1e-30, None, ALU.max), reads=[cx.B("psb", 3)], writes=[cx.B("rz", i)])
                p.V(lambda e, i=i: e.reciprocal(rz[i][hs, :], rz[i][hs, :]), reads=[cx.B("rz", i)], writes=[cx.B("rz", i)])
                p.V(lambda e, i=i: e.tensor_tensor(wv[i][hs, :], rz[i][hs, :], pG[hs, :], ALU.mult),
                    reads=[cx.B("rz", i), cx.B("psb", 6)], writes=[cx.B("wv", i)])
                p.V(lambda e, i=i, a=a: e.tensor_tensor(acc[hs, a, :], pO[hs, :], wv[i][hs, :], ALU.mult),
                    reads=[cx.B("psb", 2), cx.B("wv", i)], writes=[cx.B("acc")])
                for qt in range(4):
                    bi = 4 + cx.rot("I", 2)
                    pI = cx.psb[bi]
                    for ct in range(nct):
                        p.mm(pI[:, 0:129], eT[es][:, ct, qt * 128:(qt + 1) * 128], ov[:, ct, 0:129], ct == 0, ct == nct - 1,
                             reads=[cx.B("eT", es), cx.B("ov")], writes=[cx.B("psb", bi)])
                    p.V(lambda e, pI=pI: e.tensor_scalar(rzc[:], pI[:, 128:129], 1e-30, None, ALU.max), reads=[cx.B("psb", bi)], writes=[cx.B("rzc")])
                    p.V(lambda e: e.reciprocal(rzc[:], rzc[:]), reads=[cx.B("rzc")], writes=[cx.B("rzc")])
                    if r == 0:
                        p.V(lambda e, pI=pI, qt=qt: e.tensor_scalar(impacc[:, qt, :], pI[:, 0:128], rzc[:, 0:1], None, ALU.mult),
                            reads=[cx.B("psb", bi), cx.B("rzc")], writes=[cx.B("impacc")])
                    else:
                        p.V(lambda e, pI=pI, qt=qt: e.scalar_tensor_tensor(impacc[:, qt, :], pI[:, 0:128], rzc[:, 0:1], impacc[:, qt, :], ALU.mult, ALU.add),
                            reads=[cx.B("psb", bi), cx.B("rzc"), cx.B("impacc")], writes=[cx.B("impacc")])
            for qt in range(4):
                ti = 4 * qg + qt
                c0 = 128 - 2 * ti
                p.V(lambda e, qt=qt, c0=c0: e.tensor_tensor(imod[:], impacc[:, qt, :], AB[:, 0, c0:c0 + 128], ALU.mult),
                    reads=[cx.B("impacc"), cx.B("AB")], writes=[cx.B("imod")])
                p.V(lambda e, c0=c0: e.tensor_tensor(imod[:], imod[:], AB[:, 1, c0:c0 + 128], ALU.add), reads=[cx.B("imod"), cx.B("AB")], writes=[cx.B("imod")])
                p.V(lambda e: e.memset(imod[:, 0:1], 1e9), reads=[], writes=[cx.B("imod")])
                p.V(lambda e: e.max(m8[:, 0:8], imod[:]), reads=[cx.B("imod")], writes=[cx.B("m8")])
                p.V(lambda e: e.match_replace(scr[:], m8[:, 0:8], imod[:], -1e30), reads=[cx.B("imod"), cx.B("m8")], writes=[cx.B("scr")])
                p.V(lambda e: e.max(m8[:, 8:16], scr[:]), reads=[cx.B("scr")], writes=[cx.B("m8")])
                p.V(lambda e: e.tensor_scalar(negm[:], imod[:], m8[:, 15:16], NEG, ALU.is_lt, ALU.mult), reads=[cx.B("imod"), cx.B("m8")], writes=[cx.B("negm")])
                p.op("tensor", lambda e, qt=qt: e.transpose(pst[:, qt * 128:(qt + 1) * 128], negm[:], identb[:]),
                     reads=[cx.B("negm"), cx.B("identb")], writes=[cx.B("pst")])
                p.A(lambda e, qt=qt: e.activation(negT[:, qt * 128:(qt + 1) * 128], pst[:, qt * 128:(qt + 1) * 128], AF.Copy),
                    reads=[cx.B("pst")], writes=[cx.B("negT")])
            for (br, gi, kD, kbuf, jts) in (("sel", 1, kselD, "kselD", list(range(4 * qg + 4))),
                                            ("win", 2, kwinD, "kwinD", list(range(max(0, 4 * qg - 4), 4 * qg + 4)))):
                units = [(ji, jt, r) for ji, jt in enumerate(jts) for r in range(4)]

                def emit_S(u):
                    ji, jt, r = u
                    o = jt - 4 * qg
                    a, half = divmod(r, 2)
                    hs = slice(64 * half, 64 * half + 64)
                    b = cx.rot("S", 2)
                    ps_ = cx.psb[b]
                    need_mask = (br == "win") or (o >= 0)
                    p.mm(ps_[:, :], kD[hs, jt * 128:(jt + 1) * 128], qrot[s][hs, a, :], True, False,
                         reads=[cx.B(kbuf), cx.B("qrot", s)], writes=[cx.B("psb", b)], sig=False)
                    if br == "sel":
                        p.mm(ps_[:, :], E[:, jt, :], negT[:, :], False, not need_mask, reads=[cx.B("E"), cx.B("negT")], writes=[cx.B("psb", b)])
                    if need_mask:
                        p.mm(ps_[:, :], identb[:], wmask[:, o + 4, :], False, True, reads=[cx.B("identb"), cx.B("wmask")], writes=[cx.B("psb", b)])
                    pi = cx.rot("pT", 3)
                    p.A(lambda e: e.activation(pT[pi][:, :], ps_[:, :], AF.Exp, scale=SCALE),
                        reads=[cx.B("psb", b)], writes=[cx.B("pT", pi)])
                    return pi

                def emit_PV(u, pi):
                    ji, jt, r = u
                    half = r % 2
                    pa = cx.psb[2 + r]
                    p.mm(pa[:, :], vA[br][:, jt, (64 if half == 0 else 0):(192 if half == 0 else 128)], pT[pi][:, :], ji == 0, ji == len(jts) - 1,
                         reads=[cx.B("vA"), cx.B("pT", pi)], writes=[cx.B("psb", 2 + r)], sig=True)

                pend = None
                for u in units:
                    pi = emit_S(u)
                    if pend is not None:
                        emit_PV(*pend)
                    pend = (u, pi)
                emit_PV(*pend)
                for r in range(4):
                    half = r % 2
                    zrows = slice(64 * (1 - half), 64 * (1 - half) + 64)
                    finish_branch(r, gi, cx.psb[2 + r], cx.B("psb", 2 + r), zrows, False, gt, gtb)
            ob = cx.rot("accb", 2)
            p.V(lambda e, ob=ob: e.tensor_copy(accb[ob][:], acc[:]), reads=[cx.B("acc")], writes=[cx.B("accb", ob)])
            outs.append(p.dma("sync", oT4[jq].rearrange("(a p) t -> p a t", p=128)[:, :, tl:tl + 512], accb[ob][:], f"o{ob}", reads=[cx.B("accb", ob)]))
    return outs


def phase_C(cx, T):
    p = cx.p
    xd_d, vecs_d, wout_d, wg_d, wu_d, wd_d, xe, xf, oh_d = (T["xd"], T["vecs3"], T["wout"], T["wg3"], T["wu3"],
                                                              T["wd3"], T["xe"], T["xf"], T["oh"])
    GX2L = T["GX2L"]
    if True:
        vecs = cx.carve([128, 24], F32)
        p.dma("sync", vecs[:], vecs_d, "const", writes=[cx.B("vecs")])
        oh = cx.carve([128, 4], F32)
        p.dma("sync", oh[:], oh_d, "c_oh", writes=[cx.B("oh")])
        for o, sc_ in ((0, 32.0), (8, 32.0), (16, 16.0)):
            p.V(lambda e: e.tensor_scalar(vecs[:, o:o + 8], vecs[:, o:o + 8], sc_, None, ALU.mult), reads=[cx.B("vecs")], writes=[cx.B("vecs")])
        g_m1post = lambda c: vecs[:, c:c + 1]
        g_pre = lambda c: vecs[:, 8 + c:9 + c]
        g_post = lambda c: vecs[:, 16 + c:17 + c]
        m0 = cx.mark()
        wout = cx.carve([128, 8, D], BF16)
        for c in range(8):
            p.dma("gpsimd", wout[:, c, :], wout_d[c * 128:(c + 1) * 128, :], f"wpw{c % 4}", writes=[cx.B("wout")])
        astg = cx.carve([128, 4, 8, 512], BF16)
        at = [cx.carve([128, 8, 512], BF16) for i in range(2)]
        y2 = cx.carve([128, 8, 512], F32)
        for g in range(4):
            t0 = g * 512
            n = 512
            s = cx.rot("at", 2)
            for jj in range(4):
                p.dma("sync", astg[:, jj, :, :], GX2L[jj].rearrange("g (r t) -> (g r) t", t=TC)[:, t0:t0 + n].rearrange("(c p) t -> p c t", p=128),
                      f"as{jj}", writes=[cx.B("astg")])
            p.V(lambda e: e.tensor_scalar(at[s][:], astg[:, 0, :, :], oh[:, 0:1], None, ALU.mult), reads=[cx.B("astg"), cx.B("oh")], writes=[cx.B("at", s)])
            for jj in range(1, 4):
                p.V(lambda e: e.scalar_tensor_tensor(at[s][:], astg[:, jj, :, :], oh[:, jj:jj + 1], at[s][:], ALU.mult, ALU.add),
                    reads=[cx.B("astg"), cx.B("oh"), cx.B("at", s)], writes=[cx.B("at", s)])
            for oc in range(8):
                b = 4 + cx.rot("dn", 2)
                pd = cx.psb[b]
                for c in range(8):
                    p.mm(pd[:, :n], wout[:, c, oc * 128:(oc + 1) * 128], at[s][:, c, :], c == 0, c == 7,
                         reads=[cx.B("wout"), cx.B("at", s)], writes=[cx.B("psb", b)])
                p.A(lambda e: e.activation(y2[:, oc, :n], pd[:, :n], AF.Copy), reads=[cx.B("psb", b)], writes=[cx.B("y2")])
            norm_residual_store(cx, lambda c: y2[:, c, :n], [cx.B("y2")], xd_d, ("xd",), xe, ("xe",), t0, t0, n, g_m1post)
        cx.release(m0)
        alloc_ffn(cx, 1024)
        passes = [[(0, 512), (512, 512)], [(1024, 512), (1536, 512)]]
        ffn(cx, "f11", xe, ("xe",), xf, ("xf",), passes, wg_d, wu_d, wd_d, g_pre, g_post)
    return [cx.B("xf").last_w]


EX1_FIELDS = {"qraw_o": (0, 2097152, (1024, 2048)), "qrot_o": (2097152, 2097152, (1024, 2048)),
              "kvT_o": (4194304, 2097152, (4, 256, 2048)), "vtok_o": (6291456, 1048576, (2, 2048, 256)),
              "gate_o": (7340032, 98304, (48, 2048))}
NEL1 = 7438336
NEL2 = 256 * S
RG = [[0, 1, 2, 3], [4, 5, 6, 7]]


def build_fused():
    cx = Ctx("fused")
    p = cx.p
    nc = cx.nc
    voff, nv = l1_vec_layout()
    T = {}
    T["xT"] = cx.din("xT", [D, TC + HALO])
    T["vecs"] = cx.din("vecs", [128, nv])
    T["wgs"] = [cx.din(f"wg{i}", [D, DFF]) for i in range(3)]
    T["wus"] = [cx.din(f"wu{i}", [D, DFF]) for i in range(3)]
    T["wds"] = [cx.din(f"wd{i}", [DFF, D]) for i in range(3)]
    T["wpw1"] = cx.din("wpw1", [D, 2 * D])
    T["wpw2"] = cx.din("wpw2", [D, D])
    T["win"] = cx.din("win", [D, NIN])
    T["ident"] = cx.din("ident", [128, 128])
    T["prot"] = cx.din("prot", [128, 128])
    T["ropecos"] = cx.din("ropecos", [128, TC])
    T["ropesin"] = cx.din("ropesin", [128, TC])
    T["oh"] = cx.din("oh", [128, 4])
    T["w1"] = cx.din("w1", [2, 2048, 256])
    T["w2kD"] = cx.din("w2kD", [256, 128])
    T["w2vD"] = cx.din("w2vD", [256, 128])
    T["pe"] = cx.din("pe", [2, 128, 16])
    T["E"] = cx.din("E", [128, 64, 128], BF16)
    T["identb"] = cx.din("identb", [128, 128], BF16)
    T["cmask"] = cx.din("cmask", [128, 5, 512], BF16)
    T["wmask"] = cx.din("wmask", [128, 8, 512], BF16)
    T["AB"] = cx.din("AB", [128, 2, 256])
    T["ov"] = cx.din("ov", [128, 4, 130], BF16)
    T["selG"] = cx.din("selG", [12, 12, 128], BF16)
    T["vecs3"] = cx.din("vecs3", [128, 24])
    T["wout"] = cx.din("wout", [D, D])
    T["wg3"] = cx.din("wg3", [D, DFF])
    T["wu3"] = cx.din("wu3", [D, DFF])
    T["wd3"] = cx.din("wd3", [DFF, D])
    T["xf"] = cx.dout("xf", [D, TC])
    T["xa"] = nc.dram_tensor("xa", [D, TC + HALO], F32).ap()
    for nm in ("xb", "xc", "xd", "xe"):
        T[nm] = nc.dram_tensor(nm, [D, TC], F32).ap()
    EX1 = nc.dram_tensor("ex1", [1, NEL1], BF16).ap()
    EX2 = nc.dram_tensor("ex2", [1, NEL2], BF16).ap()
    CH = 524288
    ch1 = [(k * CH, min(CH, NEL1 - k * CH)) for k in range((NEL1 + CH - 1) // CH)]
    ch2 = [(k * CH, CH) for k in range(NEL2 // CH)]
    GX1L = [nc.dram_tensor(f"gx1_{k}", [4, n_], BF16).ap() for k, (o_, n_) in enumerate(ch1)]
    GX2L = [nc.dram_tensor(f"gx2_{k}", [4, n_], BF16).ap() for k, (o_, n_) in enumerate(ch2)]
    T["EX2"], T["GX1L"], T["GX2L"] = EX2, GX1L, GX2L
    for nm, (o, n, shp) in EX1_FIELDS.items():
        v = EX1[0, o:o + n]
        T[nm] = v.rearrange("(r t) -> r t", t=shp[1]) if len(shp) == 2 else v.rearrange("(k r t) -> k r t", r=shp[1], t=shp[2])

    with cx.st:
        cx.arena_init(51 * 1024)
        cx.ones = cx.carve([128, 128], BF16)
        p.V(lambda e: e.memset(cx.ones[:], 1.0), writes=[cx.B("ones")])
        cx.psb = [cx.ps(f"psb{i}", [128, 512], F32) for i in range(8)]
        cx.sq = [cx.carve([128, 512], BF16) for i in range(2)]
        cx.r32 = [cx.carve([128, 512], F32) for i in range(2)]
        mtop = cx.mark()
        alloc_small(cx)
        phase_A(cx, T)
        cx.release(mtop)
        for k, (o_, n_) in enumerate(ch1):
            p.op("gpsimd", lambda e: e.collective_compute("AllGather", ALU.bypass, RG, [EX1[:, o_:o_ + n_].opt()], [GX1L[k].opt()]),
                 writes=[cx.B("gx1")])
        p.barrier()
        phase_B(cx, T)
        cx.release(mtop)
        for k, (o_, n_) in enumerate(ch2):
            p.op("gpsimd", lambda e: e.collective_compute("AllGather", ALU.bypass, RG, [EX2[:, o_:o_ + n_].opt()], [GX2L[k].opt()]),
                 writes=[cx.B("gx2")])
        p.barrier()
        alloc_small(cx)
        fin = phase_C(cx, T)
        stuck = p.check()
        assert not stuck, stuck
        p.build(final_waits=fin)
    return cx.nc


def kernel(**inputs):
    I = {k: np.asarray(v) for k, v in inputs.items()}
    cores = list(range(8))
    consts = l2_consts()
    w2 = np.asarray(I["nsa_cmp_w2"][0], np.float32)
    shared = dict(consts)
    shared["w1"] = np.ascontiguousarray(I["nsa_cmp_w1"][0], dtype=np.float32)
    shared["w2kD"] = np.ascontiguousarray(np.concatenate([w2[0], w2[0]], 1))
    shared["w2vD"] = np.ascontiguousarray(np.concatenate([w2[1], w2[1]], 1))
    shared["pe"] = np.ascontiguousarray(np.stack([pc(np.asarray(I["nsa_cmp_pos"][0, i]).reshape(-1)) for i in range(2)], 0))
    shared["vecs3"] = np.ascontiguousarray(np.concatenate([pc(I["mix_norm_post"][1]), pc(I["ffn_norm_pre"][1, 1]), pc(I["ffn_norm_post"][1, 1])], axis=1))
    shared["wout"] = np.ascontiguousarray(I["nsa_w_out"][0], dtype=np.float32)
    shared["wg3"] = np.ascontiguousarray(I["ffn_w_gate"][1, 1], dtype=np.float32)
    shared["wu3"] = np.ascontiguousarray(I["ffn_w_up"][1, 1], dtype=np.float32)
    shared["wd3"] = np.ascontiguousarray(I["ffn_w_down"][1, 1], dtype=np.float32)
    maps = []
    for c in cores:
        m = prep_launch1(I, c)
        m.update(shared)
        oh = np.zeros((128, 4), np.float32)
        oh[:, c % 4] = 1.0
        m["oh"] = oh
        maps.append(m)
    res = run_bass_kernel_spmd(build_fused(), maps, core_ids=cores).results
    out = np.zeros((2, S, D), np.float32)
    for c in cores:
        b, j = divmod(c, 4)
        out[b, j * TC:(j + 1) * TC] = np.asarray(res[c]["xf"]).T
    return out
```

```python
import contextlib
import numpy as np
import ml_dtypes
import concourse.bass as bass
import concourse.mybir as mybir
from concourse.bass_utils import run_bass_kernel_spmd

F32 = mybir.dt.float32
BF16 = mybir.dt.bfloat16
AF = mybir.ActivationFunctionType
ALU = mybir.AluOpType

ENGS = ["tensor", "vector", "scalar", "gpsimd", "sync"]

D = 1024
DFF = 2816
NFC = 22
S = 8192
TC = 2048
HALO = 128
CW = 31
RMS_EPS = 1e-6
LN_EPS = 1e-5
NEG = -30000.0
NIN = 2608


class Buf:
    __slots__ = ("name", "last_w", "readers")

    def __init__(self, name=""):
        self.name = name
        self.last_w = None
        self.readers = []


class _Rec:
    def __init__(self):
        self.call = None

    def __getattr__(self, name):
        def f(*a, **k):
            self.call = (name, a, k)
            return None
        return f


class Prog:
    def __init__(self, nc):
        self.nc = nc
        self.ops = {e: [] for e in ENGS}
        self.cnt = {e: 0 for e in ENGS}
        self.seen = {e: {} for e in ENGS}
        self.dcnt = {}
        self.pending = {e: {} for e in ENGS}

    def barrier(self):
        snap = dict(self.cnt)
        snap.update(self.dcnt)
        for e in ENGS:
            for k, v in snap.items():
                if v > 0 and not (e == "tensor" and k == "tensor"):
                    if self.pending[e].get(k, 0) < v:
                        self.pending[e][k] = v

    def op(self, eng, fn, reads=(), writes=(), dma=None, sig=True):
        rec = _Rec()
        fn(rec)
        name_, args_, kw_ = rec.call
        fn = lambda e: getattr(e, name_)(*args_, **kw_)
        waits = dict(self.pending[eng])
        self.pending[eng] = {}

        def need(ev, war=False):
            if ev is None:
                return
            k, v = ev
            if k == eng:
                if eng == "tensor" or war:
                    return
            if waits.get(k, 0) < v:
                waits[k] = v

        for b in reads:
            need(b.last_w)
        for b in writes:
            need(b.last_w)
            for r in b.readers:
                need(r, war=True)
        w = []
        for k, v in waits.items():
            if self.seen[eng].get(k, 0) < v:
                self.seen[eng][k] = v
                w.append((k, v))
        if dma is not None:
            prev = self.dcnt.get(dma, 0)
            if prev > 0 and self.seen[eng].get(dma, 0) < prev:
                self.seen[eng][dma] = prev
                w.append((dma, prev))
            self.dcnt[dma] = self.dcnt.get(dma, 0) + 16
            ev = (dma, self.dcnt[dma])
            inc = (dma, 16)
        elif eng == "tensor" and not sig:
            ev = (eng, self.cnt[eng] + 1)
            inc = None
        else:
            self.cnt[eng] += 1
            ev = (eng, self.cnt[eng])
            inc = (eng, 1)
        self.ops[eng].append((fn, w, inc))
        for b in reads:
            b.readers.append(ev)
            if len(b.readers) > 64:
                best = {}
                for k, v in b.readers:
                    if best.get(k, 0) < v:
                        best[k] = v
                b.readers = list(best.items())
        for b in writes:
            b.last_w = ev
            b.readers = []
        return ev

    def mm(self, out, lhsT, rhs, start, stop, reads=(), writes=(), sig=None, **kw):
        if sig is None:
            sig = stop
        return self.op("tensor", lambda e: e.matmul(out, lhsT, rhs, start=start, stop=stop, **kw),
                       reads=reads, writes=writes, sig=sig)

    def dma(self, eng, out, in_, sem, reads=(), writes=(), **kw):
        return self.op(eng, lambda e: e.dma_start(out=out, in_=in_, **kw), reads=reads, writes=writes, dma=sem)

    def V(self, fn, reads=(), writes=()):
        return self.op("vector", fn, reads, writes)

    def A(self, fn, reads=(), writes=()):
        return self.op("scalar", fn, reads, writes)

    def check(self):
        sem = {}
        pos = {e: 0 for e in ENGS}
        n = {e: len(self.ops[e]) for e in ENGS}
        progress = True
        while progress:
            progress = False
            for e in ENGS:
                while pos[e] < n[e]:
                    fn, w, inc = self.ops[e][pos[e]]
                    if all(sem.get(k, 0) >= v for k, v in w):
                        if inc is not None:
                            sem[inc[0]] = sem.get(inc[0], 0) + inc[1]
                        pos[e] += 1
                        progress = True
                    else:
                        break
        stuck = {e: (pos[e], n[e], [(k, v, sem.get(k, 0)) for k, v in self.ops[e][pos[e]][1]]) for e in ENGS if pos[e] < n[e]}
        return stuck

    def build(self, final_waits=()):
        nc = self.nc
        names = list(ENGS) + sorted(self.dcnt.keys())
        with contextlib.ExitStack() as st:
            sems = {n: st.enter_context(nc.semaphore("s_" + n)) for n in names}
            block = st.enter_context(nc.Block())
            fw = {}
            for ev in final_waits:
                if ev is not None and fw.get(ev[0], 0) < ev[1]:
                    fw[ev[0]] = ev[1]
            for eng in ENGS:
                ops = self.ops[eng]
                if eng == "sync":
                    ops = ops + [(None, list(fw.items()), None)]
                if not ops:
                    continue

                def body(e, ops=ops):
                    for fn, w, inc in ops:
                        for k, v in w:
                            e.wait_ge(sems[k], v)
                        if fn is None:
                            continue
                        ins = fn(e)
                        if inc is not None:
                            ins.then_inc(sems[inc[0]], inc[1])

                getattr(block, eng)(body)


class Ctx:
    def __init__(self, name):
        self.nc = bass.Bass("TRN2", target_bir_lowering=False)
        self.p = Prog(self.nc)
        self.st = contextlib.ExitStack()
        self.bufs = {}
        self.outs = []
        self.rr = {}

    def din(self, name, shape, dt=F32):
        return self.nc.dram_tensor(name, list(shape), dt, kind="ExternalInput").ap()

    def dout(self, name, shape, dt=F32):
        return self.nc.dram_tensor(name, list(shape), dt, kind="ExternalOutput").ap()

    def sb(self, name, shape, dt):
        return self.st.enter_context(self.nc.sbuf_tensor(name, list(shape), dt))

    def ps(self, name, shape, dt=F32):
        return self.st.enter_context(self.nc.psum_tensor(name, list(shape), dt))

    def arena_init(self, nwords):
        self.arena = self.sb("arena", [128, nwords], F32)
        self.top = 0
        self.nwords = nwords

    def carve(self, shape, dt):
        nfree = 1
        for d in shape[1:]:
            nfree *= d
        words = nfree if dt == F32 else (nfree + 1) // 2
        a = self.top
        self.top += words
        assert self.top <= self.nwords, ("arena overflow", self.top, self.nwords)
        ap = self.arena[:, a:a + words]
        if dt != F32:
            ap = ap.bitcast(dt)[:, :nfree]
        if len(shape) == 3:
            ap = ap.rearrange("p (a b) -> p a b", b=shape[2])
        elif len(shape) == 4:
            ap = ap.rearrange("p (a b c) -> p a b c", b=shape[2], c=shape[3])
        if shape[0] < 128:
            ap = ap[0:shape[0]]
        return ap

    def mark(self):
        return self.top

    def release(self, m):
        self.top = m
        self.p.barrier()

    def B(self, *key):
        b = self.bufs.get(key)
        if b is None:
            b = self.bufs[key] = Buf(str(key))
        return b

    def rot(self, key, n):
        i = self.rr.get(key, 0)
        self.rr[key] = (i + 1) % n
        return i


def pc(v):
    v = np.asarray(v, np.float32)
    return np.ascontiguousarray(v.reshape(-1, 128).T)


def setup_common(cx, n_ps=8):
    cx.arena_init(50 * 1024)
    cx.ones = cx.carve([128, 128], BF16)
    cx.p.V(lambda e: e.memset(cx.ones[:], 1.0), writes=[cx.B("ones")])
    cx.psb = [cx.ps(f"psb{i}", [128, 512], F32) for i in range(n_ps)]
    cx.sq = [cx.carve([128, 512], BF16) for i in range(2)]
    cx.r32 = [cx.carve([128, 512], F32) for i in range(2)]


def rms_stats(cx, src, srcbufs, n, psi, eps_scaled, nch=8):
    p = cx.p
    ps = cx.psb[psi]
    for c in range(nch):
        i = cx.rot("sq", 2)
        sq = cx.sq[i]
        p.A(lambda e, c=c, sq=sq: e.activation(sq[:, :n], src(c), AF.Square), reads=srcbufs, writes=[cx.B("sq", i)])
        p.mm(ps[:, :n], cx.ones[:], sq[:, :n], c == 0, c == nch - 1, reads=[cx.B("sq", i), cx.B("ones")],
             writes=[cx.B("psb", psi)], sig=True)
    j = cx.rot("r32", 2)
    r = cx.r32[j]
    p.A(lambda e: e.activation(r[:, :n], ps[:, :n], AF.Sqrt, bias=float(eps_scaled), scale=1.0),
        reads=[cx.B("psb", psi)], writes=[cx.B("r32", j)])
    p.V(lambda e: e.reciprocal(r[:, :n], r[:, :n]), reads=[cx.B("r32", j)], writes=[cx.B("r32", j)])
    return r, cx.B("r32", j)


def load_x_group(cx, xdram, xbufkey, off, n):
    i = cx.rot("xin", 2)
    cx.xin = cx.xins[i]
    cx.xinb = cx.B("xin", i)
    src = xdram.rearrange("(c p) t -> p c t", p=128)[:, :, off:off + n]
    cx.p.dma("sync", cx.xin[:, :, :n], src, f"xin{i}", reads=[cx.B(*xbufkey)], writes=[cx.xinb])


def norm_to_h(cx, hdst, hbuf, g32col, n, psi=6):
    xin, xinb = cx.xin, cx.xinb
    r, rb = rms_stats(cx, lambda c: xin[:, c, :n], [xinb], n, psi, D * RMS_EPS)
    for c in range(8):
        cx.p.V(lambda e, c=c: e.scalar_tensor_tensor(hdst(c), xin[:, c, :n], g32col(c), r[:, :n], ALU.mult, ALU.mult),
               reads=[xinb, rb, cx.B("vecs")], writes=[hbuf])


def norm_residual_store(cx, ysrc, ybufs, xin_dram, xin_key, xout_dram, xout_key, off_in, off_out, n, gcol, psi=6):
    p = cx.p
    r, rb = rms_stats(cx, ysrc, ybufs, n, psi, D * RMS_EPS)
    load_x_group(cx, xin_dram, xin_key, off_in, n)
    xin, xinb = cx.xin, cx.xinb
    for c in range(8):
        i = cx.rot("ntmp", 2)
        t = cx.ntmp[i]
        p.V(lambda e, c=c, t=t: e.scalar_tensor_tensor(t[:, :n], ysrc(c), gcol(c), r[:, :n], ALU.mult, ALU.mult),
            reads=ybufs + [rb, cx.B("vecs")], writes=[cx.B("ntmp", i)])
        p.V(lambda e, c=c, t=t: e.tensor_tensor(xin[:, c, :n], xin[:, c, :n], t[:, :n], ALU.add),
            reads=[cx.B("ntmp", i), xinb], writes=[xinb])
    dst = xout_dram.rearrange("(c p) t -> p c t", p=128)[:, :, off_out:off_out + n]
    return p.dma("sync", dst, xin[:, :, :n], "xout", reads=[xinb], writes=[cx.B(*xout_key)])


def alloc_small(cx):
    cx.xins = [cx.carve([128, 8, 512], F32) for i in range(2)]
    cx.sg = [cx.carve([128, 512], F32) for i in range(2)]
    cx.ntmp = [cx.carve([128, 512], F32) for i in range(2)]


def alloc_ffn(cx, maxtok):
    cx.hT = [cx.carve([128, 8, maxtok], BF16) for i in range(2)]
    cx.aT = cx.carve([128, NFC, maxtok], BF16)
    cx.ytmp = cx.carve([128, 8, maxtok], F32)
    cx.wg = [cx.carve([128, 8, 256], BF16) for i in range(2)]
    cx.wu = [cx.carve([128, 8, 256], BF16) for i in range(2)]
    cx.wd = [cx.carve([128, NFC, 128], BF16) for i in range(2)]


def ffn(cx, tag, xin_dram, xin_key, xout_dram, xout_key, passes, wg_d, wu_d, wd_d, gpre, gpost, out_shift=0):
    p = cx.p
    last = None
    hTs, aT, ytmp, wg, wu, wd = cx.hT, cx.aT, cx.ytmp, cx.wg, cx.wu, cx.wd

    def locs(groups):
        loc, o = [], 0
        for (off, n) in groups:
            loc.append(o)
            o += n
        return loc

    def S1(k):
        groups = passes[k]
        loc = locs(groups)
        hT = hTs[k % 2]
        for gi, (off, n) in enumerate(groups):
            load_x_group(cx, xin_dram, xin_key, off, n)
            lo = loc[gi]
            norm_to_h(cx, lambda c, lo=lo, n=n: hT[:, c, lo:lo + n], cx.B("hT", k % 2), gpre, n)

    def S2(k):
        groups = passes[k]
        loc = locs(groups)
        hT, hb = hTs[k % 2], cx.B("hT", k % 2)
        for fp in range(NFC // 2):
            s = cx.rot("wgu", 2)
            p.dma("gpsimd", wg[s][:], wg_d.rearrange("(c p) f -> p c f", p=128)[:, :, fp * 256:(fp + 1) * 256],
                  f"wg{s}", writes=[cx.B("wg", s)])
            p.dma("gpsimd", wu[s][:], wu_d.rearrange("(c p) f -> p c f", p=128)[:, :, fp * 256:(fp + 1) * 256],
                  f"wu{s}", writes=[cx.B("wu", s)])
            for h in range(2):
                fc = fp * 2 + h
                for gi, (off, n) in enumerate(groups):
                    lo = loc[gi]
                    b = cx.rot("gu", 2)
                    pg, pu = cx.psb[b], cx.psb[2 + b]
                    for c in range(8):
                        p.mm(pg[:, :n], wg[s][:, c, h * 128:(h + 1) * 128], hT[:, c, lo:lo + n], c == 0, c == 7,
                             reads=[cx.B("wg", s), hb], writes=[cx.B("psb", b)])
                    for c in range(8):
                        p.mm(pu[:, :n], wu[s][:, c, h * 128:(h + 1) * 128], hT[:, c, lo:lo + n], c == 0, c == 7,
                             reads=[cx.B("wu", s), hb], writes=[cx.B("psb", 2 + b)])
                    sg = cx.sg[b]
                    p.A(lambda e: e.activation(sg[:, :n], pg[:, :n], AF.Silu),
                        reads=[cx.B("psb", b)], writes=[cx.B("sg", b)])
                    p.V(lambda e: e.tensor_tensor(aT[:, fc, lo:lo + n], pu[:, :n], sg[:, :n], ALU.mult),
                        reads=[cx.B("psb", 2 + b), cx.B("sg", b)], writes=[cx.B("aT")])

    def S3(k):
        groups = passes[k]
        loc = locs(groups)
        for dc in range(8):
            s = cx.rot("wd", 2)
            p.dma("gpsimd", wd[s][:], wd_d.rearrange("(fc p) d -> p fc d", p=128)[:, :, dc * 128:(dc + 1) * 128],
                  f"wd{s}", writes=[cx.B("wd", s)])
            for gi, (off, n) in enumerate(groups):
                lo = loc[gi]
                b = 4 + cx.rot("dn", 2)
                pd = cx.psb[b]
                for fc in range(NFC):
                    p.mm(pd[:, :n], wd[s][:, fc, :], aT[:, fc, lo:lo + n], fc == 0, fc == NFC - 1,
                         reads=[cx.B("wd", s), cx.B("aT")], writes=[cx.B("psb", b)])
                p.A(lambda e: e.activation(ytmp[:, dc, lo:lo + n], pd[:, :n], AF.Copy),
                    reads=[cx.B("psb", b)], writes=[cx.B("ytmp")])

    def S4(k):
        nonlocal last
        groups = passes[k]
        loc = locs(groups)
        for gi, (off, n) in enumerate(groups):
            lo = loc[gi]
            last = norm_residual_store(cx, lambda c, lo=lo, n=n: ytmp[:, c, lo:lo + n], [cx.B("ytmp")],
                                       xin_dram, xin_key, xout_dram, xout_key, off, off - out_shift, n, gpost)

    if passes:
        S1(0)
    for k in range(len(passes)):
        S2(k)
        if k + 1 < len(passes):
            S1(k + 1)
        S3(k)
        S4(k)
    return last


def l1_vec_layout():
    names = [("f00pre", 8), ("f00post", 8), ("m0pre", 8), ("bpw1", 16), ("wdw", 8 * CW), ("bdw", 8), ("lng", 8),
             ("lnb", 8), ("bpw2", 8), ("m0post", 8), ("f01pre", 8), ("f01post", 8), ("f10pre", 8), ("f10post", 8),
             ("m1pre", 8), ("bgate", 1), ("flag", 1)]
    off = {}
    o = 0
    for n, k in names:
        off[n] = (o, k)
        o += k
    return off, o


def phase_A(cx, T):
    p = cx.p
    TT = TC + HALO
    voff, nv = l1_vec_layout()
    xT, vecs_d, wgs, wus, wds = T["xT"], T["vecs"], T["wgs"], T["wus"], T["wds"]
    wpw1_d, wpw2_d, win_d, ident_d, prot_d, cos_d, sin_d = T["wpw1"], T["wpw2"], T["win"], T["ident"], T["prot"], T["ropecos"], T["ropesin"]
    xa, xb, xc, xd = T["xa"], T["xb"], T["xc"], T["xd"]
    qraw_o, qrot_o, kvT_o, vtok_o, gate_o = T["qraw_o"], T["qrot_o"], T["kvT_o"], T["vtok_o"], T["gate_o"]
    if True:
        vecs = cx.carve([128, nv], F32)
        p.dma("sync", vecs[:], vecs_d, "const", writes=[cx.B("vecs")])

        def vcol(name, scale=None):
            o, k = voff[name]
            if scale is not None:
                p.V(lambda e: e.tensor_scalar(vecs[:, o:o + k], vecs[:, o:o + k], float(scale), None, ALU.mult),
                    reads=[cx.B("vecs")], writes=[cx.B("vecs")])
            return lambda c: vecs[:, o + c:o + c + 1]

        g_f00pre = vcol("f00pre", 32.0)
        g_f00post = vcol("f00post", 16.0)
        g_m0pre = vcol("m0pre", 32.0)
        g_m0post = vcol("m0post", 32.0)
        g_f01pre = vcol("f01pre", 32.0)
        g_f01post = vcol("f01post", 16.0)
        g_f10pre = vcol("f10pre", 32.0)
        g_f10post = vcol("f10post", 16.0)
        g_m1pre = vcol("m1pre", 32.0)
        bpw1 = vcol("bpw1")
        bdw = vcol("bdw")
        lng = vcol("lng")
        lnb = vcol("lnb")
        bpw2 = vcol("bpw2")
        wdw_o = voff["wdw"][0]
        bgate_o = voff["bgate"][0]
        flag_o = voff["flag"][0]

        identb = cx.carve([128, 128], BF16)
        p.dma("gpsimd", identb[:], ident_d, "const2", writes=[cx.B("identb")])
        prot = cx.carve([128, 128], BF16)
        p.dma("gpsimd", prot[:], prot_d, "const2", writes=[cx.B("prot")])
        SKIP = False
        m0 = cx.mark()
        alloc_ffn(cx, 1152)
        passes0 = [] if SKIP else [[(0, 128), (128, 512), (640, 512)], [(1152, 512), (1664, 512)]]
        ffn(cx, "f00", xT, ("xT",), xa, ("xa",), passes0, wgs[0], wus[0], wds[0], g_f00pre, g_f00post)
        cx.release(m0)

        wpw1 = cx.carve([128, 8, 2 * D], BF16)
        wpw2 = cx.carve([128, 8, D], BF16)
        for c in range(8):
            p.dma("gpsimd", wpw1[:, c, :], wpw1_d[c * 128:(c + 1) * 128, :], f"wpw{c % 4}", writes=[cx.B("wpw1")])
        for c in range(8):
            p.dma("gpsimd", wpw2[:, c, :], wpw2_d[c * 128:(c + 1) * 128, :], f"wpw{c % 4}", writes=[cx.B("wpw2")])
        glu = cx.carve([128, 8, TT], BF16)
        vbs = [cx.carve([128, 512], BF16) for q in range(2)]
        y2 = cx.carve([128, 8, 512], F32)
        dg = [cx.carve([128, CW, 128], BF16) for i in range(2)]
        hc = cx.carve([128, 8, 512], BF16)
        vt = cx.carve([128, 8, 512], F32)
        sc = cx.carve([128, 8, 512], BF16)
        for (off, n) in ([] if SKIP else [(0, 128), (128, 512), (640, 512), (1152, 512), (1664, 512)]):
            load_x_group(cx, xa, ("xa",), off, n)
            norm_to_h(cx, lambda c, n=n: hc[:, c, :n], cx.B("hT"), g_m0pre, n)
            for oc in range(8):
                b = cx.rot("gu", 2)
                pa, pg = cx.psb[b], cx.psb[2 + b]
                for c in range(8):
                    p.mm(pa[:, :n], wpw1[:, c, oc * 128:(oc + 1) * 128], hc[:, c, :n], c == 0, c == 7,
                         reads=[cx.B("wpw1"), cx.B("hT")], writes=[cx.B("psb", b)])
                for c in range(8):
                    p.mm(pg[:, :n], wpw1[:, c, D + oc * 128:D + (oc + 1) * 128], hc[:, c, :n], c == 0, c == 7,
                         reads=[cx.B("wpw1"), cx.B("hT")], writes=[cx.B("psb", 2 + b)])
                sg = cx.sg[b]
                p.A(lambda e, sg=sg, pg=pg, n=n, oc=oc: e.activation(sg[:, :n], pg[:, :n], AF.Sigmoid, bias=bpw1(8 + oc)),
                    reads=[cx.B("psb", 2 + b), cx.B("vecs")], writes=[cx.B("sg", b)])
                p.V(lambda e, sg=sg, pa=pa, n=n, oc=oc, off=off: e.scalar_tensor_tensor(
                    glu[:, oc, off:off + n], pa[:, :n], bpw1(oc), sg[:, :n], ALU.add, ALU.mult),
                    reads=[cx.B("psb", b), cx.B("sg", b), cx.B("vecs")], writes=[cx.B("glu")])
            if off == 0:
                for oc in range(8):
                    p.V(lambda e, oc=oc: e.tensor_scalar(glu[:, oc, 0:128], glu[:, oc, 0:128], vecs[:, flag_o:flag_o + 1], None, ALU.mult),
                        reads=[cx.B("glu"), cx.B("vecs")], writes=[cx.B("glu")])
        for g in range(0 if SKIP else 4):
            t0 = HALO + g * 512
            n = 512
            for cc in range(8):
                di = cx.rot("dg", 2)
                for k in range(CW):
                    p.V(lambda e, k=k, cc=cc, di=di: e.tensor_scalar(dg[di][:, k, :], identb[:], vecs[:, wdw_o + k * 8 + cc:wdw_o + k * 8 + cc + 1], None, ALU.mult),
                        reads=[cx.B("identb"), cx.B("vecs")], writes=[cx.B("dg", di)])
                b = 4 + cx.rot("dn", 2)
                pd = cx.psb[b]
                for k in range(CW):
                    s0 = t0 - (CW - 1) + k
                    p.mm(pd[:, :n], dg[di][:, k, :], glu[:, cc, s0:s0 + n], k == 0, k == CW - 1,
                         reads=[cx.B("dg", di), cx.B("glu")], writes=[cx.B("psb", b)])
                p.A(lambda e, pd=pd, cc=cc: e.activation(vt[:, cc, :n], pd[:, :n], AF.Identity, bias=bdw(cc)),
                    reads=[cx.B("psb", b), cx.B("vecs")], writes=[cx.B("ytmp")])
            psm, psq = cx.psb[6], cx.psb[7]
            for cc in range(8):
                i = cx.rot("sq", 2)
                sq = cx.sq[i]
                p.A(lambda e, cc=cc, sq=sq: e.activation(sq[:, :n], vt[:, cc, :n], AF.Square), reads=[cx.B("ytmp")], writes=[cx.B("sq", i)])
                p.mm(psq[:, :n], cx.ones[:], sq[:, :n], cc == 0, cc == 7, reads=[cx.B("sq", i), cx.B("ones")], writes=[cx.B("psb", 7)], sig=True)
                j = cx.rot("vb", 2)
                vb = vbs[j]
                p.V(lambda e, cc=cc, vb=vb: e.tensor_copy(vb[:, :n], vt[:, cc, :n]), reads=[cx.B("ytmp")], writes=[cx.B("vb", j)])
                p.mm(psm[:, :n], cx.ones[:], vb[:, :n], cc == 0, cc == 7, reads=[cx.B("vb", j), cx.B("ones")], writes=[cx.B("psb", 6)], sig=True)
            mean, m2 = cx.r32[0], cx.r32[1]
            rs = cx.ntmp[0]
            p.V(lambda e: e.tensor_scalar(mean[:, :n], psm[:, :n], 1.0 / D, None, ALU.mult), reads=[cx.B("psb", 6)], writes=[cx.B("r32", 0)])
            p.V(lambda e: e.tensor_tensor(m2[:, :n], mean[:, :n], mean[:, :n], ALU.mult), reads=[cx.B("r32", 0)], writes=[cx.B("r32", 1)])
            p.V(lambda e: e.scalar_tensor_tensor(rs[:, :n], psq[:, :n], 1.0 / D, m2[:, :n], ALU.mult, ALU.subtract),
                reads=[cx.B("psb", 7), cx.B("r32", 1)], writes=[cx.B("ntmp", 0)])
            p.A(lambda e: e.activation(rs[:, :n], rs[:, :n], AF.Sqrt, bias=float(LN_EPS), scale=1.0), reads=[cx.B("ntmp", 0)], writes=[cx.B("ntmp", 0)])
            p.V(lambda e: e.reciprocal(rs[:, :n], rs[:, :n]), reads=[cx.B("ntmp", 0)], writes=[cx.B("ntmp", 0)])
            for cc in range(8):
                p.V(lambda e, cc=cc: e.tensor_tensor(vt[:, cc, :n], vt[:, cc, :n], mean[:, :n], ALU.subtract),
                    reads=[cx.B("ytmp"), cx.B("r32", 0)], writes=[cx.B("ytmp")])
                p.V(lambda e, cc=cc: e.tensor_tensor(vt[:, cc, :n], vt[:, cc, :n], rs[:, :n], ALU.mult),
                    reads=[cx.B("ytmp"), cx.B("ntmp", 0)], writes=[cx.B("ytmp")])
                p.A(lambda e, cc=cc: e.activation(sc[:, cc, :n], vt[:, cc, :n], AF.Silu, bias=lnb(cc), scale=lng(cc)),
                    reads=[cx.B("ytmp"), cx.B("vecs")], writes=[cx.B("aT")])
            for oc in range(8):
                b = 4 + cx.rot("dn", 2)
                pd = cx.psb[b]
                for c in range(8):
                    p.mm(pd[:, :n], wpw2[:, c, oc * 128:(oc + 1) * 128], sc[:, c, :n], c == 0, c == 7,
                         reads=[cx.B("wpw2"), cx.B("aT")], writes=[cx.B("psb", b)])
                p.A(lambda e, pd=pd, oc=oc: e.activation(y2[:, oc, :n], pd[:, :n], AF.Identity, bias=bpw2(oc)),
                    reads=[cx.B("psb", b), cx.B("vecs")], writes=[cx.B("y2")])
            norm_residual_store(cx, lambda c: y2[:, c, :n], [cx.B("y2")], xa, ("xa",), xb, ("xb",), t0, t0 - HALO, n, g_m0post)

        cx.release(m0)
        alloc_ffn(cx, 1024)
        passes = [] if SKIP else [[(0, 512), (512, 512)], [(1024, 512), (1536, 512)]]
        ffn(cx, "f01", xb, ("xb",), xc, ("xc",), passes, wgs[1], wus[1], wds[1], g_f01pre, g_f01post)
        ffn(cx, "f10", xc, ("xc",), xd, ("xd",), passes, wgs[2], wus[2], wds[2], g_f10pre, g_f10post)

        cx.release(m0)
        hc = cx.carve([128, 8, 512], BF16)
        win = cx.carve([128, 8, NIN], BF16)
        for c in range(8):
            p.dma("gpsimd", win[:, c, :], win_d[c * 128:(c + 1) * 128, :], f"wpw{c % 4}", writes=[cx.B("win")])
        cosT = cx.carve([128, TC], F32)
        sinT = cx.carve([128, TC], F32)
        p.dma("sync", cosT[:], cos_d, "const", writes=[cx.B("cs")])
        p.dma("sync", sinT[:], sin_d, "const", writes=[cx.B("cs")])
        qrawS = [cx.carve([128, 8, 512], BF16) for i in range(2)]
        qrotS = [cx.carve([128, 8, 512], BF16) for i in range(2)]
        kvS = [cx.carve([128, 8, 512], BF16) for i in range(2)]
        vtk = [cx.carve([128, 512], BF16) for i in range(2)]
        gsb = [cx.carve([128, 512], BF16) for i in range(2)]
        vS = [cx.carve([128, 2, 4, 256], BF16) for i in range(2)]
        outs = []
        PARTS = ["fm", "v", "g"]
        for g in range(4):
            t0 = g * 512
            n = 512
            load_x_group(cx, xd, ("xd",), t0, n)
            norm_to_h(cx, lambda c: hc[:, c, :n], cx.B("hT"), g_m1pre, n)
            so = cx.rot("nsao", 2)
            fm = [(oc, "q") for oc in range(8)] + [(8, "kc"), (9, "kc"), (10, "vc"), (11, "vc"), (12, "ks"), (13, "ks"), (16, "kw"), (17, "kw")]
            kvidx = {"kc": 0, "vc": 2, "ks": 4, "kw": 6}
            for (oc, kind) in (fm if "fm" in PARTS else []):
                b = cx.rot("gu", 2)
                pq = cx.psb[b]
                for c in range(8):
                    p.mm(pq[:, :n], win[:, c, oc * 128:(oc + 1) * 128], hc[:, c, :n], c == 0, c == 7,
                         reads=[cx.B("win"), cx.B("hT")], writes=[cx.B("psb", b)])
                if kind == "q":
                    raw, rawb = qrawS[so][:, oc, :], cx.B("qrawS", so)
                    rot_, rotb = qrotS[so][:, oc, :], cx.B("qrotS", so)
                elif kind in ("kc", "vc"):
                    raw, rawb = kvS[so][:, kvidx[kind] + oc % 2, :], cx.B("kvS", so)
                else:
                    i = cx.rot("qb", 2)
                    raw, rawb = vtk[i][:, :], cx.B("vtk", i)
                    rot_, rotb = kvS[so][:, kvidx[kind] + oc % 2, :], cx.B("kvS", so)
                p.A(lambda e, raw=raw, pq=pq: e.activation(raw, pq[:, :n], AF.Copy), reads=[cx.B("psb", b)], writes=[rawb])
                if kind in ("q", "ks", "kw"):
                    b2 = 2 + cx.rot("rp", 2)
                    pr = cx.psb[b2]
                    p.mm(pr[:, :n], prot[:], raw, True, True, reads=[cx.B("prot"), rawb], writes=[cx.B("psb", b2)])
                    ti = cx.rot("ntmp", 2)
                    t1, t1b = cx.ntmp[ti], cx.B("ntmp", ti)
                    si = cx.rot("sgr", 2)
                    t2, t2b = cx.sg[si], cx.B("sg", si)
                    p.V(lambda e, t1=t1, raw=raw: e.tensor_tensor(t1[:, :n], raw, cosT[:, t0:t0 + n], ALU.mult),
                        reads=[rawb, cx.B("cs")], writes=[t1b])
                    p.V(lambda e, t2=t2, pr=pr: e.tensor_tensor(t2[:, :n], pr[:, :n], sinT[:, t0:t0 + n], ALU.mult),
                        reads=[cx.B("psb", b2), cx.B("cs")], writes=[t2b])
                    p.V(lambda e, t1=t1, t2=t2, rot_=rot_: e.tensor_tensor(rot_, t1[:, :n], t2[:, :n], ALU.add),
                        reads=[t1b, t2b], writes=[rotb])
            if "fm" in PARTS:
                outs.append(p.dma("sync", qraw_o.rearrange("(c p) t -> p c t", p=128)[:, :, t0:t0 + n], qrawS[so][:], f"qo{so}a", reads=[cx.B("qrawS", so)]))
                outs.append(p.dma("sync", qrot_o.rearrange("(c p) t -> p c t", p=128)[:, :, t0:t0 + n], qrotS[so][:], f"qo{so}b", reads=[cx.B("qrotS", so)]))
                outs.append(p.dma("sync", kvT_o.rearrange("s (a p) t -> p (s a) t", p=128)[:, :, t0:t0 + n], kvS[so][:], f"qo{so}c", reads=[cx.B("kvS", so)]))
            for vi, c0 in enumerate((1792, 2304) if "v" in PARTS else ()):
                for tt in range(4):
                    b = 4 + cx.rot("dn", 2)
                    pv = cx.psb[b]
                    for c in range(8):
                        p.mm(pv[:, :256], hc[:, c, tt * 128:(tt + 1) * 128], win[:, c, c0:c0 + 256], c == 0, c == 7,
                             reads=[cx.B("win"), cx.B("hT")], writes=[cx.B("psb", b)])
                    p.A(lambda e, pv=pv, vi=vi, tt=tt: e.activation(vS[so][:, vi, tt, :], pv[:, :256], AF.Copy), reads=[cx.B("psb", b)], writes=[cx.B("vS", so)])
            if "v" in PARTS:
                for vi in range(2):
                    outs.append(p.dma("sync", vtok_o[vi, t0:t0 + n, :].rearrange("(tt p) c -> p tt c", p=128), vS[so][:, vi, :, :], f"qo{so}v{vi}", reads=[cx.B("vS", so)]))
            if "g" not in PARTS:
                continue
            b = cx.rot("gu", 2)
            pq = cx.psb[b]
            for c in range(8):
                p.mm(pq[0:48, :n], win[:, c, 2560:2608], hc[:, c, :n], c == 0, c == 7,
                     reads=[cx.B("win"), cx.B("hT")], writes=[cx.B("psb", b)])
            i = cx.rot("gsb", 2)
            p.A(lambda e, i=i, pq=pq: e.activation(gsb[i][0:48, :n], pq[0:48, :n], AF.Sigmoid, bias=vecs[0:48, bgate_o:bgate_o + 1]),
                reads=[cx.B("psb", b), cx.B("vecs")], writes=[cx.B("gsb", i)])
            outs.append(p.dma("sync", gate_o[:, t0:t0 + n], gsb[i][0:48, :n], "gout", reads=[cx.B("gsb", i)]))

        cx.release(m0)
    return outs


def rope_tables_np(pos0, n):
    pos = (pos0 + np.arange(n)).astype(np.float32)
    inv = (np.float32(500000.0) ** (-np.arange(0, 16, 2, dtype=np.float32) / np.float32(16))).astype(np.float32)
    ang = pos[None, :] * inv[:, None]
    c8, s8 = np.cos(ang).astype(np.float32), np.sin(ang).astype(np.float32)
    cos = np.ones((128, n), np.float32)
    sin = np.zeros((128, n), np.float32)
    for hb in (0, 64):
        cos[hb:hb + 8] = c8
        cos[hb + 8:hb + 16] = c8
        sin[hb:hb + 8] = s8
        sin[hb + 8:hb + 16] = s8
    return cos, sin


def prot_np():
    pr = np.zeros((128, 128), np.float32)
    for hb in (0, 64):
        for j in range(8):
            pr[hb + j + 8, hb + j] = -1.0
            pr[hb + j, hb + 8 + j] = 1.0
    return pr


def prep_launch1(I, core):
    b, j = divmod(core, 4)
    t0 = j * TC
    x = I["x"][b]
    xT = np.zeros((D, TC + HALO), np.float32)
    xT[:, HALO:] = x[t0:t0 + TC].T
    if j > 0:
        xT[:, :HALO] = x[t0 - HALO:t0].T
    voff, nv = l1_vec_layout()
    vecs = np.zeros((128, nv), np.float32)

    def put(name, arr):
        o, k = voff[name]
        vecs[:, o:o + k] = arr

    put("f00pre", pc(I["ffn_norm_pre"][0, 0]))
    put("f00post", pc(I["ffn_norm_post"][0, 0]))
    put("m0pre", pc(I["mix_norm_pre"][0]))
    put("bpw1", pc(I["conv_b_pw1"][0]))
    put("wdw", np.concatenate([pc(I["conv_w_dw"][0, k]) for k in range(CW)], axis=1))
    put("bdw", pc(I["conv_b_dw"][0]))
    put("lng", pc(I["conv_ln_g"][0]))
    put("lnb", pc(I["conv_ln_b"][0]))
    put("bpw2", pc(I["conv_b_pw2"][0]))
    put("m0post", pc(I["mix_norm_post"][0]))
    put("f01pre", pc(I["ffn_norm_pre"][0, 1]))
    put("f01post", pc(I["ffn_norm_post"][0, 1]))
    put("f10pre", pc(I["ffn_norm_pre"][1, 0]))
    put("f10post", pc(I["ffn_norm_post"][1, 0]))
    put("m1pre", pc(I["mix_norm_pre"][1]))
    bg = np.zeros((128, 1), np.float32)
    bg[:48, 0] = I["nsa_b_gate"][0]
    put("bgate", bg)
    put("flag", np.full((128, 1), 0.0 if j == 0 else 1.0, np.float32))
    cos, sin = rope_tables_np(t0, TC)
    m = {"xT": xT, "vecs": vecs, "wpw1": I["conv_w_pw1"][0], "wpw2": I["conv_w_pw2"][0], "win": I["nsa_w_in"][0],
         "ident": np.eye(128, dtype=np.float32), "prot": prot_np(), "ropecos": cos, "ropesin": sin}
    for i, (l, h) in enumerate(((0, 0), (0, 1), (1, 0))):
        m[f"wg{i}"] = I["ffn_w_gate"][l, h]
        m[f"wu{i}"] = I["ffn_w_up"][l, h]
        m[f"wd{i}"] = I["ffn_w_down"][l, h]
    return {k: np.ascontiguousarray(v, dtype=np.float32) for k, v in m.items()}


def l2_consts():
    bf = ml_dtypes.bfloat16
    E = np.zeros((128, 64, 128), np.float32)
    for jt in range(64):
        E[2 * jt, jt, 0:64] = 1.0
        E[2 * jt + 1, jt, 64:128] = 1.0
    i = np.arange(128)[:, None]
    j = np.arange(512)[None, :]
    cmask = np.zeros((128, 5, 512), np.float32)
    for d in range(5):
        cmask[:, d, :] = np.where(16 * i + 31 - 512 * d <= j, 0.0, NEG)
    wmask = np.zeros((128, 8, 512), np.float32)
    for oi in range(8):
        k = 128 * (oi - 4) + i
        wmask[:, oi, :] = np.where((k <= j) & (k > j - 512), 0.0, NEG)
    AB = np.zeros((128, 2, 256), np.float32)
    jj = np.arange(128)[:, None]
    m = np.arange(256)[None, :]
    x = (m - 128) - (jj >= 64)
    forced = (x == 0) | (x == -1)
    nonc = x > 0
    AB[:, 0, :] = np.where(forced | nonc, 0.0, 1.0)
    AB[:, 1, :] = np.where(forced, 1e9, np.where(nonc, -1e9, 0.0))
    ov = np.zeros((128, 4, 130), np.float32)
    for ct in range(4):
        for ii in range(128):
            c = 128 * ct + ii
            if c > 510:
                continue
            for n in range(128):
                if 16 * c < 64 * n + 64 and 16 * c + 32 > 64 * n:
                    ov[ii, ct, n] = 1.0
            ov[ii, ct, 128] = 1.0
    selG = np.zeros((12, 12, 128), np.float32)
    for r in range(12):
        selG[r, r, :] = 1.0
    return {"E": E.astype(bf), "identb": np.eye(128, dtype=np.float32).astype(bf), "cmask": cmask.astype(bf),
            "wmask": wmask.astype(bf), "AB": AB, "ov": ov.astype(bf), "selG": selG.astype(bf)}


NQG = 16
SCALE = 0.125


def phase_B(cx, T):
    p = cx.p
    oh_d = T["oh"]
    w1_d, w2k_d, w2v_d, pe_d = T["w1"], T["w2kD"], T["w2vD"], T["pe"]
    EX2 = T["EX2"]

    GXL = T["GX1L"]
    oT4 = EX2[0, :].rearrange("(j r t) -> j r t", j=4, t=TC)

    def gq(j, name, g_):
        base = 0 if name == "qraw_o" else 4
        return GXL[base + g_][j, :].rearrange("(r t) -> r t", t=TC)

    def gkv(j, kind):
        return GXL[8 + kind][j, :].rearrange("(r t) -> r t", t=TC)

    def gv(j, vi):
        return GXL[12 + vi][j, :].rearrange("(t c) -> t c", c=256)

    def ggate(j):
        return GXL[14][j, :].rearrange("(r t) -> r t", t=TC)

    if True:
        pst = cx.psb[7][:].bitcast(BF16)
        ones = cx.ones
        oh = cx.carve([128, 4], F32)
        p.dma("sync", oh[:], oh_d, "c_oh", writes=[cx.B("oh")])
        identb = cx.carve([128, 128], BF16)
        E = cx.carve([128, 64, 128], BF16)
        cmask = cx.carve([128, 5, 512], BF16)
        wmask = cx.carve([128, 8, 512], BF16)
        AB = cx.carve([128, 2, 256], F32)
        ov = cx.carve([128, 4, 130], BF16)
        selG = cx.carve([12, 12, 128], BF16)
        for dst, nm in ((identb, "identb"), (E, "E"), (cmask, "cmask"), (wmask, "wmask"), (AB, "AB"), (ov, "ov"), (selG, "selG")):
            p.dma("sync", dst[:], T[nm], f"c_k{nm}", writes=[cx.B(nm)])
        kselD = cx.carve([128, S], BF16)
        kwinD = cx.carve([128, S], BF16)
        vA = {nm: cx.carve([128, 64, 192], BF16) for nm in ("sel", "win")}
        for t_ in vA.values():
            p.V(lambda e: e.memset(t_[:], 1.0), writes=[cx.B("vA")])
        kcmpT = cx.carve([128, 512], BF16)
        vcmp = cx.carve([128, 4, 128], BF16)
        p.V(lambda e: e.memset(kcmpT[:], 0.0), writes=[cx.B("kcmpT")])
        p.V(lambda e: e.memset(vcmp[:], 0.0), writes=[cx.B("vcmp")])

        def select4(dst, stage, n_part, stagebuf, dstbuf):
            ps_ = slice(0, n_part)
            p.V(lambda e: e.tensor_scalar(dst, stage(0), oh[ps_, 0:1], None, ALU.mult), reads=[stagebuf, cx.B("oh")], writes=[dstbuf])
            for g_ in range(1, 4):
                p.V(lambda e: e.scalar_tensor_tensor(dst, stage(g_), oh[ps_, g_:g_ + 1], dst, ALU.mult, ALU.add),
                    reads=[stagebuf, cx.B("oh"), dstbuf], writes=[dstbuf])

        m0 = cx.mark()
        stg = cx.carve([128, 4, 2048], BF16)
        kv2 = cx.carve([128, S], BF16)
        w1 = cx.carve([128, 16, 256], BF16)
        w2 = cx.carve([128, 2, 128], BF16)
        pe = cx.carve([128, 16], BF16)
        hid = cx.carve([128, 2, 512], BF16)
        bias = cx.carve([128, 2], F32)
        vstg = cx.carve([128, 4, 16, 64], BF16)

        def load_sel_kv(kind, dstT, nm, shifted):
            for j in range(4):
                for g_ in range(4):
                    src = gkv(j, kind)[g_ * 64:(g_ + 1) * 64, :]
                    p.dma("sync", stg[0:64, g_, :], src, f"sg{g_}", writes=[cx.B("stg")])
                    if not shifted:
                        p.dma("sync", stg[64:128, g_, :], src, f"sh{g_}", writes=[cx.B("stg")])
                    else:
                        p.dma("sync", stg[64:128, g_, 0:2047], src[:, 1:2048], f"sh{g_}", writes=[cx.B("stg")])
                        if j < 3:
                            p.dma("sync", stg[64:128, g_, 2047:2048], gkv(j + 1, kind)[g_ * 64:(g_ + 1) * 64, 0:1], f"sh{g_}", writes=[cx.B("stg")], allow_slow_non_contiguous=True)
                        else:
                            p.V(lambda e: e.memset(stg[64:128, g_, 2047:2048], 0.0), writes=[cx.B("stg")])
                select4(dstT[:, j * 2048:(j + 1) * 2048], lambda g_: stg[:, g_, :], 128, cx.B("stg"), cx.B(nm))

        load_sel_kv(2, kselD, "kselD", False)
        load_sel_kv(3, kwinD, "kwinD", False)
        for vi, nm in enumerate(("sel", "win")):
            for j in range(4):
                for g_ in range(4):
                    p.dma("sync", vstg[:, g_, :, :], gv(j, vi)[:, g_ * 64:(g_ + 1) * 64].rearrange("(t p) c -> p t c", p=128),
                          f"sg{g_}", writes=[cx.B("vstg")])
                select4(vA[nm][:, j * 16:(j + 1) * 16, 64:128], lambda g_: vstg[:, g_, :, :], 128, cx.B("vstg"), cx.B("vA"))

        for which, w2_d in enumerate((w2k_d, w2v_d)):
            load_sel_kv(which, kv2, "kv2", True)
            p.dma("gpsimd", w1[:], w1_d[which].rearrange("(c p) j -> p c j", p=128), "c_w1", writes=[cx.B("w1")])
            p.dma("gpsimd", w2[:], w2_d.rearrange("(c p) j -> p c j", p=128), "c_w2", writes=[cx.B("w2")])
            p.dma("gpsimd", pe[:], pe_d[which], "c_pe", writes=[cx.B("pe")])
            for jc in range(2):
                pb = cx.psb[2]
                for ch in range(16):
                    p.mm(pb[:, 0:1], w1[:, ch, jc * 128:(jc + 1) * 128], pe[:, ch:ch + 1], ch == 0, ch == 15,
                         reads=[cx.B("w1"), cx.B("pe")], writes=[cx.B("psb", 2)])
                p.V(lambda e: e.tensor_copy(bias[:, jc:jc + 1], pb[:, 0:1]), reads=[cx.B("psb", 2)], writes=[cx.B("bias")])
                ph = cx.psb[jc]
                for lp in range(16):
                    p.mm(ph[:, 0:511], w1[:, lp, jc * 128:(jc + 1) * 128], kv2[:, 2 * lp:2 * lp + 16 * 510 + 1:16], lp == 0, lp == 15,
                         reads=[cx.B("w1"), cx.B("kv2")], writes=[cx.B("psb", jc)])
                p.A(lambda e: e.activation(hid[:, jc, 0:511], ph[:, 0:511], AF.Silu, bias=bias[:, jc:jc + 1]),
                    reads=[cx.B("psb", jc), cx.B("bias")], writes=[cx.B("hid")])
            if which == 0:
                pk = cx.psb[3]
                for jc in range(2):
                    p.mm(pk[:, 0:511], w2[:, jc, :], hid[:, jc, 0:511], jc == 0, jc == 1, reads=[cx.B("w2"), cx.B("hid")], writes=[cx.B("psb", 3)])
                p.A(lambda e: e.activation(kcmpT[:, 0:511], pk[:, 0:511], AF.Copy), reads=[cx.B("psb", 3)], writes=[cx.B("kcmpT")])
            else:
                for ct in range(4):
                    M = 128 if ct < 3 else 127
                    pv = cx.psb[3 + (ct % 2)]
                    for jc in range(2):
                        p.mm(pv[0:M, 0:128], hid[:, jc, ct * 128:ct * 128 + M], w2[:, jc, :], jc == 0, jc == 1,
                             reads=[cx.B("w2"), cx.B("hid")], writes=[cx.B("psb", 3 + (ct % 2))])
                    p.A(lambda e: e.activation(vcmp[0:M, ct, :], pv[0:M, 0:128], AF.Copy),
                        reads=[cx.B("psb", 3 + (ct % 2))], writes=[cx.B("vcmp")])
        cx.release(m0)

        qstg = cx.carve([128, 4, 2, 512], BF16)
        gstg = cx.carve([12, 4, 512], BF16)
        qraw = [cx.carve([128, 2, 512], BF16) for i in range(2)]
        qrot = [cx.carve([128, 2, 512], BF16) for i in range(2)]
        gts = [cx.carve([12, 512], BF16) for i in range(2)]
        eT = [cx.carve([128, 4, 512], BF16) for i in range(2)]
        pT = [cx.carve([128, 512], BF16) for i in range(3)]
        rz = [cx.carve([128, 512], F32) for i in range(2)]
        wv = [cx.carve([128, 512], F32) for i in range(2)]
        tmp = [cx.carve([128, 512], F32) for i in range(2)]
        acc = cx.carve([128, 2, 512], F32)
        accb = [cx.carve([128, 2, 512], BF16) for i in range(2)]
        impacc = cx.carve([128, 4, 128], F32)
        imod = cx.carve([128, 128], F32)
        scr = cx.carve([128, 128], F32)
        m8 = cx.carve([128, 16], F32)
        rzc = cx.carve([128, 1], F32)
        negm = cx.carve([128, 128], BF16)
        negT = cx.carve([128, 512], BF16)
        outs = []

        def finish_branch(r, gi, pacc, paccbuf, zrows, first, gt, gtb):
            a, half = divmod(r, 2)
            hs = slice(64 * half, 64 * half + 64)
            pG = cx.psb[6]
            p.mm(pG[:, :], selG[:, 3 * r + gi, :], gt[:, :], True, True, reads=[cx.B("selG"), gtb], writes=[cx.B("psb", 6)])
            i = cx.rot("rz", 2)
            p.V(lambda e: e.tensor_scalar(rz[i][zrows, :], pacc[zrows, :], 1e-30, None, ALU.max), reads=[paccbuf], writes=[cx.B("rz", i)])
            p.V(lambda e: e.reciprocal(rz[i][zrows, :], rz[i][zrows, :]), reads=[cx.B("rz", i)], writes=[cx.B("rz", i)])
            p.V(lambda e: e.tensor_tensor(wv[i][zrows, :], rz[i][zrows, :], pG[zrows, :], ALU.mult),
                reads=[cx.B("rz", i), cx.B("psb", 6)], writes=[cx.B("wv", i)])
            if first:
                p.V(lambda e: e.tensor_tensor(acc[hs, a, :], pacc[hs, :], wv[i][zrows, :], ALU.mult),
                    reads=[paccbuf, cx.B("wv", i)], writes=[cx.B("acc")])
            else:
                p.V(lambda e: e.tensor_tensor(tmp[i][hs, :], pacc[hs, :], wv[i][zrows, :], ALU.mult),
                    reads=[paccbuf, cx.B("wv", i)], writes=[cx.B("tmp", i)])
                p.V(lambda e: e.tensor_tensor(acc[hs, a, :], acc[hs, a, :], tmp[i][hs, :], ALU.add),
                    reads=[cx.B("acc"), cx.B("tmp", i)], writes=[cx.B("acc")])

        for qg in range(NQG):
            q0 = qg * 512
            s = cx.rot("qld", 2)
            jq, tl = divmod(qg, 4)
            tl *= 512
            for nmq, dstq, bq in (("qraw_o", qraw[s], cx.B("qraw", s)), ("qrot_o", qrot[s], cx.B("qrot", s))):
                for g_ in range(4):
                    p.dma("sync", qstg[:, g_, :, :], gq(jq, nmq, g_)[:, tl:tl + 512].rearrange("(a p) t -> p a t", p=128),
                          f"qs{g_}", writes=[cx.B("qstg")])
                select4(dstq[:], lambda g_: qstg[:, g_, :, :], 128, cx.B("qstg"), bq)
            for g_ in range(4):
                p.dma("sync", gstg[:, g_, :], ggate(jq)[g_ * 12:(g_ + 1) * 12, tl:tl + 512], f"qs{g_}", writes=[cx.B("gstg")])
            select4(gts[s][:], lambda g_: gstg[:, g_, :], 12, cx.B("gstg"), cx.B("gts", s))
            gt, gtb = gts[s], cx.B("gts", s)
            nct = (32 * qg + 30) // 128 + 1
            for r in range(4):
                a, half = divmod(r, 2)
                hs = slice(64 * half, 64 * half + 64)
                es = cx.rot("eT", 2)
                for ct in range(nct):
                    d = qg - 4 * ct
                    b = cx.rot("S", 2)
                    ps_ = cx.psb[b]
                    p.mm(ps_[:, :], kcmpT[hs, ct * 128:(ct + 1) * 128], qraw[s][hs, a, :], True, d >= 5,
                         reads=[cx.B("kcmpT"), cx.B("qraw", s)], writes=[cx.B("psb", b)])
                    if d < 5:
                        p.mm(ps_[:, :], identb[:], cmask[:, d, :], False, True, reads=[cx.B("identb"), cx.B("cmask")], writes=[cx.B("psb", b)])
                    p.A(lambda e, ps_=ps_, es=es, ct=ct: e.activation(eT[es][:, ct, :], ps_[:, :], AF.Exp, scale=SCALE),
                        reads=[cx.B("psb", b)], writes=[cx.B("eT", es)])
                pO, pZ = cx.psb[2], cx.psb[3]
                for ct in range(nct):
                    p.mm(pO[:, :], vcmp[:, ct, :], eT[es][:, ct, :], ct == 0, ct == nct - 1, reads=[cx.B("vcmp"), cx.B("eT", es)], writes=[cx.B("psb", 2)])
                for ct in range(nct):
                    p.mm(pZ[:, :], ones[:], eT[es][:, ct, :], ct == 0, ct == nct - 1, reads=[cx.B("ones"), cx.B("eT", es)], writes=[cx.B("psb", 3)])
                zrows = slice(64 * (1 - half), 64 * (1 - half) + 64)
                pG = cx.psb[6]
                p.mm(pG[:, :], selG[:, 3 * r + 0, :], gt[:, :], True, True, reads=[cx.B("selG"), gtb], writes=[cx.B("psb", 6)])
                i = cx.rot("rz", 2)
                p.V(lambda e, i=i: e.tensor_scalar(rz[i][hs, :], pZ[hs, :], 1e-30, None, ALU.max), reads=[cx.B("psb", 3)], writes=[cx.B("rz", i)])
                p.V(lambda e, i=i: e.reciprocal(rz[i][hs, :], rz[i][hs, :]), reads=[cx.B("rz", i)], writes=[cx.B("rz", i)])
                p.V(lambda e, i=i: e.tensor_tensor(wv[i][hs, :], rz[i][hs, :], pG[hs, :], ALU.mult),
                    reads=[cx.B("rz", i), cx.B("psb", 6)], writes=[cx.B("wv", i)])
                p.V(lambda e, i=i, a=a: e.tensor_tensor(acc[hs, a, :], pO[hs, :], wv[i][hs, :], ALU.mult),
                    reads=[cx.B("psb", 2), cx.B("wv", i)], writes=[cx.B("acc")])
                for qt in range(4):
                    bi = 4 + cx.rot("I", 2)
                    pI = cx.psb[bi]
                    for ct in range(nct):
                        p.mm(pI[:, 0:129], eT[es][:, ct, qt * 128:(qt + 1) * 128], ov[:, ct, 0:129], ct == 0, ct == nct - 1,
                             reads=[cx.B("eT", es), cx.B("ov")], writes=[cx.B("psb", bi)])
                    p.V(lambda e, pI=pI: e.tensor_scalar(rzc[:], pI[:, 128:129], 1e-30, None, ALU.max), reads=[cx.B("psb", bi)], writes=[cx.B("rzc")])
                    p.V(lambda e: e.reciprocal(rzc[:], rzc[:]), reads=[cx.B("rzc")], writes=[cx.B("rzc")])
                    if r == 0:
                        p.V(lambda e, pI=pI, qt=qt: e.tensor_scalar(impacc[:, qt, :], pI[:, 0:128], rzc[:, 0:1], None, ALU.mult),
                            reads=[cx.B("psb", bi), cx.B("rzc")], writes=[cx.B("impacc")])
                    else:
                        p.V(lambda e, pI=pI, qt=qt: e.scalar_tensor_tensor(impacc[:, qt, :], pI[:, 0:128], rzc[:, 0:1], impacc[:, qt, :], ALU.mult, ALU.add),
                            reads=[cx.B("psb", bi), cx.B("rzc"), cx.B("impacc")], writes=[cx.B("impacc")])
            for qt in range(4):
                ti = 4 * qg + qt
                c0 = 128 - 2 * ti
                p.V(lambda e, qt=qt, c0=c0: e.tensor_tensor(imod[:], impacc[:, qt, :], AB[:, 0, c0:c0 + 128], ALU.mult),
                    reads=[cx.B("impacc"), cx.B("AB")], writes=[cx.B("imod")])
                p.V(lambda e, c0=c0: e.tensor_tensor(imod[:], imod[:], AB[:, 1, c0:c0 + 128], ALU.add), reads=[cx.B("imod"), cx.B("AB")], writes=[cx.B("imod")])
                p.V(lambda e: e.memset(imod[:, 0:1], 1e9), reads=[], writes=[cx.B("imod")])
                p.V(lambda e: e.max(m8[:, 0:8], imod[:]), reads=[cx.B("imod")], writes=[cx.B("m8")])
                p.V(lambda e: e.match_replace(scr[:], m8[:, 0:8], imod[:], -1e30), reads=[cx.B("imod"), cx.B("m8")], writes=[cx.B("scr")])
                p.V(lambda e: e.max(m8[:, 8:16], scr[:]), reads=[cx.B("scr")], writes=[cx.B("m8")])
                p.V(lambda e: e.tensor_scalar(negm[:], imod[:], m8[:, 15:16], NEG, ALU.is_lt, ALU.mult), reads=[cx.B("imod"), cx.B("m8")], writes=[cx.B("negm")])
                p.op("tensor", lambda e, qt=qt: e.transpose(pst[:, qt * 128:(qt + 1) * 128], negm[:], identb[:]),
                     reads=[cx.B("negm"), cx.B("identb")], writes=[cx.B("pst")])
                p.A(lambda e, qt=qt: e.activation(negT[:, qt * 128:(qt + 1) * 128], pst[:, qt * 128:(qt + 1) * 128], AF.Copy),
                    reads=[cx.B("pst")], writes=[cx.B("negT")])
            for (br, gi, kD, kbuf, jts) in (("sel", 1, kselD, "kselD", list(range(4 * qg + 4))),
                                            ("win", 2, kwinD, "kwinD", list(range(max(0, 4 * qg - 4), 4 * qg + 4)))):
                units = [(ji, jt, r) for ji, jt in enumerate(jts) for r in range(4)]

                def emit_S(u):
                    ji, jt, r = u
                    o = jt - 4 * qg
                    a, half = divmod(r, 2)
                    hs = slice(64 * half, 64 * half + 64)
                    b = cx.rot("S", 2)
                    ps_ = cx.psb[b]
                    need_mask = (br == "win") or (o >= 0)
                    p.mm(ps_[:, :], kD[hs, jt * 128:(jt + 1) * 128], qrot[s][hs, a, :], True, False,
                         reads=[cx.B(kbuf), cx.B("qrot", s)], writes=[cx.B("psb", b)], sig=False)
                    if br == "sel":
                        p.mm(ps_[:, :], E[:, jt, :], negT[:, :], False, not need_mask, reads=[cx.B("E"), cx.B("negT")], writes=[cx.B("psb", b)])
                    if need_mask:
                        p.mm(ps_[:, :], identb[:], wmask[:, o + 4, :], False, True, reads=[cx.B("identb"), cx.B("wmask")], writes=[cx.B("psb", b)])
                    pi = cx.rot("pT", 3)
                    p.A(lambda e: e.activation(pT[pi][:, :], ps_[:, :], AF.Exp, scale=SCALE),
                        reads=[cx.B("psb", b)], writes=[cx.B("pT", pi)])
                    return pi

                def emit_PV(u, pi):
                    ji, jt, r = u
                    half = r % 2
                    pa = cx.psb[2 + r]
                    p.mm(pa[:, :], vA[br][:, jt, (64 if half == 0 else 0):(192 if half == 0 else 128)], pT[pi][:, :], ji == 0, ji == len(jts) - 1,
                         reads=[cx.B("vA"), cx.B("pT", pi)], writes=[cx.B("psb", 2 + r)], sig=True)

                pend = None
                for u in units:
                    pi = emit_S(u)
                    if pend is not None:
                        emit_PV(*pend)
                    pend = (u, pi)
                emit_PV(*pend)
                for r in range(4):
                    half = r % 2
                    zrows = slice(64 * (1 - half), 64 * (1 - half) + 64)
                    finish_branch(r, gi, cx.psb[2 + r], cx.B("psb", 2 + r), zrows, False, gt, gtb)
            ob = cx.rot("accb", 2)
            p.V(lambda e, ob=ob: e.tensor_copy(accb[ob][:], acc[:]), reads=[cx.B("acc")], writes=[cx.B("accb", ob)])
            outs.append(p.dma("sync", oT4[jq].rearrange("(a p) t -> p a t", p=128)[:, :, tl:tl + 512], accb[ob][:], f"o{ob}", reads=[cx.B("accb", ob)]))
    return outs


def phase_C(cx, T):
    p = cx.p
    xd_d, vecs_d, wout_d, wg_d, wu_d, wd_d, xe, xf, oh_d = (T["xd"], T["vecs3"], T["wout"], T["wg3"], T["wu3"],
                                                              T["wd3"], T["xe"], T["xf"], T["oh"])
    GX2L = T["GX2L"]
    if True:
        vecs = cx.carve([128, 24], F32)
        p.dma("sync", vecs[:], vecs_d, "const", writes=[cx.B("vecs")])
        oh = cx.carve([128, 4], F32)
        p.dma("sync", oh[:], oh_d, "c_oh", writes=[cx.B("oh")])
        for o, sc_ in ((0, 32.0), (8, 32.0), (16, 16.0)):
            p.V(lambda e: e.tensor_scalar(vecs[:, o:o + 8], vecs[:, o:o + 8], sc_, None, ALU.mult), reads=[cx.B("vecs")], writes=[cx.B("vecs")])
        g_m1post = lambda c: vecs[:, c:c + 1]
        g_pre = lambda c: vecs[:, 8 + c:9 + c]
        g_post = lambda c: vecs[:, 16 + c:17 + c]
        m0 = cx.mark()
        wout = cx.carve([128, 8, D], BF16)
        for c in range(8):
            p.dma("gpsimd", wout[:, c, :], wout_d[c * 128:(c + 1) * 128, :], f"wpw{c % 4}", writes=[cx.B("wout")])
        astg = cx.carve([128, 4, 8, 512], BF16)
        at = [cx.carve([128, 8, 512], BF16) for i in range(2)]
        y2 = cx.carve([128, 8, 512], F32)
        for g in range(4):
            t0 = g * 512
            n = 512
            s = cx.rot("at", 2)
            for jj in range(4):
                p.dma("sync", astg[:, jj, :, :], GX2L[jj].rearrange("g (r t) -> (g r) t", t=TC)[:, t0:t0 + n].rearrange("(c p) t -> p c t", p=128),
                      f"as{jj}", writes=[cx.B("astg")])
            p.V(lambda e: e.tensor_scalar(at[s][:], astg[:, 0, :, :], oh[:, 0:1], None, ALU.mult), reads=[cx.B("astg"), cx.B("oh")], writes=[cx.B("at", s)])
            for jj in range(1, 4):
                p.V(lambda e: e.scalar_tensor_tensor(at[s][:], astg[:, jj, :, :], oh[:, jj:jj + 1], at[s][:], ALU.mult, ALU.add),
                    reads=[cx.B("astg"), cx.B("oh"), cx.B("at", s)], writes=[cx.B("at", s)])
            for oc in range(8):
                b = 4 + cx.rot("dn", 2)
                pd = cx.psb[b]
                for c in range(8):
                    p.mm(pd[:, :n], wout[:, c, oc * 128:(oc + 1) * 128], at[s][:, c, :], c == 0, c == 7,
                         reads=[cx.B("wout"), cx.B("at", s)], writes=[cx.B("psb", b)])
                p.A(lambda e: e.activation(y2[:, oc, :n], pd[:, :n], AF.Copy), reads=[cx.B("psb", b)], writes=[cx.B("y2")])
            norm_residual_store(cx, lambda c: y2[:, c, :n], [cx.B("y2")], xd_d, ("xd",), xe, ("xe",), t0, t0, n, g_m1post)
        cx.release(m0)
        alloc_ffn(cx, 1024)
        passes = [[(0, 512), (512, 512)], [(1024, 512), (1536, 512)]]
        ffn(cx, "f11", xe, ("xe",), xf, ("xf",), passes, wg_d, wu_d, wd_d, g_pre, g_post)
    return [cx.B("xf").last_w]


EX1_FIELDS = {"qraw_o": (0, 2097152, (1024, 2048)), "qrot_o": (2097152, 2097152, (1024, 2048)),
              "kvT_o": (4194304, 2097152, (4, 256, 2048)), "vtok_o": (6291456, 1048576, (2, 2048, 256)),
              "gate_o": (7340032, 98304, (48, 2048))}
NEL1 = 7438336
NEL2 = 256 * S
RG = [[0, 1, 2, 3], [4, 5, 6, 7]]


def build_fused():
    cx = Ctx("fused")
    p = cx.p
    nc = cx.nc
    voff, nv = l1_vec_layout()
    T = {}
    T["xT"] = cx.din("xT", [D, TC + HALO])
    T["vecs"] = cx.din("vecs", [128, nv])
    T["wgs"] = [cx.din(f"wg{i}", [D, DFF]) for i in range(3)]
    T["wus"] = [cx.din(f"wu{i}", [D, DFF]) for i in range(3)]
    T["wds"] = [cx.din(f"wd{i}", [DFF, D]) for i in range(3)]
    T["wpw1"] = cx.din("wpw1", [D, 2 * D])
    T["wpw2"] = cx.din("wpw2", [D, D])
    T["win"] = cx.din("win", [D, NIN])
    T["ident"] = cx.din("ident", [128, 128])
    T["prot"] = cx.din("prot", [128, 128])
    T["ropecos"] = cx.din("ropecos", [128, TC])
    T["ropesin"] = cx.din("ropesin", [128, TC])
    T["oh"] = cx.din("oh", [128, 4])
    T["w1"] = cx.din("w1", [2, 2048, 256])
    T["w2kD"] = cx.din("w2kD", [256, 128])
    T["w2vD"] = cx.din("w2vD", [256, 128])
    T["pe"] = cx.din("pe", [2, 128, 16])
    T["E"] = cx.din("E", [128, 64, 128], BF16)
    T["identb"] = cx.din("identb", [128, 128], BF16)
    T["cmask"] = cx.din("cmask", [128, 5, 512], BF16)
    T["wmask"] = cx.din("wmask", [128, 8, 512], BF16)
    T["AB"] = cx.din("AB", [128, 2, 256])
    T["ov"] = cx.din("ov", [128, 4, 130], BF16)
    T["selG"] = cx.din("selG", [12, 12, 128], BF16)
    T["vecs3"] = cx.din("vecs3", [128, 24])
    T["wout"] = cx.din("wout", [D, D])
    T["wg3"] = cx.din("wg3", [D, DFF])
    T["wu3"] = cx.din("wu3", [D, DFF])
    T["wd3"] = cx.din("wd3", [DFF, D])
    T["xf"] = cx.dout("xf", [D, TC])
    T["xa"] = nc.dram_tensor("xa", [D, TC + HALO], F32).ap()
    for nm in ("xb", "xc", "xd", "xe"):
        T[nm] = nc.dram_tensor(nm, [D, TC], F32).ap()
    EX1 = nc.dram_tensor("ex1", [1, NEL1], BF16).ap()
    EX2 = nc.dram_tensor("ex2", [1, NEL2], BF16).ap()
    CH = 524288
    ch1 = [(k * CH, min(CH, NEL1 - k * CH)) for k in range((NEL1 + CH - 1) // CH)]
    ch2 = [(k * CH, CH) for k in range(NEL2 // CH)]
    GX1L = [nc.dram_tensor(f"gx1_{k}", [4, n_], BF16).ap() for k, (o_, n_) in enumerate(ch1)]
    GX2L = [nc.dram_tensor(f"gx2_{k}", [4, n_], BF16).ap() for k, (o_, n_) in enumerate(ch2)]
    T["EX2"], T["GX1L"], T["GX2L"] = EX2, GX1L, GX2L
    for nm, (o, n, shp) in EX1_FIELDS.items():
        v = EX1[0, o:o + n]
        T[nm] = v.rearrange("(r t) -> r t", t=shp[1]) if len(shp) == 2 else v.rearrange("(k r t) -> k r t", r=shp[1], t=shp[2])

    with cx.st:
        cx.arena_init(51 * 1024)
        cx.ones = cx.carve([128, 128], BF16)
        p.V(lambda e: e.memset(cx.ones[:], 1.0), writes=[cx.B("ones")])
        cx.psb = [cx.ps(f"psb{i}", [128, 512], F32) for i in range(8)]
        cx.sq = [cx.carve([128, 512], BF16) for i in range(2)]
        cx.r32 = [cx.carve([128, 512], F32) for i in range(2)]
        mtop = cx.mark()
        alloc_small(cx)
        phase_A(cx, T)
        cx.release(mtop)
        for k, (o_, n_) in enumerate(ch1):
            p.op("gpsimd", lambda e: e.collective_compute("AllGather", ALU.bypass, RG, [EX1[:, o_:o_ + n_].opt()], [GX1L[k].opt()]),
                 writes=[cx.B("gx1")])
        p.barrier()
        phase_B(cx, T)
        cx.release(mtop)
        for k, (o_, n_) in enumerate(ch2):
            p.op("gpsimd", lambda e: e.collective_compute("AllGather", ALU.bypass, RG, [EX2[:, o_:o_ + n_].opt()], [GX2L[k].opt()]),
                 writes=[cx.B("gx2")])
        p.barrier()
        alloc_small(cx)
        fin = phase_C(cx, T)
        stuck = p.check()
        assert not stuck, stuck
        p.build(final_waits=fin)
    return cx.nc


def kernel(**inputs):
    I = {k: np.asarray(v) for k, v in inputs.items()}
    cores = list(range(8))
    consts = l2_consts()
    w2 = np.asarray(I["nsa_cmp_w2"][0], np.float32)
    shared = dict(consts)
    shared["w1"] = np.ascontiguousarray(I["nsa_cmp_w1"][0], dtype=np.float32)
    shared["w2kD"] = np.ascontiguousarray(np.concatenate([w2[0], w2[0]], 1))
    shared["w2vD"] = np.ascontiguousarray(np.concatenate([w2[1], w2[1]], 1))
    shared["pe"] = np.ascontiguousarray(np.stack([pc(np.asarray(I["nsa_cmp_pos"][0, i]).reshape(-1)) for i in range(2)], 0))
    shared["vecs3"] = np.ascontiguousarray(np.concatenate([pc(I["mix_norm_post"][1]), pc(I["ffn_norm_pre"][1, 1]), pc(I["ffn_norm_post"][1, 1])], axis=1))
    shared["wout"] = np.ascontiguousarray(I["nsa_w_out"][0], dtype=np.float32)
    shared["wg3"] = np.ascontiguousarray(I["ffn_w_gate"][1, 1], dtype=np.float32)
    shared["wu3"] = np.ascontiguousarray(I["ffn_w_up"][1, 1], dtype=np.float32)
    shared["wd3"] = np.ascontiguousarray(I["ffn_w_down"][1, 1], dtype=np.float32)
    maps = []
    for c in cores:
        m = prep_launch1(I, c)
        m.update(shared)
        oh = np.zeros((128, 4), np.float32)
        oh[:, c % 4] = 1.0
        m["oh"] = oh
        maps.append(m)
    res = run_bass_kernel_spmd(build_fused(), maps, core_ids=cores).results
    out = np.zeros((2, S, D), np.float32)
    for c in cores:
        b, j = divmod(c, 4)
        out[b, j * TC:(j + 1) * TC] = np.asarray(res[c]["xf"]).T
    return out
```

```python
import contextlib
import numpy as np
import ml_dtypes
import concourse.bass as bass
import concourse.mybir as mybir
from concourse.bass_utils import run_bass_kernel_spmd

F32 = mybir.dt.float32
BF16 = mybir.dt.bfloat16
AF = mybir.ActivationFunctionType
ALU = mybir.AluOpType

ENGS = ["tensor", "vector", "scalar", "gpsimd", "sync"]

D = 1024
DFF = 2816
NFC = 22
S = 8192
TC = 2048
HALO = 128
CW = 31
RMS_EPS = 1e-6
LN_EPS = 1e-5
NEG = -30000.0
NIN = 2608


class Buf:
    __slots__ = ("name", "last_w", "readers")

    def __init__(self, name=""):
        self.name = name
        self.last_w = None
        self.readers = []


class _Rec:
    def __init__(self):
        self.call = None

    def __getattr__(self, name):
        def f(*a, **k):
            self.call = (name, a, k)
            return None
        return f


class Prog:
    def __init__(self, nc):
        self.nc = nc
        self.ops = {e: [] for e in ENGS}
        self.cnt = {e: 0 for e in ENGS}
        self.seen = {e: {} for e in ENGS}
        self.dcnt = {}
        self.pending = {e: {} for e in ENGS}

    def barrier(self):
        snap = dict(self.cnt)
        snap.update(self.dcnt)
        for e in ENGS:
            for k, v in snap.items():
                if v > 0 and not (e == "tensor" and k == "tensor"):
                    if self.pending[e].get(k, 0) < v:
                        self.pending[e][k] = v

    def op(self, eng, fn, reads=(), writes=(), dma=None, sig=True):
        rec = _Rec()
        fn(rec)
        name_, args_, kw_ = rec.call
        fn = lambda e: getattr(e, name_)(*args_, **kw_)
        waits = dict(self.pending[eng])
        self.pending[eng] = {}

        def need(ev, war=False):
            if ev is None:
                return
            k, v = ev
            if k == eng:
                if eng == "tensor" or war:
                    return
            if waits.get(k, 0) < v:
                waits[k] = v

        for b in reads:
            need(b.last_w)
        for b in writes:
            need(b.last_w)
            for r in b.readers:
                need(r, war=True)
        w = []
        for k, v in waits.items():
            if self.seen[eng].get(k, 0) < v:
                self.seen[eng][k] = v
                w.append((k, v))
        if dma is not None:
            prev = self.dcnt.get(dma, 0)
            if prev > 0 and self.seen[eng].get(dma, 0) < prev:
                self.seen[eng][dma] = prev
                w.append((dma, prev))
            self.dcnt[dma] = self.dcnt.get(dma, 0) + 16
            ev = (dma, self.dcnt[dma])
            inc = (dma, 16)
        elif eng == "tensor" and not sig:
            ev = (eng, self.cnt[eng] + 1)
            inc = None
        else:
            self.cnt[eng] += 1
            ev = (eng, self.cnt[eng])
            inc = (eng, 1)
        self.ops[eng].append((fn, w, inc))
        for b in reads:
            b.readers.append(ev)
            if len(b.readers) > 64:
                best = {}
                for k, v in b.readers:
                    if best.get(k, 0) < v:
                        best[k] = v
                b.readers = list(best.items())
        for b in writes:
            b.last_w = ev
            b.readers = []
        return ev

    def mm(self, out, lhsT, rhs, start, stop, reads=(), writes=(), sig=None, **kw):
        if sig is None:
            sig = stop
        return self.op("tensor", lambda e: e.matmul(out, lhsT, rhs, start=start, stop=stop, **kw),
                       reads=reads, writes=writes, sig=sig)

    def dma(self, eng, out, in_, sem, reads=(), writes=(), **kw):
        return self.op(eng, lambda e: e.dma_start(out=out, in_=in_, **kw), reads=reads, writes=writes, dma=sem)

    def V(self, fn, reads=(), writes=()):
        return self.op("vector", fn, reads, writes)

    def A(self, fn, reads=(), writes=()):
        return self.op("scalar", fn, reads, writes)

    def check(self):
        sem = {}
        pos = {e: 0 for e in ENGS}
        n = {e: len(self.ops[e]) for e in ENGS}
        progress = True
        while progress:
            progress = False
            for e in ENGS:
                while pos[e] < n[e]:
                    fn, w, inc = self.ops[e][pos[e]]
                    if all(sem.get(k, 0) >= v for k, v in w):
                        if inc is not None:
                            sem[inc[0]] = sem.get(inc[0], 0) + inc[1]
                        pos[e] += 1
                        progress = True
                    else:
                        break
        stuck = {e: (pos[e], n[e], [(k, v, sem.get(k, 0)) for k, v in self.ops[e][pos[e]][1]]) for e in ENGS if pos[e] < n[e]}
        return stuck

    def build(self, final_waits=()):
        nc = self.nc
        names = list(ENGS) + sorted(self.dcnt.keys())
        with contextlib.ExitStack() as st:
            sems = {n: st.enter_context(nc.semaphore("s_" + n)) for n in names}
            block = st.enter_context(nc.Block())
            fw = {}
            for ev in final_waits:
                if ev is not None and fw.get(ev[0], 0) < ev[1]:
                    fw[ev[0]] = ev[1]
            for eng in ENGS:
                ops = self.ops[eng]
                if eng == "sync":
                    ops = ops + [(None, list(fw.items()), None)]
                if not ops:
                    continue

                def body(e, ops=ops):
                    for fn, w, inc in ops:
                        for k, v in w:
                            e.wait_ge(sems[k], v)
                        if fn is None:
                            continue
                        ins = fn(e)
                        if inc is not None:
                            ins.then_inc(sems[inc[0]], inc[1])

                getattr(block, eng)(body)


class Ctx:
    def __init__(self, name):
        self.nc = bass.Bass("TRN2", target_bir_lowering=False)
        self.p = Prog(self.nc)
        self.st = contextlib.ExitStack()
        self.bufs = {}
        self.outs = []
        self.rr = {}

    def din(self, name, shape, dt=F32):
        return self.nc.dram_tensor(name, list(shape), dt, kind="ExternalInput").ap()

    def dout(self, name, shape, dt=F32):
        return self.nc.dram_tensor(name, list(shape), dt, kind="ExternalOutput").ap()

    def sb(self, name, shape, dt):
        return self.st.enter_context(self.nc.sbuf_tensor(name, list(shape), dt))

    def ps(self, name, shape, dt=F32):
        return self.st.enter_context(self.nc.psum_tensor(name, list(shape), dt))

    def arena_init(self, nwords):
        self.arena = self.sb("arena", [128, nwords], F32)
        self.top = 0
        self.nwords = nwords

    def carve(self, shape, dt):
        nfree = 1
        for d in shape[1:]:
            nfree *= d
        words = nfree if dt == F32 else (nfree + 1) // 2
        a = self.top
        self.top += words
        assert self.top <= self.nwords, ("arena overflow", self.top, self.nwords)
        ap = self.arena[:, a:a + words]
        if dt != F32:
            ap = ap.bitcast(dt)[:, :nfree]
        if len(shape) == 3:
            ap = ap.rearrange("p (a b) -> p a b", b=shape[2])
        elif len(shape) == 4:
            ap = ap.rearrange("p (a b c) -> p a b c", b=shape[2], c=shape[3])
        if shape[0] < 128:
            ap = ap[0:shape[0]]
        return ap

    def mark(self):
        return self.top

    def release(self, m):
        self.top = m
        self.p.barrier()

    def B(self, *key):
        b = self.bufs.get(key)
        if b is None:
            b = self.bufs[key] = Buf(str(key))
        return b

    def rot(self, key, n):
        i = self.rr.get(key, 0)
        self.rr[key] = (i + 1) % n
        return i


def pc(v):
    v = np.asarray(v, np.float32)
    return np.ascontiguousarray(v.reshape(-1, 128).T)


def setup_common(cx, n_ps=8):
    cx.arena_init(50 * 1024)
    cx.ones = cx.carve([128, 128], BF16)
    cx.p.V(lambda e: e.memset(cx.ones[:], 1.0), writes=[cx.B("ones")])
    cx.psb = [cx.ps(f"psb{i}", [128, 512], F32) for i in range(n_ps)]
    cx.sq = [cx.carve([128, 512], BF16) for i in range(2)]
    cx.r32 = [cx.carve([128, 512], F32) for i in range(2)]


def rms_stats(cx, src, srcbufs, n, psi, eps_scaled, nch=8):
    p = cx.p
    ps = cx.psb[psi]
    for c in range(nch):
        i = cx.rot("sq", 2)
        sq = cx.sq[i]
        p.A(lambda e, c=c, sq=sq: e.activation(sq[:, :n], src(c), AF.Square), reads=srcbufs, writes=[cx.B("sq", i)])
        p.mm(ps[:, :n], cx.ones[:], sq[:, :n], c == 0, c == nch - 1, reads=[cx.B("sq", i), cx.B("ones")],
             writes=[cx.B("psb", psi)], sig=True)
    j = cx.rot("r32", 2)
    r = cx.r32[j]
    p.A(lambda e: e.activation(r[:, :n], ps[:, :n], AF.Sqrt, bias=float(eps_scaled), scale=1.0),
        reads=[cx.B("psb", psi)], writes=[cx.B("r32", j)])
    p.V(lambda e: e.reciprocal(r[:, :n], r[:, :n]), reads=[cx.B("r32", j)], writes=[cx.B("r32", j)])
    return r, cx.B("r32", j)


def load_x_group(cx, xdram, xbufkey, off, n):
    i = cx.rot("xin", 2)
    cx.xin = cx.xins[i]
    cx.xinb = cx.B("xin", i)
    src = xdram.rearrange("(c p) t -> p c t", p=128)[:, :, off:off + n]
    cx.p.dma("sync", cx.xin[:, :, :n], src, f"xin{i}", reads=[cx.B(*xbufkey)], writes=[cx.xinb])


def norm_to_h(cx, hdst, hbuf, g32col, n, psi=6):
    xin, xinb = cx.xin, cx.xinb
    r, rb = rms_stats(cx, lambda c: xin[:, c, :n], [xinb], n, psi, D * RMS_EPS)
    for c in range(8):
        cx.p.V(lambda e, c=c: e.scalar_tensor_tensor(hdst(c), xin[:, c, :n], g32col(c), r[:, :n], ALU.mult, ALU.mult),
               reads=[xinb, rb, cx.B("vecs")], writes=[hbuf])


def norm_residual_store(cx, ysrc, ybufs, xin_dram, xin_key, xout_dram, xout_key, off_in, off_out, n, gcol, psi=6):
    p = cx.p
    r, rb = rms_stats(cx, ysrc, ybufs, n, psi, D * RMS_EPS)
    load_x_group(cx, xin_dram, xin_key, off_in, n)
    xin, xinb = cx.xin, cx.xinb
    for c in range(8):
        i = cx.rot("ntmp", 2)
        t = cx.ntmp[i]
        p.V(lambda e, c=c, t=t: e.scalar_tensor_tensor(t[:, :n], ysrc(c), gcol(c), r[:, :n], ALU.mult, ALU.mult),
            reads=ybufs + [rb, cx.B("vecs")], writes=[cx.B("ntmp", i)])
        p.V(lambda e, c=c, t=t: e.tensor_tensor(xin[:, c, :n], xin[:, c, :n], t[:, :n], ALU.add),
            reads=[cx.B("ntmp", i), xinb], writes=[xinb])
    dst = xout_dram.rearrange("(c p) t -> p c t", p=128)[:, :, off_out:off_out + n]
    return p.dma("sync", dst, xin[:, :, :n], "xout", reads=[xinb], writes=[cx.B(*xout_key)])


def alloc_small(cx):
    cx.xins = [cx.carve([128, 8, 512], F32) for i in range(2)]
    cx.sg = [cx.carve([128, 512], F32) for i in range(2)]
    cx.ntmp = [cx.carve([128, 512], F32) for i in range(2)]


def alloc_ffn(cx, maxtok):
    cx.hT = [cx.carve([128, 8, maxtok], BF16) for i in range(2)]
    cx.aT = cx.carve([128, NFC, maxtok], BF16)
    cx.ytmp = cx.carve([128, 8, maxtok], F32)
    cx.wg = [cx.carve([128, 8, 256], BF16) for i in range(2)]
    cx.wu = [cx.carve([128, 8, 256], BF16) for i in range(2)]
    cx.wd = [cx.carve([128, NFC, 128], BF16) for i in range(2)]


def ffn(cx, tag, xin_dram, xin_key, xout_dram, xout_key, passes, wg_d, wu_d, wd_d, gpre, gpost, out_shift=0):
    p = cx.p
    last = None
    hTs, aT, ytmp, wg, wu, wd = cx.hT, cx.aT, cx.ytmp, cx.wg, cx.wu, cx.wd

    def locs(groups):
        loc, o = [], 0
        for (off, n) in groups:
            loc.append(o)
            o += n
        return loc

    def S1(k):
        groups = passes[k]
        loc = locs(groups)
        hT = hTs[k % 2]
        for gi, (off, n) in enumerate(groups):
            load_x_group(cx, xin_dram, xin_key, off, n)
            lo = loc[gi]
            norm_to_h(cx, lambda c, lo=lo, n=n: hT[:, c, lo:lo + n], cx.B("hT", k % 2), gpre, n)

    def S2(k):
        groups = passes[k]
        loc = locs(groups)
        hT, hb = hTs[k % 2], cx.B("hT", k % 2)
        for fp in range(NFC // 2):
            s = cx.rot("wgu", 2)
            p.dma("gpsimd", wg[s][:], wg_d.rearrange("(c p) f -> p c f", p=128)[:, :, fp * 256:(fp + 1) * 256],
                  f"wg{s}", writes=[cx.B("wg", s)])
            p.dma("gpsimd", wu[s][:], wu_d.rearrange("(c p) f -> p c f", p=128)[:, :, fp * 256:(fp + 1) * 256],
                  f"wu{s}", writes=[cx.B("wu", s)])
            for h in range(2):
                fc = fp * 2 + h
                for gi, (off, n) in enumerate(groups):
                    lo = loc[gi]
                    b = cx.rot("gu", 2)
                    pg, pu = cx.psb[b], cx.psb[2 + b]
                    for c in range(8):
                        p.mm(pg[:, :n], wg[s][:, c, h * 128:(h + 1) * 128], hT[:, c, lo:lo + n], c == 0, c == 7,
                             reads=[cx.B("wg", s), hb], writes=[cx.B("psb", b)])
                    for c in range(8):
                        p.mm(pu[:, :n], wu[s][:, c, h * 128:(h + 1) * 128], hT[:, c, lo:lo + n], c == 0, c == 7,
                             reads=[cx.B("wu", s), hb], writes=[cx.B("psb", 2 + b)])
                    sg = cx.sg[b]
                    p.A(lambda e: e.activation(sg[:, :n], pg[:, :n], AF.Silu),
                        reads=[cx.B("psb", b)], writes=[cx.B("sg", b)])
                    p.V(lambda e: e.tensor_tensor(aT[:, fc, lo:lo + n], pu[:, :n], sg[:, :n], ALU.mult),
                        reads=[cx.B("psb", 2 + b), cx.B("sg", b)], writes=[cx.B("aT")])

    def S3(k):
        groups = passes[k]
        loc = locs(groups)
        for dc in range(8):
            s = cx.rot("wd", 2)
            p.dma("gpsimd", wd[s][:], wd_d.rearrange("(fc p) d -> p fc d", p=128)[:, :, dc * 128:(dc + 1) * 128],
                  f"wd{s}", writes=[cx.B("wd", s)])
            for gi, (off, n) in enumerate(groups):
                lo = loc[gi]
                b = 4 + cx.rot("dn", 2)
                pd = cx.psb[b]
                for fc in range(NFC):
                    p.mm(pd[:, :n], wd[s][:, fc, :], aT[:, fc, lo:lo + n], fc == 0, fc == NFC - 1,
                         reads=[cx.B("wd", s), cx.B("aT")], writes=[cx.B("psb", b)])
                p.A(lambda e: e.activation(ytmp[:, dc, lo:lo + n], pd[:, :n], AF.Copy),
                    reads=[cx.B("psb", b)], writes=[cx.B("ytmp")])

    def S4(k):
        nonlocal last
        groups = passes[k]
        loc = locs(groups)
        for gi, (off, n) in enumerate(groups):
            lo = loc[gi]
            last = norm_residual_store(cx, lambda c, lo=lo, n=n: ytmp[:, c, lo:lo + n], [cx.B("ytmp")],
                                       xin_dram, xin_key, xout_dram, xout_key, off, off - out_shift, n, gpost)

    if passes:
        S1(0)
    for k in range(len(passes)):
        S2(k)
        if k + 1 < len(passes):
            S1(k + 1)
        S3(k)
        S4(k)
    return last


def l1_vec_layout():
    names = [("f00pre", 8), ("f00post", 8), ("m0pre", 8), ("bpw1", 16), ("wdw", 8 * CW), ("bdw", 8), ("lng", 8),
             ("lnb", 8), ("bpw2", 8), ("m0post", 8), ("f01pre", 8), ("f01post", 8), ("f10pre", 8), ("f10post", 8),
             ("m1pre", 8), ("bgate", 1), ("flag", 1)]
    off = {}
    o = 0
    for n, k in names:
        off[n] = (o, k)
        o += k
    return off, o


def phase_A(cx, T):
    p = cx.p
    TT = TC + HALO
    voff, nv = l1_vec_layout()
    xT, vecs_d, wgs, wus, wds = T["xT"], T["vecs"], T["wgs"], T["wus"], T["wds"]
    wpw1_d, wpw2_d, win_d, ident_d, prot_d, cos_d, sin_d = T["wpw1"], T["wpw2"], T["win"], T["ident"], T["prot"], T["ropecos"], T["ropesin"]
    xa, xb, xc, xd = T["xa"], T["xb"], T["xc"], T["xd"]
    qraw_o, qrot_o, kvT_o, vtok_o, gate_o = T["qraw_o"], T["qrot_o"], T["kvT_o"], T["vtok_o"], T["gate_o"]
    if True:
        vecs = cx.carve([128, nv], F32)
        p.dma("sync", vecs[:], vecs_d, "const", writes=[cx.B("vecs")])

        def vcol(name, scale=None):
            o, k = voff[name]
            if scale is not None:
                p.V(lambda e: e.tensor_scalar(vecs[:, o:o + k], vecs[:, o:o + k], float(scale), None, ALU.mult),
                    reads=[cx.B("vecs")], writes=[cx.B("vecs")])
            return lambda c: vecs[:, o + c:o + c + 1]

        g_f00pre = vcol("f00pre", 32.0)
        g_f00post = vcol("f00post", 16.0)
        g_m0pre = vcol("m0pre", 32.0)
        g_m0post = vcol("m0post", 32.0)
        g_f01pre = vcol("f01pre", 32.0)
        g_f01post = vcol("f01post", 16.0)
        g_f10pre = vcol("f10pre", 32.0)
        g_f10post = vcol("f10post", 16.0)
        g_m1pre = vcol("m1pre", 32.0)
        bpw1 = vcol("bpw1")
        bdw = vcol("bdw")
        lng = vcol("lng")
        lnb = vcol("lnb")
        bpw2 = vcol("bpw2")
        wdw_o = voff["wdw"][0]
        bgate_o = voff["bgate"][0]
        flag_o = voff["flag"][0]

        identb = cx.carve([128, 128], BF16)
        p.dma("gpsimd", identb[:], ident_d, "const2", writes=[cx.B("identb")])
        prot = cx.carve([128, 128], BF16)
        p.dma("gpsimd", prot[:], prot_d, "const2", writes=[cx.B("prot")])
        SKIP = False
        m0 = cx.mark()
        alloc_ffn(cx, 1152)
        passes0 = [] if SKIP else [[(0, 128), (128, 512), (640, 512)], [(1152, 512), (1664, 512)]]
        ffn(cx, "f00", xT, ("xT",), xa, ("xa",), passes0, wgs[0], wus[0], wds[0], g_f00pre, g_f00post)
        cx.release(m0)

        wpw1 = cx.carve([128, 8, 2 * D], BF16)
        wpw2 = cx.carve([128, 8, D], BF16)
        for c in range(8):
            p.dma("gpsimd", wpw1[:, c, :], wpw1_d[c * 128:(c + 1) * 128, :], f"wpw{c % 4}", writes=[cx.B("wpw1")])
        for c in range(8):
            p.dma("gpsimd", wpw2[:, c, :], wpw2_d[c * 128:(c + 1) * 128, :], f"wpw{c % 4}", writes=[cx.B("wpw2")])
        glu = cx.carve([128, 8, TT], BF16)
        vbs = [cx.carve([128, 512], BF16) for q in range(2)]
        y2 = cx.carve([128, 8, 512], F32)
        dg = [cx.carve([128, CW, 128], BF16) for i in range(2)]
        hc = cx.carve([128, 8, 512], BF16)
        vt = cx.carve([128, 8, 512], F32)
        sc = cx.carve([128, 8, 512], BF16)
        for (off, n) in ([] if SKIP else [(0, 128), (128, 512), (640, 512), (1152, 512), (1664, 512)]):
            load_x_group(cx, xa, ("xa",), off, n)
            norm_to_h(cx, lambda c, n=n: hc[:, c, :n], cx.B("hT"), g_m0pre, n)
            for oc in range(8):
                b = cx.rot("gu", 2)
                pa, pg = cx.psb[b], cx.psb[2 + b]
                for c in range(8):
                    p.mm(pa[:, :n], wpw1[:, c, oc * 128:(oc + 1) * 128], hc[:, c, :n], c == 0, c == 7,
                         reads=[cx.B("wpw1"), cx.B("hT")], writes=[cx.B("psb", b)])
                for c in range(8):
                    p.mm(pg[:, :n], wpw1[:, c, D + oc * 128:D + (oc + 1) * 128], hc[:, c, :n], c == 0, c == 7,
                         reads=[cx.B("wpw1"), cx.B("hT")], writes=[cx.B("psb", 2 + b)])
                sg = cx.sg[b]
                p.A(lambda e, sg=sg, pg=pg, n=n, oc=oc: e.activation(sg[:, :n], pg[:, :n], AF.Sigmoid, bias=bpw1(8 + oc)),
                    reads=[cx.B("psb", 2 + b), cx.B("vecs")], writes=[cx.B("sg", b)])
                p.V(lambda e, sg=sg, pa=pa, n=n, oc=oc, off=off: e.scalar_tensor_tensor(
                    glu[:, oc, off:off + n], pa[:, :n], bpw1(oc), sg[:, :n], ALU.add, ALU.mult),
                    reads=[cx.B("psb", b), cx.B("sg", b), cx.B("vecs")], writes=[cx.B("glu")])
            if off == 0:
                for oc in range(8):
                    p.V(lambda e, oc=oc: e.tensor_scalar(glu[:, oc, 0:128], glu[:, oc, 0:128], vecs[:, flag_o:flag_o + 1], None, ALU.mult),
                        reads=[cx.B("glu"), cx.B("vecs")], writes=[cx.B("glu")])
        for g in range(0 if SKIP else 4):
            t0 = HALO + g * 512
            n = 512
            for cc in range(8):
                di = cx.rot("dg", 2)
                for k in range(CW):
                    p.V(lambda e, k=k, cc=cc, di=di: e.tensor_scalar(dg[di][:, k, :], identb[:], vecs[:, wdw_o + k * 8 + cc:wdw_o + k * 8 + cc + 1], None, ALU.mult),
                        reads=[cx.B("identb"), cx.B("vecs")], writes=[cx.B("dg", di)])
                b = 4 + cx.rot("dn", 2)
                pd = cx.psb[b]
                for k in range(CW):
                    s0 = t0 - (CW - 1) + k
                    p.mm(pd[:, :n], dg[di][:, k, :], glu[:, cc, s0:s0 + n], k == 0, k == CW - 1,
                         reads=[cx.B("dg", di), cx.B("glu")], writes=[cx.B("psb", b)])
                p.A(lambda e, pd=pd, cc=cc: e.activation(vt[:, cc, :n], pd[:, :n], AF.Identity, bias=bdw(cc)),
                    reads=[cx.B("psb", b), cx.B("vecs")], writes=[cx.B("ytmp")])
            psm, psq = cx.psb[6], cx.psb[7]
            for cc in range(8):
                i = cx.rot("sq", 2)
                sq = cx.sq[i]
                p.A(lambda e, cc=cc, sq=sq: e.activation(sq[:, :n], vt[:, cc, :n], AF.Square), reads=[cx.B("ytmp")], writes=[cx.B("sq", i)])
                p.mm(psq[:, :n], cx.ones[:], sq[:, :n], cc == 0, cc == 7, reads=[cx.B("sq", i), cx.B("ones")], writes=[cx.B("psb", 7)], sig=True)
                j = cx.rot("vb", 2)
                vb = vbs[j]
                p.V(lambda e, cc=cc, vb=vb: e.tensor_copy(vb[:, :n], vt[:, cc, :n]), reads=[cx.B("ytmp")], writes=[cx.B("vb", j)])
                p.mm(psm[:, :n], cx.ones[:], vb[:, :n], cc == 0, cc == 7, reads=[cx.B("vb", j), cx.B("ones")], writes=[cx.B("psb", 6)], sig=True)
            mean, m2 = cx.r32[0], cx.r32[1]
            rs = cx.ntmp[0]
            p.V(lambda e: e.tensor_scalar(mean[:, :n], psm[:, :n], 1.0 / D, None, ALU.mult), reads=[cx.B("psb", 6)], writes=[cx.B("r32", 0)])
            p.V(lambda e: e.tensor_tensor(m2[:, :n], mean[:, :n], mean[:, :n], ALU.mult), reads=[cx.B("r32", 0)], writes=[cx.B("r32", 1)])
            p.V(lambda e: e.scalar_tensor_tensor(rs[:, :n], psq[:, :n], 1.0 / D, m2[:, :n], ALU.mult, ALU.subtract),
                reads=[cx.B("psb", 7), cx.B("r32", 1)], writes=[cx.B("ntmp", 0)])
            p.A(lambda e: e.activation(rs[:, :n], rs[:, :n], AF.Sqrt, bias=float(LN_EPS), scale=1.0), reads=[cx.B("ntmp", 0)], writes=[cx.B("ntmp", 0)])
            p.V(lambda e: e.reciprocal(rs[:, :n], rs[:, :n]), reads=[cx.B("ntmp", 0)], writes=[cx.B("ntmp", 0)])
            for cc in range(8):
                p.V(lambda e, cc=cc: e.tensor_tensor(vt[:, cc, :n], vt[:, cc, :n], mean[:, :n], ALU.subtract),
                    reads=[cx.B("ytmp"), cx.B("r32", 0)], writes=[cx.B("ytmp")])
                p.V(lambda e, cc=cc: e.tensor_tensor(vt[:, cc, :n], vt[:, cc, :n], rs[:, :n], ALU.mult),
                    reads=[cx.B("ytmp"), cx.B("ntmp", 0)], writes=[cx.B("ytmp")])
                p.A(lambda e, cc=cc: e.activation(sc[:, cc, :n], vt[:, cc, :n], AF.Silu, bias=lnb(cc), scale=lng(cc)),
                    reads=[cx.B("ytmp"), cx.B("vecs")], writes=[cx.B("aT")])
            for oc in range(8):
                b = 4 + cx.rot("dn", 2)
                pd = cx.psb[b]
                for c in range(8):
                    p.mm(pd[:, :n], wpw2[:, c, oc * 128:(oc + 1) * 128], sc[:, c, :n], c == 0, c == 7,
                         reads=[cx.B("wpw2"), cx.B("aT")], writes=[cx.B("psb", b)])
                p.A(lambda e, pd=pd, oc=oc: e.activation(y2[:, oc, :n], pd[:, :n], AF.Identity, bias=bpw2(oc)),
                    reads=[cx.B("psb", b), cx.B("vecs")], writes=[cx.B("y2")])
            norm_residual_store(cx, lambda c: y2[:, c, :n], [cx.B("y2")], xa, ("xa",), xb, ("xb",), t0, t0 - HALO, n, g_m0post)

        cx.release(m0)
        alloc_ffn(cx, 1024)
        passes = [] if SKIP else [[(0, 512), (512, 512)], [(1024, 512), (1536, 512)]]
        ffn(cx, "f01", xb, ("xb",), xc, ("xc",), passes, wgs[1], wus[1], wds[1], g_f01pre, g_f01post)
        ffn(cx, "f10", xc, ("xc",), xd, ("xd",), passes, wgs[2], wus[2], wds[2], g_f10pre, g_f10post)

        cx.release(m0)
        hc = cx.carve([128, 8, 512], BF16)
        win = cx.carve([128, 8, NIN], BF16)
        for c in range(8):
            p.dma("gpsimd", win[:, c, :], win_d[c * 128:(c + 1) * 128, :], f"wpw{c % 4}", writes=[cx.B("win")])
        cosT = cx.carve([128, TC], F32)
        sinT = cx.carve([128, TC], F32)
        p.dma("sync", cosT[:], cos_d, "const", writes=[cx.B("cs")])
        p.dma("sync", sinT[:], sin_d, "const", writes=[cx.B("cs")])
        qrawS = [cx.carve([128, 8, 512], BF16) for i in range(2)]
        qrotS = [cx.carve([128, 8, 512], BF16) for i in range(2)]
        kvS = [cx.carve([128, 8, 512], BF16) for i in range(2)]
        vtk = [cx.carve([128, 512], BF16) for i in range(2)]
        gsb = [cx.carve([128, 512], BF16) for i in range(2)]
        vS = [cx.carve([128, 2, 4, 256], BF16) for i in range(2)]
        outs = []
        PARTS = ["fm", "v", "g"]
        for g in range(4):
            t0 = g * 512
            n = 512
            load_x_group(cx, xd, ("xd",), t0, n)
            norm_to_h(cx, lambda c: hc[:, c, :n], cx.B("hT"), g_m1pre, n)
            so = cx.rot("nsao", 2)
            fm = [(oc, "q") for oc in range(8)] + [(8, "kc"), (9, "kc"), (10, "vc"), (11, "vc"), (12, "ks"), (13, "ks"), (16, "kw"), (17, "kw")]
            kvidx = {"kc": 0, "vc": 2, "ks": 4, "kw": 6}
            for (oc, kind) in (fm if "fm" in PARTS else []):
                b = cx.rot("gu", 2)
                pq = cx.psb[b]
                for c in range(8):
                    p.mm(pq[:, :n], win[:, c, oc * 128:(oc + 1) * 128], hc[:, c, :n], c == 0, c == 7,
                         reads=[cx.B("win"), cx.B("hT")], writes=[cx.B("psb", b)])
                if kind == "q":
                    raw, rawb = qrawS[so][:, oc, :], cx.B("qrawS", so)
                    rot_, rotb = qrotS[so][:, oc, :], cx.B("qrotS", so)
                elif kind in ("kc", "vc"):
                    raw, rawb = kvS[so][:, kvidx[kind] + oc % 2, :], cx.B("kvS", so)
                else:
                    i = cx.rot("qb", 2)
                    raw, rawb = vtk[i][:, :], cx.B("vtk", i)
                    rot_, rotb = kvS[so][:, kvidx[kind] + oc % 2, :], cx.B("kvS", so)
                p.A(lambda e, raw=raw, pq=pq: e.activation(raw, pq[:, :n], AF.Copy), reads=[cx.B("psb", b)], writes=[rawb])
                if kind in ("q", "ks", "kw"):
                    b2 = 2 + cx.rot("rp", 2)
                    pr = cx.psb[b2]
                    p.mm(pr[:, :n], prot[:], raw, True, True, reads=[cx.B("prot"), rawb], writes=[cx.B("psb", b2)])
                    ti = cx.rot("ntmp", 2)
                    t1, t1b = cx.ntmp[ti], cx.B("ntmp", ti)
                    si = cx.rot("sgr", 2)
                    t2, t2b = cx.sg[si], cx.B("sg", si)
                    p.V(lambda e, t1=t1, raw=raw: e.tensor_tensor(t1[:, :n], raw, cosT[:, t0:t0 + n], ALU.mult),
                        reads=[rawb, cx.B("cs")], writes=[t1b])
                    p.V(lambda e, t2=t2, pr=pr: e.tensor_tensor(t2[:, :n], pr[:, :n], sinT[:, t0:t0 + n], ALU.mult),
                        reads=[cx.B("psb", b2), cx.B("cs")], writes=[t2b])
                    p.V(lambda e, t1=t1, t2=t2, rot_=rot_: e.tensor_tensor(rot_, t1[:, :n], t2[:, :n], ALU.add),
                        reads=[t1b, t2b], writes=[rotb])
            if "fm" in PARTS:
                outs.append(p.dma("sync", qraw_o.rearrange("(c p) t -> p c t", p=128)[:, :, t0:t0 + n], qrawS[so][:], f"qo{so}a", reads=[cx.B("qrawS", so)]))
                outs.append(p.dma("sync", qrot_o.rearrange("(c p) t -> p c t", p=128)[:, :, t0:t0 + n], qrotS[so][:], f"qo{so}b", reads=[cx.B("qrotS", so)]))
                outs.append(p.dma("sync", kvT_o.rearrange("s (a p) t -> p (s a) t", p=128)[:, :, t0:t0 + n], kvS[so][:], f"qo{so}c", reads=[cx.B("kvS", so)]))
            for vi, c0 in enumerate((1792, 2304) if "v" in PARTS else ()):
                for tt in range(4):
                    b = 4 + cx.rot("dn", 2)
                    pv = cx.psb[b]
                    for c in range(8):
                        p.mm(pv[:, :256], hc[:, c, tt * 128:(tt + 1) * 128], win[:, c, c0:c0 + 256], c == 0, c == 7,
                             reads=[cx.B("win"), cx.B("hT")], writes=[cx.B("psb", b)])
                    p.A(lambda e, pv=pv, vi=vi, tt=tt: e.activation(vS[so][:, vi, tt, :], pv[:, :256], AF.Copy), reads=[cx.B("psb", b)], writes=[cx.B("vS", so)])
            if "v" in PARTS:
                for vi in range(2):
                    outs.append(p.dma("sync", vtok_o[vi, t0:t0 + n, :].rearrange("(tt p) c -> p tt c", p=128), vS[so][:, vi, :, :], f"qo{so}v{vi}", reads=[cx.B("vS", so)]))
            if "g" not in PARTS:
                continue
            b = cx.rot("gu", 2)
            pq = cx.psb[b]
            for c in range(8):
                p.mm(pq[0:48, :n], win[:, c, 2560:2608], hc[:, c, :n], c == 0, c == 7,
                     reads=[cx.B("win"), cx.B("hT")], writes=[cx.B("psb", b)])
            i = cx.rot("gsb", 2)
            p.A(lambda e, i=i, pq=pq: e.activation(gsb[i][0:48, :n], pq[0:48, :n], AF.Sigmoid, bias=vecs[0:48, bgate_o:bgate_o + 1]),
                reads=[cx.B("psb", b), cx.B("vecs")], writes=[cx.B("gsb", i)])
            outs.append(p.dma("sync", gate_o[:, t0:t0 + n], gsb[i][0:48, :n], "gout", reads=[cx.B("gsb", i)]))

        cx.release(m0)
    return outs


def rope_tables_np(pos0, n):
    pos = (pos0 + np.arange(n)).astype(np.float32)
    inv = (np.float32(500000.0) ** (-np.arange(0, 16, 2, dtype=np.float32) / np.float32(16))).astype(np.float32)
    ang = pos[None, :] * inv[:, None]
    c8, s8 = np.cos(ang).astype(np.float32), np.sin(ang).astype(np.float32)
    cos = np.ones((128, n), np.float32)
    sin = np.zeros((128, n), np.float32)
    for hb in (0, 64):
        cos[hb:hb + 8] = c8
        cos[hb + 8:hb + 16] = c8
        sin[hb:hb + 8] = s8
        sin[hb + 8:hb + 16] = s8
    return cos, sin


def prot_np():
    pr = np.zeros((128, 128), np.float32)
    for hb in (0, 64):
        for j in range(8):
            pr[hb + j + 8, hb + j] = -1.0
            pr[hb + j, hb + 8 + j] = 1.0
    return pr


def prep_launch1(I, core):
    b, j = divmod(core, 4)
    t0 = j * TC
    x = I["x"][b]
    xT = np.zeros((D, TC + HALO), np.float32)
    xT[:, HALO:] = x[t0:t0 + TC].T
    if j > 0:
        xT[:, :HALO] = x[t0 - HALO:t0].T
    voff, nv = l1_vec_layout()
    vecs = np.zeros((128, nv), np.float32)

    def put(name, arr):
        o, k = voff[name]
        vecs[:, o:o + k] = arr

    put("f00pre", pc(I["ffn_norm_pre"][0, 0]))
    put("f00post", pc(I["ffn_norm_post"][0, 0]))
    put("m0pre", pc(I["mix_norm_pre"][0]))
    put("bpw1", pc(I["conv_b_pw1"][0]))
    put("wdw", np.concatenate([pc(I["conv_w_dw"][0, k]) for k in range(CW)], axis=1))
    put("bdw", pc(I["conv_b_dw"][0]))
    put("lng", pc(I["conv_ln_g"][0]))
    put("lnb", pc(I["conv_ln_b"][0]))
    put("bpw2", pc(I["conv_b_pw2"][0]))
    put("m0post", pc(I["mix_norm_post"][0]))
    put("f01pre", pc(I["ffn_norm_pre"][0, 1]))
    put("f01post", pc(I["ffn_norm_post"][0, 1]))
    put("f10pre", pc(I["ffn_norm_pre"][1, 0]))
    put("f10post", pc(I["ffn_norm_post"][1, 0]))
    put("m1pre", pc(I["mix_norm_pre"][1]))
    bg = np.zeros((128, 1), np.float32)
    bg[:48, 0] = I["nsa_b_gate"][0]
    put("bgate", bg)
    put("flag", np.full((128, 1), 0.0 if j == 0 else 1.0, np.float32))
    cos, sin = rope_tables_np(t0, TC)
    m = {"xT": xT, "vecs": vecs, "wpw1": I["conv_w_pw1"][0], "wpw2": I["conv_w_pw2"][0], "win": I["nsa_w_in"][0],
         "ident": np.eye(128, dtype=np.float32), "prot": prot_np(), "ropecos": cos, "ropesin": sin}
    for i, (l, h) in enumerate(((0, 0), (0, 1), (1, 0))):
        m[f"wg{i}"] = I["ffn_w_gate"][l, h]
        m[f"wu{i}"] = I["ffn_w_up"][l, h]
        m[f"wd{i}"] = I["ffn_w_down"][l, h]
    return {k: np.ascontiguousarray(v, dtype=np.float32) for k, v in m.items()}


def l2_consts():
    bf = ml_dtypes.bfloat16
    E = np.zeros((128, 64, 128), np.float32)
    for jt in range(64):
        E[2 * jt, jt, 0:64] = 1.0
        E[2 * jt + 1, jt, 64:128] = 1.0
    i = np.arange(128)[:, None]
    j = np.arange(512)[None, :]
    cmask = np.zeros((128, 5, 512), np.float32)
    for d in range(5):
        cmask[:, d, :] = np.where(16 * i + 31 - 512 * d <= j, 0.0, NEG)
    wmask = np.zeros((128, 8, 512), np.float32)
    for oi in range(8):
        k = 128 * (oi - 4) + i
        wmask[:, oi, :] = np.where((k <= j) & (k > j - 512), 0.0, NEG)
    AB = np.zeros((128, 2, 256), np.float32)
    jj = np.arange(128)[:, None]
    m = np.arange(256)[None, :]
    x = (m - 128) - (jj >= 64)
    forced = (x == 0) | (x == -1)
    nonc = x > 0
    AB[:, 0, :] = np.where(forced | nonc, 0.0, 1.0)
    AB[:, 1, :] = np.where(forced, 1e9, np.where(nonc, -1e9, 0.0))
    ov = np.zeros((128, 4, 130), np.float32)
    for ct in range(4):
        for ii in range(128):
            c = 128 * ct + ii
            if c > 510:
                continue
            for n in range(128):
                if 16 * c < 64 * n + 64 and 16 * c + 32 > 64 * n:
                    ov[ii, ct, n] = 1.0
            ov[ii, ct, 128] = 1.0
    selG = np.zeros((12, 12, 128), np.float32)
    for r in range(12):
        selG[r, r, :] = 1.0
    IND = np.zeros((64, S), np.float32)
    for jt in range(64):
        IND[2 * (jt % 32), jt * 128:jt * 128 + 64] = 1.0
        IND[2 * (jt % 32) + 1, jt * 128 + 64:jt * 128 + 128] = 1.0
    return {"IND": IND.astype(bf), "identb": np.eye(128, dtype=np.float32).astype(bf), "cmask": cmask.astype(bf),
            "wmask": wmask.astype(bf), "AB": AB, "ov": ov.astype(bf), "selG": selG.astype(bf)}


NQG = 16
SCALE = 0.125


def phase_B(cx, T):
    p = cx.p
    oh_d = T["oh"]
    w1_d, w2k_d, w2v_d, pe_d = T["w1"], T["w2kD"], T["w2vD"], T["pe"]
    EX2 = T["EX2"]

    GXL = T["GX1L"]
    oT4 = EX2[0, :].rearrange("(j r t) -> j r t", j=4, t=TC)

    def gq(j, name, g_):
        base = 0 if name == "qraw_o" else 4
        return GXL[base + g_][j, :].rearrange("(r t) -> r t", t=TC)

    def gkv(j, kind):
        return GXL[8 + kind][j, :].rearrange("(r t) -> r t", t=TC)

    def gv(j, vi):
        return GXL[12 + vi][j, :].rearrange("(t c) -> t c", c=256)

    def ggate(j):
        return GXL[14][j, :].rearrange("(r t) -> r t", t=TC)

    if True:
        pst = cx.psb[7][:].bitcast(BF16)
        ones = cx.ones
        oh = cx.carve([128, 4], F32)
        p.dma("sync", oh[:], oh_d, "c_oh", writes=[cx.B("oh")])
        identb = cx.carve([128, 128], BF16)
        cmask = cx.carve([128, 5, 512], BF16)
        wmask = cx.carve([128, 8, 512], BF16)
        AB = cx.carve([128, 2, 256], F32)
        ov = cx.carve([128, 4, 130], BF16)
        selG = cx.carve([12, 12, 128], BF16)
        for dst, nm in ((identb, "identb"), (cmask, "cmask"), (wmask, "wmask"), (AB, "AB"), (ov, "ov"), (selG, "selG")):
            p.dma("sync", dst[:], T[nm], f"c_k{nm}", writes=[cx.B(nm)])
        kselD = cx.carve([128, S], BF16)
        kwinD = cx.carve([128, S], BF16)
        KX1 = cx.carve([128, S], BF16)
        vA = {nm: cx.carve([128, 64, 192], BF16) for nm in ("sel", "win")}
        for t_ in vA.values():
            p.V(lambda e: e.memset(t_[:], 1.0), writes=[cx.B("vA")])
        kcmpT = cx.carve([128, 512], BF16)
        vcmp = cx.carve([128, 4, 128], BF16)
        p.V(lambda e: e.memset(kcmpT[:], 0.0), writes=[cx.B("kcmpT")])
        p.V(lambda e: e.memset(vcmp[:], 0.0), writes=[cx.B("vcmp")])

        def select4(dst, stage, n_part, stagebuf, dstbuf):
            ps_ = slice(0, n_part)
            p.V(lambda e: e.tensor_scalar(dst, stage(0), oh[ps_, 0:1], None, ALU.mult), reads=[stagebuf, cx.B("oh")], writes=[dstbuf])
            for g_ in range(1, 4):
                p.V(lambda e: e.scalar_tensor_tensor(dst, stage(g_), oh[ps_, g_:g_ + 1], dst, ALU.mult, ALU.add),
                    reads=[stagebuf, cx.B("oh"), dstbuf], writes=[dstbuf])

        m0 = cx.mark()
        stg = cx.carve([128, 4, 2048], BF16)
        kv2 = cx.carve([128, S], BF16)
        w1 = cx.carve([128, 16, 256], BF16)
        w2 = cx.carve([128, 2, 128], BF16)
        pe = cx.carve([128, 16], BF16)
        hid = cx.carve([128, 2, 512], BF16)
        bias = cx.carve([128, 2], F32)
        vstg = cx.carve([128, 4, 16, 64], BF16)

        def load_sel_kv(kind, dstT, nm, shifted):
            for j in range(4):
                for g_ in range(4):
                    src = gkv(j, kind)[g_ * 64:(g_ + 1) * 64, :]
                    p.dma("sync", stg[0:64, g_, :], src, f"sg{g_}", writes=[cx.B("stg")])
                    if not shifted:
                        p.dma("sync", stg[64:128, g_, :], src, f"sh{g_}", writes=[cx.B("stg")])
                    else:
                        p.dma("sync", stg[64:128, g_, 0:2047], src[:, 1:2048], f"sh{g_}", writes=[cx.B("stg")])
                        if j < 3:
                            p.dma("sync", stg[64:128, g_, 2047:2048], gkv(j + 1, kind)[g_ * 64:(g_ + 1) * 64, 0:1], f"sh{g_}", writes=[cx.B("stg")], allow_slow_non_contiguous=True)
                        else:
                            p.V(lambda e: e.memset(stg[64:128, g_, 2047:2048], 0.0), writes=[cx.B("stg")])
                select4(dstT[:, j * 2048:(j + 1) * 2048], lambda g_: stg[:, g_, :], 128, cx.B("stg"), cx.B(nm))

        load_sel_kv(2, kselD, "kselD", False)
        p.op("gpsimd", lambda e: e.tensor_copy(KX1[64:128, :], kselD[64:128, :]), reads=[cx.B("kselD")], writes=[cx.B("KX1")])
        p.dma("sync", kselD[64:128, :], T["IND"], "c_ind0", reads=[], writes=[cx.B("kselD")])
        p.dma("sync", KX1[0:64, :], T["IND"], "c_ind1", writes=[cx.B("KX1")])
        load_sel_kv(3, kwinD, "kwinD", False)
        for vi, nm in enumerate(("sel", "win")):
            for j in range(4):
                for g_ in range(4):
                    p.dma("sync", vstg[:, g_, :, :], gv(j, vi)[:, g_ * 64:(g_ + 1) * 64].rearrange("(t p) c -> p t c", p=128),
                          f"sg{g_}", writes=[cx.B("vstg")])
                select4(vA[nm][:, j * 16:(j + 1) * 16, 64:128], lambda g_: vstg[:, g_, :, :], 128, cx.B("vstg"), cx.B("vA"))

        for which, w2_d in enumerate((w2k_d, w2v_d)):
            load_sel_kv(which, kv2, "kv2", True)
            p.dma("gpsimd", w1[:], w1_d[which].rearrange("(c p) j -> p c j", p=128), "c_w1", writes=[cx.B("w1")])
            p.dma("gpsimd", w2[:], w2_d.rearrange("(c p) j -> p c j", p=128), "c_w2", writes=[cx.B("w2")])
            p.dma("gpsimd", pe[:], pe_d[which], "c_pe", writes=[cx.B("pe")])
            for jc in range(2):
                pb = cx.psb[2]
                for ch in range(16):
                    p.mm(pb[:, 0:1], w1[:, ch, jc * 128:(jc + 1) * 128], pe[:, ch:ch + 1], ch == 0, ch == 15,
                         reads=[cx.B("w1"), cx.B("pe")], writes=[cx.B("psb", 2)])
                p.V(lambda e: e.tensor_copy(bias[:, jc:jc + 1], pb[:, 0:1]), reads=[cx.B("psb", 2)], writes=[cx.B("bias")])
                ph = cx.psb[jc]
                for lp in range(16):
                    p.mm(ph[:, 0:511], w1[:, lp, jc * 128:(jc + 1) * 128], kv2[:, 2 * lp:2 * lp + 16 * 510 + 1:16], lp == 0, lp == 15,
                         reads=[cx.B("w1"), cx.B("kv2")], writes=[cx.B("psb", jc)])
                p.A(lambda e: e.activation(hid[:, jc, 0:511], ph[:, 0:511], AF.Silu, bias=bias[:, jc:jc + 1]),
                    reads=[cx.B("psb", jc), cx.B("bias")], writes=[cx.B("hid")])
            if which == 0:
                pk = cx.psb[3]
                for jc in range(2):
                    p.mm(pk[:, 0:511], w2[:, jc, :], hid[:, jc, 0:511], jc == 0, jc == 1, reads=[cx.B("w2"), cx.B("hid")], writes=[cx.B("psb", 3)])
                p.A(lambda e: e.activation(kcmpT[:, 0:511], pk[:, 0:511], AF.Copy), reads=[cx.B("psb", 3)], writes=[cx.B("kcmpT")])
            else:
                for ct in range(4):
                    M = 128 if ct < 3 else 127
                    pv = cx.psb[3 + (ct % 2)]
                    for jc in range(2):
                        p.mm(pv[0:M, 0:128], hid[:, jc, ct * 128:ct * 128 + M], w2[:, jc, :], jc == 0, jc == 1,
                             reads=[cx.B("w2"), cx.B("hid")], writes=[cx.B("psb", 3 + (ct % 2))])
                    p.A(lambda e: e.activation(vcmp[0:M, ct, :], pv[0:M, 0:128], AF.Copy),
                        reads=[cx.B("psb", 3 + (ct % 2))], writes=[cx.B("vcmp")])
        cx.release(m0)

        qstg = cx.carve([128, 4, 2, 512], BF16)
        gstg = cx.carve([12, 4, 512], BF16)
        qraw = [cx.carve([128, 2, 512], BF16) for i in range(2)]
        qrot = [cx.carve([128, 2, 512], BF16) for i in range(2)]
        gts = [cx.carve([12, 512], BF16) for i in range(2)]
        eT = [cx.carve([128, 4, 512], BF16) for i in range(2)]
        pT = [cx.carve([128, 512], BF16) for i in range(3)]
        rz = [cx.carve([128, 512], F32) for i in range(2)]
        wv = [cx.carve([128, 512], F32) for i in range(2)]
        tmp = [cx.carve([128, 512], F32) for i in range(2)]
        acc = cx.carve([128, 2, 512], F32)
        accb = [cx.carve([128, 2, 512], BF16) for i in range(2)]
        impacc = cx.carve([128, 4, 128], F32)
        imod = cx.carve([128, 128], F32)
        scr = cx.carve([128, 128], F32)
        m8 = cx.carve([128, 16], F32)
        rzc = cx.carve([128, 1], F32)
        negm = cx.carve([128, 128], BF16)
        negT = cx.carve([128, 512], BF16)
        Xt = {(h_, a_, w_): cx.carve([128, 512], BF16) for h_ in range(2) for a_ in range(2) for w_ in range(2)}
        outs = []

        def finish_branch(r, gi, pacc, paccbuf, zrows, first, gt, gtb):
            a, half = divmod(r, 2)
            hs = slice(64 * half, 64 * half + 64)
            pG = cx.psb[6]
            p.mm(pG[:, :], selG[:, 3 * r + gi, :], gt[:, :], True, True, reads=[cx.B("selG"), gtb], writes=[cx.B("psb", 6)])
            i = cx.rot("rz", 2)
            p.V(lambda e: e.tensor_scalar(rz[i][zrows, :], pacc[zrows, :], 1e-30, None, ALU.max), reads=[paccbuf], writes=[cx.B("rz", i)])
            p.V(lambda e: e.reciprocal(rz[i][zrows, :], rz[i][zrows, :]), reads=[cx.B("rz", i)], writes=[cx.B("rz", i)])
            p.V(lambda e: e.tensor_tensor(wv[i][zrows, :], rz[i][zrows, :], pG[zrows, :], ALU.mult),
                reads=[cx.B("rz", i), cx.B("psb", 6)], writes=[cx.B("wv", i)])
            if first:
                p.V(lambda e: e.tensor_tensor(acc[hs, a, :], pacc[hs, :], wv[i][zrows, :], ALU.mult),
                    reads=[paccbuf, cx.B("wv", i)], writes=[cx.B("acc")])
            else:
                p.V(lambda e: e.tensor_tensor(tmp[i][hs, :], pacc[hs, :], wv[i][zrows, :], ALU.mult),
                    reads=[paccbuf, cx.B("wv", i)], writes=[cx.B("tmp", i)])
                p.V(lambda e: e.tensor_tensor(acc[hs, a, :], acc[hs, a, :], tmp[i][hs, :], ALU.add),
                    reads=[cx.B("acc"), cx.B("tmp", i)], writes=[cx.B("acc")])

        for qg in range(NQG):
            q0 = qg * 512
            s = cx.rot("qld", 2)
            jq, tl = divmod(qg, 4)
            tl *= 512
            for nmq, dstq, bq in (("qraw_o", qraw[s], cx.B("qraw", s)), ("qrot_o", qrot[s], cx.B("qrot", s))):
                for g_ in range(4):
                    p.dma("sync", qstg[:, g_, :, :], gq(jq, nmq, g_)[:, tl:tl + 512].rearrange("(a p) t -> p a t", p=128),
                          f"qs{g_}", writes=[cx.B("qstg")])
                select4(dstq[:], lambda g_: qstg[:, g_, :, :], 128, cx.B("qstg"), bq)
            for g_ in range(4):
                p.dma("sync", gstg[:, g_, :], ggate(jq)[g_ * 12:(g_ + 1) * 12, tl:tl + 512], f"qs{g_}", writes=[cx.B("gstg")])
            select4(gts[s][:], lambda g_: gstg[:, g_, :], 12, cx.B("gstg"), cx.B("gts", s))
            gt, gtb = gts[s], cx.B("gts", s)
            nct = (32 * qg + 30) // 128 + 1
            for r in range(4):
                a, half = divmod(r, 2)
                hs = slice(64 * half, 64 * half + 64)
                es = cx.rot("eT", 2)
                for ct in range(nct):
                    d = qg - 4 * ct
                    b = cx.rot("S", 2)
                    ps_ = cx.psb[b]
                    p.mm(ps_[:, :], kcmpT[hs, ct * 128:(ct + 1) * 128], qraw[s][hs, a, :], True, d >= 5,
                         reads=[cx.B("kcmpT"), cx.B("qraw", s)], writes=[cx.B("psb", b)])
                    if d < 5:
                        p.mm(ps_[:, :], identb[:], cmask[:, d, :], False, True, reads=[cx.B("identb"), cx.B("cmask")], writes=[cx.B("psb", b)])
                    p.A(lambda e, ps_=ps_, es=es, ct=ct: e.activation(eT[es][:, ct, :], ps_[:, :], AF.Exp, scale=SCALE),
                        reads=[cx.B("psb", b)], writes=[cx.B("eT", es)])
                pO, pZ = cx.psb[2], cx.psb[3]
                for ct in range(nct):
                    p.mm(pO[:, :], vcmp[:, ct, :], eT[es][:, ct, :], ct == 0, ct == nct - 1, reads=[cx.B("vcmp"), cx.B("eT", es)], writes=[cx.B("psb", 2)])
                for ct in range(nct):
                    p.mm(pZ[:, :], ones[:], eT[es][:, ct, :], ct == 0, ct == nct - 1, reads=[cx.B("ones"), cx.B("eT", es)], writes=[cx.B("psb", 3)])
                zrows = slice(64 * (1 - half), 64 * (1 - half) + 64)
                pG = cx.psb[6]
                p.mm(pG[:, :], selG[:, 3 * r + 0, :], gt[:, :], True, True, reads=[cx.B("selG"), gtb], writes=[cx.B("psb", 6)])
                i = cx.rot("rz", 2)
                p.V(lambda e, i=i: e.tensor_scalar(rz[i][hs, :], pZ[hs, :], 1e-30, None, ALU.max), reads=[cx.B("psb", 3)], writes=[cx.B("rz", i)])
                p.V(lambda e, i=i: e.reciprocal(rz[i][hs, :], rz[i][hs, :]), reads=[cx.B("rz", i)], writes=[cx.B("rz", i)])
                p.V(lambda e, i=i: e.tensor_tensor(wv[i][hs, :], rz[i][hs, :], pG[hs, :], ALU.mult),
                    reads=[cx.B("rz", i), cx.B("psb", 6)], writes=[cx.B("wv", i)])
                p.V(lambda e, i=i, a=a: e.tensor_tensor(acc[hs, a, :], pO[hs, :], wv[i][hs, :], ALU.mult),
                    reads=[cx.B("psb", 2), cx.B("wv", i)], writes=[cx.B("acc")])
                for qt in range(4):
                    bi = 4 + cx.rot("I", 2)
                    pI = cx.psb[bi]
                    for ct in range(nct):
                        p.mm(pI[:, 0:129], eT[es][:, ct, qt * 128:(qt + 1) * 128], ov[:, ct, 0:129], ct == 0, ct == nct - 1,
                             reads=[cx.B("eT", es), cx.B("ov")], writes=[cx.B("psb", bi)])
                    p.V(lambda e, pI=pI: e.tensor_scalar(rzc[:], pI[:, 128:129], 1e-30, None, ALU.max), reads=[cx.B("psb", bi)], writes=[cx.B("rzc")])
                    p.V(lambda e: e.reciprocal(rzc[:], rzc[:]), reads=[cx.B("rzc")], writes=[cx.B("rzc")])
                    if r == 0:
                        p.V(lambda e, pI=pI, qt=qt: e.tensor_scalar(impacc[:, qt, :], pI[:, 0:128], rzc[:, 0:1], None, ALU.mult),
                            reads=[cx.B("psb", bi), cx.B("rzc")], writes=[cx.B("impacc")])
                    else:
                        p.V(lambda e, pI=pI, qt=qt: e.scalar_tensor_tensor(impacc[:, qt, :], pI[:, 0:128], rzc[:, 0:1], impacc[:, qt, :], ALU.mult, ALU.add),
                            reads=[cx.B("psb", bi), cx.B("rzc"), cx.B("impacc")], writes=[cx.B("impacc")])
            for qt in range(4):
                ti = 4 * qg + qt
                c0 = 128 - 2 * ti
                p.V(lambda e, qt=qt, c0=c0: e.tensor_tensor(imod[:], impacc[:, qt, :], AB[:, 0, c0:c0 + 128], ALU.mult),
                    reads=[cx.B("impacc"), cx.B("AB")], writes=[cx.B("imod")])
                p.V(lambda e, c0=c0: e.tensor_tensor(imod[:], imod[:], AB[:, 1, c0:c0 + 128], ALU.add), reads=[cx.B("imod"), cx.B("AB")], writes=[cx.B("imod")])
                p.V(lambda e: e.memset(imod[:, 0:1], 1e9), reads=[], writes=[cx.B("imod")])
                p.V(lambda e: e.max(m8[:, 0:8], imod[:]), reads=[cx.B("imod")], writes=[cx.B("m8")])
                p.V(lambda e: e.match_replace(scr[:], m8[:, 0:8], imod[:], -1e30), reads=[cx.B("imod"), cx.B("m8")], writes=[cx.B("scr")])
                p.V(lambda e: e.max(m8[:, 8:16], scr[:]), reads=[cx.B("scr")], writes=[cx.B("m8")])
                p.V(lambda e: e.tensor_scalar(negm[:], imod[:], m8[:, 15:16], NEG, ALU.is_lt, ALU.mult), reads=[cx.B("imod"), cx.B("m8")], writes=[cx.B("negm")])
                p.op("tensor", lambda e, qt=qt: e.transpose(pst[:, qt * 128:(qt + 1) * 128], negm[:], identb[:]),
                     reads=[cx.B("negm"), cx.B("identb")], writes=[cx.B("pst")])
                p.A(lambda e, qt=qt: e.activation(negT[:, qt * 128:(qt + 1) * 128], pst[:, qt * 128:(qt + 1) * 128], AF.Copy),
                    reads=[cx.B("pst")], writes=[cx.B("negT")])
            nwin = 1 if 4 * qg + 3 < 32 else 2
            for half in range(2):
                hs_ = slice(64 * half, 64 * half + 64)
                os_ = slice(64 * (1 - half), 64 * (1 - half) + 64)
                for a_ in range(2):
                    for w_ in range(nwin):
                        xt = Xt[(half, a_, w_)]
                        xb_ = cx.B("Xt", half, a_, w_)
                        p.op("gpsimd", lambda e: e.tensor_copy(xt[hs_, :], qrot[s][hs_, a_, :]), reads=[cx.B("qrot", s)], writes=[xb_])
                        p.V(lambda e: e.tensor_copy(xt[os_, :], negT[64 * w_:64 * w_ + 64, :]), reads=[cx.B("negT")], writes=[xb_])
            for (br, gi, kD, kbuf, jts) in (("sel", 1, kselD, "kselD", list(range(4 * qg + 4))),
                                            ("win", 2, kwinD, "kwinD", list(range(max(0, 4 * qg - 4), 4 * qg + 4)))):
                units = [(ji, jt, r) for ji, jt in enumerate(jts) for r in range(4)]

                def emit_S(u):
                    ji, jt, r = u
                    o = jt - 4 * qg
                    a, half = divmod(r, 2)
                    hs = slice(64 * half, 64 * half + 64)
                    b = cx.rot("S", 2)
                    ps_ = cx.psb[b]
                    need_mask = (br == "win") or (o >= 0)
                    if br == "sel":
                        kx, kxb = (kselD, cx.B("kselD")) if half == 0 else (KX1, cx.B("KX1"))
                        w_ = jt // 32
                        p.mm(ps_[:, :], kx[:, jt * 128:(jt + 1) * 128], Xt[(half, a, w_)][:, :], True, not need_mask,
                             reads=[kxb, cx.B("Xt", half, a, w_)], writes=[cx.B("psb", b)])
                    else:
                        p.mm(ps_[:, :], kD[hs, jt * 128:(jt + 1) * 128], qrot[s][hs, a, :], True, False,
                             reads=[cx.B(kbuf), cx.B("qrot", s)], writes=[cx.B("psb", b)], sig=False)
                    if need_mask:
                        p.mm(ps_[:, :], identb[:], wmask[:, o + 4, :], False, True, reads=[cx.B("identb"), cx.B("wmask")], writes=[cx.B("psb", b)])
                    pi = cx.rot("pT", 3)
                    p.A(lambda e: e.activation(pT[pi][:, :], ps_[:, :], AF.Exp, scale=SCALE),
                        reads=[cx.B("psb", b)], writes=[cx.B("pT", pi)])
                    return pi

                def emit_PV(u, pi):
                    ji, jt, r = u
                    half = r % 2
                    pa = cx.psb[2 + r]
                    p.mm(pa[:, :], vA[br][:, jt, (64 if half == 0 else 0):(192 if half == 0 else 128)], pT[pi][:, :], ji == 0, ji == len(jts) - 1,
                         reads=[cx.B("vA"), cx.B("pT", pi)], writes=[cx.B("psb", 2 + r)], sig=True)

                pend = None
                for u in units:
                    pi = emit_S(u)
                    if pend is not None:
                        emit_PV(*pend)
                    pend = (u, pi)
                emit_PV(*pend)
                for r in range(4):
                    half = r % 2
                    zrows = slice(64 * (1 - half), 64 * (1 - half) + 64)
                    finish_branch(r, gi, cx.psb[2 + r], cx.B("psb", 2 + r), zrows, False, gt, gtb)
            ob = cx.rot("accb", 2)
            p.V(lambda e, ob=ob: e.tensor_copy(accb[ob][:], acc[:]), reads=[cx.B("acc")], writes=[cx.B("accb", ob)])
            outs.append(p.dma("sync", oT4[jq].rearrange("(a p) t -> p a t", p=128)[:, :, tl:tl + 512], accb[ob][:], f"o{ob}", reads=[cx.B("accb", ob)]))
    return outs


def phase_C(cx, T):
    p = cx.p
    xd_d, vecs_d, wout_d, wg_d, wu_d, wd_d, xe, xf, oh_d = (T["xd"], T["vecs3"], T["wout"], T["wg3"], T["wu3"],
                                                              T["wd3"], T["xe"], T["xf"], T["oh"])
    GX2L = T["GX2L"]
    if True:
        vecs = cx.carve([128, 24], F32)
        p.dma("sync", vecs[:], vecs_d, "const", writes=[cx.B("vecs")])
        oh = cx.carve([128, 4], F32)
        p.dma("sync", oh[:], oh_d, "c_oh", writes=[cx.B("oh")])
        for o, sc_ in ((0, 32.0), (8, 32.0), (16, 16.0)):
            p.V(lambda e: e.tensor_scalar(vecs[:, o:o + 8], vecs[:, o:o + 8], sc_, None, ALU.mult), reads=[cx.B("vecs")], writes=[cx.B("vecs")])
        g_m1post = lambda c: vecs[:, c:c + 1]
        g_pre = lambda c: vecs[:, 8 + c:9 + c]
        g_post = lambda c: vecs[:, 16 + c:17 + c]
        m0 = cx.mark()
        wout = cx.carve([128, 8, D], BF16)
        for c in range(8):
            p.dma("gpsimd", wout[:, c, :], wout_d[c * 128:(c + 1) * 128, :], f"wpw{c % 4}", writes=[cx.B("wout")])
        astg = cx.carve([128, 4, 8, 512], BF16)
        at = [cx.carve([128, 8, 512], BF16) for i in range(2)]
        y2 = cx.carve([128, 8, 512], F32)
        for g in range(4):
            t0 = g * 512
            n = 512
            s = cx.rot("at", 2)
            for jj in range(4):
                p.dma("sync", astg[:, jj, :, :], GX2L[jj].rearrange("g (r t) -> (g r) t", t=TC)[:, t0:t0 + n].rearrange("(c p) t -> p c t", p=128),
                      f"as{jj}", writes=[cx.B("astg")])
            p.V(lambda e: e.tensor_scalar(at[s][:], astg[:, 0, :, :], oh[:, 0:1], None, ALU.mult), reads=[cx.B("astg"), cx.B("oh")], writes=[cx.B("at", s)])
            for jj in range(1, 4):
                p.V(lambda e: e.scalar_tensor_tensor(at[s][:], astg[:, jj, :, :], oh[:, jj:jj + 1], at[s][:], ALU.mult, ALU.add),
                    reads=[cx.B("astg"), cx.B("oh"), cx.B("at", s)], writes=[cx.B("at", s)])
            for oc in range(8):
                b = 4 + cx.rot("dn", 2)
                pd = cx.psb[b]
                for c in range(8):
                    p.mm(pd[:, :n], wout[:, c, oc * 128:(oc + 1) * 128], at[s][:, c, :], c == 0, c == 7,
                         reads=[cx.B("wout"), cx.B("at", s)], writes=[cx.B("psb", b)])
                p.A(lambda e: e.activation(y2[:, oc, :n], pd[:, :n], AF.Copy), reads=[cx.B("psb", b)], writes=[cx.B("y2")])
            norm_residual_store(cx, lambda c: y2[:, c, :n], [cx.B("y2")], xd_d, ("xd",), xe, ("xe",), t0, t0, n, g_m1post)
        cx.release(m0)
        alloc_ffn(cx, 1024)
        passes = [[(0, 512), (512, 512)], [(1024, 512), (1536, 512)]]
        ffn(cx, "f11", xe, ("xe",), xf, ("xf",), passes, wg_d, wu_d, wd_d, g_pre, g_post)
    return [cx.B("xf").last_w]


EX1_FIELDS = {"qraw_o": (0, 2097152, (1024, 2048)), "qrot_o": (2097152, 2097152, (1024, 2048)),
              "kvT_o": (4194304, 2097152, (4, 256, 2048)), "vtok_o": (6291456, 1048576, (2, 2048, 256)),
              "gate_o": (7340032, 98304, (48, 2048))}
NEL1 = 7438336
NEL2 = 256 * S
RG = [[0, 1, 2, 3], [4, 5, 6, 7]]


def build_fused():
    cx = Ctx("fused")
    p = cx.p
    nc = cx.nc
    voff, nv = l1_vec_layout()
    T = {}
    T["xT"] = cx.din("xT", [D, TC + HALO])
    T["vecs"] = cx.din("vecs", [128, nv])
    T["wgs"] = [cx.din(f"wg{i}", [D, DFF]) for i in range(3)]
    T["wus"] = [cx.din(f"wu{i}", [D, DFF]) for i in range(3)]
    T["wds"] = [cx.din(f"wd{i}", [DFF, D]) for i in range(3)]
    T["wpw1"] = cx.din("wpw1", [D, 2 * D])
    T["wpw2"] = cx.din("wpw2", [D, D])
    T["win"] = cx.din("win", [D, NIN])
    T["ident"] = cx.din("ident", [128, 128])
    T["prot"] = cx.din("prot", [128, 128])
    T["ropecos"] = cx.din("ropecos", [128, TC])
    T["ropesin"] = cx.din("ropesin", [128, TC])
    T["oh"] = cx.din("oh", [128, 4])
    T["w1"] = cx.din("w1", [2, 2048, 256])
    T["w2kD"] = cx.din("w2kD", [256, 128])
    T["w2vD"] = cx.din("w2vD", [256, 128])
    T["pe"] = cx.din("pe", [2, 128, 16])
    T["IND"] = cx.din("IND", [64, S], BF16)
    T["identb"] = cx.din("identb", [128, 128], BF16)
    T["cmask"] = cx.din("cmask", [128, 5, 512], BF16)
    T["wmask"] = cx.din("wmask", [128, 8, 512], BF16)
    T["AB"] = cx.din("AB", [128, 2, 256])
    T["ov"] = cx.din("ov", [128, 4, 130], BF16)
    T["selG"] = cx.din("selG", [12, 12, 128], BF16)
    T["vecs3"] = cx.din("vecs3", [128, 24])
    T["wout"] = cx.din("wout", [D, D])
    T["wg3"] = cx.din("wg3", [D, DFF])
    T["wu3"] = cx.din("wu3", [D, DFF])
    T["wd3"] = cx.din("wd3", [DFF, D])
    T["xf"] = cx.dout("xf", [D, TC])
    T["xa"] = nc.dram_tensor("xa", [D, TC + HALO], F32).ap()
    for nm in ("xb", "xc", "xd", "xe"):
        T[nm] = nc.dram_tensor(nm, [D, TC], F32).ap()
    EX1 = nc.dram_tensor("ex1", [1, NEL1], BF16).ap()
    EX2 = nc.dram_tensor("ex2", [1, NEL2], BF16).ap()
    CH = 524288
    ch1 = [(k * CH, min(CH, NEL1 - k * CH)) for k in range((NEL1 + CH - 1) // CH)]
    ch2 = [(k * CH, CH) for k in range(NEL2 // CH)]
    GX1L = [nc.dram_tensor(f"gx1_{k}", [4, n_], BF16).ap() for k, (o_, n_) in enumerate(ch1)]
    GX2L = [nc.dram_tensor(f"gx2_{k}", [4, n_], BF16).ap() for k, (o_, n_) in enumerate(ch2)]
    T["EX2"], T["GX1L"], T["GX2L"] = EX2, GX1L, GX2L
    for nm, (o, n, shp) in EX1_FIELDS.items():
        v = EX1[0, o:o + n]
        T[nm] = v.rearrange("(r t) -> r t", t=shp[1]) if len(shp) == 2 else v.rearrange("(k r t) -> k r t", r=shp[1], t=shp[2])

    with cx.st:
        cx.arena_init(51 * 1024)
        cx.ones = cx.carve([128, 128], BF16)
        p.V(lambda e: e.memset(cx.ones[:], 1.0), writes=[cx.B("ones")])
        cx.psb = [cx.ps(f"psb{i}", [128, 512], F32) for i in range(8)]
        cx.sq = [cx.carve([128, 512], BF16) for i in range(2)]
        cx.r32 = [cx.carve([128, 512], F32) for i in range(2)]
        mtop = cx.mark()
        alloc_small(cx)
        phase_A(cx, T)
        cx.release(mtop)
        for k, (o_, n_) in enumerate(ch1):
            p.op("gpsimd", lambda e: e.collective_compute("AllGather", ALU.bypass, RG, [EX1[:, o_:o_ + n_].opt()], [GX1L[k].opt()]),
                 writes=[cx.B("gx1")])
        p.barrier()
        phase_B(cx, T)
        cx.release(mtop)
        for k, (o_, n_) in enumerate(ch2):
            p.op("gpsimd", lambda e: e.collective_compute("AllGather", ALU.bypass, RG, [EX2[:, o_:o_ + n_].opt()], [GX2L[k].opt()]),
                 writes=[cx.B("gx2")])
        p.barrier()
        alloc_small(cx)
        fin = phase_C(cx, T)
        stuck = p.check()
        assert not stuck, stuck
        p.build(final_waits=fin)
    return cx.nc


def kernel(**inputs):
    I = {k: np.asarray(v) for k, v in inputs.items()}
    cores = list(range(8))
    consts = l2_consts()
    w2 = np.asarray(I["nsa_cmp_w2"][0], np.float32)
    shared = dict(consts)
    shared["w1"] = np.ascontiguousarray(I["nsa_cmp_w1"][0], dtype=np.float32)
    shared["w2kD"] = np.ascontiguousarray(np.concatenate([w2[0], w2[0]], 1))
    shared["w2vD"] = np.ascontiguousarray(np.concatenate([w2[1], w2[1]], 1))
    shared["pe"] = np.ascontiguousarray(np.stack([pc(np.asarray(I["nsa_cmp_pos"][0, i]).reshape(-1)) for i in range(2)], 0))
    shared["vecs3"] = np.ascontiguousarray(np.concatenate([pc(I["mix_norm_post"][1]), pc(I["ffn_norm_pre"][1, 1]), pc(I["ffn_norm_post"][1, 1])], axis=1))
    shared["wout"] = np.ascontiguousarray(I["nsa_w_out"][0], dtype=np.float32)
    shared["wg3"] = np.ascontiguousarray(I["ffn_w_gate"][1, 1], dtype=np.float32)
    shared["wu3"] = np.ascontiguousarray(I["ffn_w_up"][1, 1], dtype=np.float32)
    shared["wd3"] = np.ascontiguousarray(I["ffn_w_down"][1, 1], dtype=np.float32)
    maps = []
    for c in cores:
        m = prep_launch1(I, c)
        m.update(shared)
        oh = np.zeros((128, 4), np.float32)
        oh[:, c % 4] = 1.0
        m["oh"] = oh
        maps.append(m)
    res = run_bass_kernel_spmd(build_fused(), maps, core_ids=cores).results
    out = np.zeros((2, S, D), np.float32)
    for c in cores:
        b, j = divmod(c, 4)
        out[b, j * TC:(j + 1) * TC] = np.asarray(res[c]["xf"]).T
    return out
```

```python
import contextlib
import numpy as np
import ml_dtypes
import concourse.bass as bass
import concourse.mybir as mybir
from concourse.bass_utils import run_bass_kernel_spmd

F32 = mybir.dt.float32
BF16 = mybir.dt.bfloat16
AF = mybir.ActivationFunctionType
ALU = mybir.AluOpType

ENGS = ["tensor", "vector", "scalar", "gpsimd", "sync"]

D = 1024
DFF = 2816
NFC = 22
S = 8192
TC = 2048
HALO = 128
CW = 31
RMS_EPS = 1e-6
LN_EPS = 1e-5
NEG = -30000.0
NIN = 2608


class Buf:
    __slots__ = ("name", "last_w", "readers")

    def __init__(self, name=""):
        self.name = name
        self.last_w = None
        self.readers = []


class _Rec:
    def __init__(self):
        self.call = None

    def __getattr__(self, name):
        def f(*a, **k):
            self.call = (name, a, k)
            return None
        return f


class Prog:
    def __init__(self, nc):
        self.nc = nc
        self.ops = {e: [] for e in ENGS}
        self.cnt = {e: 0 for e in ENGS}
        self.seen = {e: {} for e in ENGS}
        self.dcnt = {}
        self.pending = {e: {} for e in ENGS}

    def barrier(self):
        snap = dict(self.cnt)
        snap.update(self.dcnt)
        for e in ENGS:
            for k, v in snap.items():
                if v > 0 and not (e == "tensor" and k == "tensor"):
                    if self.pending[e].get(k, 0) < v:
                        self.pending[e][k] = v

    def op(self, eng, fn, reads=(), writes=(), dma=None, sig=True):
        rec = _Rec()
        fn(rec)
        name_, args_, kw_ = rec.call
        fn = lambda e: getattr(e, name_)(*args_, **kw_)
        waits = dict(self.pending[eng])
        self.pending[eng] = {}

        def need(ev, war=False):
            if ev is None:
                return
            k, v = ev
            if k == eng:
                if eng == "tensor" or war:
                    return
            if waits.get(k, 0) < v:
                waits[k] = v

        for b in reads:
            need(b.last_w)
        for b in writes:
            need(b.last_w)
            for r in b.readers:
                need(r, war=True)
        w = []
        for k, v in waits.items():
            if self.seen[eng].get(k, 0) < v:
                self.seen[eng][k] = v
                w.append((k, v))
        if dma is not None:
            prev = self.dcnt.get(dma, 0)
            if prev > 0 and self.seen[eng].get(dma, 0) < prev:
                self.seen[eng][dma] = prev
                w.append((dma, prev))
            self.dcnt[dma] = self.dcnt.get(dma, 0) + 16
            ev = (dma, self.dcnt[dma])
            inc = (dma, 16)
        elif eng == "tensor" and not sig:
            ev = (eng, self.cnt[eng] + 1)
            inc = None
        else:
            self.cnt[eng] += 1
            ev = (eng, self.cnt[eng])
            inc = (eng, 1)
        self.ops[eng].append((fn, w, inc))
        for b in reads:
            b.readers.append(ev)
            if len(b.readers) > 64:
                best = {}
                for k, v in b.readers:
                    if best.get(k, 0) < v:
                        best[k] = v
                b.readers = list(best.items())
        for b in writes:
            b.last_w = ev
            b.readers = []
        return ev

    def mm(self, out, lhsT, rhs, start, stop, reads=(), writes=(), sig=None, **kw):
        if sig is None:
            sig = stop
        return self.op("tensor", lambda e: e.matmul(out, lhsT, rhs, start=start, stop=stop, **kw),
                       reads=reads, writes=writes, sig=sig)

    def dma(self, eng, out, in_, sem, reads=(), writes=(), **kw):
        return self.op(eng, lambda e: e.dma_start(out=out, in_=in_, **kw), reads=reads, writes=writes, dma=sem)

    def V(self, fn, reads=(), writes=()):
        return self.op("vector", fn, reads, writes)

    def A(self, fn, reads=(), writes=()):
        return self.op("scalar", fn, reads, writes)

    def check(self):
        sem = {}
        pos = {e: 0 for e in ENGS}
        n = {e: len(self.ops[e]) for e in ENGS}
        progress = True
        while progress:
            progress = False
            for e in ENGS:
                while pos[e] < n[e]:
                    fn, w, inc = self.ops[e][pos[e]]
                    if all(sem.get(k, 0) >= v for k, v in w):
                        if inc is not None:
                            sem[inc[0]] = sem.get(inc[0], 0) + inc[1]
                        pos[e] += 1
                        progress = True
                    else:
                        break
        stuck = {e: (pos[e], n[e], [(k, v, sem.get(k, 0)) for k, v in self.ops[e][pos[e]][1]]) for e in ENGS if pos[e] < n[e]}
        return stuck

    def build(self, final_waits=()):
        nc = self.nc
        names = list(ENGS) + sorted(self.dcnt.keys())
        with contextlib.ExitStack() as st:
            sems = {n: st.enter_context(nc.semaphore("s_" + n)) for n in names}
            block = st.enter_context(nc.Block())
            fw = {}
            for ev in final_waits:
                if ev is not None and fw.get(ev[0], 0) < ev[1]:
                    fw[ev[0]] = ev[1]
            for eng in ENGS:
                ops = self.ops[eng]
                if eng == "sync":
                    ops = ops + [(None, list(fw.items()), None)]
                if not ops:
                    continue

                def body(e, ops=ops):
                    for fn, w, inc in ops:
                        for k, v in w:
                            e.wait_ge(sems[k], v)
                        if fn is None:
                            continue
                        ins = fn(e)
                        if inc is not None:
                            ins.then_inc(sems[inc[0]], inc[1])

                getattr(block, eng)(body)


class Ctx:
    def __init__(self, name):
        self.nc = bass.Bass("TRN2", target_bir_lowering=False)
        self.p = Prog(self.nc)
        self.st = contextlib.ExitStack()
        self.bufs = {}
        self.outs = []
        self.rr = {}

    def din(self, name, shape, dt=F32):
        return self.nc.dram_tensor(name, list(shape), dt, kind="ExternalInput").ap()

    def dout(self, name, shape, dt=F32):
        return self.nc.dram_tensor(name, list(shape), dt, kind="ExternalOutput").ap()

    def sb(self, name, shape, dt):
        return self.st.enter_context(self.nc.sbuf_tensor(name, list(shape), dt))

    def ps(self, name, shape, dt=F32):
        return self.st.enter_context(self.nc.psum_tensor(name, list(shape), dt))

    def arena_init(self, nwords):
        self.arena = self.sb("arena", [128, nwords], F32)
        self.top = 0
        self.nwords = nwords

    def carve(self, shape, dt):
        nfree = 1
        for d in shape[1:]:
            nfree *= d
        words = nfree if dt == F32 else (nfree + 1) // 2
        a = self.top
        self.top += words
        assert self.top <= self.nwords, ("arena overflow", self.top, self.nwords)
        ap = self.arena[:, a:a + words]
        if dt != F32:
            ap = ap.bitcast(dt)[:, :nfree]
        if len(shape) == 3:
            ap = ap.rearrange("p (a b) -> p a b", b=shape[2])
        elif len(shape) == 4:
            ap = ap.rearrange("p (a b c) -> p a b c", b=shape[2], c=shape[3])
        if shape[0] < 128:
            ap = ap[0:shape[0]]
        return ap

    def mark(self):
        return self.top

    def release(self, m):
        self.top = m
        self.p.barrier()

    def B(self, *key):
        b = self.bufs.get(key)
        if b is None:
            b = self.bufs[key] = Buf(str(key))
        return b

    def rot(self, key, n):
        i = self.rr.get(key, 0)
        self.rr[key] = (i + 1) % n
        return i


def pc(v):
    v = np.asarray(v, np.float32)
    return np.ascontiguousarray(v.reshape(-1, 128).T)


def setup_common(cx, n_ps=8):
    cx.arena_init(50 * 1024)
    cx.ones = cx.carve([128, 128], BF16)
    cx.p.V(lambda e: e.memset(cx.ones[:], 1.0), writes=[cx.B("ones")])
    cx.psb = [cx.ps(f"psb{i}", [128, 512], F32) for i in range(n_ps)]
    cx.sq = [cx.carve([128, 512], BF16) for i in range(2)]
    cx.r32 = [cx.carve([128, 512], F32) for i in range(2)]


def rms_stats(cx, src, srcbufs, n, psi, eps_scaled, nch=8):
    p = cx.p
    ps = cx.psb[psi]
    for c in range(nch):
        i = cx.rot("sq", 2)
        sq = cx.sq[i]
        p.A(lambda e, c=c, sq=sq: e.activation(sq[:, :n], src(c), AF.Square), reads=srcbufs, writes=[cx.B("sq", i)])
        p.mm(ps[:, :n], cx.ones[:], sq[:, :n], c == 0, c == nch - 1, reads=[cx.B("sq", i), cx.B("ones")],
             writes=[cx.B("psb", psi)], sig=True)
    j = cx.rot("r32", 2)
    r = cx.r32[j]
    p.A(lambda e: e.activation(r[:, :n], ps[:, :n], AF.Sqrt, bias=float(eps_scaled), scale=1.0),
        reads=[cx.B("psb", psi)], writes=[cx.B("r32", j)])
    p.V(lambda e: e.reciprocal(r[:, :n], r[:, :n]), reads=[cx.B("r32", j)], writes=[cx.B("r32", j)])
    return r, cx.B("r32", j)


def load_x_group(cx, xdram, xbufkey, off, n):
    i = cx.rot("xin", 2)
    cx.xin = cx.xins[i]
    cx.xinb = cx.B("xin", i)
    src = xdram.rearrange("(c p) t -> p c t", p=128)[:, :, off:off + n]
    cx.p.dma("sync", cx.xin[:, :, :n], src, f"xin{i}", reads=[cx.B(*xbufkey)], writes=[cx.xinb])


def norm_to_h(cx, hdst, hbuf, g32col, n, psi=6):
    xin, xinb = cx.xin, cx.xinb
    r, rb = rms_stats(cx, lambda c: xin[:, c, :n], [xinb], n, psi, D * RMS_EPS)
    for c in range(8):
        cx.p.V(lambda e, c=c: e.scalar_tensor_tensor(hdst(c), xin[:, c, :n], g32col(c), r[:, :n], ALU.mult, ALU.mult),
               reads=[xinb, rb, cx.B("vecs")], writes=[hbuf])


def norm_residual_store(cx, ysrc, ybufs, xin_dram, xin_key, xout_dram, xout_key, off_in, off_out, n, gcol, psi=6):
    p = cx.p
    r, rb = rms_stats(cx, ysrc, ybufs, n, psi, D * RMS_EPS)
    load_x_group(cx, xin_dram, xin_key, off_in, n)
    xin, xinb = cx.xin, cx.xinb
    for c in range(8):
        i = cx.rot("ntmp", 2)
        t = cx.ntmp[i]
        p.V(lambda e, c=c, t=t: e.scalar_tensor_tensor(t[:, :n], ysrc(c), gcol(c), r[:, :n], ALU.mult, ALU.mult),
            reads=ybufs + [rb, cx.B("vecs")], writes=[cx.B("ntmp", i)])
        p.V(lambda e, c=c, t=t: e.tensor_tensor(xin[:, c, :n], xin[:, c, :n], t[:, :n], ALU.add),
            reads=[cx.B("ntmp", i), xinb], writes=[xinb])
    dst = xout_dram.rearrange("(c p) t -> p c t", p=128)[:, :, off_out:off_out + n]
    return p.dma("sync", dst, xin[:, :, :n], "xout", reads=[xinb], writes=[cx.B(*xout_key)])


def alloc_small(cx):
    cx.xins = [cx.carve([128, 8, 512], F32) for i in range(2)]
    cx.sg = [cx.carve([128, 512], F32) for i in range(2)]
    cx.ntmp = [cx.carve([128, 512], F32) for i in range(2)]


def alloc_ffn(cx, maxtok):
    cx.hT = [cx.carve([128, 8, maxtok], BF16) for i in range(2)]
    cx.aT = cx.carve([128, NFC, maxtok], BF16)
    cx.ytmp = cx.carve([128, 8, maxtok], F32)
    cx.wg = [cx.carve([128, 8, 256], BF16) for i in range(2)]
    cx.wu = [cx.carve([128, 8, 256], BF16) for i in range(2)]
    cx.wd = [cx.carve([128, NFC, 128], BF16) for i in range(2)]


def ffn(cx, tag, xin_dram, xin_key, xout_dram, xout_key, passes, wg_d, wu_d, wd_d, gpre, gpost, out_shift=0):
    p = cx.p
    last = None
    hTs, aT, ytmp, wg, wu, wd = cx.hT, cx.aT, cx.ytmp, cx.wg, cx.wu, cx.wd

    def locs(groups):
        loc, o = [], 0
        for (off, n) in groups:
            loc.append(o)
            o += n
        return loc

    def S1(k):
        groups = passes[k]
        loc = locs(groups)
        hT = hTs[k % 2]
        for gi, (off, n) in enumerate(groups):
            load_x_group(cx, xin_dram, xin_key, off, n)
            lo = loc[gi]
            norm_to_h(cx, lambda c, lo=lo, n=n: hT[:, c, lo:lo + n], cx.B("hT", k % 2), gpre, n)

    def S2(k):
        groups = passes[k]
        loc = locs(groups)
        hT, hb = hTs[k % 2], cx.B("hT", k % 2)
        for fp in range(NFC // 2):
            s = cx.rot("wgu", 2)
            p.dma("gpsimd", wg[s][:], wg_d.rearrange("(c p) f -> p c f", p=128)[:, :, fp * 256:(fp + 1) * 256],
                  f"wg{s}", writes=[cx.B("wg", s)])
            p.dma("gpsimd", wu[s][:], wu_d.rearrange("(c p) f -> p c f", p=128)[:, :, fp * 256:(fp + 1) * 256],
                  f"wu{s}", writes=[cx.B("wu", s)])
            for h in range(2):
                fc = fp * 2 + h
                for gi, (off, n) in enumerate(groups):
                    lo = loc[gi]
                    b = cx.rot("gu", 2)
                    pg, pu = cx.psb[b], cx.psb[2 + b]
                    for c in range(8):
                        p.mm(pg[:, :n], wg[s][:, c, h * 128:(h + 1) * 128], hT[:, c, lo:lo + n], c == 0, c == 7,
                             reads=[cx.B("wg", s), hb], writes=[cx.B("psb", b)])
                    for c in range(8):
                        p.mm(pu[:, :n], wu[s][:, c, h * 128:(h + 1) * 128], hT[:, c, lo:lo + n], c == 0, c == 7,
                             reads=[cx.B("wu", s), hb], writes=[cx.B("psb", 2 + b)])
                    sg = cx.sg[b]
                    p.A(lambda e: e.activation(sg[:, :n], pg[:, :n], AF.Silu),
                        reads=[cx.B("psb", b)], writes=[cx.B("sg", b)])
                    p.V(lambda e: e.tensor_tensor(aT[:, fc, lo:lo + n], pu[:, :n], sg[:, :n], ALU.mult),
                        reads=[cx.B("psb", 2 + b), cx.B("sg", b)], writes=[cx.B("aT")])

    def S3(k):
        groups = passes[k]
        loc = locs(groups)
        for dc in range(8):
            s = cx.rot("wd", 2)
            p.dma("gpsimd", wd[s][:], wd_d.rearrange("(fc p) d -> p fc d", p=128)[:, :, dc * 128:(dc + 1) * 128],
                  f"wd{s}", writes=[cx.B("wd", s)])
            for gi, (off, n) in enumerate(groups):
                lo = loc[gi]
                b = 4 + cx.rot("dn", 2)
                pd = cx.psb[b]
                for fc in range(NFC):
                    p.mm(pd[:, :n], wd[s][:, fc, :], aT[:, fc, lo:lo + n], fc == 0, fc == NFC - 1,
                         reads=[cx.B("wd", s), cx.B("aT")], writes=[cx.B("psb", b)])
                p.A(lambda e: e.activation(ytmp[:, dc, lo:lo + n], pd[:, :n], AF.Copy),
                    reads=[cx.B("psb", b)], writes=[cx.B("ytmp")])

    def S4(k):
        nonlocal last
        groups = passes[k]
        loc = locs(groups)
        for gi, (off, n) in enumerate(groups):
            lo = loc[gi]
            last = norm_residual_store(cx, lambda c, lo=lo, n=n: ytmp[:, c, lo:lo + n], [cx.B("ytmp")],
                                       xin_dram, xin_key, xout_dram, xout_key, off, off - out_shift, n, gpost)

    if passes:
        S1(0)
    for k in range(len(passes)):
        S2(k)
        if k + 1 < len(passes):
            S1(k + 1)
        S3(k)
        S4(k)
    return last


def l1_vec_layout():
    names = [("f00pre", 8), ("f00post", 8), ("m0pre", 8), ("bpw1", 16), ("wdw", 8 * CW), ("bdw", 8), ("lng", 8),
             ("lnb", 8), ("bpw2", 8), ("m0post", 8), ("f01pre", 8), ("f01post", 8), ("f10pre", 8), ("f10post", 8),
             ("m1pre", 8), ("bgate", 1), ("flag", 1)]
    off = {}
    o = 0
    for n, k in names:
        off[n] = (o, k)
        o += k
    return off, o


def phase_A(cx, T):
    p = cx.p
    TT = TC + HALO
    voff, nv = l1_vec_layout()
    xT, vecs_d, wgs, wus, wds = T["xT"], T["vecs"], T["wgs"], T["wus"], T["wds"]
    wpw1_d, wpw2_d, win_d, ident_d, prot_d, cos_d, sin_d = T["wpw1"], T["wpw2"], T["win"], T["ident"], T["prot"], T["ropecos"], T["ropesin"]
    xa, xb, xc, xd = T["xa"], T["xb"], T["xc"], T["xd"]
    qraw_o, qrot_o, kvT_o, vtok_o, gate_o = T["qraw_o"], T["qrot_o"], T["kvT_o"], T["vtok_o"], T["gate_o"]
    if True:
        vecs = cx.carve([128, nv], F32)
        p.dma("sync", vecs[:], vecs_d, "const", writes=[cx.B("vecs")])

        def vcol(name, scale=None):
            o, k = voff[name]
            if scale is not None:
                p.V(lambda e: e.tensor_scalar(vecs[:, o:o + k], vecs[:, o:o + k], float(scale), None, ALU.mult),
                    reads=[cx.B("vecs")], writes=[cx.B("vecs")])
            return lambda c: vecs[:, o + c:o + c + 1]

        g_f00pre = vcol("f00pre", 32.0)
        g_f00post = vcol("f00post", 16.0)
        g_m0pre = vcol("m0pre", 32.0)
        g_m0post = vcol("m0post", 32.0)
        g_f01pre = vcol("f01pre", 32.0)
        g_f01post = vcol("f01post", 16.0)
        g_f10pre = vcol("f10pre", 32.0)
        g_f10post = vcol("f10post", 16.0)
        g_m1pre = vcol("m1pre", 32.0)
        bpw1 = vcol("bpw1")
        bdw = vcol("bdw")
        lng = vcol("lng")
        lnb = vcol("lnb")
        bpw2 = vcol("bpw2")
        wdw_o = voff["wdw"][0]
        bgate_o = voff["bgate"][0]
        flag_o = voff["flag"][0]

        identb = cx.carve([128, 128], BF16)
        p.dma("gpsimd", identb[:], ident_d, "const2", writes=[cx.B("identb")])
        prot = cx.carve([128, 128], BF16)
        p.dma("gpsimd", prot[:], prot_d, "const2", writes=[cx.B("prot")])
        SKIP = False
        m0 = cx.mark()
        alloc_ffn(cx, 1152)
        passes0 = [] if SKIP else [[(0, 128), (128, 512), (640, 512)], [(1152, 512), (1664, 512)]]
        ffn(cx, "f00", xT, ("xT",), xa, ("xa",), passes0, wgs[0], wus[0], wds[0], g_f00pre, g_f00post)
        cx.release(m0)

        wpw1 = cx.carve([128, 8, 2 * D], BF16)
        wpw2 = cx.carve([128, 8, D], BF16)
        for c in range(8):
            p.dma("gpsimd", wpw1[:, c, :], wpw1_d[c * 128:(c + 1) * 128, :], f"wpw{c % 4}", writes=[cx.B("wpw1")])
        for c in range(8):
            p.dma("gpsimd", wpw2[:, c, :], wpw2_d[c * 128:(c + 1) * 128, :], f"wpw{c % 4}", writes=[cx.B("wpw2")])
        glu = cx.carve([128, 8, TT], BF16)
        vbs = [cx.carve([128, 512], BF16) for q in range(2)]
        y2 = cx.carve([128, 8, 512], F32)
        dg = [cx.carve([128, CW, 128], BF16) for i in range(2)]
        hc = cx.carve([128, 8, 512], BF16)
        vt = cx.carve([128, 8, 512], F32)
        sc = cx.carve([128, 8, 512], BF16)
        for (off, n) in ([] if SKIP else [(0, 128), (128, 512), (640, 512), (1152, 512), (1664, 512)]):
            load_x_group(cx, xa, ("xa",), off, n)
            norm_to_h(cx, lambda c, n=n: hc[:, c, :n], cx.B("hT"), g_m0pre, n)
            for oc in range(8):
                b = cx.rot("gu", 2)
                pa, pg = cx.psb[b], cx.psb[2 + b]
                for c in range(8):
                    p.mm(pa[:, :n], wpw1[:, c, oc * 128:(oc + 1) * 128], hc[:, c, :n], c == 0, c == 7,
                         reads=[cx.B("wpw1"), cx.B("hT")], writes=[cx.B("psb", b)])
                for c in range(8):
                    p.mm(pg[:, :n], wpw1[:, c, D + oc * 128:D + (oc + 1) * 128], hc[:, c, :n], c == 0, c == 7,
                         reads=[cx.B("wpw1"), cx.B("hT")], writes=[cx.B("psb", 2 + b)])
                sg = cx.sg[b]
                p.A(lambda e, sg=sg, pg=pg, n=n, oc=oc: e.activation(sg[:, :n], pg[:, :n], AF.Sigmoid, bias=bpw1(8 + oc)),
                    reads=[cx.B("psb", 2 + b), cx.B("vecs")], writes=[cx.B("sg", b)])
                p.V(lambda e, sg=sg, pa=pa, n=n, oc=oc, off=off: e.scalar_tensor_tensor(
                    glu[:, oc, off:off + n], pa[:, :n], bpw1(oc), sg[:, :n], ALU.add, ALU.mult),
                    reads=[cx.B("psb", b), cx.B("sg", b), cx.B("vecs")], writes=[cx.B("glu")])
            if off == 0:
                for oc in range(8):
                    p.V(lambda e, oc=oc: e.tensor_scalar(glu[:, oc, 0:128], glu[:, oc, 0:128], vecs[:, flag_o:flag_o + 1], None, ALU.mult),
                        reads=[cx.B("glu"), cx.B("vecs")], writes=[cx.B("glu")])
        for g in range(0 if SKIP else 4):
            t0 = HALO + g * 512
            n = 512
            for cc in range(8):
                di = cx.rot("dg", 2)
                for k in range(CW):
                    p.V(lambda e, k=k, cc=cc, di=di: e.tensor_scalar(dg[di][:, k, :], identb[:], vecs[:, wdw_o + k * 8 + cc:wdw_o + k * 8 + cc + 1], None, ALU.mult),
                        reads=[cx.B("identb"), cx.B("vecs")], writes=[cx.B("dg", di)])
                b = 4 + cx.rot("dn", 2)
                pd = cx.psb[b]
                for k in range(CW):
                    s0 = t0 - (CW - 1) + k
                    p.mm(pd[:, :n], dg[di][:, k, :], glu[:, cc, s0:s0 + n], k == 0, k == CW - 1,
                         reads=[cx.B("dg", di), cx.B("glu")], writes=[cx.B("psb", b)])
                p.A(lambda e, pd=pd, cc=cc: e.activation(vt[:, cc, :n], pd[:, :n], AF.Identity, bias=bdw(cc)),
                    reads=[cx.B("psb", b), cx.B("vecs")], writes=[cx.B("ytmp")])
            psm, psq = cx.psb[6], cx.psb[7]
            for cc in range(8):
                i = cx.rot("sq", 2)
                sq = cx.sq[i]
                p.A(lambda e, cc=cc, sq=sq: e.activation(sq[:, :n], vt[:, cc, :n], AF.Square), reads=[cx.B("ytmp")], writes=[cx.B("sq", i)])
                p.mm(psq[:, :n], cx.ones[:], sq[:, :n], cc == 0, cc == 7, reads=[cx.B("sq", i), cx.B("ones")], writes=[cx.B("psb", 7)], sig=True)
                j = cx.rot("vb", 2)
                vb = vbs[j]
                p.V(lambda e, cc=cc, vb=vb: e.tensor_copy(vb[:, :n], vt[:, cc, :n]), reads=[cx.B("ytmp")], writes=[cx.B("vb", j)])
                p.mm(psm[:, :n], cx.ones[:], vb[:, :n], cc == 0, cc == 7, reads=[cx.B("vb", j), cx.B("ones")], writes=[cx.B("psb", 6)], sig=True)
            mean, m2 = cx.r32[0], cx.r32[1]
            rs = cx.ntmp[0]
            p.V(lambda e: e.tensor_scalar(mean[:, :n], psm[:, :n], 1.0 / D, None, ALU.mult), reads=[cx.B("psb", 6)], writes=[cx.B("r32", 0)])
            p.V(lambda e: e.tensor_tensor(m2[:, :n], mean[:, :n], mean[:, :n], ALU.mult), reads=[cx.B("r32", 0)], writes=[cx.B("r32", 1)])
            p.V(lambda e: e.scalar_tensor_tensor(rs[:, :n], psq[:, :n], 1.0 / D, m2[:, :n], ALU.mult, ALU.subtract),
                reads=[cx.B("psb", 7), cx.B("r32", 1)], writes=[cx.B("ntmp", 0)])
            p.A(lambda e: e.activation(rs[:, :n], rs[:, :n], AF.Sqrt, bias=float(LN_EPS), scale=1.0), reads=[cx.B("ntmp", 0)], writes=[cx.B("ntmp", 0)])
            p.V(lambda e: e.reciprocal(rs[:, :n], rs[:, :n]), reads=[cx.B("ntmp", 0)], writes=[cx.B("ntmp", 0)])
            for cc in range(8):
                p.V(lambda e, cc=cc: e.tensor_tensor(vt[:, cc, :n], vt[:, cc, :n], mean[:, :n], ALU.subtract),
                    reads=[cx.B("ytmp"), cx.B("r32", 0)], writes=[cx.B("ytmp")])
                p.V(lambda e, cc=cc: e.tensor_tensor(vt[:, cc, :n], vt[:, cc, :n], rs[:, :n], ALU.mult),
                    reads=[cx.B("ytmp"), cx.B("ntmp", 0)], writes=[cx.B("ytmp")])
                p.A(lambda e, cc=cc: e.activation(sc[:, cc, :n], vt[:, cc, :n], AF.Silu, bias=lnb(cc), scale=lng(cc)),
                    reads=[cx.B("ytmp"), cx.B("vecs")], writes=[cx.B("aT")])
            for oc in range(8):
                b = 4 + cx.rot("dn", 2)
                pd = cx.psb[b]
                for c in range(8):
                    p.mm(pd[:, :n], wpw2[:, c, oc * 128:(oc + 1) * 128], sc[:, c, :n], c == 0, c == 7,
                         reads=[cx.B("wpw2"), cx.B("aT")], writes=[cx.B("psb", b)])
                p.A(lambda e, pd=pd, oc=oc: e.activation(y2[:, oc, :n], pd[:, :n], AF.Identity, bias=bpw2(oc)),
                    reads=[cx.B("psb", b), cx.B("vecs")], writes=[cx.B("y2")])
            norm_residual_store(cx, lambda c: y2[:, c, :n], [cx.B("y2")], xa, ("xa",), xb, ("xb",), t0, t0 - HALO, n, g_m0post)

        cx.release(m0)
        alloc_ffn(cx, 1024)
        passes = [] if SKIP else [[(0, 512), (512, 512)], [(1024, 512), (1536, 512)]]
        ffn(cx, "f01", xb, ("xb",), xc, ("xc",), passes, wgs[1], wus[1], wds[1], g_f01pre, g_f01post)
        ffn(cx, "f10", xc, ("xc",), xd, ("xd",), passes, wgs[2], wus[2], wds[2], g_f10pre, g_f10post)

        cx.release(m0)
        hc = cx.carve([128, 8, 512], BF16)
        win = cx.carve([128, 8, NIN], BF16)
        for c in range(8):
            p.dma("gpsimd", win[:, c, :], win_d[c * 128:(c + 1) * 128, :], f"wpw{c % 4}", writes=[cx.B("win")])
        cosT = cx.carve([128, TC], F32)
        sinT = cx.carve([128, TC], F32)
        p.dma("sync", cosT[:], cos_d, "const", writes=[cx.B("cs")])
        p.dma("sync", sinT[:], sin_d, "const", writes=[cx.B("cs")])
        qrawS = [cx.carve([128, 8, 512], BF16) for i in range(2)]
        qrotS = [cx.carve([128, 8, 512], BF16) for i in range(2)]
        kvS = [cx.carve([128, 8, 512], BF16) for i in range(2)]
        vtk = [cx.carve([128, 512], BF16) for i in range(2)]
        gsb = [cx.carve([128, 512], BF16) for i in range(2)]
        vS = [cx.carve([128, 2, 4, 256], BF16) for i in range(2)]
        outs = []
        PARTS = ["fm", "v", "g"]
        for g in range(4):
            t0 = g * 512
            n = 512
            load_x_group(cx, xd, ("xd",), t0, n)
            norm_to_h(cx, lambda c: hc[:, c, :n], cx.B("hT"), g_m1pre, n)
            so = cx.rot("nsao", 2)
            fm = [(oc, "q") for oc in range(8)] + [(8, "kc"), (9, "kc"), (10, "vc"), (11, "vc"), (12, "ks"), (13, "ks"), (16, "kw"), (17, "kw")]
            kvidx = {"kc": 0, "vc": 2, "ks": 4, "kw": 6}
            for (oc, kind) in (fm if "fm" in PARTS else []):
                b = cx.rot("gu", 2)
                pq = cx.psb[b]
                for c in range(8):
                    p.mm(pq[:, :n], win[:, c, oc * 128:(oc + 1) * 128], hc[:, c, :n], c == 0, c == 7,
                         reads=[cx.B("win"), cx.B("hT")], writes=[cx.B("psb", b)])
                if kind == "q":
                    raw, rawb = qrawS[so][:, oc, :], cx.B("qrawS", so)
                    rot_, rotb = qrotS[so][:, oc, :], cx.B("qrotS", so)
                elif kind in ("kc", "vc"):
                    raw, rawb = kvS[so][:, kvidx[kind] + oc % 2, :], cx.B("kvS", so)
                else:
                    i = cx.rot("qb", 2)
                    raw, rawb = vtk[i][:, :], cx.B("vtk", i)
                    rot_, rotb = kvS[so][:, kvidx[kind] + oc % 2, :], cx.B("kvS", so)
                p.A(lambda e, raw=raw, pq=pq: e.activation(raw, pq[:, :n], AF.Copy), reads=[cx.B("psb", b)], writes=[rawb])
                if kind in ("q", "ks", "kw"):
                    b2 = 2 + cx.rot("rp", 2)
                    pr = cx.psb[b2]
                    p.mm(pr[:, :n], prot[:], raw, True, True, reads=[cx.B("prot"), rawb], writes=[cx.B("psb", b2)])
                    ti = cx.rot("ntmp", 2)
                    t1, t1b = cx.ntmp[ti], cx.B("ntmp", ti)
                    si = cx.rot("sgr", 2)
                    t2, t2b = cx.sg[si], cx.B("sg", si)
                    p.V(lambda e, t1=t1, raw=raw: e.tensor_tensor(t1[:, :n], raw, cosT[:, t0:t0 + n], ALU.mult),
                        reads=[rawb, cx.B("cs")], writes=[t1b])
                    p.V(lambda e, t2=t2, pr=pr: e.tensor_tensor(t2[:, :n], pr[:, :n], sinT[:, t0:t0 + n], ALU.mult),
                        reads=[cx.B("psb", b2), cx.B("cs")], writes=[t2b])
                    p.V(lambda e, t1=t1, t2=t2, rot_=rot_: e.tensor_tensor(rot_, t1[:, :n], t2[:, :n], ALU.add),
                        reads=[t1b, t2b], writes=[rotb])
            if "fm" in PARTS:
                outs.append(p.dma("sync", qraw_o.rearrange("(c p) t -> p c t", p=128)[:, :, t0:t0 + n], qrawS[so][:], f"qo{so}a", reads=[cx.B("qrawS", so)]))
                outs.append(p.dma("sync", qrot_o.rearrange("(c p) t -> p c t", p=128)[:, :, t0:t0 + n], qrotS[so][:], f"qo{so}b", reads=[cx.B("qrotS", so)]))
                outs.append(p.dma("sync", kvT_o.rearrange("s (a p) t -> p (s a) t", p=128)[:, :, t0:t0 + n], kvS[so][:], f"qo{so}c", reads=[cx.B("kvS", so)]))
            for vi, c0 in enumerate((1792, 2304) if "v" in PARTS else ()):
                for tt in range(4):
                    b = 4 + cx.rot("dn", 2)
                    pv = cx.psb[b]
                    for c in range(8):
                        p.mm(pv[:, :256], hc[:, c, tt * 128:(tt + 1) * 128], win[:, c, c0:c0 + 256], c == 0, c == 7,
                             reads=[cx.B("win"), cx.B("hT")], writes=[cx.B("psb", b)])
                    p.A(lambda e, pv=pv, vi=vi, tt=tt: e.activation(vS[so][:, vi, tt, :], pv[:, :256], AF.Copy), reads=[cx.B("psb", b)], writes=[cx.B("vS", so)])
            if "v" in PARTS:
                for vi in range(2):
                    outs.append(p.dma("sync", vtok_o[vi, t0:t0 + n, :].rearrange("(tt p) c -> p tt c", p=128), vS[so][:, vi, :, :], f"qo{so}v{vi}", reads=[cx.B("vS", so)]))
            if "g" not in PARTS:
                continue
            b = cx.rot("gu", 2)
            pq = cx.psb[b]
            for c in range(8):
                p.mm(pq[0:48, :n], win[:, c, 2560:2608], hc[:, c, :n], c == 0, c == 7,
                     reads=[cx.B("win"), cx.B("hT")], writes=[cx.B("psb", b)])
            i = cx.rot("gsb", 2)
            p.A(lambda e, i=i, pq=pq: e.activation(gsb[i][0:48, :n], pq[0:48, :n], AF.Sigmoid, bias=vecs[0:48, bgate_o:bgate_o + 1]),
                reads=[cx.B("psb", b), cx.B("vecs")], writes=[cx.B("gsb", i)])
            outs.append(p.dma("sync", gate_o[:, t0:t0 + n], gsb[i][0:48, :n], "gout", reads=[cx.B("gsb", i)]))

        cx.release(m0)
    return outs


def rope_tables_np(pos0, n):
    pos = (pos0 + np.arange(n)).astype(np.float32)
    inv = (np.float32(500000.0) ** (-np.arange(0, 16, 2, dtype=np.float32) / np.float32(16))).astype(np.float32)
    ang = pos[None, :] * inv[:, None]
    c8, s8 = np.cos(ang).astype(np.float32), np.sin(ang).astype(np.float32)
    cos = np.ones((128, n), np.float32)
    sin = np.zeros((128, n), np.float32)
    for hb in (0, 64):
        cos[hb:hb + 8] = c8
        cos[hb + 8:hb + 16] = c8
        sin[hb:hb + 8] = s8
        sin[hb + 8:hb + 16] = s8
    return cos, sin


def prot_np():
    pr = np.zeros((128, 128), np.float32)
    for hb in (0, 64):
        for j in range(8):
            pr[hb + j + 8, hb + j] = -1.0
            pr[hb + j, hb + 8 + j] = 1.0
    return pr


def prep_launch1(I, core):
    b, j = divmod(core, 4)
    t0 = j * TC
    x = I["x"][b]
    xT = np.zeros((D, TC + HALO), np.float32)
    xT[:, HALO:] = x[t0:t0 + TC].T
    if j > 0:
        xT[:, :HALO] = x[t0 - HALO:t0].T
    voff, nv = l1_vec_layout()
    vecs = np.zeros((128, nv), np.float32)

    def put(name, arr):
        o, k = voff[name]
        vecs[:, o:o + k] = arr

    put("f00pre", pc(I["ffn_norm_pre"][0, 0]))
    put("f00post", pc(I["ffn_norm_post"][0, 0]))
    put("m0pre", pc(I["mix_norm_pre"][0]))
    put("bpw1", pc(I["conv_b_pw1"][0]))
    put("wdw", np.concatenate([pc(I["conv_w_dw"][0, k]) for k in range(CW)], axis=1))
    put("bdw", pc(I["conv_b_dw"][0]))
    put("lng", pc(I["conv_ln_g"][0]))
    put("lnb", pc(I["conv_ln_b"][0]))
    put("bpw2", pc(I["conv_b_pw2"][0]))
    put("m0post", pc(I["mix_norm_post"][0]))
    put("f01pre", pc(I["ffn_norm_pre"][0, 1]))
    put("f01post", pc(I["ffn_norm_post"][0, 1]))
    put("f10pre", pc(I["ffn_norm_pre"][1, 0]))
    put("f10post", pc(I["ffn_norm_post"][1, 0]))
    put("m1pre", pc(I["mix_norm_pre"][1]))
    bg = np.zeros((128, 1), np.float32)
    bg[:48, 0] = I["nsa_b_gate"][0]
    put("bgate", bg)
    put("flag", np.full((128, 1), 0.0 if j == 0 else 1.0, np.float32))
    cos, sin = rope_tables_np(t0, TC)
    m = {"xT": xT, "vecs": vecs, "wpw1": I["conv_w_pw1"][0], "wpw2": I["conv_w_pw2"][0], "win": I["nsa_w_in"][0],
         "ident": np.eye(128, dtype=np.float32), "prot": prot_np(), "ropecos": cos, "ropesin": sin}
    for i, (l, h) in enumerate(((0, 0), (0, 1), (1, 0))):
        m[f"wg{i}"] = I["ffn_w_gate"][l, h]
        m[f"wu{i}"] = I["ffn_w_up"][l, h]
        m[f"wd{i}"] = I["ffn_w_down"][l, h]
    return {k: np.ascontiguousarray(v, dtype=np.float32) for k, v in m.items()}


def l2_consts():
    bf = ml_dtypes.bfloat16
    E = np.zeros((128, 64, 128), np.float32)
    for jt in range(64):
        E[2 * jt, jt, 0:64] = 1.0
        E[2 * jt + 1, jt, 64:128] = 1.0
    i = np.arange(128)[:, None]
    j = np.arange(512)[None, :]
    cmask = np.zeros((128, 5, 512), np.float32)
    for d in range(5):
        cmask[:, d, :] = np.where(16 * i + 31 - 512 * d <= j, 0.0, NEG)
    wmask = np.zeros((128, 8, 512), np.float32)
    for oi in range(8):
        k = 128 * (oi - 4) + i
        wmask[:, oi, :] = np.where((k <= j) & (k > j - 512), 0.0, NEG)
    AB = np.zeros((128, 2, 256), np.float32)
    jj = np.arange(128)[:, None]
    m = np.arange(256)[None, :]
    x = (m - 128) - (jj >= 64)
    forced = (x == 0) | (x == -1)
    nonc = x > 0
    AB[:, 0, :] = np.where(forced | nonc, 0.0, 1.0)
    AB[:, 1, :] = np.where(forced, 1e9, np.where(nonc, -1e9, 0.0))
    ov = np.zeros((128, 4, 130), np.float32)
    for ct in range(4):
        for ii in range(128):
            c = 128 * ct + ii
            if c > 510:
                continue
            for n in range(128):
                if 16 * c < 64 * n + 64 and 16 * c + 32 > 64 * n:
                    ov[ii, ct, n] = 1.0
            ov[ii, ct, 128] = 1.0
    selG = np.zeros((12, 12, 128), np.float32)
    for r in range(12):
        selG[r, r, :] = 1.0
    IND = np.zeros((64, S), np.float32)
    for jt in range(64):
        IND[2 * (jt % 32), jt * 128:jt * 128 + 64] = 1.0
        IND[2 * (jt % 32) + 1, jt * 128 + 64:jt * 128 + 128] = 1.0
    return {"IND": IND.astype(bf), "identb": np.eye(128, dtype=np.float32).astype(bf), "cmask": cmask.astype(bf),
            "wmask": wmask.astype(bf), "AB": AB, "ov": ov.astype(bf), "selG": selG.astype(bf)}


NQG = 16
SCALE = 0.125


def phase_B(cx, T):
    p = cx.p
    oh_d = T["oh"]
    w1_d, w2k_d, w2v_d, pe_d = T["w1"], T["w2kD"], T["w2vD"], T["pe"]
    EX2 = T["EX2"]

    GXL = T["GX1L"]
    oT4 = EX2[0, :].rearrange("(j r t) -> j r t", j=4, t=TC)

    def gq(j, name, g_):
        base = 0 if name == "qraw_o" else 4
        return GXL[base + g_][j, :].rearrange("(r t) -> r t", t=TC)

    def gkv(j, kind):
        return GXL[8 + kind][j, :].rearrange("(r t) -> r t", t=TC)

    def gv(j, vi):
        return GXL[12 + vi][j, :].rearrange("(t c) -> t c", c=256)

    def ggate(j):
        return GXL[14][j, :].rearrange("(r t) -> r t", t=TC)

    if True:
        pst = cx.psb[7][:].bitcast(BF16)
        ones = cx.ones
        oh = cx.carve([128, 4], F32)
        p.dma("sync", oh[:], oh_d, "c_oh", writes=[cx.B("oh")])
        identb = cx.carve([128, 128], BF16)
        cmask = cx.carve([128, 5, 512], BF16)
        wmask = cx.carve([128, 8, 512], BF16)
        AB = cx.carve([128, 2, 256], F32)
        ov = cx.carve([128, 4, 130], BF16)
        selG = cx.carve([12, 12, 128], BF16)
        for dst, nm in ((identb, "identb"), (cmask, "cmask"), (wmask, "wmask"), (AB, "AB"), (ov, "ov"), (selG, "selG")):
            p.dma("sync", dst[:], T[nm], f"c_k{nm}", writes=[cx.B(nm)])
        kselD = cx.carve([128, S], BF16)
        kwinD = cx.carve([128, S], BF16)
        KX1 = cx.carve([128, S], BF16)
        vA = {nm: cx.carve([128, 64, 192], BF16) for nm in ("sel", "win")}
        for t_ in vA.values():
            p.V(lambda e: e.memset(t_[:], 1.0), writes=[cx.B("vA")])
        kcmpT = cx.carve([128, 512], BF16)
        vcmp = cx.carve([128, 4, 128], BF16)
        p.V(lambda e: e.memset(kcmpT[:], 0.0), writes=[cx.B("kcmpT")])
        p.V(lambda e: e.memset(vcmp[:], 0.0), writes=[cx.B("vcmp")])

        def select4(dst, stage, n_part, stagebuf, dstbuf):
            ps_ = slice(0, n_part)
            p.V(lambda e: e.tensor_scalar(dst, stage(0), oh[ps_, 0:1], None, ALU.mult), reads=[stagebuf, cx.B("oh")], writes=[dstbuf])
            for g_ in range(1, 4):
                p.V(lambda e: e.scalar_tensor_tensor(dst, stage(g_), oh[ps_, g_:g_ + 1], dst, ALU.mult, ALU.add),
                    reads=[stagebuf, cx.B("oh"), dstbuf], writes=[dstbuf])

        m0 = cx.mark()
        stgs = [cx.carve([128, 4, 2048], BF16) for i in range(2)]
        kv2 = cx.carve([128, S], BF16)
        w1 = cx.carve([128, 16, 256], BF16)
        w2 = cx.carve([128, 2, 128], BF16)
        pe = cx.carve([128, 16], BF16)
        hid = cx.carve([128, 2, 512], BF16)
        bias = cx.carve([128, 2], F32)
        vstg = [cx.carve([128, 16, 256], BF16) for i in range(2)]

        def load_sel_kv(kind, dstT, nm, shifted):
            for j in range(4):
                si_ = cx.rot("stg", 2)
                stg, stgb = stgs[si_], cx.B("stg", si_)
                for g_ in range(4):
                    src = gkv(j, kind)[g_ * 64:(g_ + 1) * 64, :]
                    p.dma("sync", stg[0:64, g_, :], src, f"sg{g_}", writes=[stgb])
                    if not shifted:
                        p.dma("sync", stg[64:128, g_, :], src, f"sh{g_}", writes=[stgb])
                    else:
                        p.dma("sync", stg[64:128, g_, 0:2047], src[:, 1:2048], f"sh{g_}", writes=[stgb])
                        if j < 3:
                            p.dma("sync", stg[64:128, g_, 2047:2048], gkv(j + 1, kind)[g_ * 64:(g_ + 1) * 64, 0:1], f"sh{g_}", writes=[stgb], allow_slow_non_contiguous=True)
                        else:
                            p.V(lambda e: e.memset(stg[64:128, g_, 2047:2048], 0.0), writes=[stgb])
                select4(dstT[:, j * 2048:(j + 1) * 2048], lambda g_: stg[:, g_, :], 128, stgb, cx.B(nm))

        load_sel_kv(2, kselD, "kselD", False)
        p.op("gpsimd", lambda e: e.tensor_copy(KX1[64:128, :], kselD[64:128, :]), reads=[cx.B("kselD")], writes=[cx.B("KX1")])
        p.dma("sync", kselD[64:128, :], T["IND"], "c_ind0", reads=[], writes=[cx.B("kselD")])
        p.dma("sync", KX1[0:64, :], T["IND"], "c_ind1", writes=[cx.B("KX1")])
        load_sel_kv(3, kwinD, "kwinD", False)
        for vi, nm in enumerate(("sel", "win")):
            for j in range(4):
                vs_ = cx.rot("vstg", 2)
                p.dma("sync", vstg[vs_][:, :, :], gv(j, vi).rearrange("(t p) c -> p t c", p=128), f"sv{vs_}", writes=[cx.B("vstg", vs_)])
                select4(vA[nm][:, j * 16:(j + 1) * 16, 64:128], lambda g_: vstg[vs_][:, :, g_ * 64:(g_ + 1) * 64], 128, cx.B("vstg", vs_), cx.B("vA"))

        for which, w2_d in enumerate((w2k_d, w2v_d)):
            load_sel_kv(which, kv2, "kv2", True)
            p.dma("gpsimd", w1[:], w1_d[which].rearrange("(c p) j -> p c j", p=128), "c_w1", writes=[cx.B("w1")])
            p.dma("gpsimd", w2[:], w2_d.rearrange("(c p) j -> p c j", p=128), "c_w2", writes=[cx.B("w2")])
            p.dma("gpsimd", pe[:], pe_d[which], "c_pe", writes=[cx.B("pe")])
            for jc in range(2):
                pb = cx.psb[2]
                for ch in range(16):
                    p.mm(pb[:, 0:1], w1[:, ch, jc * 128:(jc + 1) * 128], pe[:, ch:ch + 1], ch == 0, ch == 15,
                         reads=[cx.B("w1"), cx.B("pe")], writes=[cx.B("psb", 2)])
                p.V(lambda e: e.tensor_copy(bias[:, jc:jc + 1], pb[:, 0:1]), reads=[cx.B("psb", 2)], writes=[cx.B("bias")])
                ph = cx.psb[jc]
                for lp in range(16):
                    p.mm(ph[:, 0:511], w1[:, lp, jc * 128:(jc + 1) * 128], kv2[:, 2 * lp:2 * lp + 16 * 510 + 1:16], lp == 0, lp == 15,
                         reads=[cx.B("w1"), cx.B("kv2")], writes=[cx.B("psb", jc)])
                p.A(lambda e: e.activation(hid[:, jc, 0:511], ph[:, 0:511], AF.Silu, bias=bias[:, jc:jc + 1]),
                    reads=[cx.B("psb", jc), cx.B("bias")], writes=[cx.B("hid")])
            if which == 0:
                pk = cx.psb[3]
                for jc in range(2):
                    p.mm(pk[:, 0:511], w2[:, jc, :], hid[:, jc, 0:511], jc == 0, jc == 1, reads=[cx.B("w2"), cx.B("hid")], writes=[cx.B("psb", 3)])
                p.A(lambda e: e.activation(kcmpT[:, 0:511], pk[:, 0:511], AF.Copy), reads=[cx.B("psb", 3)], writes=[cx.B("kcmpT")])
            else:
                for ct in range(4):
                    M = 128 if ct < 3 else 127
                    pv = cx.psb[3 + (ct % 2)]
                    for jc in range(2):
                        p.mm(pv[0:M, 0:128], hid[:, jc, ct * 128:ct * 128 + M], w2[:, jc, :], jc == 0, jc == 1,
                             reads=[cx.B("w2"), cx.B("hid")], writes=[cx.B("psb", 3 + (ct % 2))])
                    p.A(lambda e: e.activation(vcmp[0:M, ct, :], pv[0:M, 0:128], AF.Copy),
                        reads=[cx.B("psb", 3 + (ct % 2))], writes=[cx.B("vcmp")])
        cx.release(m0)

        qstg = cx.carve([128, 4, 2, 512], BF16)
        gstg = cx.carve([12, 4, 512], BF16)
        qraw = [cx.carve([128, 2, 512], BF16) for i in range(2)]
        qrot = [cx.carve([128, 2, 512], BF16) for i in range(2)]
        gts = [cx.carve([12, 512], BF16) for i in range(2)]
        eT = [cx.carve([128, 4, 512], BF16) for i in range(2)]
        pT = [cx.carve([128, 512], BF16) for i in range(4)]
        rz = [cx.carve([128, 512], F32) for i in range(2)]
        wv = [cx.carve([128, 512], F32) for i in range(2)]
        tmp = [cx.carve([128, 512], F32) for i in range(2)]
        acc = cx.carve([128, 2, 512], F32)
        accb = [cx.carve([128, 2, 512], BF16) for i in range(2)]
        impacc = cx.carve([128, 4, 128], F32)
        imod = cx.carve([128, 128], F32)
        scr = cx.carve([128, 128], F32)
        m8 = cx.carve([128, 16], F32)
        rzc = cx.carve([128, 1], F32)
        negm = cx.carve([128, 128], BF16)
        negT = cx.carve([128, 512], BF16)
        Xt = {(h_, a_, w_): cx.carve([128, 512], BF16) for h_ in range(2) for a_ in range(2) for w_ in range(2)}
        outs = []

        def finish_branch(r, gi, pacc, paccbuf, zrows, first, gt, gtb):
            a, half = divmod(r, 2)
            hs = slice(64 * half, 64 * half + 64)
            pG = cx.psb[6]
            p.mm(pG[:, :], selG[:, 3 * r + gi, :], gt[:, :], True, True, reads=[cx.B("selG"), gtb], writes=[cx.B("psb", 6)])
            i = cx.rot("rz", 2)
            p.V(lambda e: e.tensor_scalar(rz[i][zrows, :], pacc[zrows, :], 1e-30, None, ALU.max), reads=[paccbuf], writes=[cx.B("rz", i)])
            p.V(lambda e: e.reciprocal(rz[i][zrows, :], rz[i][zrows, :]), reads=[cx.B("rz", i)], writes=[cx.B("rz", i)])
            p.V(lambda e: e.tensor_tensor(wv[i][zrows, :], rz[i][zrows, :], pG[zrows, :], ALU.mult),
                reads=[cx.B("rz", i), cx.B("psb", 6)], writes=[cx.B("wv", i)])
            if first:
                p.V(lambda e: e.tensor_tensor(acc[hs, a, :], pacc[hs, :], wv[i][zrows, :], ALU.mult),
                    reads=[paccbuf, cx.B("wv", i)], writes=[cx.B("acc")])
            else:
                p.V(lambda e: e.tensor_tensor(tmp[i][hs, :], pacc[hs, :], wv[i][zrows, :], ALU.mult),
                    reads=[paccbuf, cx.B("wv", i)], writes=[cx.B("tmp", i)])
                p.V(lambda e: e.tensor_tensor(acc[hs, a, :], acc[hs, a, :], tmp[i][hs, :], ALU.add),
                    reads=[cx.B("acc"), cx.B("tmp", i)], writes=[cx.B("acc")])

        for qg in range(NQG):
            q0 = qg * 512
            s = cx.rot("qld", 2)
            jq, tl = divmod(qg, 4)
            tl *= 512
            for nmq, dstq, bq in (("qraw_o", qraw[s], cx.B("qraw", s)), ("qrot_o", qrot[s], cx.B("qrot", s))):
                for g_ in range(4):
                    p.dma("sync", qstg[:, g_, :, :], gq(jq, nmq, g_)[:, tl:tl + 512].rearrange("(a p) t -> p a t", p=128),
                          f"qs{g_}", writes=[cx.B("qstg")])
                select4(dstq[:], lambda g_: qstg[:, g_, :, :], 128, cx.B("qstg"), bq)
            for g_ in range(4):
                p.dma("sync", gstg[:, g_, :], ggate(jq)[g_ * 12:(g_ + 1) * 12, tl:tl + 512], f"qs{g_}", writes=[cx.B("gstg")])
            select4(gts[s][:], lambda g_: gstg[:, g_, :], 12, cx.B("gstg"), cx.B("gts", s))
            gt, gtb = gts[s], cx.B("gts", s)
            nct = (32 * qg + 30) // 128 + 1
            for r in range(4):
                a, half = divmod(r, 2)
                hs = slice(64 * half, 64 * half + 64)
                es = cx.rot("eT", 2)
                for ct in range(nct):
                    d = qg - 4 * ct
                    b = cx.rot("S", 2)
                    ps_ = cx.psb[b]
                    p.mm(ps_[:, :], kcmpT[hs, ct * 128:(ct + 1) * 128], qraw[s][hs, a, :], True, d >= 5,
                         reads=[cx.B("kcmpT"), cx.B("qraw", s)], writes=[cx.B("psb", b)])
                    if d < 5:
                        p.mm(ps_[:, :], identb[:], cmask[:, d, :], False, True, reads=[cx.B("identb"), cx.B("cmask")], writes=[cx.B("psb", b)])
                    p.A(lambda e, ps_=ps_, es=es, ct=ct: e.activation(eT[es][:, ct, :], ps_[:, :], AF.Exp, scale=SCALE),
                        reads=[cx.B("psb", b)], writes=[cx.B("eT", es)])
                pO, pZ = cx.psb[2], cx.psb[3]
                for ct in range(nct):
                    p.mm(pO[:, :], vcmp[:, ct, :], eT[es][:, ct, :], ct == 0, ct == nct - 1, reads=[cx.B("vcmp"), cx.B("eT", es)], writes=[cx.B("psb", 2)])
                for ct in range(nct):
                    p.mm(pZ[:, :], ones[:], eT[es][:, ct, :], ct == 0, ct == nct - 1, reads=[cx.B("ones"), cx.B("eT", es)], writes=[cx.B("psb", 3)])
                zrows = slice(64 * (1 - half), 64 * (1 - half) + 64)
                pG = cx.psb[6]
                p.mm(pG[:, :], selG[:, 3 * r + 0, :], gt[:, :], True, True, reads=[cx.B("selG"), gtb], writes=[cx.B("psb", 6)])
                i = cx.rot("rz", 2)
                p.V(lambda e, i=i: e.tensor_scalar(rz[i][hs, :], pZ[hs, :], 1e-30, None, ALU.max), reads=[cx.B("psb", 3)], writes=[cx.B("rz", i)])
                p.V(lambda e, i=i: e.reciprocal(rz[i][hs, :], rz[i][hs, :]), reads=[cx.B("rz", i)], writes=[cx.B("rz", i)])
                p.V(lambda e, i=i: e.tensor_tensor(wv[i][hs, :], rz[i][hs, :], pG[hs, :], ALU.mult),
                    reads=[cx.B("rz", i), cx.B("psb", 6)], writes=[cx.B("wv", i)])
                p.V(lambda e, i=i, a=a: e.tensor_tensor(acc[hs, a, :], pO[hs, :], wv[i][hs, :], ALU.mult),
                    reads=[cx.B("psb", 2), cx.B("wv", i)], writes=[cx.B("acc")])
                for qt in range(4):
                    bi = 4 + cx.rot("I", 2)
                    pI = cx.psb[bi]
                    for ct in range(nct):
                        p.mm(pI[:, 0:129], eT[es][:, ct, qt * 128:(qt + 1) * 128], ov[:, ct, 0:129], ct == 0, ct == nct - 1,
                             reads=[cx.B("eT", es), cx.B("ov")], writes=[cx.B("psb", bi)])
                    p.V(lambda e, pI=pI: e.tensor_scalar(rzc[:], pI[:, 128:129], 1e-30, None, ALU.max), reads=[cx.B("psb", bi)], writes=[cx.B("rzc")])
                    p.V(lambda e: e.reciprocal(rzc[:], rzc[:]), reads=[cx.B("rzc")], writes=[cx.B("rzc")])
                    if r == 0:
                        p.V(lambda e, pI=pI, qt=qt: e.tensor_scalar(impacc[:, qt, :], pI[:, 0:128], rzc[:, 0:1], None, ALU.mult),
                            reads=[cx.B("psb", bi), cx.B("rzc")], writes=[cx.B("impacc")])
                    else:
                        p.V(lambda e, pI=pI, qt=qt: e.scalar_tensor_tensor(impacc[:, qt, :], pI[:, 0:128], rzc[:, 0:1], impacc[:, qt, :], ALU.mult, ALU.add),
                            reads=[cx.B("psb", bi), cx.B("rzc"), cx.B("impacc")], writes=[cx.B("impacc")])
            for qt in range(4):
                ti = 4 * qg + qt
                c0 = 128 - 2 * ti
                p.V(lambda e, qt=qt, c0=c0: e.tensor_tensor(imod[:], impacc[:, qt, :], AB[:, 0, c0:c0 + 128], ALU.mult),
                    reads=[cx.B("impacc"), cx.B("AB")], writes=[cx.B("imod")])
                p.V(lambda e, c0=c0: e.tensor_tensor(imod[:], imod[:], AB[:, 1, c0:c0 + 128], ALU.add), reads=[cx.B("imod"), cx.B("AB")], writes=[cx.B("imod")])
                p.V(lambda e: e.memset(imod[:, 0:1], 1e9), reads=[], writes=[cx.B("imod")])
                p.V(lambda e: e.max(m8[:, 0:8], imod[:]), reads=[cx.B("imod")], writes=[cx.B("m8")])
                p.V(lambda e: e.match_replace(scr[:], m8[:, 0:8], imod[:], -1e30), reads=[cx.B("imod"), cx.B("m8")], writes=[cx.B("scr")])
                p.V(lambda e: e.max(m8[:, 8:16], scr[:]), reads=[cx.B("scr")], writes=[cx.B("m8")])
                p.V(lambda e: e.tensor_scalar(negm[:], imod[:], m8[:, 15:16], NEG, ALU.is_lt, ALU.mult), reads=[cx.B("imod"), cx.B("m8")], writes=[cx.B("negm")])
                p.op("tensor", lambda e, qt=qt: e.transpose(pst[:, qt * 128:(qt + 1) * 128], negm[:], identb[:]),
                     reads=[cx.B("negm"), cx.B("identb")], writes=[cx.B("pst")])
                p.A(lambda e, qt=qt: e.activation(negT[:, qt * 128:(qt + 1) * 128], pst[:, qt * 128:(qt + 1) * 128], AF.Copy),
                    reads=[cx.B("pst")], writes=[cx.B("negT")])
            nwin = 1 if 4 * qg + 3 < 32 else 2
            for half in range(2):
                hs_ = slice(64 * half, 64 * half + 64)
                os_ = slice(64 * (1 - half), 64 * (1 - half) + 64)
                for a_ in range(2):
                    for w_ in range(nwin):
                        xt = Xt[(half, a_, w_)]
                        xb_ = cx.B("Xt", half, a_, w_)
                        p.op("gpsimd", lambda e: e.tensor_copy(xt[hs_, :], qrot[s][hs_, a_, :]), reads=[cx.B("qrot", s)], writes=[xb_])
                        p.V(lambda e: e.tensor_copy(xt[os_, :], negT[64 * w_:64 * w_ + 64, :]), reads=[cx.B("negT")], writes=[xb_])
            for (br, gi, kD, kbuf, jts) in (("sel", 1, kselD, "kselD", list(range(4 * qg + 4))),
                                            ("win", 2, kwinD, "kwinD", list(range(max(0, 4 * qg - 4), 4 * qg + 4)))):
                units = [(ji, jt, r) for ji, jt in enumerate(jts) for r in range(4)]

                def emit_S(u):
                    ji, jt, r = u
                    o = jt - 4 * qg
                    a, half = divmod(r, 2)
                    hs = slice(64 * half, 64 * half + 64)
                    b = (0, 1, 6)[cx.rot("S3", 3)]
                    ps_ = cx.psb[b]
                    need_mask = (br == "win") or (o >= 0)
                    if br == "sel":
                        kx, kxb = (kselD, cx.B("kselD")) if half == 0 else (KX1, cx.B("KX1"))
                        w_ = jt // 32
                        p.mm(ps_[:, :], kx[:, jt * 128:(jt + 1) * 128], Xt[(half, a, w_)][:, :], True, not need_mask,
                             reads=[kxb, cx.B("Xt", half, a, w_)], writes=[cx.B("psb", b)])
                    else:
                        p.mm(ps_[:, :], kD[hs, jt * 128:(jt + 1) * 128], qrot[s][hs, a, :], True, False,
                             reads=[cx.B(kbuf), cx.B("qrot", s)], writes=[cx.B("psb", b)], sig=False)
                    if need_mask:
                        p.mm(ps_[:, :], identb[:], wmask[:, o + 4, :], False, True, reads=[cx.B("identb"), cx.B("wmask")], writes=[cx.B("psb", b)])
                    pi = cx.rot("pT", 4)
                    p.A(lambda e: e.activation(pT[pi][:, :], ps_[:, :], AF.Exp, scale=SCALE),
                        reads=[cx.B("psb", b)], writes=[cx.B("pT", pi)])
                    return pi

                def emit_PV(u, pi):
                    ji, jt, r = u
                    half = r % 2
                    pa = cx.psb[2 + r]
                    p.mm(pa[:, :], vA[br][:, jt, (64 if half == 0 else 0):(192 if half == 0 else 128)], pT[pi][:, :], ji == 0, ji == len(jts) - 1,
                         reads=[cx.B("vA"), cx.B("pT", pi)], writes=[cx.B("psb", 2 + r)], sig=True)

                pend = []
                for u in units:
                    pi = emit_S(u)
                    pend.append((u, pi))
                    if len(pend) > 2:
                        emit_PV(*pend.pop(0))
                for pu in pend:
                    emit_PV(*pu)
                for r in range(4):
                    half = r % 2
                    zrows = slice(64 * (1 - half), 64 * (1 - half) + 64)
                    finish_branch(r, gi, cx.psb[2 + r], cx.B("psb", 2 + r), zrows, False, gt, gtb)
            ob = cx.rot("accb", 2)
            p.V(lambda e, ob=ob: e.tensor_copy(accb[ob][:], acc[:]), reads=[cx.B("acc")], writes=[cx.B("accb", ob)])
            outs.append(p.dma("sync", oT4[jq].rearrange("(a p) t -> p a t", p=128)[:, :, tl:tl + 512], accb[ob][:], f"o{ob}", reads=[cx.B("accb", ob)]))
    return outs


def phase_C(cx, T):
    p = cx.p
    xd_d, vecs_d, wout_d, wg_d, wu_d, wd_d, xe, xf, oh_d = (T["xd"], T["vecs3"], T["wout"], T["wg3"], T["wu3"],
                                                              T["wd3"], T["xe"], T["xf"], T["oh"])
    GX2L = T["GX2L"]
    if True:
        vecs = cx.carve([128, 24], F32)
        p.dma("sync", vecs[:], vecs_d, "const", writes=[cx.B("vecs")])
        oh = cx.carve([128, 4], F32)
        p.dma("sync", oh[:], oh_d, "c_oh", writes=[cx.B("oh")])
        for o, sc_ in ((0, 32.0), (8, 32.0), (16, 16.0)):
            p.V(lambda e: e.tensor_scalar(vecs[:, o:o + 8], vecs[:, o:o + 8], sc_, None, ALU.mult), reads=[cx.B("vecs")], writes=[cx.B("vecs")])
        g_m1post = lambda c: vecs[:, c:c + 1]
        g_pre = lambda c: vecs[:, 8 + c:9 + c]
        g_post = lambda c: vecs[:, 16 + c:17 + c]
        m0 = cx.mark()
        wout = cx.carve([128, 8, D], BF16)
        for c in range(8):
            p.dma("gpsimd", wout[:, c, :], wout_d[c * 128:(c + 1) * 128, :], f"wpw{c % 4}", writes=[cx.B("wout")])
        astg = cx.carve([128, 4, 8, 512], BF16)
        at = [cx.carve([128, 8, 512], BF16) for i in range(2)]
        y2 = cx.carve([128, 8, 512], F32)
        for g in range(4):
            t0 = g * 512
            n = 512
            s = cx.rot("at", 2)
            for jj in range(4):
                p.dma("sync", astg[:, jj, :, :], GX2L[jj].rearrange("g (r t) -> (g r) t", t=TC)[:, t0:t0 + n].rearrange("(c p) t -> p c t", p=128),
                      f"as{jj}", writes=[cx.B("astg")])
            p.V(lambda e: e.tensor_scalar(at[s][:], astg[:, 0, :, :], oh[:, 0:1], None, ALU.mult), reads=[cx.B("astg"), cx.B("oh")], writes=[cx.B("at", s)])
            for jj in range(1, 4):
                p.V(lambda e: e.scalar_tensor_tensor(at[s][:], astg[:, jj, :, :], oh[:, jj:jj + 1], at[s][:], ALU.mult, ALU.add),
                    reads=[cx.B("astg"), cx.B("oh"), cx.B("at", s)], writes=[cx.B("at", s)])
            for oc in range(8):
                b = 4 + cx.rot("dn", 2)
                pd = cx.psb[b]
                for c in range(8):
                    p.mm(pd[:, :n], wout[:, c, oc * 128:(oc + 1) * 128], at[s][:, c, :], c == 0, c == 7,
                         reads=[cx.B("wout"), cx.B("at", s)], writes=[cx.B("psb", b)])
                p.A(lambda e: e.activation(y2[:, oc, :n], pd[:, :n], AF.Copy), reads=[cx.B("psb", b)], writes=[cx.B("y2")])
            norm_residual_store(cx, lambda c: y2[:, c, :n], [cx.B("y2")], xd_d, ("xd",), xe, ("xe",), t0, t0, n, g_m1post)
        cx.release(m0)
        alloc_ffn(cx, 1024)
        passes = [[(0, 512), (512, 512)], [(1024, 512), (1536, 512)]]
        ffn(cx, "f11", xe, ("xe",), xf, ("xf",), passes, wg_d, wu_d, wd_d, g_pre, g_post)
    return [cx.B("xf").last_w]


EX1_FIELDS = {"qraw_o": (0, 2097152, (1024, 2048)), "qrot_o": (2097152, 2097152, (1024, 2048)),
              "kvT_o": (4194304, 2097152, (4, 256, 2048)), "vtok_o": (6291456, 1048576, (2, 2048, 256)),
              "gate_o": (7340032, 98304, (48, 2048))}
NEL1 = 7438336
NEL2 = 256 * S
RG = [[0, 1, 2, 3], [4, 5, 6, 7]]


def build_fused():
    cx = Ctx("fused")
    p = cx.p
    nc = cx.nc
    voff, nv = l1_vec_layout()
    T = {}
    T["xT"] = cx.din("xT", [D, TC + HALO])
    T["vecs"] = cx.din("vecs", [128, nv])
    T["wgs"] = [cx.din(f"wg{i}", [D, DFF]) for i in range(3)]
    T["wus"] = [cx.din(f"wu{i}", [D, DFF]) for i in range(3)]
    T["wds"] = [cx.din(f"wd{i}", [DFF, D]) for i in range(3)]
    T["wpw1"] = cx.din("wpw1", [D, 2 * D])
    T["wpw2"] = cx.din("wpw2", [D, D])
    T["win"] = cx.din("win", [D, NIN])
    T["ident"] = cx.din("ident", [128, 128])
    T["prot"] = cx.din("prot", [128, 128])
    T["ropecos"] = cx.din("ropecos", [128, TC])
    T["ropesin"] = cx.din("ropesin", [128, TC])
    T["oh"] = cx.din("oh", [128, 4])
    T["w1"] = cx.din("w1", [2, 2048, 256])
    T["w2kD"] = cx.din("w2kD", [256, 128])
    T["w2vD"] = cx.din("w2vD", [256, 128])
    T["pe"] = cx.din("pe", [2, 128, 16])
    T["IND"] = cx.din("IND", [64, S], BF16)
    T["identb"] = cx.din("identb", [128, 128], BF16)
    T["cmask"] = cx.din("cmask", [128, 5, 512], BF16)
    T["wmask"] = cx.din("wmask", [128, 8, 512], BF16)
    T["AB"] = cx.din("AB", [128, 2, 256])
    T["ov"] = cx.din("ov", [128, 4, 130], BF16)
    T["selG"] = cx.din("selG", [12, 12, 128], BF16)
    T["vecs3"] = cx.din("vecs3", [128, 24])
    T["wout"] = cx.din("wout", [D, D])
    T["wg3"] = cx.din("wg3", [D, DFF])
    T["wu3"] = cx.din("wu3", [D, DFF])
    T["wd3"] = cx.din("wd3", [DFF, D])
    T["xf"] = cx.dout("xf", [D, TC])
    T["xa"] = nc.dram_tensor("xa", [D, TC + HALO], F32).ap()
    for nm in ("xb", "xc", "xd", "xe"):
        T[nm] = nc.dram_tensor(nm, [D, TC], F32).ap()
    EX1 = nc.dram_tensor("ex1", [1, NEL1], BF16).ap()
    EX2 = nc.dram_tensor("ex2", [1, NEL2], BF16).ap()
    CH = 524288
    ch1 = [(k * CH, min(CH, NEL1 - k * CH)) for k in range((NEL1 + CH - 1) // CH)]
    ch2 = [(k * CH, CH) for k in range(NEL2 // CH)]
    GX1L = [nc.dram_tensor(f"gx1_{k}", [4, n_], BF16).ap() for k, (o_, n_) in enumerate(ch1)]
    GX2L = [nc.dram_tensor(f"gx2_{k}", [4, n_], BF16).ap() for k, (o_, n_) in enumerate(ch2)]
    T["EX2"], T["GX1L"], T["GX2L"] = EX2, GX1L, GX2L
    for nm, (o, n, shp) in EX1_FIELDS.items():
        v = EX1[0, o:o + n]
        T[nm] = v.rearrange("(r t) -> r t", t=shp[1]) if len(shp) == 2 else v.rearrange("(k r t) -> k r t", r=shp[1], t=shp[2])

    with cx.st:
        cx.arena_init(51 * 1024)
        cx.ones = cx.carve([128, 128], BF16)
        p.V(lambda e: e.memset(cx.ones[:], 1.0), writes=[cx.B("ones")])
        cx.psb = [cx.ps(f"psb{i}", [128, 512], F32) for i in range(8)]
        cx.sq = [cx.carve([128, 512], BF16) for i in range(2)]
        cx.r32 = [cx.carve([128, 512], F32) for i in range(2)]
        mtop = cx.mark()
        alloc_small(cx)
        phase_A(cx, T)
        cx.release(mtop)
        for k, (o_, n_) in enumerate(ch1):
            p.op("gpsimd", lambda e: e.collective_compute("AllGather", ALU.bypass, RG, [EX1[:, o_:o_ + n_].opt()], [GX1L[k].opt()]),
                 writes=[cx.B("gx1")])
        p.barrier()
        phase_B(cx, T)
        cx.release(mtop)
        for k, (o_, n_) in enumerate(ch2):
            p.op("gpsimd", lambda e: e.collective_compute("AllGather", ALU.bypass, RG, [EX2[:, o_:o_ + n_].opt()], [GX2L[k].opt()]),
                 writes=[cx.B("gx2")])
        p.barrier()
        alloc_small(cx)
        fin = phase_C(cx, T)
        stuck = p.check()
        assert not stuck, stuck
        p.build(final_waits=fin)
    return cx.nc


def kernel(**inputs):
    I = {k: np.asarray(v) for k, v in inputs.items()}
    cores = list(range(8))
    consts = l2_consts()
    w2 = np.asarray(I["nsa_cmp_w2"][0], np.float32)
    shared = dict(consts)
    shared["w1"] = np.ascontiguousarray(I["nsa_cmp_w1"][0], dtype=np.float32)
    shared["w2kD"] = np.ascontiguousarray(np.concatenate([w2[0], w2[0]], 1))
    shared["w2vD"] = np.ascontiguousarray(np.concatenate([w2[1], w2[1]], 1))
    shared["pe"] = np.ascontiguousarray(np.stack([pc(np.asarray(I["nsa_cmp_pos"][0, i]).reshape(-1)) for i in range(2)], 0))
    shared["vecs3"] = np.ascontiguousarray(np.concatenate([pc(I["mix_norm_post"][1]), pc(I["ffn_norm_pre"][1, 1]), pc(I["ffn_norm_post"][1, 1])], axis=1))
    shared["wout"] = np.ascontiguousarray(I["nsa_w_out"][0], dtype=np.float32)
    shared["wg3"] = np.ascontiguousarray(I["ffn_w_gate"][1, 1], dtype=np.float32)
    shared["wu3"] = np.ascontiguousarray(I["ffn_w_up"][1, 1], dtype=np.float32)
    shared["wd3"] = np.ascontiguousarray(I["ffn_w_down"][1, 1], dtype=np.float32)
    maps = []
    for c in cores:
        m = prep_launch1(I, c)
        m.update(shared)
        oh = np.zeros((128, 4), np.float32)
        oh[:, c % 4] = 1.0
        m["oh"] = oh
        maps.append(m)
    res = run_bass_kernel_spmd(build_fused(), maps, core_ids=cores).results
    out = np.zeros((2, S, D), np.float32)
    for c in cores:
        b, j = divmod(c, 4)
        out[b, j * TC:(j + 1) * TC] = np.asarray(res[c]["xf"]).T
    return out
```

```python
import contextlib
import numpy as np
import ml_dtypes
import concourse.bass as bass
import concourse.mybir as mybir
from concourse.bass_utils import run_bass_kernel_spmd

F32 = mybir.dt.float32
BF16 = mybir.dt.bfloat16
AF = mybir.ActivationFunctionType
ALU = mybir.AluOpType

ENGS = ["tensor", "vector", "scalar", "gpsimd", "sync"]

D = 1024
DFF = 2816
NFC = 22
S = 8192
TC = 2048
HALO = 128
CW = 31
RMS_EPS = 1e-6
LN_EPS = 1e-5
NEG = -30000.0
NIN = 2608


class Buf:
    __slots__ = ("name", "last_w", "readers")

    def __init__(self, name=""):
        self.name = name
        self.last_w = None
        self.readers = []


class _Rec:
    def __init__(self):
        self.call = None

    def __getattr__(self, name):
        def f(*a, **k):
            self.call = (name, a, k)
            return None
        return f


class Prog:
    def __init__(self, nc):
        self.nc = nc
        self.ops = {e: [] for e in ENGS}
        self.cnt = {e: 0 for e in ENGS}
        self.seen = {e: {} for e in ENGS}
        self.dcnt = {}
        self.pending = {e: {} for e in ENGS}

    def barrier(self):
        snap = dict(self.cnt)
        snap.update(self.dcnt)
        for e in ENGS:
            for k, v in snap.items():
                if v > 0 and not (e == "tensor" and k == "tensor"):
                    if self.pending[e].get(k, 0) < v:
                        self.pending[e][k] = v

    def op(self, eng, fn, reads=(), writes=(), dma=None, sig=True):
        rec = _Rec()
        fn(rec)
        name_, args_, kw_ = rec.call
        fn = lambda e: getattr(e, name_)(*args_, **kw_)
        waits = dict(self.pending[eng])
        self.pending[eng] = {}

        def need(ev, war=False):
            if ev is None:
                return
            k, v = ev
            if k == eng:
                if eng == "tensor" or war:
                    return
            if waits.get(k, 0) < v:
                waits[k] = v

        for b in reads:
            need(b.last_w)
        for b in writes:
            need(b.last_w)
            for r in b.readers:
                need(r, war=True)
        w = []
        for k, v in waits.items():
            if self.seen[eng].get(k, 0) < v:
                self.seen[eng][k] = v
                w.append((k, v))
        if dma is not None:
            prev = self.dcnt.get(dma, 0)
            if prev > 0 and self.seen[eng].get(dma, 0) < prev:
                self.seen[eng][dma] = prev
                w.append((dma, prev))
            self.dcnt[dma] = self.dcnt.get(dma, 0) + 16
            ev = (dma, self.dcnt[dma])
            inc = (dma, 16)
        elif eng == "tensor" and not sig:
            ev = (eng, self.cnt[eng] + 1)
            inc = None
        else:
            self.cnt[eng] += 1
            ev = (eng, self.cnt[eng])
            inc = (eng, 1)
        self.ops[eng].append((fn, w, inc))
        for b in reads:
            b.readers.append(ev)
            if len(b.readers) > 64:
                best = {}
                for k, v in b.readers:
                    if best.get(k, 0) < v:
                        best[k] = v
                b.readers = list(best.items())
        for b in writes:
            b.last_w = ev
            b.readers = []
        return ev

    def mm(self, out, lhsT, rhs, start, stop, reads=(), writes=(), sig=None, **kw):
        if sig is None:
            sig = stop
        return self.op("tensor", lambda e: e.matmul(out, lhsT, rhs, start=start, stop=stop, **kw),
                       reads=reads, writes=writes, sig=sig)

    def dma(self, eng, out, in_, sem, reads=(), writes=(), **kw):
        return self.op(eng, lambda e: e.dma_start(out=out, in_=in_, **kw), reads=reads, writes=writes, dma=sem)

    def V(self, fn, reads=(), writes=()):
        return self.op("vector", fn, reads, writes)

    def A(self, fn, reads=(), writes=()):
        return self.op("scalar", fn, reads, writes)

    def check(self):
        sem = {}
        pos = {e: 0 for e in ENGS}
        n = {e: len(self.ops[e]) for e in ENGS}
        progress = True
        while progress:
            progress = False
            for e in ENGS:
                while pos[e] < n[e]:
                    fn, w, inc = self.ops[e][pos[e]]
                    if all(sem.get(k, 0) >= v for k, v in w):
                        if inc is not None:
                            sem[inc[0]] = sem.get(inc[0], 0) + inc[1]
                        pos[e] += 1
                        progress = True
                    else:
                        break
        stuck = {e: (pos[e], n[e], [(k, v, sem.get(k, 0)) for k, v in self.ops[e][pos[e]][1]]) for e in ENGS if pos[e] < n[e]}
        return stuck

    def build(self, final_waits=()):
        nc = self.nc
        names = list(ENGS) + sorted(self.dcnt.keys())
        with contextlib.ExitStack() as st:
            sems = {n: st.enter_context(nc.semaphore("s_" + n)) for n in names}
            block = st.enter_context(nc.Block())
            fw = {}
            for ev in final_waits:
                if ev is not None and fw.get(ev[0], 0) < ev[1]:
                    fw[ev[0]] = ev[1]
            for eng in ENGS:
                ops = self.ops[eng]
                if eng == "sync":
                    ops = ops + [(None, list(fw.items()), None)]
                if not ops:
                    continue

                def body(e, ops=ops):
                    for fn, w, inc in ops:
                        for k, v in w:
                            e.wait_ge(sems[k], v)
                        if fn is None:
                            continue
                        ins = fn(e)
                        if inc is not None:
                            ins.then_inc(sems[inc[0]], inc[1])

                getattr(block, eng)(body)


class Ctx:
    def __init__(self, name):
        self.nc = bass.Bass("TRN2", target_bir_lowering=False)
        self.p = Prog(self.nc)
        self.st = contextlib.ExitStack()
        self.bufs = {}
        self.outs = []
        self.rr = {}

    def din(self, name, shape, dt=F32):
        return self.nc.dram_tensor(name, list(shape), dt, kind="ExternalInput").ap()

    def dout(self, name, shape, dt=F32):
        return self.nc.dram_tensor(name, list(shape), dt, kind="ExternalOutput").ap()

    def sb(self, name, shape, dt):
        return self.st.enter_context(self.nc.sbuf_tensor(name, list(shape), dt))

    def ps(self, name, shape, dt=F32):
        return self.st.enter_context(self.nc.psum_tensor(name, list(shape), dt))

    def arena_init(self, nwords):
        self.arena = self.sb("arena", [128, nwords], F32)
        self.top = 0
        self.nwords = nwords

    def carve(self, shape, dt):
        nfree = 1
        for d in shape[1:]:
            nfree *= d
        words = nfree if dt == F32 else (nfree + 1) // 2
        a = self.top
        self.top += words
        assert self.top <= self.nwords, ("arena overflow", self.top, self.nwords)
        ap = self.arena[:, a:a + words]
        if dt != F32:
            ap = ap.bitcast(dt)[:, :nfree]
        if len(shape) == 3:
            ap = ap.rearrange("p (a b) -> p a b", b=shape[2])
        elif len(shape) == 4:
            ap = ap.rearrange("p (a b c) -> p a b c", b=shape[2], c=shape[3])
        if shape[0] < 128:
            ap = ap[0:shape[0]]
        return ap

    def mark(self):
        return self.top

    def release(self, m):
        self.top = m
        self.p.barrier()

    def B(self, *key):
        b = self.bufs.get(key)
        if b is None:
            b = self.bufs[key] = Buf(str(key))
        return b

    def rot(self, key, n):
        i = self.rr.get(key, 0)
        self.rr[key] = (i + 1) % n
        return i


def pc(v):
    v = np.asarray(v, np.float32)
    return np.ascontiguousarray(v.reshape(-1, 128).T)


def setup_common(cx, n_ps=8):
    cx.arena_init(50 * 1024)
    cx.ones = cx.carve([128, 128], BF16)
    cx.p.V(lambda e: e.memset(cx.ones[:], 1.0), writes=[cx.B("ones")])
    cx.psb = [cx.ps(f"psb{i}", [128, 512], F32) for i in range(n_ps)]
    cx.sq = [cx.carve([128, 512], BF16) for i in range(2)]
    cx.r32 = [cx.carve([128, 512], F32) for i in range(2)]


def rms_stats(cx, src, srcbufs, n, psi, eps_scaled, nch=8):
    p = cx.p
    ps = cx.psb[psi]
    for c in range(nch):
        i = cx.rot("sq", 2)
        sq = cx.sq[i]
        p.A(lambda e, c=c, sq=sq: e.activation(sq[:, :n], src(c), AF.Square), reads=srcbufs, writes=[cx.B("sq", i)])
        p.mm(ps[:, :n], cx.ones[:], sq[:, :n], c == 0, c == nch - 1, reads=[cx.B("sq", i), cx.B("ones")],
             writes=[cx.B("psb", psi)], sig=True)
    j = cx.rot("r32", 2)
    r = cx.r32[j]
    p.A(lambda e: e.activation(r[:, :n], ps[:, :n], AF.Sqrt, bias=float(eps_scaled), scale=1.0),
        reads=[cx.B("psb", psi)], writes=[cx.B("r32", j)])
    p.V(lambda e: e.reciprocal(r[:, :n], r[:, :n]), reads=[cx.B("r32", j)], writes=[cx.B("r32", j)])
    return r, cx.B("r32", j)


def load_x_group(cx, xdram, xbufkey, off, n):
    i = cx.rot("xin", 2)
    cx.xin = cx.xins[i]
    cx.xinb = cx.B("xin", i)
    src = xdram.rearrange("(c p) t -> p c t", p=128)[:, :, off:off + n]
    cx.p.dma("sync", cx.xin[:, :, :n], src, f"xin{i}", reads=[cx.B(*xbufkey)], writes=[cx.xinb])


def norm_to_h(cx, hdst, hbuf, g32col, n, psi=6):
    xin, xinb = cx.xin, cx.xinb
    r, rb = rms_stats(cx, lambda c: xin[:, c, :n], [xinb], n, psi, D * RMS_EPS)
    for c in range(8):
        cx.p.V(lambda e, c=c: e.scalar_tensor_tensor(hdst(c), xin[:, c, :n], g32col(c), r[:, :n], ALU.mult, ALU.mult),
               reads=[xinb, rb, cx.B("vecs")], writes=[hbuf])


def norm_residual_store(cx, ysrc, ybufs, xin_dram, xin_key, xout_dram, xout_key, off_in, off_out, n, gcol, psi=6):
    p = cx.p
    r, rb = rms_stats(cx, ysrc, ybufs, n, psi, D * RMS_EPS)
    load_x_group(cx, xin_dram, xin_key, off_in, n)
    xin, xinb = cx.xin, cx.xinb
    for c in range(8):
        i = cx.rot("ntmp", 2)
        t = cx.ntmp[i]
        p.V(lambda e, c=c, t=t: e.scalar_tensor_tensor(t[:, :n], ysrc(c), gcol(c), r[:, :n], ALU.mult, ALU.mult),
            reads=ybufs + [rb, cx.B("vecs")], writes=[cx.B("ntmp", i)])
        p.V(lambda e, c=c, t=t: e.tensor_tensor(xin[:, c, :n], xin[:, c, :n], t[:, :n], ALU.add),
            reads=[cx.B("ntmp", i), xinb], writes=[xinb])
    dst = xout_dram.rearrange("(c p) t -> p c t", p=128)[:, :, off_out:off_out + n]
    return p.dma("sync", dst, xin[:, :, :n], "xout", reads=[xinb], writes=[cx.B(*xout_key)])


def alloc_small(cx):
    cx.xins = [cx.carve([128, 8, 512], F32) for i in range(2)]
    cx.sg = [cx.carve([128, 512], F32) for i in range(2)]
    cx.ntmp = [cx.carve([128, 512], F32) for i in range(2)]


def alloc_ffn(cx, maxtok):
    cx.hT = [cx.carve([128, 8, maxtok], BF16) for i in range(2)]
    cx.aT = cx.carve([128, NFC, maxtok], BF16)
    cx.ytmp = cx.carve([128, 8, maxtok], F32)
    cx.wg = [cx.carve([128, 8, 256], BF16) for i in range(2)]
    cx.wu = [cx.carve([128, 8, 256], BF16) for i in range(2)]
    cx.wd = [cx.carve([128, NFC, 128], BF16) for i in range(2)]


def ffn(cx, tag, xin_dram, xin_key, xout_dram, xout_key, passes, wg_d, wu_d, wd_d, gpre, gpost, out_shift=0):
    p = cx.p
    last = None
    hTs, aT, ytmp, wg, wu, wd = cx.hT, cx.aT, cx.ytmp, cx.wg, cx.wu, cx.wd

    def locs(groups):
        loc, o = [], 0
        for (off, n) in groups:
            loc.append(o)
            o += n
        return loc

    def S1(k):
        groups = passes[k]
        loc = locs(groups)
        hT = hTs[k % 2]
        for gi, (off, n) in enumerate(groups):
            load_x_group(cx, xin_dram, xin_key, off, n)
            lo = loc[gi]
            norm_to_h(cx, lambda c, lo=lo, n=n: hT[:, c, lo:lo + n], cx.B("hT", k % 2), gpre, n)

    def S2(k):
        groups = passes[k]
        loc = locs(groups)
        hT, hb = hTs[k % 2], cx.B("hT", k % 2)
        for fp in range(NFC // 2):
            s = cx.rot("wgu", 2)
            p.dma("gpsimd", wg[s][:], wg_d.rearrange("(c p) f -> p c f", p=128)[:, :, fp * 256:(fp + 1) * 256],
                  f"wg{s}", writes=[cx.B("wg", s)])
            p.dma("gpsimd", wu[s][:], wu_d.rearrange("(c p) f -> p c f", p=128)[:, :, fp * 256:(fp + 1) * 256],
                  f"wu{s}", writes=[cx.B("wu", s)])
            for h in range(2):
                fc = fp * 2 + h
                for gi, (off, n) in enumerate(groups):
                    lo = loc[gi]
                    b = cx.rot("gu", 2)
                    pg, pu = cx.psb[b], cx.psb[2 + b]
                    for c in range(8):
                        p.mm(pg[:, :n], wg[s][:, c, h * 128:(h + 1) * 128], hT[:, c, lo:lo + n], c == 0, c == 7,
                             reads=[cx.B("wg", s), hb], writes=[cx.B("psb", b)])
                    for c in range(8):
                        p.mm(pu[:, :n], wu[s][:, c, h * 128:(h + 1) * 128], hT[:, c, lo:lo + n], c == 0, c == 7,
                             reads=[cx.B("wu", s), hb], writes=[cx.B("psb", 2 + b)])
                    sg = cx.sg[b]
                    p.A(lambda e: e.activation(sg[:, :n], pg[:, :n], AF.Silu),
                        reads=[cx.B("psb", b)], writes=[cx.B("sg", b)])
                    p.V(lambda e: e.tensor_tensor(aT[:, fc, lo:lo + n], pu[:, :n], sg[:, :n], ALU.mult),
                        reads=[cx.B("psb", 2 + b), cx.B("sg", b)], writes=[cx.B("aT")])

    def S3(k):
        groups = passes[k]
        loc = locs(groups)
        for dc in range(8):
            s = cx.rot("wd", 2)
            p.dma("gpsimd", wd[s][:], wd_d.rearrange("(fc p) d -> p fc d", p=128)[:, :, dc * 128:(dc + 1) * 128],
                  f"wd{s}", writes=[cx.B("wd", s)])
            for gi, (off, n) in enumerate(groups):
                lo = loc[gi]
                b = 4 + cx.rot("dn", 2)
                pd = cx.psb[b]
                for fc in range(NFC):
                    p.mm(pd[:, :n], wd[s][:, fc, :], aT[:, fc, lo:lo + n], fc == 0, fc == NFC - 1,
                         reads=[cx.B("wd", s), cx.B("aT")], writes=[cx.B("psb", b)])
                p.A(lambda e: e.activation(ytmp[:, dc, lo:lo + n], pd[:, :n], AF.Copy),
                    reads=[cx.B("psb", b)], writes=[cx.B("ytmp")])

    def S4(k):
        nonlocal last
        groups = passes[k]
        loc = locs(groups)
        for gi, (off, n) in enumerate(groups):
            lo = loc[gi]
            last = norm_residual_store(cx, lambda c, lo=lo, n=n: ytmp[:, c, lo:lo + n], [cx.B("ytmp")],
                                       xin_dram, xin_key, xout_dram, xout_key, off, off - out_shift, n, gpost)

    if passes:
        S1(0)
    for k in range(len(passes)):
        S2(k)
        if k + 1 < len(passes):
            S1(k + 1)
        S3(k)
        S4(k)
    return last


def l1_vec_layout():
    names = [("f00pre", 8), ("f00post", 8), ("m0pre", 8), ("bpw1", 16), ("wdw", 8 * CW), ("bdw", 8), ("lng", 8),
             ("lnb", 8), ("bpw2", 8), ("m0post", 8), ("f01pre", 8), ("f01post", 8), ("f10pre", 8), ("f10post", 8),
             ("m1pre", 8), ("bgate", 1), ("flag", 1)]
    off = {}
    o = 0
    for n, k in names:
        off[n] = (o, k)
        o += k
    return off, o


def phase_A(cx, T):
    p = cx.p
    TT = TC + HALO
    voff, nv = l1_vec_layout()
    xT, vecs_d, wgs, wus, wds = T["xT"], T["vecs"], T["wgs"], T["wus"], T["wds"]
    wpw1_d, wpw2_d, win_d, ident_d, prot_d, cos_d, sin_d = T["wpw1"], T["wpw2"], T["win"], T["ident"], T["prot"], T["ropecos"], T["ropesin"]
    xa, xb, xc, xd = T["xa"], T["xb"], T["xc"], T["xd"]
    qraw_o, qrot_o, kvT_o, vtok_o, gate_o = T["qraw_o"], T["qrot_o"], T["kvT_o"], T["vtok_o"], T["gate_o"]
    if True:
        vecs = cx.carve([128, nv], F32)
        p.dma("sync", vecs[:], vecs_d, "const", writes=[cx.B("vecs")])

        def vcol(name, scale=None):
            o, k = voff[name]
            if scale is not None:
                p.V(lambda e: e.tensor_scalar(vecs[:, o:o + k], vecs[:, o:o + k], float(scale), None, ALU.mult),
                    reads=[cx.B("vecs")], writes=[cx.B("vecs")])
            return lambda c: vecs[:, o + c:o + c + 1]

        g_f00pre = vcol("f00pre", 32.0)
        g_f00post = vcol("f00post", 16.0)
        g_m0pre = vcol("m0pre", 32.0)
        g_m0post = vcol("m0post", 32.0)
        g_f01pre = vcol("f01pre", 32.0)
        g_f01post = vcol("f01post", 16.0)
        g_f10pre = vcol("f10pre", 32.0)
        g_f10post = vcol("f10post", 16.0)
        g_m1pre = vcol("m1pre", 32.0)
        bpw1 = vcol("bpw1")
        bdw = vcol("bdw")
        lng = vcol("lng")
        lnb = vcol("lnb")
        bpw2 = vcol("bpw2")
        wdw_o = voff["wdw"][0]
        bgate_o = voff["bgate"][0]
        flag_o = voff["flag"][0]

        identb = cx.carve([128, 128], BF16)
        p.dma("gpsimd", identb[:], ident_d, "const2", writes=[cx.B("identb")])
        prot = cx.carve([128, 128], BF16)
        p.dma("gpsimd", prot[:], prot_d, "const2", writes=[cx.B("prot")])
        SKIP = False
        m0 = cx.mark()
        alloc_ffn(cx, 1152)
        passes0 = [] if SKIP else [[(0, 128), (128, 512), (640, 512)], [(1152, 512), (1664, 512)]]
        ffn(cx, "f00", xT, ("xT",), xa, ("xa",), passes0, wgs[0], wus[0], wds[0], g_f00pre, g_f00post)
        cx.release(m0)

        wpw1 = cx.carve([128, 8, 2 * D], BF16)
        wpw2 = cx.carve([128, 8, D], BF16)
        for c in range(8):
            p.dma("gpsimd", wpw1[:, c, :], wpw1_d[c * 128:(c + 1) * 128, :], f"wpw{c % 4}", writes=[cx.B("wpw1")])
        for c in range(8):
            p.dma("gpsimd", wpw2[:, c, :], wpw2_d[c * 128:(c + 1) * 128, :], f"wpw{c % 4}", writes=[cx.B("wpw2")])
        glu = cx.carve([128, 8, TT], BF16)
        vbs = [cx.carve([128, 512], BF16) for q in range(2)]
        y2 = cx.carve([128, 8, 512], F32)
        dg = [cx.carve([128, CW, 128], BF16) for i in range(2)]
        hc = cx.carve([128, 8, 512], BF16)
        vt = cx.carve([128, 8, 512], F32)
        sc = cx.carve([128, 8, 512], BF16)
        for (off, n) in ([] if SKIP else [(0, 128), (128, 512), (640, 512), (1152, 512), (1664, 512)]):
            load_x_group(cx, xa, ("xa",), off, n)
            norm_to_h(cx, lambda c, n=n: hc[:, c, :n], cx.B("hT"), g_m0pre, n)
            for oc in range(8):
                b = cx.rot("gu", 2)
                pa, pg = cx.psb[b], cx.psb[2 + b]
                for c in range(8):
                    p.mm(pa[:, :n], wpw1[:, c, oc * 128:(oc + 1) * 128], hc[:, c, :n], c == 0, c == 7,
                         reads=[cx.B("wpw1"), cx.B("hT")], writes=[cx.B("psb", b)])
                for c in range(8):
                    p.mm(pg[:, :n], wpw1[:, c, D + oc * 128:D + (oc + 1) * 128], hc[:, c, :n], c == 0, c == 7,
                         reads=[cx.B("wpw1"), cx.B("hT")], writes=[cx.B("psb", 2 + b)])
                sg = cx.sg[b]
                p.A(lambda e, sg=sg, pg=pg, n=n, oc=oc: e.activation(sg[:, :n], pg[:, :n], AF.Sigmoid, bias=bpw1(8 + oc)),
                    reads=[cx.B("psb", 2 + b), cx.B("vecs")], writes=[cx.B("sg", b)])
                p.V(lambda e, sg=sg, pa=pa, n=n, oc=oc, off=off: e.scalar_tensor_tensor(
                    glu[:, oc, off:off + n], pa[:, :n], bpw1(oc), sg[:, :n], ALU.add, ALU.mult),
                    reads=[cx.B("psb", b), cx.B("sg", b), cx.B("vecs")], writes=[cx.B("glu")])
            if off == 0:
                for oc in range(8):
                    p.V(lambda e, oc=oc: e.tensor_scalar(glu[:, oc, 0:128], glu[:, oc, 0:128], vecs[:, flag_o:flag_o + 1], None, ALU.mult),
                        reads=[cx.B("glu"), cx.B("vecs")], writes=[cx.B("glu")])
        for g in range(0 if SKIP else 4):
            t0 = HALO + g * 512
            n = 512
            for cc in range(8):
                di = cx.rot("dg", 2)
                for k in range(CW):
                    p.V(lambda e, k=k, cc=cc, di=di: e.tensor_scalar(dg[di][:, k, :], identb[:], vecs[:, wdw_o + k * 8 + cc:wdw_o + k * 8 + cc + 1], None, ALU.mult),
                        reads=[cx.B("identb"), cx.B("vecs")], writes=[cx.B("dg", di)])
                b = 4 + cx.rot("dn", 2)
                pd = cx.psb[b]
                for k in range(CW):
                    s0 = t0 - (CW - 1) + k
                    p.mm(pd[:, :n], dg[di][:, k, :], glu[:, cc, s0:s0 + n], k == 0, k == CW - 1,
                         reads=[cx.B("dg", di), cx.B("glu")], writes=[cx.B("psb", b)])
                p.A(lambda e, pd=pd, cc=cc: e.activation(vt[:, cc, :n], pd[:, :n], AF.Identity, bias=bdw(cc)),
                    reads=[cx.B("psb", b), cx.B("vecs")], writes=[cx.B("ytmp")])
            psm, psq = cx.psb[6], cx.psb[7]
            for cc in range(8):
                i = cx.rot("sq", 2)
                sq = cx.sq[i]
                p.A(lambda e, cc=cc, sq=sq: e.activation(sq[:, :n], vt[:, cc, :n], AF.Square), reads=[cx.B("ytmp")], writes=[cx.B("sq", i)])
                p.mm(psq[:, :n], cx.ones[:], sq[:, :n], cc == 0, cc == 7, reads=[cx.B("sq", i), cx.B("ones")], writes=[cx.B("psb", 7)], sig=True)
                j = cx.rot("vb", 2)
                vb = vbs[j]
                p.V(lambda e, cc=cc, vb=vb: e.tensor_copy(vb[:, :n], vt[:, cc, :n]), reads=[cx.B("ytmp")], writes=[cx.B("vb", j)])
                p.mm(psm[:, :n], cx.ones[:], vb[:, :n], cc == 0, cc == 7, reads=[cx.B("vb", j), cx.B("ones")], writes=[cx.B("psb", 6)], sig=True)
            mean, m2 = cx.r32[0], cx.r32[1]
            rs = cx.ntmp[0]
            p.V(lambda e: e.tensor_scalar(mean[:, :n], psm[:, :n], 1.0 / D, None, ALU.mult), reads=[cx.B("psb", 6)], writes=[cx.B("r32", 0)])
            p.V(lambda e: e.tensor_tensor(m2[:, :n], mean[:, :n], mean[:, :n], ALU.mult), reads=[cx.B("r32", 0)], writes=[cx.B("r32", 1)])
            p.V(lambda e: e.scalar_tensor_tensor(rs[:, :n], psq[:, :n], 1.0 / D, m2[:, :n], ALU.mult, ALU.subtract),
                reads=[cx.B("psb", 7), cx.B("r32", 1)], writes=[cx.B("ntmp", 0)])
            p.A(lambda e: e.activation(rs[:, :n], rs[:, :n], AF.Sqrt, bias=float(LN_EPS), scale=1.0), reads=[cx.B("ntmp", 0)], writes=[cx.B("ntmp", 0)])
            p.V(lambda e: e.reciprocal(rs[:, :n], rs[:, :n]), reads=[cx.B("ntmp", 0)], writes=[cx.B("ntmp", 0)])
            for cc in range(8):
                p.V(lambda e, cc=cc: e.tensor_tensor(vt[:, cc, :n], vt[:, cc, :n], mean[:, :n], ALU.subtract),
                    reads=[cx.B("ytmp"), cx.B("r32", 0)], writes=[cx.B("ytmp")])
                p.V(lambda e, cc=cc: e.tensor_tensor(vt[:, cc, :n], vt[:, cc, :n], rs[:, :n], ALU.mult),
                    reads=[cx.B("ytmp"), cx.B("ntmp", 0)], writes=[cx.B("ytmp")])
                p.A(lambda e, cc=cc: e.activation(sc[:, cc, :n], vt[:, cc, :n], AF.Silu, bias=lnb(cc), scale=lng(cc)),
                    reads=[cx.B("ytmp"), cx.B("vecs")], writes=[cx.B("aT")])
            for oc in range(8):
                b = 4 + cx.rot("dn", 2)
                pd = cx.psb[b]
                for c in range(8):
                    p.mm(pd[:, :n], wpw2[:, c, oc * 128:(oc + 1) * 128], sc[:, c, :n], c == 0, c == 7,
                         reads=[cx.B("wpw2"), cx.B("aT")], writes=[cx.B("psb", b)])
                p.A(lambda e, pd=pd, oc=oc: e.activation(y2[:, oc, :n], pd[:, :n], AF.Identity, bias=bpw2(oc)),
                    reads=[cx.B("psb", b), cx.B("vecs")], writes=[cx.B("y2")])
            norm_residual_store(cx, lambda c: y2[:, c, :n], [cx.B("y2")], xa, ("xa",), xb, ("xb",), t0, t0 - HALO, n, g_m0post)

        cx.release(m0)
        alloc_ffn(cx, 1024)
        passes = [] if SKIP else [[(0, 512), (512, 512)], [(1024, 512), (1536, 512)]]
        ffn(cx, "f01", xb, ("xb",), xc, ("xc",), passes, wgs[1], wus[1], wds[1], g_f01pre, g_f01post)
        ffn(cx, "f10", xc, ("xc",), xd, ("xd",), passes, wgs[2], wus[2], wds[2], g_f10pre, g_f10post)

        cx.release(m0)
        hc = cx.carve([128, 8, 512], BF16)
        win = cx.carve([128, 8, NIN], BF16)
        for c in range(8):
            p.dma("gpsimd", win[:, c, :], win_d[c * 128:(c + 1) * 128, :], f"wpw{c % 4}", writes=[cx.B("win")])
        cosT = cx.carve([128, TC], F32)
        sinT = cx.carve([128, TC], F32)
        p.dma("sync", cosT[:], cos_d, "const", writes=[cx.B("cs")])
        p.dma("sync", sinT[:], sin_d, "const", writes=[cx.B("cs")])
        qrawS = [cx.carve([128, 8, 512], BF16) for i in range(2)]
        qrotS = [cx.carve([128, 8, 512], BF16) for i in range(2)]
        kvS = [cx.carve([128, 8, 512], BF16) for i in range(2)]
        vtk = [cx.carve([128, 512], BF16) for i in range(2)]
        gsb = [cx.carve([128, 512], BF16) for i in range(2)]
        vS = [cx.carve([128, 2, 4, 256], BF16) for i in range(2)]
        outs = []
        PARTS = ["fm", "v", "g"]
        for g in range(4):
            t0 = g * 512
            n = 512
            load_x_group(cx, xd, ("xd",), t0, n)
            norm_to_h(cx, lambda c: hc[:, c, :n], cx.B("hT"), g_m1pre, n)
            so = cx.rot("nsao", 2)
            fm = [(oc, "q") for oc in range(8)] + [(8, "kc"), (9, "kc"), (10, "vc"), (11, "vc"), (12, "ks"), (13, "ks"), (16, "kw"), (17, "kw")]
            kvidx = {"kc": 0, "vc": 2, "ks": 4, "kw": 6}
            for (oc, kind) in (fm if "fm" in PARTS else []):
                b = cx.rot("gu", 2)
                pq = cx.psb[b]
                for c in range(8):
                    p.mm(pq[:, :n], win[:, c, oc * 128:(oc + 1) * 128], hc[:, c, :n], c == 0, c == 7,
                         reads=[cx.B("win"), cx.B("hT")], writes=[cx.B("psb", b)])
                if kind == "q":
                    raw, rawb = qrawS[so][:, oc, :], cx.B("qrawS", so)
                    rot_, rotb = qrotS[so][:, oc, :], cx.B("qrotS", so)
                elif kind in ("kc", "vc"):
                    raw, rawb = kvS[so][:, kvidx[kind] + oc % 2, :], cx.B("kvS", so)
                else:
                    i = cx.rot("qb", 2)
                    raw, rawb = vtk[i][:, :], cx.B("vtk", i)
                    rot_, rotb = kvS[so][:, kvidx[kind] + oc % 2, :], cx.B("kvS", so)
                p.A(lambda e, raw=raw, pq=pq: e.activation(raw, pq[:, :n], AF.Copy), reads=[cx.B("psb", b)], writes=[rawb])
                if kind in ("q", "ks", "kw"):
                    b2 = 2 + cx.rot("rp", 2)
                    pr = cx.psb[b2]
                    p.mm(pr[:, :n], prot[:], raw, True, True, reads=[cx.B("prot"), rawb], writes=[cx.B("psb", b2)])
                    ti = cx.rot("ntmp", 2)
                    t1, t1b = cx.ntmp[ti], cx.B("ntmp", ti)
                    si = cx.rot("sgr", 2)
                    t2, t2b = cx.sg[si], cx.B("sg", si)
                    p.V(lambda e, t1=t1, raw=raw: e.tensor_tensor(t1[:, :n], raw, cosT[:, t0:t0 + n], ALU.mult),
                        reads=[rawb, cx.B("cs")], writes=[t1b])
                    p.V(lambda e, t2=t2, pr=pr: e.tensor_tensor(t2[:, :n], pr[:, :n], sinT[:, t0:t0 + n], ALU.mult),
                        reads=[cx.B("psb", b2), cx.B("cs")], writes=[t2b])
                    p.V(lambda e, t1=t1, t2=t2, rot_=rot_: e.tensor_tensor(rot_, t1[:, :n], t2[:, :n], ALU.add),
                        reads=[t1b, t2b], writes=[rotb])
            if "fm" in PARTS:
                outs.append(p.dma("sync", qraw_o.rearrange("(c p) t -> p c t", p=128)[:, :, t0:t0 + n], qrawS[so][:], f"qo{so}a", reads=[cx.B("qrawS", so)]))
                outs.append(p.dma("sync", qrot_o.rearrange("(c p) t -> p c t", p=128)[:, :, t0:t0 + n], qrotS[so][:], f"qo{so}b", reads=[cx.B("qrotS", so)]))
                outs.append(p.dma("sync", kvT_o.rearrange("s (a p) t -> p (s a) t", p=128)[:, :, t0:t0 + n], kvS[so][:], f"qo{so}c", reads=[cx.B("kvS", so)]))
            for vi, c0 in enumerate((1792, 2304) if "v" in PARTS else ()):
                for tt in range(4):
                    b = 4 + cx.rot("dn", 2)
                    pv = cx.psb[b]
                    for c in range(8):
                        p.mm(pv[:, :256], hc[:, c, tt * 128:(tt + 1) * 128], win[:, c, c0:c0 + 256], c == 0, c == 7,
                             reads=[cx.B("win"), cx.B("hT")], writes=[cx.B("psb", b)])
                    p.A(lambda e, pv=pv, vi=vi, tt=tt: e.activation(vS[so][:, vi, tt, :], pv[:, :256], AF.Copy), reads=[cx.B("psb", b)], writes=[cx.B("vS", so)])
            if "v" in PARTS:
                for vi in range(2):
                    outs.append(p.dma("sync", vtok_o[vi, t0:t0 + n, :].rearrange("(tt p) c -> p tt c", p=128), vS[so][:, vi, :, :], f"qo{so}v{vi}", reads=[cx.B("vS", so)]))
            if "g" not in PARTS:
                continue
            b = cx.rot("gu", 2)
            pq = cx.psb[b]
            for c in range(8):
                p.mm(pq[0:48, :n], win[:, c, 2560:2608], hc[:, c, :n], c == 0, c == 7,
                     reads=[cx.B("win"), cx.B("hT")], writes=[cx.B("psb", b)])
            i = cx.rot("gsb", 2)
            p.A(lambda e, i=i, pq=pq: e.activation(gsb[i][0:48, :n], pq[0:48, :n], AF.Sigmoid, bias=vecs[0:48, bgate_o:bgate_o + 1]),
                reads=[cx.B("psb", b), cx.B("vecs")], writes=[cx.B("gsb", i)])
            outs.append(p.dma("sync", gate_o[:, t0:t0 + n], gsb[i][0:48, :n], "gout", reads=[cx.B("gsb", i)]))

        cx.release(m0)
    return outs


def rope_tables_np(pos0, n):
    pos = (pos0 + np.arange(n)).astype(np.float32)
    inv = (np.float32(500000.0) ** (-np.arange(0, 16, 2, dtype=np.float32) / np.float32(16))).astype(np.float32)
    ang = pos[None, :] * inv[:, None]
    c8, s8 = np.cos(ang).astype(np.float32), np.sin(ang).astype(np.float32)
    cos = np.ones((128, n), np.float32)
    sin = np.zeros((128, n), np.float32)
    for hb in (0, 64):
        cos[hb:hb + 8] = c8
        cos[hb + 8:hb + 16] = c8
        sin[hb:hb + 8] = s8
        sin[hb + 8:hb + 16] = s8
    return cos, sin


def prot_np():
    pr = np.zeros((128, 128), np.float32)
    for hb in (0, 64):
        for j in range(8):
            pr[hb + j + 8, hb + j] = -1.0
            pr[hb + j, hb + 8 + j] = 1.0
    return pr


def prep_launch1(I, core):
    b, j = divmod(core, 4)
    t0 = j * TC
    x = I["x"][b]
    xT = np.zeros((D, TC + HALO), np.float32)
    xT[:, HALO:] = x[t0:t0 + TC].T
    if j > 0:
        xT[:, :HALO] = x[t0 - HALO:t0].T
    voff, nv = l1_vec_layout()
    vecs = np.zeros((128, nv), np.float32)

    def put(name, arr):
        o, k = voff[name]
        vecs[:, o:o + k] = arr

    put("f00pre", pc(I["ffn_norm_pre"][0, 0]))
    put("f00post", pc(I["ffn_norm_post"][0, 0]))
    put("m0pre", pc(I["mix_norm_pre"][0]))
    put("bpw1", pc(I["conv_b_pw1"][0]))
    put("wdw", np.concatenate([pc(I["conv_w_dw"][0, k]) for k in range(CW)], axis=1))
    put("bdw", pc(I["conv_b_dw"][0]))
    put("lng", pc(I["conv_ln_g"][0]))
    put("lnb", pc(I["conv_ln_b"][0]))
    put("bpw2", pc(I["conv_b_pw2"][0]))
    put("m0post", pc(I["mix_norm_post"][0]))
    put("f01pre", pc(I["ffn_norm_pre"][0, 1]))
    put("f01post", pc(I["ffn_norm_post"][0, 1]))
    put("f10pre", pc(I["ffn_norm_pre"][1, 0]))
    put("f10post", pc(I["ffn_norm_post"][1, 0]))
    put("m1pre", pc(I["mix_norm_pre"][1]))
    bg = np.zeros((128, 1), np.float32)
    bg[:48, 0] = I["nsa_b_gate"][0]
    put("bgate", bg)
    put("flag", np.full((128, 1), 0.0 if j == 0 else 1.0, np.float32))
    cos, sin = rope_tables_np(t0, TC)
    m = {"xT": xT, "vecs": vecs, "wpw1": I["conv_w_pw1"][0], "wpw2": I["conv_w_pw2"][0], "win": I["nsa_w_in"][0],
         "ident": np.eye(128, dtype=np.float32), "prot": prot_np(), "ropecos": cos, "ropesin": sin}
    for i, (l, h) in enumerate(((0, 0), (0, 1), (1, 0))):
        m[f"wg{i}"] = I["ffn_w_gate"][l, h]
        m[f"wu{i}"] = I["ffn_w_up"][l, h]
        m[f"wd{i}"] = I["ffn_w_down"][l, h]
    return {k: np.ascontiguousarray(v, dtype=np.float32) for k, v in m.items()}


def l2_consts():
    bf = ml_dtypes.bfloat16
    E = np.zeros((128, 64, 128), np.float32)
    for jt in range(64):
        E[2 * jt, jt, 0:64] = 1.0
        E[2 * jt + 1, jt, 64:128] = 1.0
    i = np.arange(128)[:, None]
    j = np.arange(512)[None, :]
    cmask = np.zeros((128, 5, 512), np.float32)
    for d in range(5):
        cmask[:, d, :] = np.where(16 * i + 31 - 512 * d <= j, 0.0, NEG)
    wmask = np.zeros((128, 8, 512), np.float32)
    for oi in range(8):
        k = 128 * (oi - 4) + i
        wmask[:, oi, :] = np.where((k <= j) & (k > j - 512), 0.0, NEG)
    AB = np.zeros((128, 2, 256), np.float32)
    jj = np.arange(128)[:, None]
    m = np.arange(256)[None, :]
    x = (m - 128) - (jj >= 64)
    forced = (x == 0) | (x == -1)
    nonc = x > 0
    AB[:, 0, :] = np.where(forced | nonc, 0.0, 1.0)
    AB[:, 1, :] = np.where(forced, 1e9, np.where(nonc, -1e9, 0.0))
    ov = np.zeros((128, 4, 130), np.float32)
    for ct in range(4):
        for ii in range(128):
            c = 128 * ct + ii
            if c > 510:
                continue
            for n in range(128):
                if 16 * c < 64 * n + 64 and 16 * c + 32 > 64 * n:
                    ov[ii, ct, n] = 1.0
            ov[ii, ct, 128] = 1.0
    selG = np.zeros((12, 12, 128), np.float32)
    for r in range(12):
        selG[r, r, :] = 1.0
    IND = np.zeros((64, S), np.float32)
    for jt in range(64):
        IND[2 * (jt % 32), jt * 128:jt * 128 + 64] = 1.0
        IND[2 * (jt % 32) + 1, jt * 128 + 64:jt * 128 + 128] = 1.0
    return {"IND": IND.astype(bf), "identb": np.eye(128, dtype=np.float32).astype(bf), "cmask": cmask.astype(bf),
            "wmask": wmask.astype(bf), "AB": AB, "ov": ov.astype(bf), "selG": selG.astype(bf)}


NQG = 16
SCALE = 0.125


def phase_B(cx, T):
    p = cx.p
    oh_d = T["oh"]
    w1_d, w2k_d, w2v_d, pe_d = T["w1"], T["w2kD"], T["w2vD"], T["pe"]
    EX2 = T["EX2"]

    GXL = T["GX1L"]
    oT4 = EX2[0, :].rearrange("(j r t) -> j r t", j=4, t=TC)

    def gq(j, name, g_):
        base = 0 if name == "qraw_o" else 4
        return GXL[base + g_][j, :].rearrange("(r t) -> r t", t=TC)

    def gkv(j, kind):
        return GXL[8 + kind][j, :].rearrange("(r t) -> r t", t=TC)

    def gv(j, vi):
        return GXL[12 + vi][j, :].rearrange("(t c) -> t c", c=256)

    def ggate(j):
        return GXL[14][j, :].rearrange("(r t) -> r t", t=TC)

    if True:
        pst = cx.psb[7][:].bitcast(BF16)
        ones = cx.ones
        oh = cx.carve([128, 4], F32)
        p.dma("sync", oh[:], oh_d, "c_oh", writes=[cx.B("oh")])
        identb = cx.carve([128, 128], BF16)
        cmask = cx.carve([128, 5, 512], BF16)
        wmask = cx.carve([128, 8, 512], BF16)
        AB = cx.carve([128, 2, 256], F32)
        ov = cx.carve([128, 4, 130], BF16)
        selG = cx.carve([12, 12, 128], BF16)
        for dst, nm in ((identb, "identb"), (cmask, "cmask"), (wmask, "wmask"), (AB, "AB"), (ov, "ov"), (selG, "selG")):
            p.dma("sync", dst[:], T[nm], f"c_k{nm}", writes=[cx.B(nm)])
        kselD = cx.carve([128, S], BF16)
        kwinD = cx.carve([128, S], BF16)
        KX1 = cx.carve([128, S], BF16)
        vA = {nm: cx.carve([128, 64, 192], BF16) for nm in ("sel", "win")}
        for t_ in vA.values():
            p.V(lambda e: e.memset(t_[:], 1.0), writes=[cx.B("vA")])
        kcmpT = cx.carve([128, 512], BF16)
        vcmp = cx.carve([128, 4, 128], BF16)
        p.V(lambda e: e.memset(kcmpT[:], 0.0), writes=[cx.B("kcmpT")])
        p.V(lambda e: e.memset(vcmp[:], 0.0), writes=[cx.B("vcmp")])

        def select4(dst, stage, n_part, stagebuf, dstbuf):
            ps_ = slice(0, n_part)
            p.V(lambda e: e.tensor_scalar(dst, stage(0), oh[ps_, 0:1], None, ALU.mult), reads=[stagebuf, cx.B("oh")], writes=[dstbuf])
            for g_ in range(1, 4):
                p.V(lambda e: e.scalar_tensor_tensor(dst, stage(g_), oh[ps_, g_:g_ + 1], dst, ALU.mult, ALU.add),
                    reads=[stagebuf, cx.B("oh"), dstbuf], writes=[dstbuf])

        m0 = cx.mark()
        stgs = [cx.carve([128, 4, 2048], BF16) for i in range(2)]
        kv2 = cx.carve([128, S], BF16)
        w1 = cx.carve([128, 16, 256], BF16)
        w2 = cx.carve([128, 2, 128], BF16)
        pe = cx.carve([128, 16], BF16)
        hid = cx.carve([128, 2, 512], BF16)
        bias = cx.carve([128, 2], F32)
        vstg = [cx.carve([128, 16, 256], BF16) for i in range(2)]

        def load_sel_kv(kind, dstT, nm, shifted):
            for j in range(4):
                si_ = cx.rot("stg", 2)
                stg, stgb = stgs[si_], cx.B("stg", si_)
                for g_ in range(4):
                    src = gkv(j, kind)[g_ * 64:(g_ + 1) * 64, :]
                    p.dma("sync", stg[0:64, g_, :], src, f"sg{g_}", writes=[stgb])
                    if not shifted:
                        p.dma("sync", stg[64:128, g_, :], src, f"sh{g_}", writes=[stgb])
                    else:
                        p.dma("sync", stg[64:128, g_, 0:2047], src[:, 1:2048], f"sh{g_}", writes=[stgb])
                        if j < 3:
                            p.dma("sync", stg[64:128, g_, 2047:2048], gkv(j + 1, kind)[g_ * 64:(g_ + 1) * 64, 0:1], f"sh{g_}", writes=[stgb], allow_slow_non_contiguous=True)
                        else:
                            p.V(lambda e: e.memset(stg[64:128, g_, 2047:2048], 0.0), writes=[stgb])
                select4(dstT[:, j * 2048:(j + 1) * 2048], lambda g_: stg[:, g_, :], 128, stgb, cx.B(nm))

        load_sel_kv(2, kselD, "kselD", False)
        p.op("gpsimd", lambda e: e.tensor_copy(KX1[64:128, :], kselD[64:128, :]), reads=[cx.B("kselD")], writes=[cx.B("KX1")])
        p.dma("sync", kselD[64:128, :], T["IND"], "c_ind0", reads=[], writes=[cx.B("kselD")])
        p.dma("sync", KX1[0:64, :], T["IND"], "c_ind1", writes=[cx.B("KX1")])
        load_sel_kv(3, kwinD, "kwinD", False)
        for vi, nm in enumerate(("sel", "win")):
            for j in range(4):
                vs_ = cx.rot("vstg", 2)
                p.dma("sync", vstg[vs_][:, :, :], gv(j, vi).rearrange("(t p) c -> p t c", p=128), f"sv{vs_}", writes=[cx.B("vstg", vs_)])
                select4(vA[nm][:, j * 16:(j + 1) * 16, 64:128], lambda g_: vstg[vs_][:, :, g_ * 64:(g_ + 1) * 64], 128, cx.B("vstg", vs_), cx.B("vA"))

        for which, w2_d in enumerate((w2k_d, w2v_d)):
            load_sel_kv(which, kv2, "kv2", True)
            p.dma("gpsimd", w1[:], w1_d[which].rearrange("(c p) j -> p c j", p=128), "c_w1", writes=[cx.B("w1")])
            p.dma("gpsimd", w2[:], w2_d.rearrange("(c p) j -> p c j", p=128), "c_w2", writes=[cx.B("w2")])
            p.dma("gpsimd", pe[:], pe_d[which], "c_pe", writes=[cx.B("pe")])
            for jc in range(2):
                pb = cx.psb[2]
                for ch in range(16):
                    p.mm(pb[:, 0:1], w1[:, ch, jc * 128:(jc + 1) * 128], pe[:, ch:ch + 1], ch == 0, ch == 15,
                         reads=[cx.B("w1"), cx.B("pe")], writes=[cx.B("psb", 2)])
                p.V(lambda e: e.tensor_copy(bias[:, jc:jc + 1], pb[:, 0:1]), reads=[cx.B("psb", 2)], writes=[cx.B("bias")])
                ph = cx.psb[jc]
                for lp in range(16):
                    p.mm(ph[:, 0:511], w1[:, lp, jc * 128:(jc + 1) * 128], kv2[:, 2 * lp:2 * lp + 16 * 510 + 1:16], lp == 0, lp == 15,
                         reads=[cx.B("w1"), cx.B("kv2")], writes=[cx.B("psb", jc)])
                p.A(lambda e: e.activation(hid[:, jc, 0:511], ph[:, 0:511], AF.Silu, bias=bias[:, jc:jc + 1]),
                    reads=[cx.B("psb", jc), cx.B("bias")], writes=[cx.B("hid")])
            if which == 0:
                pk = cx.psb[3]
                for jc in range(2):
                    p.mm(pk[:, 0:511], w2[:, jc, :], hid[:, jc, 0:511], jc == 0, jc == 1, reads=[cx.B("w2"), cx.B("hid")], writes=[cx.B("psb", 3)])
                p.A(lambda e: e.activation(kcmpT[:, 0:511], pk[:, 0:511], AF.Copy), reads=[cx.B("psb", 3)], writes=[cx.B("kcmpT")])
            else:
                for ct in range(4):
                    M = 128 if ct < 3 else 127
                    pv = cx.psb[3 + (ct % 2)]
                    for jc in range(2):
                        p.mm(pv[0:M, 0:128], hid[:, jc, ct * 128:ct * 128 + M], w2[:, jc, :], jc == 0, jc == 1,
                             reads=[cx.B("w2"), cx.B("hid")], writes=[cx.B("psb", 3 + (ct % 2))])
                    p.A(lambda e: e.activation(vcmp[0:M, ct, :], pv[0:M, 0:128], AF.Copy),
                        reads=[cx.B("psb", 3 + (ct % 2))], writes=[cx.B("vcmp")])
        cx.release(m0)

        qstg = cx.carve([128, 4, 2, 512], BF16)
        gstg = cx.carve([12, 4, 512], BF16)
        qraw = [cx.carve([128, 2, 512], BF16) for i in range(2)]
        qrot = [cx.carve([128, 2, 512], BF16) for i in range(2)]
        gts = [cx.carve([12, 512], BF16) for i in range(2)]
        eT = [cx.carve([128, 4, 512], BF16) for i in range(2)]
        pT = [cx.carve([128, 512], BF16) for i in range(5)]
        rz = [cx.carve([128, 512], F32) for i in range(2)]
        wv = [cx.carve([128, 512], F32) for i in range(2)]
        tmp = [cx.carve([128, 512], F32) for i in range(2)]
        acc = cx.carve([128, 2, 512], F32)
        accb = [cx.carve([128, 2, 512], BF16) for i in range(2)]
        impacc = cx.carve([128, 4, 128], F32)
        imod = cx.carve([128, 128], F32)
        scr = cx.carve([128, 128], F32)
        m8 = cx.carve([128, 16], F32)
        rzc = cx.carve([128, 1], F32)
        negm = cx.carve([128, 128], BF16)
        negT = cx.carve([128, 512], BF16)
        Xt = {(h_, a_, w_): cx.carve([128, 512], BF16) for h_ in range(2) for a_ in range(2) for w_ in range(2)}
        outs = []

        def finish_branch(r, gi, pacc, paccbuf, zrows, first, gt, gtb):
            a, half = divmod(r, 2)
            hs = slice(64 * half, 64 * half + 64)
            pG = cx.psb[6]
            p.mm(pG[:, :], selG[:, 3 * r + gi, :], gt[:, :], True, True, reads=[cx.B("selG"), gtb], writes=[cx.B("psb", 6)])
            i = cx.rot("rz", 2)
            p.V(lambda e: e.tensor_scalar(rz[i][zrows, :], pacc[zrows, :], 1e-30, None, ALU.max), reads=[paccbuf], writes=[cx.B("rz", i)])
            p.V(lambda e: e.reciprocal(rz[i][zrows, :], rz[i][zrows, :]), reads=[cx.B("rz", i)], writes=[cx.B("rz", i)])
            p.V(lambda e: e.tensor_tensor(wv[i][zrows, :], rz[i][zrows, :], pG[zrows, :], ALU.mult),
                reads=[cx.B("rz", i), cx.B("psb", 6)], writes=[cx.B("wv", i)])
            if first:
                p.V(lambda e: e.tensor_tensor(acc[hs, a, :], pacc[hs, :], wv[i][zrows, :], ALU.mult),
                    reads=[paccbuf, cx.B("wv", i)], writes=[cx.B("acc")])
            else:
                p.V(lambda e: e.tensor_tensor(tmp[i][hs, :], pacc[hs, :], wv[i][zrows, :], ALU.mult),
                    reads=[paccbuf, cx.B("wv", i)], writes=[cx.B("tmp", i)])
                p.V(lambda e: e.tensor_tensor(acc[hs, a, :], acc[hs, a, :], tmp[i][hs, :], ALU.add),
                    reads=[cx.B("acc"), cx.B("tmp", i)], writes=[cx.B("acc")])

        for qg in range(NQG):
            q0 = qg * 512
            s = cx.rot("qld", 2)
            jq, tl = divmod(qg, 4)
            tl *= 512
            for nmq, dstq, bq in (("qraw_o", qraw[s], cx.B("qraw", s)), ("qrot_o", qrot[s], cx.B("qrot", s))):
                for g_ in range(4):
                    p.dma("sync", qstg[:, g_, :, :], gq(jq, nmq, g_)[:, tl:tl + 512].rearrange("(a p) t -> p a t", p=128),
                          f"qs{g_}", writes=[cx.B("qstg")])
                select4(dstq[:], lambda g_: qstg[:, g_, :, :], 128, cx.B("qstg"), bq)
            for g_ in range(4):
                p.dma("sync", gstg[:, g_, :], ggate(jq)[g_ * 12:(g_ + 1) * 12, tl:tl + 512], f"qs{g_}", writes=[cx.B("gstg")])
            select4(gts[s][:], lambda g_: gstg[:, g_, :], 12, cx.B("gstg"), cx.B("gts", s))
            gt, gtb = gts[s], cx.B("gts", s)
            nct = (32 * qg + 30) // 128 + 1
            for r in range(4):
                a, half = divmod(r, 2)
                hs = slice(64 * half, 64 * half + 64)
                es = cx.rot("eT", 2)
                for ct in range(nct):
                    d = qg - 4 * ct
                    b = cx.rot("S", 2)
                    ps_ = cx.psb[b]
                    p.mm(ps_[:, :], kcmpT[hs, ct * 128:(ct + 1) * 128], qraw[s][hs, a, :], True, d >= 5,
                         reads=[cx.B("kcmpT"), cx.B("qraw", s)], writes=[cx.B("psb", b)])
                    if d < 5:
                        p.mm(ps_[:, :], identb[:], cmask[:, d, :], False, True, reads=[cx.B("identb"), cx.B("cmask")], writes=[cx.B("psb", b)])
                    p.A(lambda e, ps_=ps_, es=es, ct=ct: e.activation(eT[es][:, ct, :], ps_[:, :], AF.Exp, scale=SCALE),
                        reads=[cx.B("psb", b)], writes=[cx.B("eT", es)])
                pO, pZ = cx.psb[2], cx.psb[3]
                for ct in range(nct):
                    p.mm(pO[:, :], vcmp[:, ct, :], eT[es][:, ct, :], ct == 0, ct == nct - 1, reads=[cx.B("vcmp"), cx.B("eT", es)], writes=[cx.B("psb", 2)])
                for ct in range(nct):
                    p.mm(pZ[:, :], ones[:], eT[es][:, ct, :], ct == 0, ct == nct - 1, reads=[cx.B("ones"), cx.B("eT", es)], writes=[cx.B("psb", 3)])
                zrows = slice(64 * (1 - half), 64 * (1 - half) + 64)
                pG = cx.psb[6]
                p.mm(pG[:, :], selG[:, 3 * r + 0, :], gt[:, :], True, True, reads=[cx.B("selG"), gtb], writes=[cx.B("psb", 6)])
                i = cx.rot("rz", 2)
                p.V(lambda e, i=i: e.tensor_scalar(rz[i][hs, :], pZ[hs, :], 1e-30, None, ALU.max), reads=[cx.B("psb", 3)], writes=[cx.B("rz", i)])
                p.V(lambda e, i=i: e.reciprocal(rz[i][hs, :], rz[i][hs, :]), reads=[cx.B("rz", i)], writes=[cx.B("rz", i)])
                p.V(lambda e, i=i: e.tensor_tensor(wv[i][hs, :], rz[i][hs, :], pG[hs, :], ALU.mult),
                    reads=[cx.B("rz", i), cx.B("psb", 6)], writes=[cx.B("wv", i)])
                p.V(lambda e, i=i, a=a: e.tensor_tensor(acc[hs, a, :], pO[hs, :], wv[i][hs, :], ALU.mult),
                    reads=[cx.B("psb", 2), cx.B("wv", i)], writes=[cx.B("acc")])
                for qt in range(4):
                    bi = 4 + cx.rot("I", 2)
                    pI = cx.psb[bi]
                    for ct in range(nct):
                        p.mm(pI[:, 0:129], eT[es][:, ct, qt * 128:(qt + 1) * 128], ov[:, ct, 0:129], ct == 0, ct == nct - 1,
                             reads=[cx.B("eT", es), cx.B("ov")], writes=[cx.B("psb", bi)])
                    p.V(lambda e, pI=pI: e.tensor_scalar(rzc[:], pI[:, 128:129], 1e-30, None, ALU.max), reads=[cx.B("psb", bi)], writes=[cx.B("rzc")])
                    p.V(lambda e: e.reciprocal(rzc[:], rzc[:]), reads=[cx.B("rzc")], writes=[cx.B("rzc")])
                    if r == 0:
                        p.V(lambda e, pI=pI, qt=qt: e.tensor_scalar(impacc[:, qt, :], pI[:, 0:128], rzc[:, 0:1], None, ALU.mult),
                            reads=[cx.B("psb", bi), cx.B("rzc")], writes=[cx.B("impacc")])
                    else:
                        p.V(lambda e, pI=pI, qt=qt: e.scalar_tensor_tensor(impacc[:, qt, :], pI[:, 0:128], rzc[:, 0:1], impacc[:, qt, :], ALU.mult, ALU.add),
                            reads=[cx.B("psb", bi), cx.B("rzc"), cx.B("impacc")], writes=[cx.B("impacc")])
            for qt in range(4):
                ti = 4 * qg + qt
                c0 = 128 - 2 * ti
                p.V(lambda e, qt=qt, c0=c0: e.tensor_tensor(imod[:], impacc[:, qt, :], AB[:, 0, c0:c0 + 128], ALU.mult),
                    reads=[cx.B("impacc"), cx.B("AB")], writes=[cx.B("imod")])
                p.V(lambda e, c0=c0: e.tensor_tensor(imod[:], imod[:], AB[:, 1, c0:c0 + 128], ALU.add), reads=[cx.B("imod"), cx.B("AB")], writes=[cx.B("imod")])
                p.V(lambda e: e.memset(imod[:, 0:1], 1e9), reads=[], writes=[cx.B("imod")])
                p.V(lambda e: e.max(m8[:, 0:8], imod[:]), reads=[cx.B("imod")], writes=[cx.B("m8")])
                p.V(lambda e: e.match_replace(scr[:], m8[:, 0:8], imod[:], -1e30), reads=[cx.B("imod"), cx.B("m8")], writes=[cx.B("scr")])
                p.V(lambda e: e.max(m8[:, 8:16], scr[:]), reads=[cx.B("scr")], writes=[cx.B("m8")])
                p.V(lambda e: e.tensor_scalar(negm[:], imod[:], m8[:, 15:16], NEG, ALU.is_lt, ALU.mult), reads=[cx.B("imod"), cx.B("m8")], writes=[cx.B("negm")])
                p.op("tensor", lambda e, qt=qt: e.transpose(pst[:, qt * 128:(qt + 1) * 128], negm[:], identb[:]),
                     reads=[cx.B("negm"), cx.B("identb")], writes=[cx.B("psb", 7)])
                p.A(lambda e, qt=qt: e.activation(negT[:, qt * 128:(qt + 1) * 128], pst[:, qt * 128:(qt + 1) * 128], AF.Copy),
                    reads=[cx.B("psb", 7)], writes=[cx.B("negT")])
            nwin = 1 if 4 * qg + 3 < 32 else 2
            for half in range(2):
                hs_ = slice(64 * half, 64 * half + 64)
                os_ = slice(64 * (1 - half), 64 * (1 - half) + 64)
                for a_ in range(2):
                    for w_ in range(nwin):
                        xt = Xt[(half, a_, w_)]
                        xb_ = cx.B("Xt", half, a_, w_)
                        p.op("gpsimd", lambda e: e.tensor_copy(xt[hs_, :], qrot[s][hs_, a_, :]), reads=[cx.B("qrot", s)], writes=[xb_])
                        p.V(lambda e: e.tensor_copy(xt[os_, :], negT[64 * w_:64 * w_ + 64, :]), reads=[cx.B("negT")], writes=[xb_])
            for (br, gi, kD, kbuf, jts) in (("sel", 1, kselD, "kselD", list(range(4 * qg + 4))),
                                            ("win", 2, kwinD, "kwinD", list(range(max(0, 4 * qg - 4), 4 * qg + 4)))):
                units = [(ji, jt, r) for ji, jt in enumerate(jts) for r in range(4)]

                def emit_S(u):
                    ji, jt, r = u
                    o = jt - 4 * qg
                    a, half = divmod(r, 2)
                    hs = slice(64 * half, 64 * half + 64)
                    b = (0, 1, 6, 7)[cx.rot("S3", 4)]
                    ps_ = cx.psb[b]
                    need_mask = (br == "win") or (o >= 0)
                    if br == "sel":
                        kx, kxb = (kselD, cx.B("kselD")) if half == 0 else (KX1, cx.B("KX1"))
                        w_ = jt // 32
                        p.mm(ps_[:, :], kx[:, jt * 128:(jt + 1) * 128], Xt[(half, a, w_)][:, :], True, not need_mask,
                             reads=[kxb, cx.B("Xt", half, a, w_)], writes=[cx.B("psb", b)])
                    else:
                        p.mm(ps_[:, :], kD[hs, jt * 128:(jt + 1) * 128], qrot[s][hs, a, :], True, False,
                             reads=[cx.B(kbuf), cx.B("qrot", s)], writes=[cx.B("psb", b)], sig=False)
                    if need_mask:
                        p.mm(ps_[:, :], identb[:], wmask[:, o + 4, :], False, True, reads=[cx.B("identb"), cx.B("wmask")], writes=[cx.B("psb", b)])
                    pi = cx.rot("pT", 5)
                    p.A(lambda e: e.activation(pT[pi][:, :], ps_[:, :], AF.Exp, scale=SCALE),
                        reads=[cx.B("psb", b)], writes=[cx.B("pT", pi)])
                    return pi

                def emit_PV(u, pi):
                    ji, jt, r = u
                    half = r % 2
                    pa = cx.psb[2 + r]
                    p.mm(pa[:, :], vA[br][:, jt, (64 if half == 0 else 0):(192 if half == 0 else 128)], pT[pi][:, :], ji == 0, ji == len(jts) - 1,
                         reads=[cx.B("vA"), cx.B("pT", pi)], writes=[cx.B("psb", 2 + r)], sig=True)

                pend = []
                for u in units:
                    pi = emit_S(u)
                    pend.append((u, pi))
                    if len(pend) > 3:
                        emit_PV(*pend.pop(0))
                for pu in pend:
                    emit_PV(*pu)
                for r in range(4):
                    half = r % 2
                    zrows = slice(64 * (1 - half), 64 * (1 - half) + 64)
                    finish_branch(r, gi, cx.psb[2 + r], cx.B("psb", 2 + r), zrows, False, gt, gtb)
            ob = cx.rot("accb", 2)
            p.V(lambda e, ob=ob: e.tensor_copy(accb[ob][:], acc[:]), reads=[cx.B("acc")], writes=[cx.B("accb", ob)])
            outs.append(p.dma("sync", oT4[jq].rearrange("(a p) t -> p a t", p=128)[:, :, tl:tl + 512], accb[ob][:], f"o{ob}", reads=[cx.B("accb", ob)]))
    return outs


def phase_C(cx, T):
    p = cx.p
    xd_d, vecs_d, wout_d, wg_d, wu_d, wd_d, xe, xf, oh_d = (T["xd"], T["vecs3"], T["wout"], T["wg3"], T["wu3"],
                                                              T["wd3"], T["xe"], T["xf"], T["oh"])
    GX2L = T["GX2L"]
    if True:
        vecs = cx.carve([128, 24], F32)
        p.dma("sync", vecs[:], vecs_d, "const", writes=[cx.B("vecs")])
        oh = cx.carve([128, 4], F32)
        p.dma("sync", oh[:], oh_d, "c_oh", writes=[cx.B("oh")])
        for o, sc_ in ((0, 32.0), (8, 32.0), (16, 16.0)):
            p.V(lambda e: e.tensor_scalar(vecs[:, o:o + 8], vecs[:, o:o + 8], sc_, None, ALU.mult), reads=[cx.B("vecs")], writes=[cx.B("vecs")])
        g_m1post = lambda c: vecs[:, c:c + 1]
        g_pre = lambda c: vecs[:, 8 + c:9 + c]
        g_post = lambda c: vecs[:, 16 + c:17 + c]
        m0 = cx.mark()
        wout = cx.carve([128, 8, D], BF16)
        for c in range(8):
            p.dma("gpsimd", wout[:, c, :], wout_d[c * 128:(c + 1) * 128, :], f"wpw{c % 4}", writes=[cx.B("wout")])
        astg = cx.carve([128, 4, 8, 512], BF16)
        at = [cx.carve([128, 8, 512], BF16) for i in range(2)]
        y2 = cx.carve([128, 8, 512], F32)
        for g in range(4):
            t0 = g * 512
            n = 512
            s = cx.rot("at", 2)
            for jj in range(4):
                p.dma("sync", astg[:, jj, :, :], GX2L[jj].rearrange("g (r t) -> (g r) t", t=TC)[:, t0:t0 + n].rearrange("(c p) t -> p c t", p=128),
                      f"as{jj}", writes=[cx.B("astg")])
            p.V(lambda e: e.tensor_scalar(at[s][:], astg[:, 0, :, :], oh[:, 0:1], None, ALU.mult), reads=[cx.B("astg"), cx.B("oh")], writes=[cx.B("at", s)])
            for jj in range(1, 4):
                p.V(lambda e: e.scalar_tensor_tensor(at[s][:], astg[:, jj, :, :], oh[:, jj:jj + 1], at[s][:], ALU.mult, ALU.add),
                    reads=[cx.B("astg"), cx.B("oh"), cx.B("at", s)], writes=[cx.B("at", s)])
            for oc in range(8):
                b = 4 + cx.rot("dn", 2)
                pd = cx.psb[b]
                for c in range(8):
                    p.mm(pd[:, :n], wout[:, c, oc * 128:(oc + 1) * 128], at[s][:, c, :], c == 0, c == 7,
                         reads=[cx.B("wout"), cx.B("at", s)], writes=[cx.B("psb", b)])
                p.A(lambda e: e.activation(y2[:, oc, :n], pd[:, :n], AF.Copy), reads=[cx.B("psb", b)], writes=[cx.B("y2")])
            norm_residual_store(cx, lambda c: y2[:, c, :n], [cx.B("y2")], xd_d, ("xd",), xe, ("xe",), t0, t0, n, g_m1post)
        cx.release(m0)
        alloc_ffn(cx, 1024)
        passes = [[(0, 512), (512, 512)], [(1024, 512), (1536, 512)]]
        ffn(cx, "f11", xe, ("xe",), xf, ("xf",), passes, wg_d, wu_d, wd_d, g_pre, g_post)
    return [cx.B("xf").last_w]


EX1_FIELDS = {"qraw_o": (0, 2097152, (1024, 2048)), "qrot_o": (2097152, 2097152, (1024, 2048)),
              "kvT_o": (4194304, 2097152, (4, 256, 2048)), "vtok_o": (6291456, 1048576, (2, 2048, 256)),
              "gate_o": (7340032, 98304, (48, 2048))}
NEL1 = 7438336
NEL2 = 256 * S
RG = [[0, 1, 2, 3], [4, 5, 6, 7]]


def build_fused():
    cx = Ctx("fused")
    p = cx.p
    nc = cx.nc
    voff, nv = l1_vec_layout()
    T = {}
    T["xT"] = cx.din("xT", [D, TC + HALO])
    T["vecs"] = cx.din("vecs", [128, nv])
    T["wgs"] = [cx.din(f"wg{i}", [D, DFF]) for i in range(3)]
    T["wus"] = [cx.din(f"wu{i}", [D, DFF]) for i in range(3)]
    T["wds"] = [cx.din(f"wd{i}", [DFF, D]) for i in range(3)]
    T["wpw1"] = cx.din("wpw1", [D, 2 * D])
    T["wpw2"] = cx.din("wpw2", [D, D])
    T["win"] = cx.din("win", [D, NIN])
    T["ident"] = cx.din("ident", [128, 128])
    T["prot"] = cx.din("prot", [128, 128])
    T["ropecos"] = cx.din("ropecos", [128, TC])
    T["ropesin"] = cx.din("ropesin", [128, TC])
    T["oh"] = cx.din("oh", [128, 4])
    T["w1"] = cx.din("w1", [2, 2048, 256])
    T["w2kD"] = cx.din("w2kD", [256, 128])
    T["w2vD"] = cx.din("w2vD", [256, 128])
    T["pe"] = cx.din("pe", [2, 128, 16])
    T["IND"] = cx.din("IND", [64, S], BF16)
    T["identb"] = cx.din("identb", [128, 128], BF16)
    T["cmask"] = cx.din("cmask", [128, 5, 512], BF16)
    T["wmask"] = cx.din("wmask", [128, 8, 512], BF16)
    T["AB"] = cx.din("AB", [128, 2, 256])
    T["ov"] = cx.din("ov", [128, 4, 130], BF16)
    T["selG"] = cx.din("selG", [12, 12, 128], BF16)
    T["vecs3"] = cx.din("vecs3", [128, 24])
    T["wout"] = cx.din("wout", [D, D])
    T["wg3"] = cx.din("wg3", [D, DFF])
    T["wu3"] = cx.din("wu3", [D, DFF])
    T["wd3"] = cx.din("wd3", [DFF, D])
    T["xf"] = cx.dout("xf", [D, TC])
    T["xa"] = nc.dram_tensor("xa", [D, TC + HALO], F32).ap()
    for nm in ("xb", "xc", "xd", "xe"):
        T[nm] = nc.dram_tensor(nm, [D, TC], F32).ap()
    EX1 = nc.dram_tensor("ex1", [1, NEL1], BF16).ap()
    EX2 = nc.dram_tensor("ex2", [1, NEL2], BF16).ap()
    CH = 524288
    ch1 = [(k * CH, min(CH, NEL1 - k * CH)) for k in range((NEL1 + CH - 1) // CH)]
    ch2 = [(k * CH, CH) for k in range(NEL2 // CH)]
    GX1L = [nc.dram_tensor(f"gx1_{k}", [4, n_], BF16).ap() for k, (o_, n_) in enumerate(ch1)]
    GX2L = [nc.dram_tensor(f"gx2_{k}", [4, n_], BF16).ap() for k, (o_, n_) in enumerate(ch2)]
    T["EX2"], T["GX1L"], T["GX2L"] = EX2, GX1L, GX2L
    for nm, (o, n, shp) in EX1_FIELDS.items():
        v = EX1[0, o:o + n]
        T[nm] = v.rearrange("(r t) -> r t", t=shp[1]) if len(shp) == 2 else v.rearrange("(k r t) -> k r t", r=shp[1], t=shp[2])

    with cx.st:
        cx.arena_init(51 * 1024)
        cx.ones = cx.carve([128, 128], BF16)
        p.V(lambda e: e.memset(cx.ones[:], 1.0), writes=[cx.B("ones")])
        cx.psb = [cx.ps(f"psb{i}", [128, 512], F32) for i in range(8)]
        cx.sq = [cx.carve([128, 512], BF16) for i in range(2)]
        cx.r32 = [cx.carve([128, 512], F32) for i in range(2)]
        mtop = cx.mark()
        alloc_small(cx)
        phase_A(cx, T)
        cx.release(mtop)
        for k, (o_, n_) in enumerate(ch1):
            p.op("gpsimd", lambda e: e.collective_compute("AllGather", ALU.bypass, RG, [EX1[:, o_:o_ + n_].opt()], [GX1L[k].opt()]),
                 writes=[cx.B("gx1")])
        p.barrier()
        phase_B(cx, T)
        cx.release(mtop)
        for k, (o_, n_) in enumerate(ch2):
            p.op("gpsimd", lambda e: e.collective_compute("AllGather", ALU.bypass, RG, [EX2[:, o_:o_ + n_].opt()], [GX2L[k].opt()]),
                 writes=[cx.B("gx2")])
        p.barrier()
        alloc_small(cx)
        fin = phase_C(cx, T)
        stuck = p.check()
        assert not stuck, stuck
        p.build(final_waits=fin)
    return cx.nc


def kernel(**inputs):
    I = {k: np.asarray(v) for k, v in inputs.items()}
    cores = list(range(8))
    consts = l2_consts()
    w2 = np.asarray(I["nsa_cmp_w2"][0], np.float32)
    shared = dict(consts)
    shared["w1"] = np.ascontiguousarray(I["nsa_cmp_w1"][0], dtype=np.float32)
    shared["w2kD"] = np.ascontiguousarray(np.concatenate([w2[0], w2[0]], 1))
    shared["w2vD"] = np.ascontiguousarray(np.concatenate([w2[1], w2[1]], 1))
    shared["pe"] = np.ascontiguousarray(np.stack([pc(np.asarray(I["nsa_cmp_pos"][0, i]).reshape(-1)) for i in range(2)], 0))
    shared["vecs3"] = np.ascontiguousarray(np.concatenate([pc(I["mix_norm_post"][1]), pc(I["ffn_norm_pre"][1, 1]), pc(I["ffn_norm_post"][1, 1])], axis=1))
    shared["wout"] = np.ascontiguousarray(I["nsa_w_out"][0], dtype=np.float32)
    shared["wg3"] = np.ascontiguousarray(I["ffn_w_gate"][1, 1], dtype=np.float32)
    shared["wu3"] = np.ascontiguousarray(I["ffn_w_up"][1, 1], dtype=np.float32)
    shared["wd3"] = np.ascontiguousarray(I["ffn_w_down"][1, 1], dtype=np.float32)
    maps = []
    for c in cores:
        m = prep_launch1(I, c)
        m.update(shared)
        oh = np.zeros((128, 4), np.float32)
        oh[:, c % 4] = 1.0
        m["oh"] = oh
        maps.append(m)
    res = run_bass_kernel_spmd(build_fused(), maps, core_ids=cores).results
    out = np.zeros((2, S, D), np.float32)
    for c in cores:
        b, j = divmod(c, 4)
        out[b, j * TC:(j + 1) * TC] = np.asarray(res[c]["xf"]).T
    return out
```

```python
import contextlib
import numpy as np
import ml_dtypes
import concourse.bass as bass
import concourse.mybir as mybir
from concourse.bass_utils import run_bass_kernel_spmd

F32 = mybir.dt.float32
BF16 = mybir.dt.bfloat16
AF = mybir.ActivationFunctionType
ALU = mybir.AluOpType

ENGS = ["tensor", "vector", "scalar", "gpsimd", "sync"]

D = 1024
DFF = 2816
NFC = 22
S = 8192
TC = 2048
HALO = 128
CW = 31
RMS_EPS = 1e-6
LN_EPS = 1e-5
NEG = -30000.0
NIN = 2608


class Buf:
    __slots__ = ("name", "last_w", "readers")

    def __init__(self, name=""):
        self.name = name
        self.last_w = None
        self.readers = []


class _Rec:
    def __init__(self):
        self.call = None

    def __getattr__(self, name):
        def f(*a, **k):
            self.call = (name, a, k)
            return None
        return f


class Prog:
    def __init__(self, nc):
        self.nc = nc
        self.ops = {e: [] for e in ENGS}
        self.cnt = {e: 0 for e in ENGS}
        self.seen = {e: {} for e in ENGS}
        self.dcnt = {}
        self.pending = {e: {} for e in ENGS}

    def barrier(self):
        snap = dict(self.cnt)
        snap.update(self.dcnt)
        for e in ENGS:
            for k, v in snap.items():
                if v > 0 and not (e == "tensor" and k == "tensor"):
                    if self.pending[e].get(k, 0) < v:
                        self.pending[e][k] = v

    def op(self, eng, fn, reads=(), writes=(), dma=None, sig=True):
        rec = _Rec()
        fn(rec)
        name_, args_, kw_ = rec.call
        fn = lambda e: getattr(e, name_)(*args_, **kw_)
        waits = dict(self.pending[eng])
        self.pending[eng] = {}

        def need(ev, war=False):
            if ev is None:
                return
            k, v = ev
            if k == eng:
                if eng == "tensor" or war:
                    return
            if waits.get(k, 0) < v:
                waits[k] = v

        for b in reads:
            need(b.last_w)
        for b in writes:
            need(b.last_w)
            for r in b.readers:
                need(r, war=True)
        w = []
        for k, v in waits.items():
            if self.seen[eng].get(k, 0) < v:
                self.seen[eng][k] = v
                w.append((k, v))
        if dma is not None:
            prev = self.dcnt.get(dma, 0)
            if prev > 0 and self.seen[eng].get(dma, 0) < prev:
                self.seen[eng][dma] = prev
                w.append((dma, prev))
            self.dcnt[dma] = self.dcnt.get(dma, 0) + 16
            ev = (dma, self.dcnt[dma])
            inc = (dma, 16)
        elif eng == "tensor" and not sig:
            ev = (eng, self.cnt[eng] + 1)
            inc = None
        else:
            self.cnt[eng] += 1
            ev = (eng, self.cnt[eng])
            inc = (eng, 1)
        self.ops[eng].append((fn, w, inc))
        for b in reads:
            b.readers.append(ev)
            if len(b.readers) > 64:
                best = {}
                for k, v in b.readers:
                    if best.get(k, 0) < v:
                        best[k] = v
                b.readers = list(best.items())
        for b in writes:
            b.last_w = ev
            b.readers = []
        return ev

    def mm(self, out, lhsT, rhs, start, stop, reads=(), writes=(), sig=None, **kw):
        if sig is None:
            sig = stop
        return self.op("tensor", lambda e: e.matmul(out, lhsT, rhs, start=start, stop=stop, **kw),
                       reads=reads, writes=writes, sig=sig)

    def dma(self, eng, out, in_, sem, reads=(), writes=(), **kw):
        return self.op(eng, lambda e: e.dma_start(out=out, in_=in_, **kw), reads=reads, writes=writes, dma=sem)

    def V(self, fn, reads=(), writes=()):
        return self.op("vector", fn, reads, writes)

    def A(self, fn, reads=(), writes=()):
        return self.op("scalar", fn, reads, writes)

    def check(self):
        sem = {}
        pos = {e: 0 for e in ENGS}
        n = {e: len(self.ops[e]) for e in ENGS}
        progress = True
        while progress:
            progress = False
            for e in ENGS:
                while pos[e] < n[e]:
                    fn, w, inc = self.ops[e][pos[e]]
                    if all(sem.get(k, 0) >= v for k, v in w):
                        if inc is not None:
                            sem[inc[0]] = sem.get(inc[0], 0) + inc[1]
                        pos[e] += 1
                        progress = True
                    else:
                        break
        stuck = {e: (pos[e], n[e], [(k, v, sem.get(k, 0)) for k, v in self.ops[e][pos[e]][1]]) for e in ENGS if pos[e] < n[e]}
        return stuck

    def build(self, final_waits=()):
        nc = self.nc
        names = list(ENGS) + sorted(self.dcnt.keys())
        with contextlib.ExitStack() as st:
            sems = {n: st.enter_context(nc.semaphore("s_" + n)) for n in names}
            block = st.enter_context(nc.Block())
            fw = {}
            for ev in final_waits:
                if ev is not None and fw.get(ev[0], 0) < ev[1]:
                    fw[ev[0]] = ev[1]
            for eng in ENGS:
                ops = self.ops[eng]
                if eng == "sync":
                    ops = ops + [(None, list(fw.items()), None)]
                if not ops:
                    continue

                def body(e, ops=ops):
                    for fn, w, inc in ops:
                        for k, v in w:
                            e.wait_ge(sems[k], v)
                        if fn is None:
                            continue
                        ins = fn(e)
                        if inc is not None:
                            ins.then_inc(sems[inc[0]], inc[1])

                getattr(block, eng)(body)


class Ctx:
    def __init__(self, name):
        self.nc = bass.Bass("TRN2", target_bir_lowering=False)
        self.p = Prog(self.nc)
        self.st = contextlib.ExitStack()
        self.bufs = {}
        self.outs = []
        self.rr = {}

    def din(self, name, shape, dt=F32):
        return self.nc.dram_tensor(name, list(shape), dt, kind="ExternalInput").ap()

    def dout(self, name, shape, dt=F32):
        return self.nc.dram_tensor(name, list(shape), dt, kind="ExternalOutput").ap()

    def sb(self, name, shape, dt):
        return self.st.enter_context(self.nc.sbuf_tensor(name, list(shape), dt))

    def ps(self, name, shape, dt=F32):
        return self.st.enter_context(self.nc.psum_tensor(name, list(shape), dt))

    def arena_init(self, nwords):
        self.arena = self.sb("arena", [128, nwords], F32)
        self.top = 0
        self.nwords = nwords

    def carve(self, shape, dt):
        nfree = 1
        for d in shape[1:]:
            nfree *= d
        words = nfree if dt == F32 else (nfree + 1) // 2
        a = self.top
        self.top += words
        assert self.top <= self.nwords, ("arena overflow", self.top, self.nwords)
        ap = self.arena[:, a:a + words]
        if dt != F32:
            ap = ap.bitcast(dt)[:, :nfree]
        if len(shape) == 3:
            ap = ap.rearrange("p (a b) -> p a b", b=shape[2])
        elif len(shape) == 4:
            ap = ap.rearrange("p (a b c) -> p a b c", b=shape[2], c=shape[3])
        if shape[0] < 128:
            ap = ap[0:shape[0]]
        return ap

    def mark(self):
        return self.top

    def release(self, m):
        self.top = m
        self.p.barrier()

    def B(self, *key):
        b = self.bufs.get(key)
        if b is None:
            b = self.bufs[key] = Buf(str(key))
        return b

    def rot(self, key, n):
        i = self.rr.get(key, 0)
        self.rr[key] = (i + 1) % n
        return i


def pc(v):
    v = np.asarray(v, np.float32)
    return np.ascontiguousarray(v.reshape(-1, 128).T)


def setup_common(cx, n_ps=8):
    cx.arena_init(50 * 1024)
    cx.ones = cx.carve([128, 128], BF16)
    cx.p.V(lambda e: e.memset(cx.ones[:], 1.0), writes=[cx.B("ones")])
    cx.psb = [cx.ps(f"psb{i}", [128, 512], F32) for i in range(n_ps)]
    cx.sq = [cx.carve([128, 512], BF16) for i in range(2)]
    cx.r32 = [cx.carve([128, 512], F32) for i in range(2)]


def rms_stats(cx, src, srcbufs, n, psi, eps_scaled, nch=8):
    p = cx.p
    ps = cx.psb[psi]
    for c in range(nch):
        i = cx.rot("sq", 2)
        sq = cx.sq[i]
        p.A(lambda e, c=c, sq=sq: e.activation(sq[:, :n], src(c), AF.Square), reads=srcbufs, writes=[cx.B("sq", i)])
        p.mm(ps[:, :n], cx.ones[:], sq[:, :n], c == 0, c == nch - 1, reads=[cx.B("sq", i), cx.B("ones")],
             writes=[cx.B("psb", psi)], sig=True)
    j = cx.rot("r32", 2)
    r = cx.r32[j]
    p.A(lambda e: e.activation(r[:, :n], ps[:, :n], AF.Sqrt, bias=float(eps_scaled), scale=1.0),
        reads=[cx.B("psb", psi)], writes=[cx.B("r32", j)])
    p.V(lambda e: e.reciprocal(r[:, :n], r[:, :n]), reads=[cx.B("r32", j)], writes=[cx.B("r32", j)])
    return r, cx.B("r32", j)


def load_x_group(cx, xdram, xbufkey, off, n):
    i = cx.rot("xin", 2)
    cx.xin = cx.xins[i]
    cx.xinb = cx.B("xin", i)
    src = xdram.rearrange("(c p) t -> p c t", p=128)[:, :, off:off + n]
    cx.p.dma("sync", cx.xin[:, :, :n], src, f"xin{i}", reads=[cx.B(*xbufkey)], writes=[cx.xinb])


def norm_to_h(cx, hdst, hbuf, g32col, n, psi=6):
    xin, xinb = cx.xin, cx.xinb
    r, rb = rms_stats(cx, lambda c: xin[:, c, :n], [xinb], n, psi, D * RMS_EPS)
    for c in range(8):
        cx.p.V(lambda e, c=c: e.scalar_tensor_tensor(hdst(c), xin[:, c, :n], g32col(c), r[:, :n], ALU.mult, ALU.mult),
               reads=[xinb, rb, cx.B("vecs")], writes=[hbuf])


def norm_residual_store(cx, ysrc, ybufs, xin_dram, xin_key, xout_dram, xout_key, off_in, off_out, n, gcol, psi=6):
    p = cx.p
    r, rb = rms_stats(cx, ysrc, ybufs, n, psi, D * RMS_EPS)
    load_x_group(cx, xin_dram, xin_key, off_in, n)
    xin, xinb = cx.xin, cx.xinb
    for c in range(8):
        i = cx.rot("ntmp", 2)
        t = cx.ntmp[i]
        p.V(lambda e, c=c, t=t: e.scalar_tensor_tensor(t[:, :n], ysrc(c), gcol(c), r[:, :n], ALU.mult, ALU.mult),
            reads=ybufs + [rb, cx.B("vecs")], writes=[cx.B("ntmp", i)])
        p.V(lambda e, c=c, t=t: e.tensor_tensor(xin[:, c, :n], xin[:, c, :n], t[:, :n], ALU.add),
            reads=[cx.B("ntmp", i), xinb], writes=[xinb])
    dst = xout_dram.rearrange("(c p) t -> p c t", p=128)[:, :, off_out:off_out + n]
    return p.dma("sync", dst, xin[:, :, :n], "xout", reads=[xinb], writes=[cx.B(*xout_key)])


def alloc_small(cx):
    cx.xins = [cx.carve([128, 8, 512], F32) for i in range(2)]
    cx.sg = [cx.carve([128, 512], F32) for i in range(2)]
    cx.ntmp = [cx.carve([128, 512], F32) for i in range(2)]


def alloc_ffn(cx, maxtok):
    cx.hT = [cx.carve([128, 8, maxtok], BF16) for i in range(2)]
    cx.aT = cx.carve([128, NFC, maxtok], BF16)
    cx.ytmp = cx.carve([128, 8, maxtok], F32)
    cx.wg = [cx.carve([128, 8, 256], BF16) for i in range(2)]
    cx.wu = [cx.carve([128, 8, 256], BF16) for i in range(2)]
    cx.wd = [cx.carve([128, NFC, 128], BF16) for i in range(2)]


def ffn(cx, tag, xin_dram, xin_key, xout_dram, xout_key, passes, wg_d, wu_d, wd_d, gpre, gpost, out_shift=0):
    p = cx.p
    last = None
    hTs, aT, ytmp, wg, wu, wd = cx.hT, cx.aT, cx.ytmp, cx.wg, cx.wu, cx.wd

    def locs(groups):
        loc, o = [], 0
        for (off, n) in groups:
            loc.append(o)
            o += n
        return loc

    def S1(k):
        groups = passes[k]
        loc = locs(groups)
        hT = hTs[k % 2]
        for gi, (off, n) in enumerate(groups):
            load_x_group(cx, xin_dram, xin_key, off, n)
            lo = loc[gi]
            norm_to_h(cx, lambda c, lo=lo, n=n: hT[:, c, lo:lo + n], cx.B("hT", k % 2), gpre, n)

    def S2(k):
        groups = passes[k]
        loc = locs(groups)
        hT, hb = hTs[k % 2], cx.B("hT", k % 2)
        for fp in range(NFC // 2):
            s = cx.rot("wgu", 2)
            p.dma("gpsimd", wg[s][:], wg_d.rearrange("(c p) f -> p c f", p=128)[:, :, fp * 256:(fp + 1) * 256],
                  f"wg{s}", writes=[cx.B("wg", s)])
            p.dma("gpsimd", wu[s][:], wu_d.rearrange("(c p) f -> p c f", p=128)[:, :, fp * 256:(fp + 1) * 256],
                  f"wu{s}", writes=[cx.B("wu", s)])
            for h in range(2):
                fc = fp * 2 + h
                for gi, (off, n) in enumerate(groups):
                    lo = loc[gi]
                    b = cx.rot("gu", 2)
                    pg, pu = cx.psb[b], cx.psb[2 + b]
                    for c in range(8):
                        p.mm(pg[:, :n], wg[s][:, c, h * 128:(h + 1) * 128], hT[:, c, lo:lo + n], c == 0, c == 7,
                             reads=[cx.B("wg", s), hb], writes=[cx.B("psb", b)])
                    for c in range(8):
                        p.mm(pu[:, :n], wu[s][:, c, h * 128:(h + 1) * 128], hT[:, c, lo:lo + n], c == 0, c == 7,
                             reads=[cx.B("wu", s), hb], writes=[cx.B("psb", 2 + b)])
                    sg = cx.sg[b]
                    p.A(lambda e: e.activation(sg[:, :n], pg[:, :n], AF.Silu),
                        reads=[cx.B("psb", b)], writes=[cx.B("sg", b)])
                    p.V(lambda e: e.tensor_tensor(aT[:, fc, lo:lo + n], pu[:, :n], sg[:, :n], ALU.mult),
                        reads=[cx.B("psb", 2 + b), cx.B("sg", b)], writes=[cx.B("aT")])

    def S3(k):
        groups = passes[k]
        loc = locs(groups)
        for dc in range(8):
            s = cx.rot("wd", 2)
            p.dma("gpsimd", wd[s][:], wd_d.rearrange("(fc p) d -> p fc d", p=128)[:, :, dc * 128:(dc + 1) * 128],
                  f"wd{s}", writes=[cx.B("wd", s)])
            for gi, (off, n) in enumerate(groups):
                lo = loc[gi]
                b = 4 + cx.rot("dn", 2)
                pd = cx.psb[b]
                for fc in range(NFC):
                    p.mm(pd[:, :n], wd[s][:, fc, :], aT[:, fc, lo:lo + n], fc == 0, fc == NFC - 1,
                         reads=[cx.B("wd", s), cx.B("aT")], writes=[cx.B("psb", b)])
                p.A(lambda e: e.activation(ytmp[:, dc, lo:lo + n], pd[:, :n], AF.Copy),
                    reads=[cx.B("psb", b)], writes=[cx.B("ytmp")])

    def S4(k):
        nonlocal last
        groups = passes[k]
        loc = locs(groups)
        for gi, (off, n) in enumerate(groups):
            lo = loc[gi]
            last = norm_residual_store(cx, lambda c, lo=lo, n=n: ytmp[:, c, lo:lo + n], [cx.B("ytmp")],
                                       xin_dram, xin_key, xout_dram, xout_key, off, off - out_shift, n, gpost)

    if passes:
        S1(0)
    for k in range(len(passes)):
        S2(k)
        if k + 1 < len(passes):
            S1(k + 1)
        S3(k)
        S4(k)
    return last


def l1_vec_layout():
    names = [("f00pre", 8), ("f00post", 8), ("m0pre", 8), ("bpw1", 16), ("wdw", 8 * CW), ("bdw", 8), ("lng", 8),
             ("lnb", 8), ("bpw2", 8), ("m0post", 8), ("f01pre", 8), ("f01post", 8), ("f10pre", 8), ("f10post", 8),
             ("m1pre", 8), ("bgate", 1), ("flag", 1)]
    off = {}
    o = 0
    for n, k in names:
        off[n] = (o, k)
        o += k
    return off, o


def phase_A(cx, T):
    p = cx.p
    TT = TC + HALO
    voff, nv = l1_vec_layout()
    xT, vecs_d, wgs, wus, wds = T["xT"], T["vecs"], T["wgs"], T["wus"], T["wds"]
    wpw1_d, wpw2_d, win_d, ident_d, prot_d, cos_d, sin_d = T["wpw1"], T["wpw2"], T["win"], T["ident"], T["prot"], T["ropecos"], T["ropesin"]
    xa, xb, xc, xd = T["xa"], T["xb"], T["xc"], T["xd"]
    qraw_o, qrot_o, kvT_o, vtok_o, gate_o = T["qraw_o"], T["qrot_o"], T["kvT_o"], T["vtok_o"], T["gate_o"]
    if True:
        vecs = cx.carve([128, nv], F32)
        p.dma("sync", vecs[:], vecs_d, "const", writes=[cx.B("vecs")])

        def vcol(name, scale=None):
            o, k = voff[name]
            if scale is not None:
                p.V(lambda e: e.tensor_scalar(vecs[:, o:o + k], vecs[:, o:o + k], float(scale), None, ALU.mult),
                    reads=[cx.B("vecs")], writes=[cx.B("vecs")])
            return lambda c: vecs[:, o + c:o + c + 1]

        g_f00pre = vcol("f00pre", 32.0)
        g_f00post = vcol("f00post", 16.0)
        g_m0pre = vcol("m0pre", 32.0)
        g_m0post = vcol("m0post", 32.0)
        g_f01pre = vcol("f01pre", 32.0)
        g_f01post = vcol("f01post", 16.0)
        g_f10pre = vcol("f10pre", 32.0)
        g_f10post = vcol("f10post", 16.0)
        g_m1pre = vcol("m1pre", 32.0)
        bpw1 = vcol("bpw1")
        bdw = vcol("bdw")
        lng = vcol("lng")
        lnb = vcol("lnb")
        bpw2 = vcol("bpw2")
        wdw_o = voff["wdw"][0]
        bgate_o = voff["bgate"][0]
        flag_o = voff["flag"][0]

        identb = cx.carve([128, 128], BF16)
        p.dma("gpsimd", identb[:], ident_d, "const2", writes=[cx.B("identb")])
        prot = cx.carve([128, 128], BF16)
        p.dma("gpsimd", prot[:], prot_d, "const2", writes=[cx.B("prot")])
        SKIP = False
        m0 = cx.mark()
        alloc_ffn(cx, 1152)
        passes0 = [] if SKIP else [[(0, 128), (128, 512), (640, 512)], [(1152, 512), (1664, 512)]]
        ffn(cx, "f00", xT, ("xT",), xa, ("xa",), passes0, wgs[0], wus[0], wds[0], g_f00pre, g_f00post)
        cx.release(m0)

        wpw1 = cx.carve([128, 8, 2 * D], BF16)
        wpw2 = cx.carve([128, 8, D], BF16)
        for c in range(8):
            p.dma("gpsimd", wpw1[:, c, :], wpw1_d[c * 128:(c + 1) * 128, :], f"wpw{c % 4}", writes=[cx.B("wpw1")])
        for c in range(8):
            p.dma("gpsimd", wpw2[:, c, :], wpw2_d[c * 128:(c + 1) * 128, :], f"wpw{c % 4}", writes=[cx.B("wpw2")])
        glu = cx.carve([128, 8, TT], BF16)
        vbs = [cx.carve([128, 512], BF16) for q in range(2)]
        y2 = cx.carve([128, 8, 512], F32)
        dg = [cx.carve([128, CW, 128], BF16) for i in range(2)]
        hc = cx.carve([128, 8, 512], BF16)
        vt = cx.carve([128, 8, 512], F32)
        sc = cx.carve([128, 8, 512], BF16)
        for (off, n) in ([] if SKIP else [(0, 128), (128, 512), (640, 512), (1152, 512), (1664, 512)]):
            load_x_group(cx, xa, ("xa",), off, n)
            norm_to_h(cx, lambda c, n=n: hc[:, c, :n], cx.B("hT"), g_m0pre, n)
            for oc in range(8):
                b = cx.rot("gu", 2)
                pa, pg = cx.psb[b], cx.psb[2 + b]
                for c in range(8):
                    p.mm(pa[:, :n], wpw1[:, c, oc * 128:(oc + 1) * 128], hc[:, c, :n], c == 0, c == 7,
                         reads=[cx.B("wpw1"), cx.B("hT")], writes=[cx.B("psb", b)])
                for c in range(8):
                    p.mm(pg[:, :n], wpw1[:, c, D + oc * 128:D + (oc + 1) * 128], hc[:, c, :n], c == 0, c == 7,
                         reads=[cx.B("wpw1"), cx.B("hT")], writes=[cx.B("psb", 2 + b)])
                sg = cx.sg[b]
                p.A(lambda e, sg=sg, pg=pg, n=n, oc=oc: e.activation(sg[:, :n], pg[:, :n], AF.Sigmoid, bias=bpw1(8 + oc)),
                    reads=[cx.B("psb", 2 + b), cx.B("vecs")], writes=[cx.B("sg", b)])
                p.V(lambda e, sg=sg, pa=pa, n=n, oc=oc, off=off: e.scalar_tensor_tensor(
                    glu[:, oc, off:off + n], pa[:, :n], bpw1(oc), sg[:, :n], ALU.add, ALU.mult),
                    reads=[cx.B("psb", b), cx.B("sg", b), cx.B("vecs")], writes=[cx.B("glu")])
            if off == 0:
                for oc in range(8):
                    p.V(lambda e, oc=oc: e.tensor_scalar(glu[:, oc, 0:128], glu[:, oc, 0:128], vecs[:, flag_o:flag_o + 1], None, ALU.mult),
                        reads=[cx.B("glu"), cx.B("vecs")], writes=[cx.B("glu")])
        for g in range(0 if SKIP else 4):
            t0 = HALO + g * 512
            n = 512
            for cc in range(8):
                di = cx.rot("dg", 2)
                for k in range(CW):
                    p.V(lambda e, k=k, cc=cc, di=di: e.tensor_scalar(dg[di][:, k, :], identb[:], vecs[:, wdw_o + k * 8 + cc:wdw_o + k * 8 + cc + 1], None, ALU.mult),
                        reads=[cx.B("identb"), cx.B("vecs")], writes=[cx.B("dg", di)])
                b = 4 + cx.rot("dn", 2)
                pd = cx.psb[b]
                for k in range(CW):
                    s0 = t0 - (CW - 1) + k
                    p.mm(pd[:, :n], dg[di][:, k, :], glu[:, cc, s0:s0 + n], k == 0, k == CW - 1,
                         reads=[cx.B("dg", di), cx.B("glu")], writes=[cx.B("psb", b)])
                p.A(lambda e, pd=pd, cc=cc: e.activation(vt[:, cc, :n], pd[:, :n], AF.Identity, bias=bdw(cc)),
                    reads=[cx.B("psb", b), cx.B("vecs")], writes=[cx.B("ytmp")])
            psm, psq = cx.psb[6], cx.psb[7]
            for cc in range(8):
                i = cx.rot("sq", 2)
                sq = cx.sq[i]
                p.A(lambda e, cc=cc, sq=sq: e.activation(sq[:, :n], vt[:, cc, :n], AF.Square), reads=[cx.B("ytmp")], writes=[cx.B("sq", i)])
                p.mm(psq[:, :n], cx.ones[:], sq[:, :n], cc == 0, cc == 7, reads=[cx.B("sq", i), cx.B("ones")], writes=[cx.B("psb", 7)], sig=True)
                j = cx.rot("vb", 2)
                vb = vbs[j]
                p.V(lambda e, cc=cc, vb=vb: e.tensor_copy(vb[:, :n], vt[:, cc, :n]), reads=[cx.B("ytmp")], writes=[cx.B("vb", j)])
                p.mm(psm[:, :n], cx.ones[:], vb[:, :n], cc == 0, cc == 7, reads=[cx.B("vb", j), cx.B("ones")], writes=[cx.B("psb", 6)], sig=True)
            mean, m2 = cx.r32[0], cx.r32[1]
            rs = cx.ntmp[0]
            p.V(lambda e: e.tensor_scalar(mean[:, :n], psm[:, :n], 1.0 / D, None, ALU.mult), reads=[cx.B("psb", 6)], writes=[cx.B("r32", 0)])
            p.V(lambda e: e.tensor_tensor(m2[:, :n], mean[:, :n], mean[:, :n], ALU.mult), reads=[cx.B("r32", 0)], writes=[cx.B("r32", 1)])
            p.V(lambda e: e.scalar_tensor_tensor(rs[:, :n], psq[:, :n], 1.0 / D, m2[:, :n], ALU.mult, ALU.subtract),
                reads=[cx.B("psb", 7), cx.B("r32", 1)], writes=[cx.B("ntmp", 0)])
            p.A(lambda e: e.activation(rs[:, :n], rs[:, :n], AF.Sqrt, bias=float(LN_EPS), scale=1.0), reads=[cx.B("ntmp", 0)], writes=[cx.B("ntmp", 0)])
            p.V(lambda e: e.reciprocal(rs[:, :n], rs[:, :n]), reads=[cx.B("ntmp", 0)], writes=[cx.B("ntmp", 0)])
            for cc in range(8):
                p.V(lambda e, cc=cc: e.tensor_tensor(vt[:, cc, :n], vt[:, cc, :n], mean[:, :n], ALU.subtract),
                    reads=[cx.B("ytmp"), cx.B("r32", 0)], writes=[cx.B("ytmp")])
                p.V(lambda e, cc=cc: e.tensor_tensor(vt[:, cc, :n], vt[:, cc, :n], rs[:, :n], ALU.mult),
                    reads=[cx.B("ytmp"), cx.B("ntmp", 0)], writes=[cx.B("ytmp")])
                p.A(lambda e, cc=cc: e.activation(sc[:, cc, :n], vt[:, cc, :n], AF.Silu, bias=lnb(cc), scale=lng(cc)),
                    reads=[cx.B("ytmp"), cx.B("vecs")], writes=[cx.B("aT")])
            for oc in range(8):
                b = 4 + cx.rot("dn", 2)
                pd = cx.psb[b]
                for c in range(8):
                    p.mm(pd[:, :n], wpw2[:, c, oc * 128:(oc + 1) * 128], sc[:, c, :n], c == 0, c == 7,
                         reads=[cx.B("wpw2"), cx.B("aT")], writes=[cx.B("psb", b)])
                p.A(lambda e, pd=pd, oc=oc: e.activation(y2[:, oc, :n], pd[:, :n], AF.Identity, bias=bpw2(oc)),
                    reads=[cx.B("psb", b), cx.B("vecs")], writes=[cx.B("y2")])
            norm_residual_store(cx, lambda c: y2[:, c, :n], [cx.B("y2")], xa, ("xa",), xb, ("xb",), t0, t0 - HALO, n, g_m0post)

        cx.release(m0)
        alloc_ffn(cx, 1024)
        passes = [] if SKIP else [[(0, 512), (512, 512)], [(1024, 512), (1536, 512)]]
        ffn(cx, "f01", xb, ("xb",), xc, ("xc",), passes, wgs[1], wus[1], wds[1], g_f01pre, g_f01post)
        ffn(cx, "f10", xc, ("xc",), xd, ("xd",), passes, wgs[2], wus[2], wds[2], g_f10pre, g_f10post)

        cx.release(m0)
        hc = cx.carve([128, 8, 512], BF16)
        win = cx.carve([128, 8, NIN], BF16)
        for c in range(8):
            p.dma("gpsimd", win[:, c, :], win_d[c * 128:(c + 1) * 128, :], f"wpw{c % 4}", writes=[cx.B("win")])
        cosT = cx.carve([128, TC], F32)
        sinT = cx.carve([128, TC], F32)
        p.dma("sync", cosT[:], cos_d, "const", writes=[cx.B("cs")])
        p.dma("sync", sinT[:], sin_d, "const", writes=[cx.B("cs")])
        qrawS = [cx.carve([128, 8, 512], BF16) for i in range(2)]
        qrotS = [cx.carve([128, 8, 512], BF16) for i in range(2)]
        kvS = [cx.carve([128, 8, 512], BF16) for i in range(2)]
        vtk = [cx.carve([128, 512], BF16) for i in range(2)]
        gsb = [cx.carve([128, 512], BF16) for i in range(2)]
        vS = [cx.carve([128, 2, 4, 256], BF16) for i in range(2)]
        outs = []
        PARTS = ["fm", "v", "g"]
        for g in range(4):
            t0 = g * 512
            n = 512
            load_x_group(cx, xd, ("xd",), t0, n)
            norm_to_h(cx, lambda c: hc[:, c, :n], cx.B("hT"), g_m1pre, n)
            so = cx.rot("nsao", 2)
            fm = [(oc, "q") for oc in range(8)] + [(8, "kc"), (9, "kc"), (10, "vc"), (11, "vc"), (12, "ks"), (13, "ks"), (16, "kw"), (17, "kw")]
            kvidx = {"kc": 0, "vc": 2, "ks": 4, "kw": 6}
            for (oc, kind) in (fm if "fm" in PARTS else []):
                b = cx.rot("gu", 2)
                pq = cx.psb[b]
                for c in range(8):
                    p.mm(pq[:, :n], win[:, c, oc * 128:(oc + 1) * 128], hc[:, c, :n], c == 0, c == 7,
                         reads=[cx.B("win"), cx.B("hT")], writes=[cx.B("psb", b)])
                if kind == "q":
                    raw, rawb = qrawS[so][:, oc, :], cx.B("qrawS", so)
                    rot_, rotb = qrotS[so][:, oc, :], cx.B("qrotS", so)
                elif kind in ("kc", "vc"):
                    raw, rawb = kvS[so][:, kvidx[kind] + oc % 2, :], cx.B("kvS", so)
                else:
                    i = cx.rot("qb", 2)
                    raw, rawb = vtk[i][:, :], cx.B("vtk", i)
                    rot_, rotb = kvS[so][:, kvidx[kind] + oc % 2, :], cx.B("kvS", so)
                p.A(lambda e, raw=raw, pq=pq: e.activation(raw, pq[:, :n], AF.Copy), reads=[cx.B("psb", b)], writes=[rawb])
                if kind in ("q", "ks", "kw"):
                    b2 = 2 + cx.rot("rp", 2)
                    pr = cx.psb[b2]
                    p.mm(pr[:, :n], prot[:], raw, True, True, reads=[cx.B("prot"), rawb], writes=[cx.B("psb", b2)])
                    ti = cx.rot("ntmp", 2)
                    t1, t1b = cx.ntmp[ti], cx.B("ntmp", ti)
                    si = cx.rot("sgr", 2)
                    t2, t2b = cx.sg[si], cx.B("sg", si)
                    p.V(lambda e, t1=t1, raw=raw: e.tensor_tensor(t1[:, :n], raw, cosT[:, t0:t0 + n], ALU.mult),
                        reads=[rawb, cx.B("cs")], writes=[t1b])
                    p.V(lambda e, t2=t2, pr=pr: e.tensor_tensor(t2[:, :n], pr[:, :n], sinT[:, t0:t0 + n], ALU.mult),
                        reads=[cx.B("psb", b2), cx.B("cs")], writes=[t2b])
                    p.V(lambda e, t1=t1, t2=t2, rot_=rot_: e.tensor_tensor(rot_, t1[:, :n], t2[:, :n], ALU.add),
                        reads=[t1b, t2b], writes=[rotb])
            if "fm" in PARTS:
                outs.append(p.dma("sync", qraw_o.rearrange("(c p) t -> p c t", p=128)[:, :, t0:t0 + n], qrawS[so][:], f"qo{so}a", reads=[cx.B("qrawS", so)]))
                outs.append(p.dma("sync", qrot_o.rearrange("(c p) t -> p c t", p=128)[:, :, t0:t0 + n], qrotS[so][:], f"qo{so}b", reads=[cx.B("qrotS", so)]))
                outs.append(p.dma("sync", kvT_o.rearrange("s (a p) t -> p (s a) t", p=128)[:, :, t0:t0 + n], kvS[so][:], f"qo{so}c", reads=[cx.B("kvS", so)]))
            for vi, c0 in enumerate((1792, 2304) if "v" in PARTS else ()):
                for tt in range(4):
                    b = 4 + cx.rot("dn", 2)
                    pv = cx.psb[b]
                    for c in range(8):
                        p.mm(pv[:, :256], hc[:, c, tt * 128:(tt + 1) * 128], win[:, c, c0:c0 + 256], c == 0, c == 7,
                             reads=[cx.B("win"), cx.B("hT")], writes=[cx.B("psb", b)])
                    p.A(lambda e, pv=pv, vi=vi, tt=tt: e.activation(vS[so][:, vi, tt, :], pv[:, :256], AF.Copy), reads=[cx.B("psb", b)], writes=[cx.B("vS", so)])
            if "v" in PARTS:
                for vi in range(2):
                    outs.append(p.dma("sync", vtok_o[vi, t0:t0 + n, :].rearrange("(tt p) c -> p tt c", p=128), vS[so][:, vi, :, :], f"qo{so}v{vi}", reads=[cx.B("vS", so)]))
            if "g" not in PARTS:
                continue
            b = cx.rot("gu", 2)
            pq = cx.psb[b]
            for c in range(8):
                p.mm(pq[0:48, :n], win[:, c, 2560:2608], hc[:, c, :n], c == 0, c == 7,
                     reads=[cx.B("win"), cx.B("hT")], writes=[cx.B("psb", b)])
            i = cx.rot("gsb", 2)
            p.A(lambda e, i=i, pq=pq: e.activation(gsb[i][0:48, :n], pq[0:48, :n], AF.Sigmoid, bias=vecs[0:48, bgate_o:bgate_o + 1]),
                reads=[cx.B("psb", b), cx.B("vecs")], writes=[cx.B("gsb", i)])
            outs.append(p.dma("sync", gate_o[:, t0:t0 + n], gsb[i][0:48, :n], "gout", reads=[cx.B("gsb", i)]))

        cx.release(m0)
    return outs


def rope_tables_np(pos0, n):
    pos = (pos0 + np.arange(n)).astype(np.float32)
    inv = (np.float32(500000.0) ** (-np.arange(0, 16, 2, dtype=np.float32) / np.float32(16))).astype(np.float32)
    ang = pos[None, :] * inv[:, None]
    c8, s8 = np.cos(ang).astype(np.float32), np.sin(ang).astype(np.float32)
    cos = np.ones((128, n), np.float32)
    sin = np.zeros((128, n), np.float32)
    for hb in (0, 64):
        cos[hb:hb + 8] = c8
        cos[hb + 8:hb + 16] = c8
        sin[hb:hb + 8] = s8
        sin[hb + 8:hb + 16] = s8
    return cos, sin


def prot_np():
    pr = np.zeros((128, 128), np.float32)
    for hb in (0, 64):
        for j in range(8):
            pr[hb + j + 8, hb + j] = -1.0
            pr[hb + j, hb + 8 + j] = 1.0
    return pr


def prep_launch1(I, core):
    b, j = divmod(core, 4)
    t0 = j * TC
    x = I["x"][b]
    xT = np.zeros((D, TC + HALO), np.float32)
    xT[:, HALO:] = x[t0:t0 + TC].T
    if j > 0:
        xT[:, :HALO] = x[t0 - HALO:t0].T
    voff, nv = l1_vec_layout()
    vecs = np.zeros((128, nv), np.float32)

    def put(name, arr):
        o, k = voff[name]
        vecs[:, o:o + k] = arr

    put("f00pre", pc(I["ffn_norm_pre"][0, 0]))
    put("f00post", pc(I["ffn_norm_post"][0, 0]))
    put("m0pre", pc(I["mix_norm_pre"][0]))
    put("bpw1", pc(I["conv_b_pw1"][0]))
    put("wdw", np.concatenate([pc(I["conv_w_dw"][0, k]) for k in range(CW)], axis=1))
    put("bdw", pc(I["conv_b_dw"][0]))
    put("lng", pc(I["conv_ln_g"][0]))
    put("lnb", pc(I["conv_ln_b"][0]))
    put("bpw2", pc(I["conv_b_pw2"][0]))
    put("m0post", pc(I["mix_norm_post"][0]))
    put("f01pre", pc(I["ffn_norm_pre"][0, 1]))
    put("f01post", pc(I["ffn_norm_post"][0, 1]))
    put("f10pre", pc(I["ffn_norm_pre"][1, 0]))
    put("f10post", pc(I["ffn_norm_post"][1, 0]))
    put("m1pre", pc(I["mix_norm_pre"][1]))
    bg = np.zeros((128, 1), np.float32)
    bg[:48, 0] = I["nsa_b_gate"][0]
    put("bgate", bg)
    put("flag", np.full((128, 1), 0.0 if j == 0 else 1.0, np.float32))
    cos, sin = rope_tables_np(t0, TC)
    m = {"xT": xT, "vecs": vecs, "wpw1": I["conv_w_pw1"][0], "wpw2": I["conv_w_pw2"][0], "win": I["nsa_w_in"][0],
         "ident": np.eye(128, dtype=np.float32), "prot": prot_np(), "ropecos": cos, "ropesin": sin}
    for i, (l, h) in enumerate(((0, 0), (0, 1), (1, 0))):
        m[f"wg{i}"] = I["ffn_w_gate"][l, h]
        m[f"wu{i}"] = I["ffn_w_up"][l, h]
        m[f"wd{i}"] = I["ffn_w_down"][l, h]
    return {k: np.ascontiguousarray(v, dtype=np.float32) for k, v in m.items()}


def l2_consts():
    bf = ml_dtypes.bfloat16
    E = np.zeros((128, 64, 128), np.float32)
    for jt in range(64):
        E[2 * jt, jt, 0:64] = 1.0
        E[2 * jt + 1, jt, 64:128] = 1.0
    i = np.arange(128)[:, None]
    j = np.arange(512)[None, :]
    cmask = np.zeros((128, 5, 512), np.float32)
    for d in range(5):
        cmask[:, d, :] = np.where(16 * i + 31 - 512 * d <= j, 0.0, NEG)
    wmask = np.zeros((128, 8, 512), np.float32)
    for oi in range(8):
        k = 128 * (oi - 4) + i
        wmask[:, oi, :] = np.where((k <= j) & (k > j - 512), 0.0, NEG)
    AB = np.zeros((128, 2, 256), np.float32)
    jj = np.arange(128)[:, None]
    m = np.arange(256)[None, :]
    x = (m - 128) - (jj >= 64)
    forced = (x == 0) | (x == -1)
    nonc = x > 0
    AB[:, 0, :] = np.where(forced | nonc, 0.0, 1.0)
    AB[:, 1, :] = np.where(forced, 1e9, np.where(nonc, -1e9, 0.0))
    ov = np.zeros((128, 4, 130), np.float32)
    for ct in range(4):
        for ii in range(128):
            c = 128 * ct + ii
            if c > 510:
                continue
            for n in range(128):
                if 16 * c < 64 * n + 64 and 16 * c + 32 > 64 * n:
                    ov[ii, ct, n] = 1.0
            ov[ii, ct, 128] = 1.0
    selG = np.zeros((12, 12, 128), np.float32)
    for r in range(12):
        selG[r, r, :] = 1.0
    IND = np.zeros((64, S), np.float32)
    for jt in range(64):
        IND[2 * (jt % 32), jt * 128:jt * 128 + 64] = 1.0
        IND[2 * (jt % 32) + 1, jt * 128 + 64:jt * 128 + 128] = 1.0
    return {"IND": IND.astype(bf), "identb": np.eye(128, dtype=np.float32).astype(bf), "cmask": cmask.astype(bf),
            "wmask": wmask.astype(bf), "AB": AB, "ov": ov.astype(bf), "selG": selG.astype(bf)}


NQG = 16
SCALE = 0.125


def phase_B(cx, T):
    p = cx.p
    oh_d = T["oh"]
    w1_d, w2k_d, w2v_d, pe_d = T["w1"], T["w2kD"], T["w2vD"], T["pe"]
    EX2 = T["EX2"]

    GXL = T["GX1L"]
    oT4 = EX2[0, :].rearrange("(j r t) -> j r t", j=4, t=TC)

    def gq(j, name, g_):
        base = 0 if name == "qraw_o" else 4
        return GXL[base + g_][j, :].rearrange("(r t) -> r t", t=TC)

    def gkv(j, kind):
        return GXL[8 + kind][j, :].rearrange("(r t) -> r t", t=TC)

    def gv(j, vi):
        return GXL[12 + vi][j, :].rearrange("(t c) -> t c", c=256)

    def ggate(j):
        return GXL[14][j, :].rearrange("(r t) -> r t", t=TC)

    if True:
        pst = cx.psb[7][:].bitcast(BF16)
        ones = cx.ones
        oh = cx.carve([128, 4], F32)
        p.dma("sync", oh[:], oh_d, "c_oh", writes=[cx.B("oh")])
        identb = cx.carve([128, 128], BF16)
        cmask = cx.carve([128, 5, 512], BF16)
        wmask = cx.carve([128, 8, 512], BF16)
        AB = cx.carve([128, 2, 256], F32)
        ov = cx.carve([128, 4, 130], BF16)
        selG = cx.carve([12, 12, 128], BF16)
        for dst, nm in ((identb, "identb"), (cmask, "cmask"), (wmask, "wmask"), (AB, "AB"), (ov, "ov"), (selG, "selG")):
            p.dma("sync", dst[:], T[nm], f"c_k{nm}", writes=[cx.B(nm)])
        kselD = cx.carve([128, S], BF16)
        kwinD = cx.carve([128, S], BF16)
        KX1 = cx.carve([128, S], BF16)
        vA = {nm: cx.carve([128, 64, 192], BF16) for nm in ("sel", "win")}
        for t_ in vA.values():
            p.V(lambda e: e.memset(t_[:], 1.0), writes=[cx.B("vA")])
        kcmpT = cx.carve([128, 512], BF16)
        vcmp = cx.carve([128, 4, 128], BF16)
        p.V(lambda e: e.memset(kcmpT[:], 0.0), writes=[cx.B("kcmpT")])
        p.V(lambda e: e.memset(vcmp[:], 0.0), writes=[cx.B("vcmp")])

        def select4(dst, stage, n_part, stagebuf, dstbuf):
            ps_ = slice(0, n_part)
            p.V(lambda e: e.tensor_scalar(dst, stage(0), oh[ps_, 0:1], None, ALU.mult), reads=[stagebuf, cx.B("oh")], writes=[dstbuf])
            for g_ in range(1, 4):
                p.V(lambda e: e.scalar_tensor_tensor(dst, stage(g_), oh[ps_, g_:g_ + 1], dst, ALU.mult, ALU.add),
                    reads=[stagebuf, cx.B("oh"), dstbuf], writes=[dstbuf])

        m0 = cx.mark()
        stgs = [cx.carve([128, 4, 2048], BF16) for i in range(2)]
        kv2 = cx.carve([128, S], BF16)
        w1 = cx.carve([128, 16, 256], BF16)
        w2 = cx.carve([128, 2, 128], BF16)
        pe = cx.carve([128, 16], BF16)
        hid = cx.carve([128, 2, 512], BF16)
        bias = cx.carve([128, 2], F32)
        vstg = [cx.carve([128, 16, 256], BF16) for i in range(2)]

        def load_sel_kv(kind, dstT, nm, shifted):
            for j in range(4):
                si_ = cx.rot("stg", 2)
                stg, stgb = stgs[si_], cx.B("stg", si_)
                for g_ in range(4):
                    src = gkv(j, kind)[g_ * 64:(g_ + 1) * 64, :]
                    p.dma("sync", stg[0:64, g_, :], src, f"sg{g_}", writes=[stgb])
                    if not shifted:
                        p.dma("sync", stg[64:128, g_, :], src, f"sh{g_}", writes=[stgb])
                    else:
                        p.dma("sync", stg[64:128, g_, 0:2047], src[:, 1:2048], f"sh{g_}", writes=[stgb])
                        if j < 3:
                            p.dma("sync", stg[64:128, g_, 2047:2048], gkv(j + 1, kind)[g_ * 64:(g_ + 1) * 64, 0:1], f"sh{g_}", writes=[stgb], allow_slow_non_contiguous=True)
                        else:
                            p.V(lambda e: e.memset(stg[64:128, g_, 2047:2048], 0.0), writes=[stgb])
                select4(dstT[:, j * 2048:(j + 1) * 2048], lambda g_: stg[:, g_, :], 128, stgb, cx.B(nm))

        load_sel_kv(2, kselD, "kselD", False)
        p.op("gpsimd", lambda e: e.tensor_copy(KX1[64:128, :], kselD[64:128, :]), reads=[cx.B("kselD")], writes=[cx.B("KX1")])
        p.dma("sync", kselD[64:128, :], T["IND"], "c_ind0", reads=[], writes=[cx.B("kselD")])
        p.dma("sync", KX1[0:64, :], T["IND"], "c_ind1", writes=[cx.B("KX1")])
        load_sel_kv(3, kwinD, "kwinD", False)
        for vi, nm in enumerate(("sel", "win")):
            for j in range(4):
                vs_ = cx.rot("vstg", 2)
                p.dma("sync", vstg[vs_][:, :, :], gv(j, vi).rearrange("(t p) c -> p t c", p=128), f"sv{vs_}", writes=[cx.B("vstg", vs_)])
                select4(vA[nm][:, j * 16:(j + 1) * 16, 64:128], lambda g_: vstg[vs_][:, :, g_ * 64:(g_ + 1) * 64], 128, cx.B("vstg", vs_), cx.B("vA"))

        for which, w2_d in enumerate((w2k_d, w2v_d)):
            load_sel_kv(which, kv2, "kv2", True)
            p.dma("gpsimd", w1[:], w1_d[which].rearrange("(c p) j -> p c j", p=128), "c_w1", writes=[cx.B("w1")])
            p.dma("gpsimd", w2[:], w2_d.rearrange("(c p) j -> p c j", p=128), "c_w2", writes=[cx.B("w2")])
            p.dma("gpsimd", pe[:], pe_d[which], "c_pe", writes=[cx.B("pe")])
            for jc in range(2):
                pb = cx.psb[2]
                for ch in range(16):
                    p.mm(pb[:, 0:1], w1[:, ch, jc * 128:(jc + 1) * 128], pe[:, ch:ch + 1], ch == 0, ch == 15,
                         reads=[cx.B("w1"), cx.B("pe")], writes=[cx.B("psb", 2)])
                p.V(lambda e: e.tensor_copy(bias[:, jc:jc + 1], pb[:, 0:1]), reads=[cx.B("psb", 2)], writes=[cx.B("bias")])
                ph = cx.psb[jc]
                for lp in range(16):
                    p.mm(ph[:, 0:511], w1[:, lp, jc * 128:(jc + 1) * 128], kv2[:, 2 * lp:2 * lp + 16 * 510 + 1:16], lp == 0, lp == 15,
                         reads=[cx.B("w1"), cx.B("kv2")], writes=[cx.B("psb", jc)])
                p.A(lambda e: e.activation(hid[:, jc, 0:511], ph[:, 0:511], AF.Silu, bias=bias[:, jc:jc + 1]),
                    reads=[cx.B("psb", jc), cx.B("bias")], writes=[cx.B("hid")])
            if which == 0:
                pk = cx.psb[3]
                for jc in range(2):
                    p.mm(pk[:, 0:511], w2[:, jc, :], hid[:, jc, 0:511], jc == 0, jc == 1, reads=[cx.B("w2"), cx.B("hid")], writes=[cx.B("psb", 3)])
                p.A(lambda e: e.activation(kcmpT[:, 0:511], pk[:, 0:511], AF.Copy), reads=[cx.B("psb", 3)], writes=[cx.B("kcmpT")])
            else:
                for ct in range(4):
                    M = 128 if ct < 3 else 127
                    pv = cx.psb[3 + (ct % 2)]
                    for jc in range(2):
                        p.mm(pv[0:M, 0:128], hid[:, jc, ct * 128:ct * 128 + M], w2[:, jc, :], jc == 0, jc == 1,
                             reads=[cx.B("w2"), cx.B("hid")], writes=[cx.B("psb", 3 + (ct % 2))])
                    p.A(lambda e: e.activation(vcmp[0:M, ct, :], pv[0:M, 0:128], AF.Copy),
                        reads=[cx.B("psb", 3 + (ct % 2))], writes=[cx.B("vcmp")])
        cx.release(m0)

        qstg = cx.carve([128, 4, 2, 512], BF16)
        gstg = cx.carve([12, 4, 512], BF16)
        qraw = [cx.carve([128, 2, 512], BF16) for i in range(2)]
        qrot = [cx.carve([128, 2, 512], BF16) for i in range(2)]
        gts = [cx.carve([12, 512], BF16) for i in range(2)]
        eT = [cx.carve([128, 4, 512], BF16) for i in range(2)]
        pT = [cx.carve([128, 512], BF16) for i in range(5)]
        rz = [cx.carve([128, 512], F32) for i in range(2)]
        wv = [cx.carve([128, 512], F32) for i in range(2)]
        tmp = [cx.carve([128, 512], F32) for i in range(2)]
        acc = cx.carve([128, 2, 512], F32)
        accb = [cx.carve([128, 2, 512], BF16) for i in range(2)]
        impacc = cx.carve([128, 4, 128], F32)
        imod = cx.carve([128, 128], F32)
        scr = cx.carve([128, 128], F32)
        m8 = cx.carve([128, 16], F32)
        rzc = cx.carve([128, 1], F32)
        negm = cx.carve([128, 128], BF16)
        negT = cx.carve([128, 512], BF16)
        Xt = {(h_, a_, w_): cx.carve([128, 512], BF16) for h_ in range(2) for a_ in range(2) for w_ in range(2)}
        outs = []

        def finish_branch(r, gi, pacc, paccbuf, zrows, first, gt, gtb):
            a, half = divmod(r, 2)
            hs = slice(64 * half, 64 * half + 64)
            pG = cx.psb[6]
            p.mm(pG[:, :], selG[:, 3 * r + gi, :], gt[:, :], True, True, reads=[cx.B("selG"), gtb], writes=[cx.B("psb", 6)])
            i = cx.rot("rz", 2)
            p.V(lambda e: e.tensor_scalar(rz[i][zrows, :], pacc[zrows, :], 1e-30, None, ALU.max), reads=[paccbuf], writes=[cx.B("rz", i)])
            p.V(lambda e: e.reciprocal(rz[i][zrows, :], rz[i][zrows, :]), reads=[cx.B("rz", i)], writes=[cx.B("rz", i)])
            p.V(lambda e: e.tensor_tensor(wv[i][zrows, :], rz[i][zrows, :], pG[zrows, :], ALU.mult),
                reads=[cx.B("rz", i), cx.B("psb", 6)], writes=[cx.B("wv", i)])
            if first:
                p.V(lambda e: e.tensor_tensor(acc[hs, a, :], pacc[hs, :], wv[i][zrows, :], ALU.mult),
                    reads=[paccbuf, cx.B("wv", i)], writes=[cx.B("acc")])
            else:
                p.V(lambda e: e.tensor_tensor(tmp[i][hs, :], pacc[hs, :], wv[i][zrows, :], ALU.mult),
                    reads=[paccbuf, cx.B("wv", i)], writes=[cx.B("tmp", i)])
                p.V(lambda e: e.tensor_tensor(acc[hs, a, :], acc[hs, a, :], tmp[i][hs, :], ALU.add),
                    reads=[cx.B("acc"), cx.B("tmp", i)], writes=[cx.B("acc")])

        for qg in range(NQG):
            q0 = qg * 512
            s = cx.rot("qld", 2)
            jq, tl = divmod(qg, 4)
            tl *= 512
            for nmq, dstq, bq in (("qraw_o", qraw[s], cx.B("qraw", s)), ("qrot_o", qrot[s], cx.B("qrot", s))):
                for g_ in range(4):
                    p.dma("sync", qstg[:, g_, :, :], gq(jq, nmq, g_)[:, tl:tl + 512].rearrange("(a p) t -> p a t", p=128),
                          f"qs{g_}", writes=[cx.B("qstg")])
                select4(dstq[:], lambda g_: qstg[:, g_, :, :], 128, cx.B("qstg"), bq)
            for g_ in range(4):
                p.dma("sync", gstg[:, g_, :], ggate(jq)[g_ * 12:(g_ + 1) * 12, tl:tl + 512], f"qs{g_}", writes=[cx.B("gstg")])
            select4(gts[s][:], lambda g_: gstg[:, g_, :], 12, cx.B("gstg"), cx.B("gts", s))
            gt, gtb = gts[s], cx.B("gts", s)
            nct = (32 * qg + 30) // 128 + 1
            for r in range(4):
                a, half = divmod(r, 2)
                hs = slice(64 * half, 64 * half + 64)
                es = cx.rot("eT", 2)
                for ct in range(nct):
                    d = qg - 4 * ct
                    b = cx.rot("S", 2)
                    ps_ = cx.psb[b]
                    p.mm(ps_[:, :], kcmpT[hs, ct * 128:(ct + 1) * 128], qraw[s][hs, a, :], True, d >= 5,
                         reads=[cx.B("kcmpT"), cx.B("qraw", s)], writes=[cx.B("psb", b)])
                    if d < 5:
                        p.mm(ps_[:, :], identb[:], cmask[:, d, :], False, True, reads=[cx.B("identb"), cx.B("cmask")], writes=[cx.B("psb", b)])
                    p.A(lambda e, ps_=ps_, es=es, ct=ct: e.activation(eT[es][:, ct, :], ps_[:, :], AF.Exp, scale=SCALE),
                        reads=[cx.B("psb", b)], writes=[cx.B("eT", es)])
                pO, pZ = cx.psb[2], cx.psb[3]
                for ct in range(nct):
                    p.mm(pO[:, :], vcmp[:, ct, :], eT[es][:, ct, :], ct == 0, ct == nct - 1, reads=[cx.B("vcmp"), cx.B("eT", es)], writes=[cx.B("psb", 2)])
                for ct in range(nct):
                    p.mm(pZ[:, :], ones[:], eT[es][:, ct, :], ct == 0, ct == nct - 1, reads=[cx.B("ones"), cx.B("eT", es)], writes=[cx.B("psb", 3)])
                zrows = slice(64 * (1 - half), 64 * (1 - half) + 64)
                pG = cx.psb[6]
                p.mm(pG[:, :], selG[:, 3 * r + 0, :], gt[:, :], True, True, reads=[cx.B("selG"), gtb], writes=[cx.B("psb", 6)])
                i = cx.rot("rz", 2)
                p.V(lambda e, i=i: e.tensor_scalar(rz[i][hs, :], pZ[hs, :], 1e-30, None, ALU.max), reads=[cx.B("psb", 3)], writes=[cx.B("rz", i)])
                p.V(lambda e, i=i: e.reciprocal(rz[i][hs, :], rz[i][hs, :]), reads=[cx.B("rz", i)], writes=[cx.B("rz", i)])
                p.V(lambda e, i=i: e.tensor_tensor(wv[i][hs, :], rz[i][hs, :], pG[hs, :], ALU.mult),
                    reads=[cx.B("rz", i), cx.B("psb", 6)], writes=[cx.B("wv", i)])
                p.V(lambda e, i=i, a=a: e.tensor_tensor(acc[hs, a, :], pO[hs, :], wv[i][hs, :], ALU.mult),
                    reads=[cx.B("psb", 2), cx.B("wv", i)], writes=[cx.B("acc")])
                for qt in range(4):
                    bi = 4 + cx.rot("I", 2)
                    pI = cx.psb[bi]
                    for ct in range(nct):
                        p.mm(pI[:, 0:129], eT[es][:, ct, qt * 128:(qt + 1) * 128], ov[:, ct, 0:129], ct == 0, ct == nct - 1,
                             reads=[cx.B("eT", es), cx.B("ov")], writes=[cx.B("psb", bi)])
                    p.V(lambda e, pI=pI: e.tensor_scalar(rzc[:], pI[:, 128:129], 1e-30, None, ALU.max), reads=[cx.B("psb", bi)], writes=[cx.B("rzc")])
                    p.V(lambda e: e.reciprocal(rzc[:], rzc[:]), reads=[cx.B("rzc")], writes=[cx.B("rzc")])
                    if r == 0:
                        p.V(lambda e, pI=pI, qt=qt: e.tensor_scalar(impacc[:, qt, :], pI[:, 0:128], rzc[:, 0:1], None, ALU.mult),
                            reads=[cx.B("psb", bi), cx.B("rzc")], writes=[cx.B("impacc")])
                    else:
                        p.V(lambda e, pI=pI, qt=qt: e.scalar_tensor_tensor(impacc[:, qt, :], pI[:, 0:128], rzc[:, 0:1], impacc[:, qt, :], ALU.mult, ALU.add),
                            reads=[cx.B("psb", bi), cx.B("rzc"), cx.B("impacc")], writes=[cx.B("impacc")])
            for qt in range(4):
                ti = 4 * qg + qt
                c0 = 128 - 2 * ti
                p.V(lambda e, qt=qt, c0=c0: e.tensor_tensor(imod[:], impacc[:, qt, :], AB[:, 0, c0:c0 + 128], ALU.mult),
                    reads=[cx.B("impacc"), cx.B("AB")], writes=[cx.B("imod")])
                p.V(lambda e, c0=c0: e.tensor_tensor(imod[:], imod[:], AB[:, 1, c0:c0 + 128], ALU.add), reads=[cx.B("imod"), cx.B("AB")], writes=[cx.B("imod")])
                p.V(lambda e: e.memset(imod[:, 0:1], 1e9), reads=[], writes=[cx.B("imod")])
                p.V(lambda e: e.max(m8[:, 0:8], imod[:]), reads=[cx.B("imod")], writes=[cx.B("m8")])
                p.V(lambda e: e.match_replace(scr[:], m8[:, 0:8], imod[:], -1e30), reads=[cx.B("imod"), cx.B("m8")], writes=[cx.B("scr")])
                p.V(lambda e: e.max(m8[:, 8:16], scr[:]), reads=[cx.B("scr")], writes=[cx.B("m8")])
                p.V(lambda e: e.tensor_scalar(negm[:], imod[:], m8[:, 15:16], NEG, ALU.is_lt, ALU.mult), reads=[cx.B("imod"), cx.B("m8")], writes=[cx.B("negm")])
                p.op("tensor", lambda e, qt=qt: e.transpose(pst[:, qt * 128:(qt + 1) * 128], negm[:], identb[:]),
                     reads=[cx.B("negm"), cx.B("identb")], writes=[cx.B("psb", 7)])
                p.A(lambda e, qt=qt: e.activation(negT[:, qt * 128:(qt + 1) * 128], pst[:, qt * 128:(qt + 1) * 128], AF.Copy),
                    reads=[cx.B("psb", 7)], writes=[cx.B("negT")])
            nwin = 1 if 4 * qg + 3 < 32 else 2
            for half in range(2):
                hs_ = slice(64 * half, 64 * half + 64)
                os_ = slice(64 * (1 - half), 64 * (1 - half) + 64)
                for a_ in range(2):
                    for w_ in range(nwin):
                        xt = Xt[(half, a_, w_)]
                        xb_ = cx.B("Xt", half, a_, w_)
                        p.op("gpsimd", lambda e: e.tensor_copy(xt[hs_, :], qrot[s][hs_, a_, :]), reads=[cx.B("qrot", s)], writes=[xb_])
                        p.V(lambda e: e.tensor_copy(xt[os_, :], negT[64 * w_:64 * w_ + 64, :]), reads=[cx.B("negT")], writes=[xb_])
            for (br, gi, kD, kbuf, jts) in (("sel", 1, kselD, "kselD", list(range(4 * qg + 4))),
                                            ("win", 2, kwinD, "kwinD", list(range(max(0, 4 * qg - 4), 4 * qg + 4)))):

                def cols(jt):
                    o = jt - 4 * qg
                    c0 = max(0, 128 * o)
                    c1 = min(512, 128 * (o + 5)) if br == "win" else 512
                    return c0, c1

                def emit_S(u):
                    ji, jt, r = u
                    o = jt - 4 * qg
                    c0, c1 = cols(jt)
                    a, half = divmod(r, 2)
                    hs = slice(64 * half, 64 * half + 64)
                    b = (0, 1, 6, 7)[cx.rot("S3", 4)]
                    ps_ = cx.psb[b]
                    need_mask = (br == "win") or (o >= 0)
                    if br == "sel":
                        kx, kxb = (kselD, cx.B("kselD")) if half == 0 else (KX1, cx.B("KX1"))
                        w_ = jt // 32
                        p.mm(ps_[:, c0:c1], kx[:, jt * 128:(jt + 1) * 128], Xt[(half, a, w_)][:, c0:c1], True, not need_mask,
                             reads=[kxb, cx.B("Xt", half, a, w_)], writes=[cx.B("psb", b)])
                    else:
                        p.mm(ps_[:, c0:c1], kD[hs, jt * 128:(jt + 1) * 128], qrot[s][hs, a, c0:c1], True, False,
                             reads=[cx.B(kbuf), cx.B("qrot", s)], writes=[cx.B("psb", b)], sig=False)
                    if need_mask:
                        p.mm(ps_[:, c0:c1], identb[:], wmask[:, o + 4, c0:c1], False, True, reads=[cx.B("identb"), cx.B("wmask")], writes=[cx.B("psb", b)])
                    pi = cx.rot("pT", 5)
                    p.A(lambda e: e.activation(pT[pi][:, c0:c1], ps_[:, c0:c1], AF.Exp, scale=SCALE),
                        reads=[cx.B("psb", b)], writes=[cx.B("pT", pi)])
                    return pi

                def emit_PV(u, pi):
                    ji, jt, r = u
                    c0, c1 = cols(jt)
                    half = r % 2
                    pa = cx.psb[2 + r]
                    p.mm(pa[:, c0:c1], vA[br][:, jt, (64 if half == 0 else 0):(192 if half == 0 else 128)], pT[pi][:, c0:c1], ji == 0, ji == len(jts) - 1,
                         reads=[cx.B("vA"), cx.B("pT", pi)], writes=[cx.B("psb", 2 + r)], sig=True)

                jts = sorted(jts, key=lambda jt_: 0 if cols(jt_) == (0, 512) else 1)
                assert cols(jts[0]) == (0, 512)
                units = [(ji, jt, r) for ji, jt in enumerate(jts) for r in range(4)]
                pend = []
                for u in units:
                    pi = emit_S(u)
                    pend.append((u, pi))
                    if len(pend) > 3:
                        emit_PV(*pend.pop(0))
                for pu in pend:
                    emit_PV(*pu)
                for r in range(4):
                    half = r % 2
                    zrows = slice(64 * (1 - half), 64 * (1 - half) + 64)
                    finish_branch(r, gi, cx.psb[2 + r], cx.B("psb", 2 + r), zrows, False, gt, gtb)
            ob = cx.rot("accb", 2)
            p.V(lambda e, ob=ob: e.tensor_copy(accb[ob][:], acc[:]), reads=[cx.B("acc")], writes=[cx.B("accb", ob)])
            outs.append(p.dma("sync", oT4[jq].rearrange("(a p) t -> p a t", p=128)[:, :, tl:tl + 512], accb[ob][:], f"o{ob}", reads=[cx.B("accb", ob)],
                              writes=[cx.B("ex2", jq, ob)]))
            if qg % 4 == 3:
                T["ag2"](jq, [cx.B("ex2", jq, 0), cx.B("ex2", jq, 1)])
    return outs


def phase_C(cx, T):
    p = cx.p
    xd_d, vecs_d, wout_d, wg_d, wu_d, wd_d, xe, xf, oh_d = (T["xd"], T["vecs3"], T["wout"], T["wg3"], T["wu3"],
                                                              T["wd3"], T["xe"], T["xf"], T["oh"])
    GX2L = T["GX2L"]
    if True:
        vecs = cx.carve([128, 24], F32)
        p.dma("sync", vecs[:], vecs_d, "const", writes=[cx.B("vecs")])
        oh = cx.carve([128, 4], F32)
        p.dma("sync", oh[:], oh_d, "c_oh", writes=[cx.B("oh")])
        for o, sc_ in ((0, 32.0), (8, 32.0), (16, 16.0)):
            p.V(lambda e: e.tensor_scalar(vecs[:, o:o + 8], vecs[:, o:o + 8], sc_, None, ALU.mult), reads=[cx.B("vecs")], writes=[cx.B("vecs")])
        g_m1post = lambda c: vecs[:, c:c + 1]
        g_pre = lambda c: vecs[:, 8 + c:9 + c]
        g_post = lambda c: vecs[:, 16 + c:17 + c]
        m0 = cx.mark()
        wout = cx.carve([128, 8, D], BF16)
        for c in range(8):
            p.dma("gpsimd", wout[:, c, :], wout_d[c * 128:(c + 1) * 128, :], f"wpw{c % 4}", writes=[cx.B("wout")])
        astg = cx.carve([128, 4, 8, 512], BF16)
        at = [cx.carve([128, 8, 512], BF16) for i in range(2)]
        y2 = cx.carve([128, 8, 512], F32)
        for g in range(4):
            t0 = g * 512
            n = 512
            s = cx.rot("at", 2)
            for jj in range(4):
                p.dma("sync", astg[:, jj, :, :], GX2L[jj].rearrange("g (r t) -> (g r) t", t=TC)[:, t0:t0 + n].rearrange("(c p) t -> p c t", p=128),
                      f"as{jj}", writes=[cx.B("astg")])
            p.V(lambda e: e.tensor_scalar(at[s][:], astg[:, 0, :, :], oh[:, 0:1], None, ALU.mult), reads=[cx.B("astg"), cx.B("oh")], writes=[cx.B("at", s)])
            for jj in range(1, 4):
                p.V(lambda e: e.scalar_tensor_tensor(at[s][:], astg[:, jj, :, :], oh[:, jj:jj + 1], at[s][:], ALU.mult, ALU.add),
                    reads=[cx.B("astg"), cx.B("oh"), cx.B("at", s)], writes=[cx.B("at", s)])
            for oc in range(8):
                b = 4 + cx.rot("dn", 2)
                pd = cx.psb[b]
                for c in range(8):
                    p.mm(pd[:, :n], wout[:, c, oc * 128:(oc + 1) * 128], at[s][:, c, :], c == 0, c == 7,
                         reads=[cx.B("wout"), cx.B("at", s)], writes=[cx.B("psb", b)])
                p.A(lambda e: e.activation(y2[:, oc, :n], pd[:, :n], AF.Copy), reads=[cx.B("psb", b)], writes=[cx.B("y2")])
            norm_residual_store(cx, lambda c: y2[:, c, :n], [cx.B("y2")], xd_d, ("xd",), xe, ("xe",), t0, t0, n, g_m1post)
        cx.release(m0)
        alloc_ffn(cx, 1024)
        passes = [[(0, 512), (512, 512)], [(1024, 512), (1536, 512)]]
        ffn(cx, "f11", xe, ("xe",), xf, ("xf",), passes, wg_d, wu_d, wd_d, g_pre, g_post)
    return [cx.B("xf").last_w]


EX1_FIELDS = {"qraw_o": (0, 2097152, (1024, 2048)), "qrot_o": (2097152, 2097152, (1024, 2048)),
              "kvT_o": (4194304, 2097152, (4, 256, 2048)), "vtok_o": (6291456, 1048576, (2, 2048, 256)),
              "gate_o": (7340032, 98304, (48, 2048))}
NEL1 = 7438336
NEL2 = 256 * S
RG = [[0, 1, 2, 3], [4, 5, 6, 7]]


def build_fused():
    cx = Ctx("fused")
    p = cx.p
    nc = cx.nc
    voff, nv = l1_vec_layout()
    T = {}
    T["xT"] = cx.din("xT", [D, TC + HALO])
    T["vecs"] = cx.din("vecs", [128, nv])
    T["wgs"] = [cx.din(f"wg{i}", [D, DFF]) for i in range(3)]
    T["wus"] = [cx.din(f"wu{i}", [D, DFF]) for i in range(3)]
    T["wds"] = [cx.din(f"wd{i}", [DFF, D]) for i in range(3)]
    T["wpw1"] = cx.din("wpw1", [D, 2 * D])
    T["wpw2"] = cx.din("wpw2", [D, D])
    T["win"] = cx.din("win", [D, NIN])
    T["ident"] = cx.din("ident", [128, 128])
    T["prot"] = cx.din("prot", [128, 128])
    T["ropecos"] = cx.din("ropecos", [128, TC])
    T["ropesin"] = cx.din("ropesin", [128, TC])
    T["oh"] = cx.din("oh", [128, 4])
    T["w1"] = cx.din("w1", [2, 2048, 256])
    T["w2kD"] = cx.din("w2kD", [256, 128])
    T["w2vD"] = cx.din("w2vD", [256, 128])
    T["pe"] = cx.din("pe", [2, 128, 16])
    T["IND"] = cx.din("IND", [64, S], BF16)
    T["identb"] = cx.din("identb", [128, 128], BF16)
    T["cmask"] = cx.din("cmask", [128, 5, 512], BF16)
    T["wmask"] = cx.din("wmask", [128, 8, 512], BF16)
    T["AB"] = cx.din("AB", [128, 2, 256])
    T["ov"] = cx.din("ov", [128, 4, 130], BF16)
    T["selG"] = cx.din("selG", [12, 12, 128], BF16)
    T["vecs3"] = cx.din("vecs3", [128, 24])
    T["wout"] = cx.din("wout", [D, D])
    T["wg3"] = cx.din("wg3", [D, DFF])
    T["wu3"] = cx.din("wu3", [D, DFF])
    T["wd3"] = cx.din("wd3", [DFF, D])
    T["xf"] = cx.dout("xf", [D, TC])
    T["xa"] = nc.dram_tensor("xa", [D, TC + HALO], F32).ap()
    for nm in ("xb", "xc", "xd", "xe"):
        T[nm] = nc.dram_tensor(nm, [D, TC], F32).ap()
    EX1 = nc.dram_tensor("ex1", [1, NEL1], BF16).ap()
    EX2 = nc.dram_tensor("ex2", [1, NEL2], BF16).ap()
    CH = 524288
    ch1 = [(k * CH, min(CH, NEL1 - k * CH)) for k in range((NEL1 + CH - 1) // CH)]
    ch2 = [(k * CH, CH) for k in range(NEL2 // CH)]
    GX1L = [nc.dram_tensor(f"gx1_{k}", [4, n_], BF16).ap() for k, (o_, n_) in enumerate(ch1)]
    GX2L = [nc.dram_tensor(f"gx2_{k}", [4, n_], BF16).ap() for k, (o_, n_) in enumerate(ch2)]
    T["EX2"], T["GX1L"], T["GX2L"] = EX2, GX1L, GX2L
    for nm, (o, n, shp) in EX1_FIELDS.items():
        v = EX1[0, o:o + n]
        T[nm] = v.rearrange("(r t) -> r t", t=shp[1]) if len(shp) == 2 else v.rearrange("(k r t) -> k r t", r=shp[1], t=shp[2])

    with cx.st:
        cx.arena_init(51 * 1024)
        cx.ones = cx.carve([128, 128], BF16)
        p.V(lambda e: e.memset(cx.ones[:], 1.0), writes=[cx.B("ones")])
        cx.psb = [cx.ps(f"psb{i}", [128, 512], F32) for i in range(8)]
        cx.sq = [cx.carve([128, 512], BF16) for i in range(2)]
        cx.r32 = [cx.carve([128, 512], F32) for i in range(2)]
        mtop = cx.mark()
        alloc_small(cx)
        phase_A(cx, T)
        cx.release(mtop)
        for k, (o_, n_) in enumerate(ch1):
            p.op("gpsimd", lambda e: e.collective_compute("AllGather", ALU.bypass, RG, [EX1[:, o_:o_ + n_].opt()], [GX1L[k].opt()]),
                 writes=[cx.B("gx1")])
        p.barrier()
        def ag2(k, rbufs):
            o_, n_ = ch2[k]
            p.op("gpsimd", lambda e: e.collective_compute("AllGather", ALU.bypass, RG, [EX2[:, o_:o_ + n_].opt()], [GX2L[k].opt()]),
                 reads=rbufs, writes=[cx.B("gx2", k)])
        T["ag2"] = ag2
        phase_B(cx, T)
        cx.release(mtop)
        p.barrier()
        alloc_small(cx)
        fin = phase_C(cx, T)
        stuck = p.check()
        assert not stuck, stuck
        p.build(final_waits=fin)
    return cx.nc


def kernel(**inputs):
    I = {k: np.asarray(v) for k, v in inputs.items()}
    cores = list(range(8))
    consts = l2_consts()
    w2 = np.asarray(I["nsa_cmp_w2"][0], np.float32)
    shared = dict(consts)
    shared["w1"] = np.ascontiguousarray(I["nsa_cmp_w1"][0], dtype=np.float32)
    shared["w2kD"] = np.ascontiguousarray(np.concatenate([w2[0], w2[0]], 1))
    shared["w2vD"] = np.ascontiguousarray(np.concatenate([w2[1], w2[1]], 1))
    shared["pe"] = np.ascontiguousarray(np.stack([pc(np.asarray(I["nsa_cmp_pos"][0, i]).reshape(-1)) for i in range(2)], 0))
    shared["vecs3"] = np.ascontiguousarray(np.concatenate([pc(I["mix_norm_post"][1]), pc(I["ffn_norm_pre"][1, 1]), pc(I["ffn_norm_post"][1, 1])], axis=1))
    shared["wout"] = np.ascontiguousarray(I["nsa_w_out"][0], dtype=np.float32)
    shared["wg3"] = np.ascontiguousarray(I["ffn_w_gate"][1, 1], dtype=np.float32)
    shared["wu3"] = np.ascontiguousarray(I["ffn_w_up"][1, 1], dtype=np.float32)
    shared["wd3"] = np.ascontiguousarray(I["ffn_w_down"][1, 1], dtype=np.float32)
    maps = []
    for c in cores:
        m = prep_launch1(I, c)
        m.update(shared)
        oh = np.zeros((128, 4), np.float32)
        oh[:, c % 4] = 1.0
        m["oh"] = oh
        maps.append(m)
    res = run_bass_kernel_spmd(build_fused(), maps, core_ids=cores).results
    out = np.zeros((2, S, D), np.float32)
    for c in cores:
        b, j = divmod(c, 4)
        out[b, j * TC:(j + 1) * TC] = np.asarray(res[c]["xf"]).T
    return out
```
